# Optimizing a Trainium2 kernel written in Bass

```python
import math
import jax, jax.numpy as jnp
from jax import lax
import numpy as np

D_MODEL = 4096
BATCH = 1
SEQ = 16384
DEPTH = 4

CHUNK = 64
N_META = 16
FRONT_PAD = (-N_META) % CHUNK
NORM_EPS = 1e-6
F32 = jnp.float32

SSD_HEADDIM = 64
SSD_WIDTH = D_MODEL
SSD_HEADS = SSD_WIDTH // SSD_HEADDIM
SSD_GROUPS = 8
SSD_STATE = 128
SSD_CONV = 4
SSD_CONV_DIM = SSD_WIDTH + 2 * SSD_GROUPS * SSD_STATE

S5_GROUP = 16
S5_WIDTH = D_MODEL // 2
S5_GROUPS = S5_WIDTH // S5_GROUP
S5_STATE = 64

HYB_IN = SSD_WIDTH + SSD_CONV_DIM + SSD_HEADS + 2 * S5_WIDTH
HYB_SPLITS = (SSD_WIDTH, SSD_WIDTH + SSD_CONV_DIM, SSD_WIDTH + SSD_CONV_DIM + SSD_HEADS,
              SSD_WIDTH + SSD_CONV_DIM + SSD_HEADS + S5_WIDTH)
HYB_OUT = SSD_WIDTH + S5_WIDTH

MLA_HEADS = D_MODEL // 128
MLA_Q_RANK = D_MODEL // 4
MLA_KV_RANK = D_MODEL // 8
MLA_NOPE = 128
MLA_ROPE = 64
MLA_V = 128
MLA_WIDTH = MLA_HEADS * MLA_V
MLA_IN = MLA_Q_RANK + MLA_KV_RANK + MLA_ROPE + MLA_WIDTH
MLA_SPLITS = (MLA_Q_RANK, MLA_Q_RANK + MLA_KV_RANK, MLA_Q_RANK + MLA_KV_RANK + MLA_ROPE)
ROPE_BASE = 10000.0
Q_BLOCK = 128

kernel_name = 'hybrid_ssd_s5_mla_meta_stream'


def rms_norm(x, g):
    xf = x.astype(F32)
    y = xf * lax.rsqrt(jnp.mean(xf * xf, axis=-1, keepdims=True) + NORM_EPS)
    return (y * g.astype(F32)).astype(x.dtype)


def chunk_ids(length):
    p = jnp.arange(length)
    return jnp.where(p < N_META, 0, 1 + (p - N_META) // CHUNK)


def causal_conv(x, w, b):
    y = lax.conv_general_dilated(x, w[:, None, :].astype(x.dtype), window_strides=(1,),
                                 padding=[(SSD_CONV - 1, 0)],
                                 dimension_numbers=('NWC', 'WIO', 'NWC'),
                                 feature_group_count=x.shape[-1])
    return y + b.astype(x.dtype)


def ssd_scan(xs, dt, a, bm, cm):
    bsz = xs.shape[0]
    f = FRONT_PAD
    padf = lambda t: jnp.pad(t, [(0, 0), (f, 0)] + [(0, 0)] * (t.ndim - 2))
    xs, dt, bm, cm = padf(xs), padf(dt), padf(bm), padf(cm)
    nc = xs.shape[1] // CHUNK
    e = SSD_HEADS // SSD_GROUPS
    x = (xs * dt[..., None]).reshape(bsz, nc, CHUNK, SSD_GROUPS, e, SSD_HEADDIM)
    da = (dt * a).reshape(bsz, nc, CHUNK, SSD_GROUPS, e)
    bm = bm.reshape(bsz, nc, CHUNK, SSD_GROUPS, SSD_STATE)
    cm = cm.reshape(bsz, nc, CHUNK, SSD_GROUPS, SSD_STATE)
    acum = jnp.cumsum(da, axis=2)
    seg = acum[:, :, :, None] - acum[:, :, None, :]
    causal = jnp.tril(jnp.ones((CHUNK, CHUNK), bool))[:, :, None, None]
    decay = jnp.exp(jnp.where(causal, seg, -jnp.inf))
    cb = jnp.einsum('bclgn,bcsgn->bclsg', cm, bm)
    y_diag = jnp.einsum('bclsg,bclsge,bcsgep->bclgep', cb, decay, x)
    decay_to_end = jnp.exp(acum[:, :, -1:] - acum)
    states = jnp.einsum('bclgn,bclge,bclgep->bcgepn', bm, decay_to_end, x)
    chunk_decay = jnp.exp(acum[:, :, -1])

    def step(h, inp):
        s, d = inp
        return h * d[..., None, None] + s, h

    _, prev = lax.scan(step, jnp.zeros_like(states[:, 0]),
                       (jnp.moveaxis(states, 1, 0), jnp.moveaxis(chunk_decay, 1, 0)))
    prev = jnp.moveaxis(prev, 0, 1)
    y_off = jnp.einsum('bclgn,bcgepn,bclge->bclgep', cm, prev, jnp.exp(acum))
    y = (y_diag + y_off).reshape(bsz, nc * CHUNK, SSD_HEADS, SSD_HEADDIM)
    return y[:, f:]


def ssd_branch(z, xbc, dt_raw, conv_w, conv_b, dt_bias, a_log, d_skip, norm_g):
    bsz, length, _ = z.shape
    xbc = jax.nn.silu(causal_conv(xbc, conv_w, conv_b)).astype(F32)
    xs, bm, cm = jnp.split(xbc, [SSD_WIDTH, SSD_WIDTH + SSD_GROUPS * SSD_STATE], axis=-1)
    xs = xs.reshape(bsz, length, SSD_HEADS, SSD_HEADDIM)
    bm = bm.reshape(bsz, length, SSD_GROUPS, SSD_STATE)
    cm = cm.reshape(bsz, length, SSD_GROUPS, SSD_STATE)
    dt = jax.nn.softplus(dt_raw.astype(F32) + dt_bias.astype(F32))
    a = -jnp.exp(a_log.astype(F32))
    y = ssd_scan(xs, dt, a, bm, cm) + d_skip.astype(F32)[:, None] * xs
    y = y.reshape(bsz, length, SSD_WIDTH) * jax.nn.silu(z.astype(F32))
    yg = y.reshape(bsz, length, SSD_GROUPS, -1)
    yg = yg * lax.rsqrt(jnp.mean(yg * yg, axis=-1, keepdims=True) + NORM_EPS)
    return (yg.reshape(bsz, length, SSD_WIDTH) * norm_g.astype(F32)).astype(z.dtype)


def s5_branch(u, gate, a_re, a_im, log_dt, b_re, b_im, c_re, c_im, d_skip, w_glu):
    bsz, length, _ = u.shape
    a_re, a_im = a_re.astype(F32), a_im.astype(F32)
    b_re, b_im = b_re.astype(F32), b_im.astype(F32)
    c_re, c_im = c_re.astype(F32), c_im.astype(F32)
    uf = u.astype(F32)
    ug = uf.reshape(bsz, length, S5_GROUPS, S5_GROUP)
    dt = jnp.exp(log_dt.astype(F32))[:, None]
    mag = jnp.exp(dt * a_re)
    ab_re, ab_im = mag * jnp.cos(dt * a_im), mag * jnp.sin(dt * a_im)
    den = a_re * a_re + a_im * a_im
    k_re = ((ab_re - 1.0) * a_re + ab_im * a_im) / den
    k_im = (ab_im * a_re - (ab_re - 1.0) * a_im) / den
    bb_re = k_re[..., None] * b_re - k_im[..., None] * b_im
    bb_im = k_re[..., None] * b_im + k_im[..., None] * b_re
    bu_re = jnp.einsum('gpi,blgi->blgp', bb_re, ug)
    bu_im = jnp.einsum('gpi,blgi->blgp', bb_im, ug)
    shp = bu_re.shape

    def combine(e1, e2):
        a1r, a1i, b1r, b1i = e1
        a2r, a2i, b2r, b2i = e2
        return (a2r * a1r - a2i * a1i, a2r * a1i + a2i * a1r,
                a2r * b1r - a2i * b1i + b2r, a2r * b1i + a2i * b1r + b2i)

    _, _, s_re, s_im = lax.associative_scan(
        combine, (jnp.broadcast_to(ab_re, shp), jnp.broadcast_to(ab_im, shp), bu_re, bu_im), axis=1)
    y = jnp.einsum('gip,blgp->blgi', c_re, s_re) - jnp.einsum('gip,blgp->blgi', c_im, s_im)
    y = y.reshape(bsz, length, S5_WIDTH) + d_skip.astype(F32) * uf
    g = jax.nn.gelu(y)
    y = g * jax.nn.sigmoid(g @ w_glu.astype(F32))
    return (y * jax.nn.silu(gate.astype(F32))).astype(u.dtype)


def rope(x, cos, sin):
    x1, x2 = jnp.split(x, 2, axis=-1)
    return jnp.concatenate([x1 * cos - x2 * sin, x2 * cos + x1 * sin], axis=-1)


def chunk_causal_attention(q_nope, q_rope, k_nope, k_rope, v):
    bsz, length = q_nope.shape[:2]
    nblk = -(-length // Q_BLOCK)
    pad = nblk * Q_BLOCK - length
    cid = chunk_ids(length)
    cid_q = jnp.pad(cid, (0, pad), constant_values=0).reshape(nblk, Q_BLOCK)

    def to_blocks(t):
        t = jnp.pad(t, [(0, 0), (0, pad)] + [(0, 0)] * (t.ndim - 2))
        return jnp.moveaxis(t.reshape(bsz, nblk, Q_BLOCK, *t.shape[2:]), 1, 0)

    scale = (MLA_NOPE + MLA_ROPE) ** -0.5

    def one_block(args):
        qn, qr, cq = args
        s = jnp.einsum('bqhd,bkhd->bhqk', qn, k_nope) + jnp.einsum('bqhr,bkr->bhqk', qr, k_rope)
        s = jnp.where(cid[None, :] <= cq[:, None], s.astype(F32) * scale, -jnp.inf)
        p = jax.nn.softmax(s, axis=-1).astype(v.dtype)
        return jnp.einsum('bhqk,bkhd->bqhd', p, v)

    o = lax.map(one_block, (to_blocks(q_nope), to_blocks(q_rope), cid_q))
    o = jnp.moveaxis(o, 0, 1).reshape(bsz, nblk * Q_BLOCK, MLA_HEADS, MLA_V)
    return o[:, :length]


def mla_branch(h, w_in, q_norm, w_uq, kv_norm, w_ukv):
    bsz, length, _ = h.shape
    c_q, c_kv, k_rope, gate = jnp.split(h @ w_in, MLA_SPLITS, axis=-1)
    q = (rms_norm(c_q, q_norm) @ w_uq).reshape(bsz, length, MLA_HEADS, MLA_NOPE + MLA_ROPE)
    kv = (rms_norm(c_kv, kv_norm) @ w_ukv).reshape(bsz, length, MLA_HEADS, MLA_NOPE + MLA_V)
    q_nope, q_rope = q[..., :MLA_NOPE], q[..., MLA_NOPE:]
    k_nope, v = kv[..., :MLA_NOPE], kv[..., MLA_NOPE:]
    pos = jnp.arange(length, dtype=F32)
    inv_freq = ROPE_BASE ** (-jnp.arange(0, MLA_ROPE, 2, dtype=F32) / MLA_ROPE)
    ang = pos[:, None] * inv_freq[None, :]
    cos, sin = jnp.cos(ang).astype(h.dtype), jnp.sin(ang).astype(h.dtype)
    q_rope = rope(q_rope, cos[:, None, :], sin[:, None, :])
    k_rope = rope(k_rope, cos, sin)
    o = chunk_causal_attention(q_nope, q_rope, k_nope, k_rope, v)
    return o.reshape(bsz, length, MLA_WIDTH) * jax.nn.silu(gate)


def setup_inputs(seed: int = 0) -> dict:
    key = jax.random.key(seed)
    ks = iter(jax.random.split(key, 40))
    ne, no = (DEPTH + 1) // 2, DEPTH // 2

    def nrm(shape, scale):
        return jax.random.normal(next(ks), shape, F32) * scale

    def gain(shape):
        return 1.0 + nrm(shape, 0.02)

    def log_uniform(shape, lo, hi):
        return jax.random.uniform(next(ks), shape, F32, math.log(lo), math.log(hi))

    ssd_dt = jnp.exp(log_uniform((ne, SSD_HEADS), 1e-3, 1e-1))
    n_idx = jnp.arange(S5_STATE, dtype=F32)
    return {
        'x': nrm((BATCH, SEQ, D_MODEL), 1.0),
        'meta': nrm((N_META, D_MODEL), 1.0),
        'hyb_norm': gain((ne, D_MODEL)),
        'hyb_w_in': nrm((ne, D_MODEL, HYB_IN), D_MODEL ** -0.5),
        'ssd_conv_w': jax.random.uniform(next(ks), (ne, SSD_CONV, SSD_CONV_DIM), F32, -0.5, 0.5),
        'ssd_conv_b': nrm((ne, SSD_CONV_DIM), 0.02),
        'ssd_dt_bias': ssd_dt + jnp.log(-jnp.expm1(-ssd_dt)),
        'ssd_a_log': jnp.log(jax.random.uniform(next(ks), (ne, SSD_HEADS), F32, 1.0, 16.0)),
        'ssd_d': 1.0 + nrm((ne, SSD_HEADS), 0.1),
        'ssd_norm': gain((ne, SSD_WIDTH)),
        's5_a_re': -0.5 * jnp.exp(nrm((ne, S5_GROUPS, S5_STATE), 0.02)),
        's5_a_im': math.pi * n_idx + nrm((ne, S5_GROUPS, S5_STATE), 0.01),
        's5_log_dt': log_uniform((ne, S5_GROUPS), 1e-3, 1e-1),
        's5_b_re': nrm((ne, S5_GROUPS, S5_STATE, S5_GROUP), (2 * S5_GROUP) ** -0.5),
        's5_b_im': nrm((ne, S5_GROUPS, S5_STATE, S5_GROUP), (2 * S5_GROUP) ** -0.5),
        's5_c_re': nrm((ne, S5_GROUPS, S5_GROUP, S5_STATE), S5_STATE ** -0.5),
        's5_c_im': nrm((ne, S5_GROUPS, S5_GROUP, S5_STATE), S5_STATE ** -0.5),
        's5_d': nrm((ne, S5_WIDTH), 1.0),
        's5_w_glu': nrm((ne, S5_WIDTH, S5_WIDTH), S5_WIDTH ** -0.5),
        'hyb_w_out': nrm((ne, HYB_OUT, D_MODEL), HYB_OUT ** -0.5),
        'mla_norm': gain((no, D_MODEL)),
        'mla_w_in': nrm((no, D_MODEL, MLA_IN), D_MODEL ** -0.5),
        'mla_q_norm': gain((no, MLA_Q_RANK)),
        'mla_w_uq': nrm((no, MLA_Q_RANK, MLA_HEADS * (MLA_NOPE + MLA_ROPE)), MLA_Q_RANK ** -0.5),
        'mla_kv_norm': gain((no, MLA_KV_RANK)),
        'mla_w_ukv': nrm((no, MLA_KV_RANK, MLA_HEADS * (MLA_NOPE + MLA_V)), MLA_KV_RANK ** -0.5),
        'mla_w_out': nrm((no, MLA_WIDTH, D_MODEL), MLA_WIDTH ** -0.5),
        'final_norm': gain((D_MODEL,)),
    }


def reference(x, meta, hyb_norm, hyb_w_in, ssd_conv_w, ssd_conv_b, ssd_dt_bias, ssd_a_log, ssd_d,
              ssd_norm, s5_a_re, s5_a_im, s5_log_dt, s5_b_re, s5_b_im, s5_c_re, s5_c_im, s5_d,
              s5_w_glu, hyb_w_out, mla_norm, mla_w_in, mla_q_norm, mla_w_uq, mla_kv_norm, mla_w_ukv,
              mla_w_out, final_norm):
    bsz = x.shape[0]
    h = jnp.concatenate([jnp.broadcast_to(meta.astype(x.dtype)[None], (bsz, N_META, D_MODEL)), x], axis=1)
    for layer in range(DEPTH):
        i = layer // 2
        if layer % 2 == 0:
            hn = rms_norm(h, hyb_norm[i])
            z, xbc, dt_raw, u, gate = jnp.split(hn @ hyb_w_in[i], HYB_SPLITS, axis=-1)
            y_a = ssd_branch(z, xbc, dt_raw, ssd_conv_w[i], ssd_conv_b[i], ssd_dt_bias[i],
                             ssd_a_log[i], ssd_d[i], ssd_norm[i])
            y_b = s5_branch(u, gate, s5_a_re[i], s5_a_im[i], s5_log_dt[i], s5_b_re[i], s5_b_im[i],
                            s5_c_re[i], s5_c_im[i], s5_d[i], s5_w_glu[i])
            h = h + jnp.concatenate([y_a, y_b], axis=-1) @ hyb_w_out[i]
        else:
            hn = rms_norm(h, mla_norm[i])
            y_c = mla_branch(hn, mla_w_in[i], mla_q_norm[i], mla_w_uq[i], mla_kv_norm[i], mla_w_ukv[i])
            h = h + y_c @ mla_w_out[i]
    return rms_norm(h[:, N_META:], final_norm)
```

```python
import contextlib
import math
import os
import numpy as np
import ml_dtypes


import concourse.bass as bass
import concourse.mybir as mybir

F32 = mybir.dt.float32
BF16 = mybir.dt.bfloat16
I32 = mybir.dt.int32
AF = mybir.ActivationFunctionType
ALU = mybir.AluOpType
AX = mybir.AxisListType

COMPUTE = ("pe", "act", "dve", "pool")


class View:
    __slots__ = ("tl", "ap")

    def __init__(self, tl, ap):
        self.tl = tl
        self.ap = ap

    def __getitem__(self, idx):
        return View(self.tl, self.ap[idx])

    def rearrange(self, pat, **kw):
        return View(self.tl, self.ap.rearrange(pat, **kw))

    def broadcast_to(self, shape):
        return View(self.tl, self.ap.broadcast_to(list(shape)))

    def unsqueeze(self, ax):
        return View(self.tl, self.ap.unsqueeze(ax))

    def partition_broadcast(self, n):
        return View(self.tl, self.ap.partition_broadcast(n))

    def bitcast(self, dt):
        return View(self.tl, self.ap.bitcast(dt))

    @property
    def shape(self):
        return self.ap.shape


class Tl:
    __slots__ = ("t", "name", "lw", "rd", "dsem", "dcnt", "is_dram", "is_psum")

    def __init__(self, t, name, is_dram=False, is_psum=False):
        self.is_psum = is_psum
        self.t = t
        self.name = name
        self.lw = {}
        self.rd = {}
        self.dsem = None
        self.dcnt = 0
        self.is_dram = is_dram

    def __getitem__(self, idx):
        return View(self, self.t[idx])

    def rearrange(self, pat, **kw):
        return View(self, self.t.rearrange(pat, **kw))

    @property
    def v(self):
        return View(self, self.t[:])


def _is_view(x):
    return isinstance(x, View)


class KB:
    def __init__(self, nc, stack):
        self.nc = nc
        self.stack = stack
        self.root = stack
        self.lists = {e: [] for e in ("pe", "act", "dve", "pool", "sp")}
        self.psem = {}
        self.pcnt = {}
        for e in COMPUTE:
            self.psem[e] = stack.enter_context(nc.semaphore("prog_" + e))
            self.pcnt[e] = 0
        self.known = {e: {} for e in self.lists}
        self.cinst = {e: [] for e in COMPUTE}
        self.ntile = 0
        self.tiles = []
        self.sem_pool = []
        self.n_sem = 4
        self.n_inst = 0
        self.n_wait = 0

    def sb(self, shape, dt, name=None):
        self.ntile += 1
        name = name or f"t{self.ntile}"
        t = self.stack.enter_context(self.nc.sbuf_tensor(name, list(shape), dt))
        tl = Tl(t, name)
        self.tiles.append(tl)
        return tl

    def ps(self, shape, dt, name=None):
        self.ntile += 1
        name = name or f"p{self.ntile}"
        t = self.stack.enter_context(self.nc.psum_tensor(name, list(shape), dt))
        return Tl(t, name, is_psum=True)

    def dram(self, name, shape, dt, kind="Internal", **kw):
        t = self.nc.dram_tensor(name, list(shape), dt, kind=kind, **kw)
        tl = Tl(t.ap(), name, is_dram=True)
        self.tiles.append(tl)
        return tl

    def _need(self, eng, ev, waits):
        if ev is None:
            return
        if ev[0] == "c":
            _, src, idx = ev
            if src == "pe" and eng == "pe":
                return
            key = ("c", src)
        else:
            _, sem, idx = ev
            key = id(sem)
        kn = self.known[eng]
        if kn.get(key, 0) >= idx:
            return
        kn[key] = idx
        if ev[0] == "c":
            self.cinst[src][idx - 1][4] = True
        waits[key] = ev

    def _deps(self, eng, reads, writes):
        waits = {}
        for t in reads:
            for ev in t.lw.values():
                self._need(eng, ev, waits)
            if t.is_psum:
                for ev in t.rd.values():
                    if not (ev[0] == "c" and ev[1] == eng):
                        self._need(eng, ev, waits)
        for t in writes:
            for ev in t.lw.values():
                self._need(eng, ev, waits)
            for ev in t.rd.values():
                self._need(eng, ev, waits)
        return list(waits.values())

    @staticmethod
    def _evkey(ev):
        return ("c", ev[1]) if ev[0] == "c" else id(ev[1])

    def _record(self, ev, reads, writes):
        key = self._evkey(ev)
        for t in reads:
            t.rd[key] = ev
        for t in writes:
            if t.is_dram and ev[0] == "d":
                t.lw[key] = ev
            else:
                t.lw = {key: ev}
            t.rd = {}

    def op(self, eng, fn, reads=(), writes=(), inc=True):
        reads = [r.tl if _is_view(r) else r for r in reads]
        writes = [w.tl if _is_view(w) else w for w in writes]
        waits = self._deps(eng, reads, writes)
        ent = [waits, fn, "c", eng, False]
        self.cinst[eng].append(ent)
        ev = ("c", eng, len(self.cinst[eng]))
        self.lists[eng].append(ent)
        self._record(ev, reads, writes)
        self.n_inst += 1
        self.n_wait += len(waits)

    def dma(self, q, out, in_, sem_tile=None, **kw):
        reads = [in_.tl]
        writes = [out.tl]
        waits = self._deps(q, reads, writes)
        st = sem_tile or (in_.tl if out.tl.is_dram and not in_.tl.is_dram else out.tl)
        if st.dsem is None:
            st.dsem, st.dcnt = self.get_sem("d_" + st.name)
        st.dcnt += 16
        ev = ("d", st.dsem, st.dcnt)
        oap, iap = out.ap, in_.ap
        self.lists[q].append([waits, lambda e: e.dma_start(out=oap, in_=iap, **kw), "d", st.dsem, True])
        self._record(ev, reads, writes)
        self.n_inst += 1
        self.n_wait += len(waits)

    def collective(self, kind, out, in_, op=None):
        reads = [in_.tl]
        writes = [out.tl]
        waits = self._deps("pool", reads, writes)
        st = out.tl
        if st.dsem is None:
            st.dsem, st.dcnt = self.get_sem("c_" + st.name)
        st.dcnt += 16
        ev = ("d", st.dsem, st.dcnt)
        oap, iap = out.ap, in_.ap
        aop = op if op is not None else ALU.bypass
        groups = [list(range(8))]
        self.lists["pool"].append([waits, lambda e: e.collective_compute(kind, aop, replica_groups=groups, ins=[iap], outs=[oap]), "d", st.dsem, True])
        self._record(ev, reads, writes)
        self.n_inst += 1

    def get_sem(self, name):
        if self.sem_pool:
            return self.sem_pool.pop()
        self.n_sem += 1
        return self.root.enter_context(self.nc.semaphore(name)), 0

    @contextlib.contextmanager
    def scope(self):
        old_stack, old_tiles = self.stack, self.tiles
        with contextlib.ExitStack() as st:
            self.stack = st
            self.tiles = []
            yield
            self.tiles = old_tiles + self.tiles
            self.barrier()
            new = self.tiles[len(old_tiles):]
            self.release([t for t in new if not t.is_dram])
            self.tiles = old_tiles + [t for t in new if t.is_dram]
            self.stack = old_stack

    def release(self, tiles):
        for t in tiles:
            if t.dsem is not None:
                self.sem_pool.append((t.dsem, t.dcnt))
                t.dsem = None

    def barrier(self):
        evs = [("c", e, len(self.cinst[e])) for e in COMPUTE if self.cinst[e]]
        evs += [("d", s, c) for (s, c) in self.all_dsems() if c > 0]
        for eng in self.lists:
            waits = {}
            for ev in evs:
                if ev[0] == "c" and ev[1] == eng:
                    continue
                saved = None
                if ev[0] == "c" and ev[1] == "pe" and eng == "pe":
                    continue
                self._need(eng, ev, waits)
            if waits:
                self.lists[eng].append([list(waits.values()), None, None, None, False])

    def all_dsems(self):
        out = [(t.dsem, t.dcnt) for t in self.tiles if t.dsem is not None]
        out += list(self.sem_pool)
        return out

    def final_wait(self, eng, tiles):
        waits = {}
        for t in tiles:
            for ev in t.lw.values():
                self._need(eng, ev, waits)
        self.lists[eng].append([list(waits.values()), None, None, None, False])

    def matmul(self, out, lhsT, rhs, start=True, stop=True):
        o, l, r = out.ap, lhsT.ap, rhs.ap
        self.op("pe", lambda e: e.matmul(o, lhsT=l, rhs=r, start=start, stop=stop), [lhsT, rhs], [out], inc=bool(stop))

    def transpose(self, out, in_, ident):
        o, i, d = out.ap, in_.ap, ident.ap
        self.op("pe", lambda e: e.transpose(o, i, d), [in_, ident], [out])

    def act(self, out, in_, func, bias=None, scale=None, accum_out=None):
        o, i = out.ap, in_.ap
        kw = {}
        rd = [in_]
        wr = [out]
        if bias is not None:
            if _is_view(bias):
                rd.append(bias)
                kw["bias"] = bias.ap
            else:
                kw["bias"] = bias
        if scale is not None:
            if _is_view(scale):
                rd.append(scale)
                kw["scale"] = scale.ap
            else:
                kw["scale"] = scale
        if accum_out is not None:
            wr.append(accum_out)
            kw["accum_out"] = accum_out.ap
        self.op("act", lambda e: e.activation(out=o, in_=i, func=func, **kw), rd, wr)

    def tt(self, out, in0, in1, op, eng="dve"):
        o, a, b = out.ap, in0.ap, in1.ap
        self.op(eng, lambda e: e.tensor_tensor(out=o, in0=a, in1=b, op=op), [in0, in1], [out])

    def ts(self, out, in0, s1, op0, s2=None, op1=None, eng="dve", accum_out=None):
        o, a = out.ap, in0.ap
        rd = [in0]
        wr = [out]
        a1 = s1
        a2 = s2
        if _is_view(s1):
            rd.append(s1)
            a1 = s1.ap
        if _is_view(s2):
            rd.append(s2)
            a2 = s2.ap
        kw = {}
        if op1 is not None:
            kw["op1"] = op1
        if accum_out is not None:
            wr.append(accum_out)
            kw["accum_out"] = accum_out.ap
        self.op(eng, lambda e: e.tensor_scalar(out=o, in0=a, scalar1=a1, scalar2=a2, op0=op0, **kw), rd, wr)

    def stt(self, out, in0, scalar, in1, op0, op1):
        o, a, b = out.ap, in0.ap, in1.ap
        rd = [in0, in1]
        s = scalar
        if _is_view(scalar):
            rd.append(scalar)
            s = scalar.ap
        self.op("dve", lambda e: e.scalar_tensor_tensor(out=o, in0=a, scalar=s, in1=b, op0=op0, op1=op1), rd, [out])

    def copy(self, out, in_, eng="dve"):
        o, i = out.ap, in_.ap
        if eng == "act":
            self.op("act", lambda e: e.copy(out=o, in_=i), [in_], [out])
        else:
            self.op(eng, lambda e: e.tensor_copy(out=o, in_=i), [in_], [out])

    def memset(self, out, val, eng="pool"):
        o = out.ap
        self.op(eng, lambda e: e.memset(o, val), [], [out])

    def recip(self, out, in_):
        o, i = out.ap, in_.ap
        self.op("dve", lambda e: e.reciprocal(out=o, in_=i), [in_], [out])

    def scan(self, out, d0, d1, initial, op0=ALU.mult, op1=ALU.add):
        o, a, b = out.ap, d0.ap, d1.ap
        rd = [d0, d1]
        ini = initial
        if _is_view(initial):
            rd.append(initial)
            ini = initial.ap
        self.op("dve", lambda e: e.tensor_tensor_scan(out=o, data0=a, data1=b, initial=ini, op0=op0, op1=op1), rd, [out])

    def reduce(self, out, in_, op, axis=AX.X):
        o, i = out.ap, in_.ap
        self.op("dve", lambda e: e.tensor_reduce(out=o, in_=i, axis=axis, op=op), [in_], [out])

    def emit(self):
        nc = self.nc
        lists = self.lists
        cum = {}
        for eng in COMPUTE:
            c = 0
            arr = []
            for ent in self.cinst[eng]:
                if ent[4]:
                    c += 1
                arr.append(c)
            cum[eng] = arr
        self.n_marked = {e: (cum[e][-1] if cum[e] else 0) for e in COMPUTE}
        psem = self.psem

        def run(e, items):
            for ent in items:
                waits, fn, kind, who, mark = ent
                for ev in waits:
                    if ev[0] == "c":
                        e.wait_ge(psem[ev[1]], cum[ev[1]][ev[2] - 1])
                    else:
                        e.wait_ge(ev[1], ev[2])
                if fn is None:
                    continue
                if kind == "d":
                    fn(e).then_inc(who, 16)
                elif mark:
                    fn(e).then_inc(psem[who], 1)
                else:
                    fn(e)

        with nc.Block() as block:
            @block.tensor
            def _(e):
                run(e, lists["pe"])

            @block.scalar
            def _(e):
                run(e, lists["act"])

            @block.vector
            def _(e):
                run(e, lists["dve"])

            @block.gpsimd
            def _(e):
                run(e, lists["pool"])

            @block.sync
            def _(e):
                run(e, lists["sp"])


EPS = 1e-6
D = 4096
NDT = 32


def col_groups(Tc, gmax=1024):
    groups = []
    s = 0
    while s < Tc:
        e = min(s + gmax, Tc)
        if 0 < Tc - e < 64:
            e = Tc
        groups.append((s, e))
        s = e
    return groups


def stage_out(k, hT, yT, wl, g_l, hT_new, hnT, Tc, nkt, first, out_dt, pfx="o", glu=None):
    ones = k.sb([128, 128], F32, pfx + "ones")
    k.memset(ones.v, 1.0)
    gt = k.sb([128, NDT], F32, pfx + "g")
    k.dma("sp", gt.v, g_l.v)
    GW = 528 if glu is not None else 1040
    GMAX = 512 if glu is not None else 1024
    if not first:
        yb = k.sb([128, nkt, GW], BF16, pfx + "yb")
        KH = nkt // 2
        wst = [k.sb([128, KH * 128], F32, pfx + f"wst{i}") for i in range(2)]
        wbf = [k.sb([128, nkt * 128], BF16, pfx + f"wbf{i}") for i in range(2)]
        acc = [k.ps([128, 512], F32, pfx + f"acc{i}") for i in range(4)]
        yTv = yT.rearrange("(kt p) t -> p kt t", p=128)
    ssq = [k.ps([128, 512], F32, pfx + f"ssq{i}") for i in range(3)]
    hin = [k.sb([128, 512], F32, pfx + f"hin{i}") for i in range(3)]
    hnw = [k.sb([128, 512], F32, pfx + f"hnw{i}") for i in range(3)]
    sq = [k.sb([128, 512], F32, pfx + f"sq{i}") for i in range(2)]
    rstd = k.sb([128, GW], F32, pfx + "rstd")
    hno = [k.sb([128, 512], out_dt, pfx + f"hno{i}") for i in range(3)]
    hsrc = hT if first else hT_new
    u = 0
    wcnt = 0
    if glu is not None:
        g_all, sg_all, wglu_l = glu
        gb = k.sb([128, 16, GW], BF16, pfx + "gb")
        sgb = k.sb([128, 16, GW], BF16, pfx + "sgb")
        gst = [k.sb([128, 2048], F32, pfx + f"gst{i}") for i in range(2)]
        gwb = [k.sb([128, 2048], BF16, pfx + f"gwb{i}") for i in range(2)]
        sig = [k.sb([128, 512], F32, pfx + f"sig{i}") for i in range(2)]
        gv = g_all.rearrange("(kt p) t -> p kt t", p=128)
        sgv = sg_all.rearrange("(kt p) t -> p kt t", p=128)
        gcnt = 0
    for (c0, c1) in col_groups(Tc, GMAX):
        gw = c1 - c0
        chunks = [(s, min(s + 512, c1)) for s in range(c0, c1, 512)]
        assert len(chunks) <= 3 and gw <= GW
        if not first:
            nkt_y = nkt - 16 if glu is not None else nkt
            for kt0 in range(0, nkt_y, 8):
                k.dma("sp", yb[:, kt0:kt0 + 8, 0:gw], yTv[:, kt0:kt0 + 8, c0:c1])
        if glu is not None:
            for kt0 in range(0, 16, 8):
                k.dma("sp", gb[:, kt0:kt0 + 8, 0:gw], gv[:, kt0:kt0 + 8, c0:c1])
                k.dma("sp", sgb[:, kt0:kt0 + 8, 0:gw], sgv[:, kt0:kt0 + 8, c0:c1])
            for mt in range(16):
                gs, gw_ = gst[gcnt % 2], gwb[gcnt % 2]
                k.dma("act", gs.v, wglu_l[mt])
                k.copy(gw_.v, gs.v, eng="pool")
                gcnt += 1
                for ci, (s0, s1) in enumerate(chunks):
                    n = s1 - s0
                    ac = acc[(mt * 2 + ci) % 4]
                    for kt in range(16):
                        k.matmul(ac[:, 0:n], gw_[:, kt * 128:(kt + 1) * 128], gb[:, kt, s0 - c0:s1 - c0], start=(kt == 0), stop=(kt == 15))
                    sg_ = sig[(mt * 2 + ci) % 2]
                    k.act(sg_[:, 0:n], ac[:, 0:n], AF.Sigmoid)
                    k.tt(sg_[:, 0:n], sg_[:, 0:n], gb[:, mt, s0 - c0:s1 - c0], ALU.mult)
                    k.tt(yb[:, 32 + mt, s0 - c0:s1 - c0], sg_[:, 0:n], sgb[:, mt, s0 - c0:s1 - c0], ALU.mult, eng="pool")
        for d in range(NDT):
            if not first:
                wb = wbf[wcnt % 2]
                for hh in range(2):
                    ws = wst[(2 * wcnt + hh) % 2]
                    k.dma("act", ws.v, wl[d, :, hh * KH * 128:(hh + 1) * KH * 128])
                    k.copy(wb[:, hh * KH * 128:(hh + 1) * KH * 128], ws.v, eng="pool")
                wcnt += 1
            for ci, (s0, s1) in enumerate(chunks):
                n = s1 - s0
                hi = hin[u % 3]
                hw = hnw[u % 3]
                sqt = sq[u % 2]
                k.dma("sp", hi[:, 0:n], hT[d * 128:(d + 1) * 128, s0:s1])
                if not first:
                    ac = acc[u % 4]
                    for kt in range(nkt):
                        k.matmul(ac[:, 0:n], wb[:, kt * 128:(kt + 1) * 128], yb[:, kt, s0 - c0:s1 - c0],
                                 start=(kt == 0), stop=(kt == nkt - 1))
                    k.tt(hw[:, 0:n], ac[:, 0:n], hi[:, 0:n], ALU.add)
                    k.dma("sp", hT_new[d * 128:(d + 1) * 128, s0:s1], hw[:, 0:n])
                    src = hw
                else:
                    src = hi
                k.act(sqt[:, 0:n], src[:, 0:n], AF.Square)
                k.matmul(ssq[ci][:, 0:n], ones.v, sqt[:, 0:n], start=(d == 0), stop=(d == NDT - 1))
                u += 1
        for ci, (s0, s1) in enumerate(chunks):
            n = s1 - s0
            k.ts(rstd[:, s0 - c0:s1 - c0], ssq[ci][:, 0:n], 1.0 / D, ALU.mult, EPS, ALU.add)
            k.act(rstd[:, s0 - c0:s1 - c0], rstd[:, s0 - c0:s1 - c0], AF.Sqrt)
            k.recip(rstd[:, s0 - c0:s1 - c0], rstd[:, s0 - c0:s1 - c0])
        for d in range(NDT):
            for ci, (s0, s1) in enumerate(chunks):
                n = s1 - s0
                hi = hin[u % 3]
                ho = hno[u % 3]
                k.dma("sp", hi[:, 0:n], hsrc[d * 128:(d + 1) * 128, s0:s1])
                k.stt(ho[:, 0:n], hi[:, 0:n], gt[:, d:d + 1], rstd[:, s0 - c0:s1 - c0], ALU.mult, ALU.mult)
                k.dma("act", hnT[d * 128:(d + 1) * 128, s0:s1], ho[:, 0:n])
                u += 1

DBG_NB = int(os.environ.get('DBG_NB', '0'))
DBG_SKIP = os.environ.get('DBG_SKIP', '')
DBG_START = int(os.environ.get('DBG_START', '0'))

EPS = 1e-6
NH = 4
QSCALE = 192 ** -0.5


def tok_blocks(T, bs=512):
    assert (T - 16) % bs == 0
    return [(0, 16)] + [(s, s + bs) for s in range(16, T, bs)]


def load_cast(k, dst, src_dram, nkt, ncols, stg, q="act", ceng="pool"):
    if 'lc' in DBG_SKIP:
        k.memset(dst.v, 0.01)
        return
    for kt in range(nkt):
        s = stg[kt % len(stg)]
        k.dma(q, s[:, 0:ncols], src_dram[:, kt, :])
        k.copy(dst[:, kt, :], s[:, 0:ncols], eng=ceng)


def rstd_from_ssq(k, rstd, ssq, n, dim):
    k.ts(rstd[:, 0:n], ssq[:, 0:n], 1.0 / dim, ALU.mult, EPS, ALU.add)
    k.act(rstd[:, 0:n], rstd[:, 0:n], AF.Sqrt)
    k.recip(rstd[:, 0:n], rstd[:, 0:n])


BS_A = 256


def _a_common(k, pfx):
    ones = k.sb([128, 128], BF16, pfx + "ones")
    k.memset(ones.v, 1.0)
    stg = [k.sb([128, 1152], F32, pfx + f"stg{i}") for i in range(2)]
    hb = [k.sb([128, 32, BS_A], BF16, pfx + f"hb{i}") for i in range(2)]
    if os.environ.get("HB1"): hb = [hb[0], hb[0]]
    cst = [k.sb([128, BS_A], F32, pfx + f"cos{i}") for i in range(2)]
    snt = [k.sb([128, BS_A], F32, pfx + f"sin{i}") for i in range(2)]
    rstd = k.sb([128, BS_A], F32, pfx + "rstd")
    acc = [k.ps([128, 512], F32, pfx + f"acc{i}") for i in range(3)]
    ssq = k.ps([128, 512], F32, pfx + "ssq")
    up = [k.ps([128, 512], F32, pfx + f"up{i}") for i in range(3)]
    sqb = [k.sb([128, BS_A], BF16, pfx + f"sqb{i}") for i in range(2)]
    ra = [k.sb([128, BS_A], F32, pfx + f"ra{i}") for i in range(2)]
    rb = [k.sb([128, BS_A], F32, pfx + f"rb{i}") for i in range(2)]
    ob = [k.sb([128, 512], BF16, pfx + f"ob{i}") for i in range(4)]
    return ones, stg, hb, cst, snt, rstd, acc, ssq, up, sqb, ra, rb, ob


def mla_a1(k, T, hn_meta, hn_own, wq_in, wuq, gq, cos2, sin2s, qnT, qrT, pfx="m1"):
    blocks = tok_blocks(T, BS_A)
    hbm = k.sb([128, 32, 16], BF16, pfx + "hbm")
    ones, stg, hb, cst, snt, rstd, acc, ssq, up, sqb, ra, rb, ob = _a_common(k, pfx)
    oc = [0]

    def nob():
        oc[0] += 1
        return ob[oc[0] % 4]

    w1 = k.sb([128, 32, 1024], BF16, pfx + "w1")
    load_cast(k, w1, wq_in, 32, 1024, stg)
    wu = k.sb([128, 8, NH * 256], BF16, pfx + "wu")
    load_cast(k, wu, wuq, 8, NH * 256, stg)
    gqt = k.sb([128, 8], F32, pfx + "gq")
    if 'gq' not in DBG_SKIP:
        k.dma("sp", gqt.v, gq.v)
    cq = k.sb([128, 8, BS_A], F32, pfx + "cq")
    cqn = k.sb([128, 8, BS_A], BF16, pfx + "cqn")
    for bi, (t0, t1) in enumerate(blocks):
        if DBG_NB and bi >= DBG_NB:
            break
        if bi < DBG_START:
            continue
        n = t1 - t0
        if bi == 0:
            h = hbm
            k.dma("sp", h.v, hn_meta.v)
        else:
            h = hb[bi % 2]
            k.dma(os.environ.get("HQ", "sp"), h.v, hn_own[bi - 1])
        ct, sn = cst[bi % 2], snt[bi % 2]
        if 'cs' not in DBG_SKIP:
            _q = os.environ.get("CSQ", "sp")
            _o = 0 if os.environ.get("CS0") else t0
            k.dma(_q, ct[0:64, 0:n], cos2[:, _o:_o + n])
            k.dma(_q, sn[0:64, 0:n], sin2s[:, _o:_o + n])
        for m in range(8):
            a = acc[m % 3]
            for kt in range(32):
                k.matmul(a[:, 0:n], w1[:, kt, m * 128:(m + 1) * 128], h[:, kt, 0:n], start=(kt == 0), stop=(kt == 31))
            sq = sqb[m % 2]
            if 'sq' not in DBG_SKIP:
                k.act(sq[:, 0:n], a[:, 0:n], AF.Square)
            if 'cp' not in DBG_SKIP:
                k.copy(cq[:, m, 0:n], a[:, 0:n], eng=os.environ.get("CPENG","dve"))
            if 'ssq' not in DBG_SKIP:
                k.matmul(ssq[:, 0:n], ones.v, sq[:, 0:n], start=(m == 0), stop=(m == 7))
        if 'rstd' not in DBG_SKIP:
            rstd_from_ssq(k, rstd, ssq, n, 1024)
        for m in range(8):
            if 'stt' in DBG_SKIP:
                break
            k.stt(cqn[:, m, 0:n], cq[:, m, 0:n], gqt[:, m:m + 1], rstd[:, 0:n], ALU.mult, ALU.mult)
        for hd in range(NH):
            if 'up' in DBG_SKIP:
                break
            c0 = hd * 256
            u0 = up[0]
            for kt in range(8):
                k.matmul(u0[:, 0:n], wu[:, kt, c0:c0 + 128], cqn[:, kt, 0:n], start=(kt == 0), stop=(kt == 7))
            o = nob()
            k.act(o[:, 0:n], u0[:, 0:n], AF.Copy, scale=QSCALE)
            k.dma("act", qnT[hd, :, t0:t1], o[:, 0:n])
            u1, u2 = up[1], up[2]
            for kt in range(8):
                k.matmul(u1[0:64, 0:n], wu[:, kt, c0 + 128:c0 + 192], cqn[:, kt, 0:n], start=(kt == 0), stop=(kt == 7))
            for kt in range(8):
                k.matmul(u2[0:64, 0:n], wu[:, kt, c0 + 192:c0 + 256], cqn[:, kt, 0:n], start=(kt == 0), stop=(kt == 7))
            a_, b_ = ra[hd % 2], rb[hd % 2]
            k.tt(a_[0:64, 0:n], u1[0:64, 0:n], ct[0:64, 0:n], ALU.mult)
            k.tt(b_[0:64, 0:n], u2[0:64, 0:n], sn[0:64, 0:n], ALU.mult)
            o = nob()
            k.tt(a_[0:64, 0:n], a_[0:64, 0:n], b_[0:64, 0:n], ALU.add, eng="pool")
            k.act(o[0:64, 0:n], a_[0:64, 0:n], AF.Copy, scale=QSCALE)
            k.dma("act", qrT[hd, :, t0:t1], o[0:64, 0:n])


def mla_a2(k, T, hn_meta, hn_own, wkv_in, wukv, gkv, cos2, sin2s, knT, krT, vtok, gT, pfx="m2"):
    blocks = tok_blocks(T, BS_A)
    hbm = k.sb([128, 32, 16], BF16, pfx + "hbm")
    ones, stg, hb, cst, snt, rstd, acc, ssq, up, sqb, ra, rb, ob = _a_common(k, pfx)
    oc = [0]

    def nob():
        oc[0] += 1
        return ob[oc[0] % 4]

    w2 = k.sb([128, 32, 1152], BF16, pfx + "w2")
    load_cast(k, w2, wkv_in, 32, 1152, stg)
    wk = k.sb([128, 4, 1024], BF16, pfx + "wk")
    load_cast(k, wk, wukv, 4, 1024, stg)
    gkt = k.sb([128, 4], F32, pfx + "gk")
    k.dma("sp", gkt.v, gkv.v)
    ckv = k.sb([128, 4, BS_A], F32, pfx + "ckv")
    ckn = k.sb([128, 4, BS_A], BF16, pfx + "ckn")
    for bi, (t0, t1) in enumerate(blocks):
        n = t1 - t0
        if bi == 0:
            h = hbm
            k.dma("sp", h.v, hn_meta.v)
        else:
            h = hb[bi % 2]
            k.dma(os.environ.get("HQ", "sp"), h.v, hn_own[bi - 1])
        ct, sn = cst[bi % 2], snt[bi % 2]
        k.dma("sp", ct[0:64, 0:n], cos2[:, t0:t1])
        k.dma("sp", sn[0:64, 0:n], sin2s[:, t0:t1])
        for m in range(4):
            a = acc[m % 3]
            for kt in range(32):
                k.matmul(a[:, 0:n], w2[:, kt, m * 128:(m + 1) * 128], h[:, kt, 0:n], start=(kt == 0), stop=(kt == 31))
            sq = sqb[m % 2]
            k.act(sq[:, 0:n], a[:, 0:n], AF.Square)
            k.copy(ckv[:, m, 0:n], a[:, 0:n], eng="dve")
            k.matmul(ssq[:, 0:n], ones.v, sq[:, 0:n], start=(m == 0), stop=(m == 3))
        rstd_from_ssq(k, rstd, ssq, n, 512)
        for m in range(4):
            k.stt(ckn[:, m, 0:n], ckv[:, m, 0:n], gkt[:, m:m + 1], rstd[:, 0:n], ALU.mult, ALU.mult)
        u1, u2 = up[1], up[2]
        for kt in range(32):
            k.matmul(u1[0:64, 0:n], w2[:, kt, 512:576], h[:, kt, 0:n], start=(kt == 0), stop=(kt == 31))
        for kt in range(32):
            k.matmul(u2[0:64, 0:n], w2[:, kt, 576:640], h[:, kt, 0:n], start=(kt == 0), stop=(kt == 31))
        a_, b_ = ra[0], rb[0]
        k.tt(a_[0:64, 0:n], u1[0:64, 0:n], ct[0:64, 0:n], ALU.mult)
        k.tt(b_[0:64, 0:n], u2[0:64, 0:n], sn[0:64, 0:n], ALU.mult)
        o = nob()
        k.tt(o[0:64, 0:n], a_[0:64, 0:n], b_[0:64, 0:n], ALU.add)
        k.dma("act", krT[:, t0:t1], o[0:64, 0:n])
        for m in range(4):
            a = acc[m % 3]
            for kt in range(32):
                k.matmul(a[:, 0:n], w2[:, kt, 640 + m * 128:640 + (m + 1) * 128], h[:, kt, 0:n], start=(kt == 0), stop=(kt == 31))
            o = nob()
            k.act(o[:, 0:n], a[:, 0:n], AF.Silu)
            k.dma("act", gT[m * 128:(m + 1) * 128, t0:t1], o[:, 0:n])
        for hd in range(NH):
            u0 = up[0]
            for kt in range(4):
                k.matmul(u0[:, 0:n], wk[:, kt, hd * 128:(hd + 1) * 128], ckn[:, kt, 0:n], start=(kt == 0), stop=(kt == 3))
            o = nob()
            k.copy(o[:, 0:n], u0[:, 0:n], eng="act")
            k.dma("act", knT[hd, :, t0:t1], o[:, 0:n])
        for s0 in range(0, n, 128):
            ns = min(128, n - s0)
            a = acc[(s0 // 128) % 3]
            for kt in range(4):
                k.matmul(a[0:ns, :], ckn[:, kt, s0:s0 + ns], wk[:, kt, 512:1024], start=(kt == 0), stop=(kt == 3))
            o = nob()
            k.copy(o[0:ns, :], a[0:ns, :], eng="dve")
            kb = 0 if bi == 0 else 1 + (t0 + s0 - 16) // 128
            for hd in range(NH):
                k.dma("act", vtok[hd, 0:ns, kb, :], o[0:ns, hd * 128:(hd + 1) * 128])


def mla_phase_b(k, T, qnT, qrT, knT, krT, vtok, gT, yT, pfx="mb"):
    NB = (T - 16) // 512
    NKB = (T - 16) // 128
    ones = k.sb([128, 128], BF16, pfx + "ones")
    k.memset(ones.v, 1.0)
    kr = k.sb([128, T], BF16, pfx + "kr")
    k.dma("sp", kr[0:64, :], krT.v)
    kn = k.sb([128, T], BF16, pfx + "kn")
    vv = k.sb([128, NKB + 1, 128], BF16, pfx + "vv")
    qn = [k.sb([128, 512], BF16, pfx + f"qn{i}") for i in range(2)]
    qr = [k.sb([128, 512], BF16, pfx + f"qr{i}") for i in range(2)]
    gt = [k.sb([128, 512], BF16, pfx + f"gt{i}") for i in range(2)]
    pt = [k.sb([128, 512], BF16, pfx + f"pt{i}") for i in range(3)]
    sc = [k.ps([128, 512], F32, pfx + f"sc{i}") for i in range(3)]
    oT = [k.ps([128, 512], F32, pfx + f"oT{i}") for i in range(2)]
    dn = [k.ps([128, 512], F32, pfx + f"dn{i}") for i in range(2)]
    rden = [k.sb([128, 512], F32, pfx + f"rden{i}") for i in range(2)]
    yo = [k.sb([128, 512], F32, pfx + f"yo{i}") for i in range(2)]
    yb = [k.sb([128, 512], BF16, pfx + f"yb{i}") for i in range(2)]
    u = 0
    g = 0
    for hd in range(NH):
        k.dma("sp", kn.v, knT[hd, :, :])
        k.dma("act", vv.v, vtok[hd])
        groups = [(0, 16, -1)] + [(16 + 512 * i, 16 + 512 * (i + 1), i) for i in range(NB)]
        for (q0, q1, gi) in groups:
            n = q1 - q0
            qnt, qrt, gtt = qn[g % 2], qr[g % 2], gt[g % 2]
            o_, d_ = oT[g % 2], dn[g % 2]
            k.dma("sp", qnt[:, 0:n], qnT[hd, :, q0:q1])
            k.dma("sp", qrt[0:64, 0:n], qrT[hd, :, q0:q1])
            k.dma("sp", gtt[:, 0:n], gT[hd * 128:(hd + 1) * 128, q0:q1])
            kbs = [(0, 16, 0, 0)]
            if gi >= 0:
                for j in range(4 * gi):
                    kbs.append((16 + 128 * j, 128, 1 + j, 0))
                for dgi in range(4):
                    j = 4 * gi + dgi
                    kbs.append((16 + 128 * j, 128, 1 + j, 128 * dgi))
            for idx, (kc, nk, vb, qs) in enumerate(kbs):
                s_ = sc[u % 3]
                p_ = pt[u % 3]
                k.matmul(s_[0:nk, qs:n], kn[:, kc:kc + nk], qnt[:, qs:n], start=True, stop=False)
                k.matmul(s_[0:nk, qs:n], kr[0:64, kc:kc + nk], qrt[0:64, qs:n], start=False, stop=True)
                k.act(p_[0:nk, qs:n], s_[0:nk, qs:n], AF.Exp)
                if gi >= 0 and idx >= 1 + 4 * gi:
                    k.memset(p_[64:128, qs:qs + 64], 0.0, eng="pool")
                last = (idx == len(kbs) - 1)
                k.matmul(o_[:, qs:n], vv[0:nk, vb, :], p_[0:nk, qs:n], start=(idx == 0), stop=last)
                k.matmul(d_[:, qs:n], ones[0:nk, :], p_[0:nk, qs:n], start=(idx == 0), stop=last)
                u += 1
            rd, y1, y2 = rden[g % 2], yo[g % 2], yb[g % 2]
            k.recip(rd[:, 0:n], d_[:, 0:n])
            k.tt(y1[:, 0:n], o_[:, 0:n], rd[:, 0:n], ALU.mult)
            k.tt(y2[:, 0:n], y1[:, 0:n], gtt[:, 0:n], ALU.mult, eng="pool")
            k.dma("act", yT[hd * 128:(hd + 1) * 128, q0:q1], y2[:, 0:n])
            g += 1


def stage_mla(k, T, hn_meta, hn_own, wq_in, wkv_in, wuq, wukv, gq, gkv, cos2, sin2s, yT, pfx="ml", kind="Internal", phases="12b"):
    qnT = k.dram(pfx + "_qnT", [NH, 128, T], BF16, kind=kind)
    qrT = k.dram(pfx + "_qrT", [NH, 64, T], BF16, kind=kind)
    knT = k.dram(pfx + "_knT", [NH, 128, T], BF16, kind=kind)
    krT = k.dram(pfx + "_krT", [64, T], BF16, kind=kind)
    vtok = k.dram(pfx + "_vtok", [NH, 128, (T - 16) // 128 + 1, 128], BF16, kind=kind)
    gT = k.dram(pfx + "_gT", [512, T], BF16, kind=kind)
    if "1" in phases:
      with (contextlib.nullcontext() if os.environ.get("NOSCOPE") else k.scope()):
        mla_a1(k, T, hn_meta, hn_own, wq_in, wuq, gq, cos2, sin2s, qnT, qrT, pfx + "1")
    if "2" in phases:
      with k.scope():
        mla_a2(k, T, hn_meta, hn_own, wkv_in, wukv, gkv, cos2, sin2s, knT, krT, vtok, gT, pfx + "2")
    if "b" in phases:
      with k.scope():
        mla_phase_b(k, T, qnT, qrT, knT, krT, vtok, gT, yT, pfx + "b")
    return dict(qnT=qnT, qrT=qrT, knT=knT, krT=krT, vtok=vtok, gT=gT)


EPS = 1e-6
GELU_C = 1.5957691216057308


def hyb_ssd(k, T, hn_meta, hn_own, w_ssd, convw, convb, dtb_bc, alog_bc, d_bc, ng_bc, Umat, ident, yT, pfx="hs"):
    blocks = tok_blocks(T, BS_A)
    NW = 1288
    stg = [k.sb([128, NW], F32, pfx + f"stg{i}") for i in range(2)]
    w = k.sb([128, 32, NW], BF16, pfx + "w")
    load_cast(k, w, w_ssd, 32, NW, stg)
    hbm = k.sb([128, 32, 16], BF16, pfx + "hbm")
    hb = [k.sb([128, 32, BS_A], BF16, pfx + f"hb{i}") for i in range(2)]
    cw = k.sb([128, 6, 4], F32, pfx + "cw")
    cb = k.sb([128, 6], F32, pfx + "cb")
    k.dma("sp", cw.v, convw.v)
    k.dma("sp", cb.v, convb.v)
    dtb = k.sb([128, 8], F32, pfx + "dtb")
    aneg = k.sb([128, 8], F32, pfx + "aneg")
    dbc = k.sb([128, 512], F32, pfx + "dbc")
    ngb = k.sb([128, 512], F32, pfx + "ngb")
    U = k.sb([128, 128], F32, pfx + "U")
    idb = k.sb([128, 128], BF16, pfx + "idb")
    idf = k.sb([128, 128], F32, pfx + "idf")
    k.dma("sp", dtb.v, dtb_bc.v)
    k.dma("sp", aneg.v, alog_bc.v)
    k.dma("sp", dbc.v, d_bc.v)
    k.dma("sp", ngb.v, ng_bc.v)
    k.dma("sp", U.v, Umat.v)
    k.dma("sp", idf.v, ident.v)
    k.copy(idb.v, idf.v, eng="pool")
    k.act(aneg.v, aneg.v, AF.Exp)
    k.ts(aneg.v, aneg.v, -1.0, ALU.mult)
    ones = k.sb([128, 128], F32, pfx + "ones")
    k.memset(ones.v, 1.0)

    cin = [k.sb([128, 3 + BS_A], F32, pfx + f"cin{m}") for m in range(6)]
    for m in range(6):
        k.memset(cin[m].v, 0.0)
    cacc = [k.sb([128, BS_A], F32, pfx + f"cacc{i}") for i in range(2)]
    fT = [k.sb([128, BS_A], BF16, pfx + f"fT{m}") for m in range(6)]
    zs = k.sb([128, 512], F32, pfx + "zs")
    dt = k.sb([128, 8], F32, pfx + "dt")
    da = k.sb([128, 8], F32, pfx + "da")
    dab = k.sb([128, 8, 128], F32, pfx + "dab")
    acum = k.sb([128, 8], F32, pfx + "acum")
    nacum = k.sb([128, 8], F32, pfx + "nacum")
    aend = k.sb([128, 8], F32, pfx + "aend")
    eend = k.sb([128, 8], F32, pfx + "eend")
    eac = k.sb([128, 8], F32, pfx + "eac")
    dte = k.sb([128, 8], F32, pfx + "dte")
    xtok = k.sb([128, 512], BF16, pfx + "xtok")
    btok = k.sb([128, 128], BF16, pfx + "btok")
    xdt = k.sb([128, 512], BF16, pfx + "xdt")
    xw = k.sb([128, 512], BF16, pfx + "xw")
    segc = k.sb([128, 8, 128], F32, pfx + "segc")
    cbm = k.sb([128, 128], F32, pfx + "cbm")
    MT = k.sb([128, 8, 128], BF16, pfx + "MT")
    S = k.sb([128, 512], F32, pfx + "S")
    Sb = k.sb([128, 512], BF16, pfx + "Sb")
    k.memset(S.v, 0.0)
    k.memset(Sb.v, 0.0)
    t1 = k.sb([128, 512], F32, pfx + "t1")
    t2 = k.sb([128, 512], F32, pfx + "t2")
    ssq = k.sb([128, 1], F32, pfx + "ssq")
    yn = k.sb([128, 512], BF16, pfx + "yn")
    yTs = [k.sb([128, 128], BF16, pfx + f"yTs{i}") for i in range(4)]

    accA = k.ps([128, 512], F32, pfx + "accA")
    accB = k.ps([128, 512], F32, pfx + "accB")
    misc = k.ps([128, 512], F32, pfx + "misc")
    AB = k.ps([128, 8, 128], F32, pfx + "AB")
    ydg = k.ps([128, 512], F32, pfx + "ydg")
    yof = k.ps([128, 512], F32, pfx + "yof")
    tr = k.ps([128, 512], BF16, pfx + "tr")
    accs = [accA, accB]

    for bi, (t0, t1_) in enumerate(blocks):
        n = t1_ - t0
        if bi == 0:
            h = hbm
            k.dma("sp", h.v, hn_meta.v)
        else:
            h = hb[bi % 2]
            k.dma("sp", h.v, hn_own[bi - 1])
        for m in range(6):
            a = accs[m % 2]
            c0 = 512 + m * 128
            for kt in range(32):
                k.matmul(a[:, 0:n], w[:, kt, c0:c0 + 128], h[:, kt, 0:n], start=(kt == 0), stop=(kt == 31))
            ci = cin[m]
            k.copy(ci[:, 3:3 + n], a[:, 0:n], eng="act")
            ca = cacc[m % 2]
            k.ts(ca[:, 0:n], ci[:, 0:n], cw[:, m, 0:1], ALU.mult, cb[:, m:m + 1], ALU.add)
            for j in range(1, 4):
                k.stt(ca[:, 0:n], ci[:, j:j + n], cw[:, m, j:j + 1], ca[:, 0:n], ALU.mult, ALU.add)
            k.act(fT[m][:, 0:n], ca[:, 0:n], AF.Silu)
            k.copy(ci[:, 0:3], ci[:, n:n + 3], eng="pool")
        for s0 in range(0, n, 128):
            cl = min(128, n - s0)
            cs = slice(s0, s0 + cl)
            for kt in range(32):
                k.matmul(accA[0:cl, :], h[:, kt, cs], w[:, kt, 0:512], start=(kt == 0), stop=(kt == 31))
            k.act(zs[0:cl, :], accA[0:cl, :], AF.Silu)
            for kt in range(32):
                k.matmul(misc[0:cl, 0:8], h[:, kt, cs], w[:, kt, 1280:1288], start=(kt == 0), stop=(kt == 31))
            k.tt(dt[0:cl, :], misc[0:cl, 0:8], dtb[0:cl, :], ALU.add)
            k.act(dt[0:cl, :], dt[0:cl, :], AF.Exp)
            k.act(dt[0:cl, :], dt[0:cl, :], AF.Ln, bias=1.0)
            k.tt(da[0:cl, :], dt[0:cl, :], aneg[0:cl, :], ALU.mult)
            for m in range(4):
                k.transpose(tr[0:cl, m * 128:(m + 1) * 128], fT[m][:, cs], idb.v)
            k.copy(xtok[0:cl, :], tr[0:cl, :], eng="act")
            k.transpose(tr[0:cl, 0:128], fT[4][:, cs], idb.v)
            k.copy(btok[0:cl, :], tr[0:cl, 0:128], eng="act")
            k.tt(xdt[0:cl, :].rearrange("p (h d) -> p h d", h=8), xtok[0:cl, :].rearrange("p (h d) -> p h d", h=8),
                 dt[0:cl, :].unsqueeze(2).broadcast_to([cl, 8, 64]), ALU.mult)
            k.matmul(misc[0:cl, 8:16], U[0:cl, 0:cl], da[0:cl, :])
            k.copy(acum[0:cl, :], misc[0:cl, 8:16], eng="dve")
            k.ts(nacum[0:cl, :], acum[0:cl, :], -1.0, ALU.mult)
            k.act(eac[0:cl, :], acum[0:cl, :], AF.Exp)
            k.tt(dab[0:cl, :, 0:cl], ones[0:cl, 0:cl].unsqueeze(1).broadcast_to([cl, 8, cl]),
                 da[0:cl, :].unsqueeze(2).broadcast_to([cl, 8, cl]), ALU.mult, eng="pool")
            for hh in range(8):
                k.matmul(AB[0:cl, hh, 0:cl], dab[0:cl, hh, 0:cl], U[0:cl, 0:cl])
            k.tt(segc[0:cl, :, 0:cl], AB[0:cl, :, 0:cl], nacum[0:cl, :].unsqueeze(2).broadcast_to([cl, 8, cl]), ALU.add)
            k.copy(aend[0:cl, :], AB[0:cl, :, cl - 1], eng="dve")
            k.ts(segc[0:cl, :, 0:cl], segc[0:cl, :, 0:cl], 0.0, ALU.min, eng="pool")
            k.act(segc[0:cl, :, 0:cl], segc[0:cl, :, 0:cl], AF.Exp)
            k.matmul(misc[0:cl, 128:128 + cl], fT[4][:, cs], fT[5][:, cs])
            k.tt(cbm[0:cl, 0:cl], misc[0:cl, 128:128 + cl], U[0:cl, 0:cl], ALU.mult)
            k.tt(MT[0:cl, :, 0:cl], segc[0:cl, :, 0:cl], cbm[0:cl, 0:cl].unsqueeze(1).broadcast_to([cl, 8, cl]), ALU.mult, eng="pool")
            for hh in range(8):
                k.matmul(ydg[0:cl, hh * 64:(hh + 1) * 64], MT[0:cl, hh, 0:cl], xdt[0:cl, hh * 64:(hh + 1) * 64])
            k.matmul(yof[0:cl, :], fT[5][:, cs], Sb.v)
            k.tt(t1[0:cl, :].rearrange("p (h d) -> p h d", h=8), yof[0:cl, :].rearrange("p (h d) -> p h d", h=8),
                 eac[0:cl, :].unsqueeze(2).broadcast_to([cl, 8, 64]), ALU.mult)
            k.tt(t1[0:cl, :], t1[0:cl, :], ydg[0:cl, :], ALU.add)
            k.tt(t2[0:cl, :], xtok[0:cl, :], dbc[0:cl, :], ALU.mult, eng="pool")
            k.tt(t1[0:cl, :], t1[0:cl, :], t2[0:cl, :], ALU.add)
            k.tt(t1[0:cl, :], t1[0:cl, :], zs[0:cl, :], ALU.mult)
            k.act(t2[0:cl, :], t1[0:cl, :], AF.Square, accum_out=ssq[0:cl, :])
            k.ts(ssq[0:cl, :], ssq[0:cl, :], 1.0 / 512, ALU.mult, EPS, ALU.add)
            k.act(ssq[0:cl, :], ssq[0:cl, :], AF.Sqrt)
            k.recip(ssq[0:cl, :], ssq[0:cl, :])
            k.stt(yn[0:cl, :], t1[0:cl, :], ssq[0:cl, 0:1], ngb[0:cl, :], ALU.mult, ALU.mult)
            for m in range(4):
                k.transpose(tr[:, m * 128:m * 128 + cl], yn[0:cl, m * 128:(m + 1) * 128], idb[0:cl, 0:cl])
            for m in range(4):
                k.copy(yTs[m][:, 0:cl], tr[:, m * 128:m * 128 + cl], eng=("act" if m % 2 else "dve"))
                k.dma("act", yT[m * 128:(m + 1) * 128, t0 + s0:t0 + s0 + cl], yTs[m][:, 0:cl])
            k.ts(dte[0:cl, :], aend[0:cl, :], 1.0 / cl, ALU.mult)
            k.matmul(misc[:, 16:24], ones[0:cl, :], dte[0:cl, :])
            k.act(eend.v, misc[:, 16:24], AF.Exp)
            k.tt(dte[0:cl, :], aend[0:cl, :], acum[0:cl, :], ALU.subtract)
            k.act(dte[0:cl, :], dte[0:cl, :], AF.Exp)
            k.tt(xw[0:cl, :].rearrange("p (h d) -> p h d", h=8), xdt[0:cl, :].rearrange("p (h d) -> p h d", h=8),
                 dte[0:cl, :].unsqueeze(2).broadcast_to([cl, 8, 64]), ALU.mult)
            k.matmul(yof.v, btok[0:cl, :], xw[0:cl, :])
            k.tt(S.v.rearrange("p (h d) -> p h d", h=8), S.v.rearrange("p (h d) -> p h d", h=8),
                 eend.v.unsqueeze(2).broadcast_to([128, 8, 64]), ALU.mult)
            k.tt(S.v, S.v, yof.v, ALU.add)
            k.copy(Sb.v, S.v, eng="pool")


def hyb_s5(k, T, hn_meta, hn_own, w_s5, bre, bim, cre, cim, are_l, aim_l, ldt_l, d_l, gT, sgT, pfx="h5"):
    blocks = tok_blocks(T, BS_A)
    L = BS_A
    stg = [k.sb([128, 512], F32, pfx + f"stg{i}") for i in range(2)]
    w = k.sb([128, 32, 512], BF16, pfx + "w")
    load_cast(k, w, w_s5, 32, 512, stg)
    hbm = k.sb([128, 32, 16], BF16, pfx + "hbm")
    hb = [k.sb([128, 32, BS_A], BF16, pfx + f"hb{i}") for i in range(2)]
    f_bre = k.sb([128, 8, 128], F32, pfx + "fbre"); f_bim = k.sb([128, 8, 128], F32, pfx + "fbim")
    f_cre = k.sb([128, 8, 128], F32, pfx + "fcre"); f_cim = k.sb([128, 8, 128], F32, pfx + "fcim")
    Bre = k.sb([128, 8, 128], BF16, pfx + "Bre"); Bim = k.sb([128, 8, 128], BF16, pfx + "Bim")
    Cre = k.sb([128, 8, 128], BF16, pfx + "Cre"); Cim = k.sb([128, 8, 128], BF16, pfx + "Cim")
    for (dst, f, src) in ((Bre, f_bre, bre), (Bim, f_bim, bim), (Cre, f_cre, cre), (Cim, f_cim, cim)):
        k.dma("sp", f.v, src.v)
        k.copy(dst.v, f.v, eng="pool")
    are = k.sb([128, 8], F32, pfx + "are"); aim = k.sb([128, 8], F32, pfx + "aim"); dtt = k.sb([128, 8], F32, pfx + "dtt")
    dsk = k.sb([128, 2], F32, pfx + "dsk")
    k.dma("sp", are.v, are_l.v); k.dma("sp", aim.v, aim_l.v); k.dma("sp", dtt.v, ldt_l.v); k.dma("sp", dsk.v, d_l.v)
    k.act(dtt.v, dtt.v, AF.Exp)
    th = k.sb([128, 8], F32, pfx + "th"); rho = k.sb([128, 8], F32, pfx + "rho")
    k.tt(th.v, dtt.v, aim.v, ALU.mult)
    k.tt(rho.v, dtt.v, are.v, ALU.mult)
    k.act(rho.v, rho.v, AF.Exp)
    ki = k.sb([128, 8], I32, pfx + "ki"); kf = k.sb([128, 8], F32, pfx + "kf")
    hh_ = k.sb([128, 8], F32, pfx + "hh"); sh = k.sb([128, 8], F32, pfx + "sh"); ch = k.sb([128, 8], F32, pfx + "ch")
    k.ts(kf.v, th.v, 1.0 / (2 * math.pi), ALU.mult)
    k.copy(ki.v, kf.v, eng="dve")
    k.copy(kf.v, ki.v, eng="dve")
    k.stt(hh_.v, kf.v, -2 * math.pi, th.v, ALU.mult, ALU.add)
    k.ts(hh_.v, hh_.v, 0.5, ALU.mult)
    k.act(sh.v, hh_.v, AF.Sin)
    q4 = k.sb([128, 8], F32, pfx + "q4")
    k.act(q4.v, hh_.v, AF.Sin, scale=0.5)
    k.tt(q4.v, q4.v, q4.v, ALU.mult)
    k.ts(ch.v, q4.v, -2.0, ALU.mult, 1.0, ALU.add)
    zr = k.sb([128, 8, 9], F32, pfx + "zr"); zi = k.sb([128, 8, 9], F32, pfx + "zi"); nzi = k.sb([128, 8, 9], F32, pfx + "nzi")
    tmp8 = k.sb([128, 8], F32, pfx + "tmp8"); tmp8b = k.sb([128, 8], F32, pfx + "tmp8b")
    k.tt(tmp8.v, sh.v, sh.v, ALU.mult)
    k.ts(zr[:, :, 0], tmp8.v, -2.0, ALU.mult, 1.0, ALU.add)
    k.tt(tmp8.v, sh.v, ch.v, ALU.mult)
    k.ts(zi[:, :, 0], tmp8.v, 2.0, ALU.mult)
    for m_ in range(8):
        k.tt(tmp8.v, zr[:, :, m_], zr[:, :, m_], ALU.mult)
        k.tt(tmp8b.v, zi[:, :, m_], zi[:, :, m_], ALU.mult)
        k.tt(zr[:, :, m_ + 1], tmp8.v, tmp8b.v, ALU.subtract)
        k.tt(tmp8.v, zr[:, :, m_], zi[:, :, m_], ALU.mult)
        k.ts(zi[:, :, m_ + 1], tmp8.v, 2.0, ALU.mult)
    k.ts(nzi.v, zi.v, -1.0, ALU.mult)
    abr = k.sb([128, 8], F32, pfx + "abr"); abi = k.sb([128, 8], F32, pfx + "abi"); den = k.sb([128, 8], F32, pfx + "den")
    kre = k.sb([128, 8], F32, pfx + "kre"); kim = k.sb([128, 8], F32, pfx + "kim"); nkre = k.sb([128, 8], F32, pfx + "nkre")
    k.tt(abr.v, rho.v, zr[:, :, 0], ALU.mult)
    k.ts(abr.v, abr.v, -1.0, ALU.add)
    k.tt(abi.v, rho.v, zi[:, :, 0], ALU.mult)
    k.tt(den.v, are.v, are.v, ALU.mult)
    k.tt(tmp8.v, aim.v, aim.v, ALU.mult)
    k.tt(den.v, den.v, tmp8.v, ALU.add)
    k.recip(den.v, den.v)
    k.tt(kre.v, abr.v, are.v, ALU.mult)
    k.tt(tmp8.v, abi.v, aim.v, ALU.mult)
    k.tt(kre.v, kre.v, tmp8.v, ALU.add)
    k.tt(kre.v, kre.v, den.v, ALU.mult)
    k.tt(kim.v, abi.v, are.v, ALU.mult)
    k.tt(tmp8.v, abr.v, aim.v, ALU.mult)
    k.tt(kim.v, kim.v, tmp8.v, ALU.subtract)
    k.tt(kim.v, kim.v, den.v, ALU.mult)
    k.ts(nkre.v, kre.v, -1.0, ALU.mult)
    Fc = k.sb([128, 8, L], F32, pfx + "Fc"); Fs = k.sb([128, 8, L], F32, pfx + "Fs")
    Ere = k.sb([128, 8, L], F32, pfx + "Ere"); Eim = k.sb([128, 8, L], F32, pfx + "Eim")
    rhoT = k.sb([128, 8, L], F32, pfx + "rhoT")
    tl = k.sb([128, L], F32, pfx + "tl")
    k.memset(Fc.v, 1.0)
    k.memset(Fs.v, 0.0)
    k.memset(rhoT.v, 1.0)
    for j in range(8):
        for m_ in range(8):
            lo = slice(0, 2 ** m_)
            hi = slice(2 ** m_, 2 ** (m_ + 1))
            w_ = 2 ** m_
            k.ts(tl[:, 0:w_], Fs[:, j, lo], zi[:, j, m_:m_ + 1], ALU.mult)
            k.stt(Fc[:, j, hi], Fc[:, j, lo], zr[:, j, m_:m_ + 1], tl[:, 0:w_], ALU.mult, ALU.subtract)
            k.ts(tl[:, 0:w_], Fc[:, j, lo], zi[:, j, m_:m_ + 1], ALU.mult)
            k.stt(Fs[:, j, hi], Fs[:, j, lo], zr[:, j, m_:m_ + 1], tl[:, 0:w_], ALU.mult, ALU.add)
        k.ts(tl.v, Fs[:, j, :], kim[:, j:j + 1], ALU.mult)
        k.stt(Ere[:, j, :], Fc[:, j, :], kre[:, j:j + 1], tl.v, ALU.mult, ALU.add)
        k.ts(tl.v, Fs[:, j, :], nkre[:, j:j + 1], ALU.mult)
        k.stt(Eim[:, j, :], Fc[:, j, :], kim[:, j:j + 1], tl.v, ALU.mult, ALU.add)
        k.ts(rhoT[:, j, :], rhoT[:, j, :], rho[:, j:j + 1], ALU.mult)
    uf = [k.sb([128, BS_A], F32, pfx + f"uf{a}") for a in range(2)]
    ub = [k.sb([128, BS_A], BF16, pfx + f"ub{a}") for a in range(2)]
    go = [k.sb([128, BS_A], BF16, pfx + f"go{i}") for i in range(2)]
    vre = k.sb([128, BS_A], F32, pfx + "vre"); vim = k.sb([128, BS_A], F32, pfx + "vim")
    p1 = k.sb([128, BS_A], F32, pfx + "p1"); p2 = k.sb([128, BS_A], F32, pfx + "p2")
    p3 = k.sb([128, BS_A], F32, pfx + "p3"); p4 = k.sb([128, BS_A], F32, pfx + "p4")
    wre = k.sb([128, BS_A], F32, pfx + "wre"); wim = k.sb([128, BS_A], F32, pfx + "wim")
    sre = [k.sb([128, BS_A], BF16, pfx + f"sre{i}") for i in range(2)]
    sim_ = [k.sb([128, BS_A], BF16, pfx + f"sim{i}") for i in range(2)]
    wir = k.sb([128, 8], F32, pfx + "wir"); wii = k.sb([128, 8], F32, pfx + "wii")
    k.memset(wir.v, 0.0)
    k.memset(wii.v, 0.0)
    c1 = k.sb([128, 1], F32, pfx + "c1")
    x2 = k.sb([128, BS_A], F32, pfx + "x2"); x3 = k.sb([128, BS_A], F32, pfx + "x3"); yv = k.sb([128, BS_A], F32, pfx + "yv")
    acc = [k.ps([128, 512], F32, pfx + f"acc{i}") for i in range(2)]
    Pp = k.ps([128, 512], F32, pfx + "Pp"); Qp = k.ps([128, 512], F32, pfx + "Qp")
    yp = [k.ps([128, 512], F32, pfx + f"yp{a}") for a in range(2)]
    for bi, (t0, t1_) in enumerate(blocks):
        n = t1_ - t0
        if bi == 0:
            h = hbm
            k.dma("sp", h.v, hn_meta.v)
        else:
            h = hb[bi % 2]
            k.dma("sp", h.v, hn_own[bi - 1])
        lev = 4 if bi == 0 else 8
        for a in range(2):
            for kt in range(32):
                k.matmul(acc[0][:, 0:n], w[:, kt, a * 128:(a + 1) * 128], h[:, kt, 0:n], start=(kt == 0), stop=(kt == 31))
            k.copy(uf[a][:, 0:n], acc[0][:, 0:n], eng="dve")
            k.copy(ub[a][:, 0:n], uf[a][:, 0:n], eng="pool")
            for kt in range(32):
                k.matmul(acc[1][:, 0:n], w[:, kt, 256 + a * 128:256 + (a + 1) * 128], h[:, kt, 0:n], start=(kt == 0), stop=(kt == 31))
            o = go[a]
            k.act(o[:, 0:n], acc[1][:, 0:n], AF.Silu)
            k.dma("act", sgT[a * 128:(a + 1) * 128, t0:t1_], o[:, 0:n])
        for j in range(8):
            a, q = j // 4, (j % 4) * 32
            k.matmul(Pp[:, 0:n], Bre[:, j, :], ub[a][:, 0:n])
            k.matmul(Qp[:, 0:n], Bim[:, j, :], ub[a][:, 0:n])
            k.tt(p1[:, 0:n], Pp[:, 0:n], Ere[:, j, 0:n], ALU.mult)
            k.tt(p2[:, 0:n], Qp[:, 0:n], Eim[:, j, 0:n], ALU.mult)
            k.tt(p3[:, 0:n], Qp[:, 0:n], Ere[:, j, 0:n], ALU.mult)
            k.tt(p4[:, 0:n], Pp[:, 0:n], Eim[:, j, 0:n], ALU.mult)
            k.tt(vre[:, 0:n], p1[:, 0:n], p2[:, 0:n], ALU.subtract, eng="pool")
            k.tt(vim[:, 0:n], p3[:, 0:n], p4[:, 0:n], ALU.add, eng="pool")
            k.scan(wre[:, 0:n], rhoT[:, j, 0:n], vre[:, 0:n], wir[:, j:j + 1])
            k.scan(wim[:, 0:n], rhoT[:, j, 0:n], vim[:, 0:n], wii[:, j:j + 1])
            k.ts(c1.v, wim[:, n - 1:n], nzi[:, j, lev:lev + 1], ALU.mult)
            k.stt(wir[:, j:j + 1], wre[:, n - 1:n], zr[:, j, lev:lev + 1], c1.v, ALU.mult, ALU.add)
            k.ts(c1.v, wre[:, n - 1:n], zi[:, j, lev:lev + 1], ALU.mult)
            k.stt(wii[:, j:j + 1], wim[:, n - 1:n], zr[:, j, lev:lev + 1], c1.v, ALU.mult, ALU.add)
            sr, si = sre[j % 2], sim_[j % 2]
            k.tt(p1[:, 0:n], wre[:, 0:n], Fc[:, j, 0:n], ALU.mult, eng="pool")
            k.tt(p2[:, 0:n], wim[:, 0:n], Fs[:, j, 0:n], ALU.mult, eng="pool")
            k.tt(sr[:, 0:n], p1[:, 0:n], p2[:, 0:n], ALU.subtract, eng="pool")
            k.tt(p3[:, 0:n], wre[:, 0:n], Fs[:, j, 0:n], ALU.mult, eng="pool")
            k.tt(p4[:, 0:n], wim[:, 0:n], Fc[:, j, 0:n], ALU.mult, eng="pool")
            k.stt(si[:, 0:n], p3[:, 0:n], -1.0, p4[:, 0:n], ALU.mult, ALU.subtract)
            k.matmul(yp[a][:, 0:n], Cre[:, j, :], sr[:, 0:n], start=(j % 4 == 0), stop=False)
            k.matmul(yp[a][:, 0:n], Cim[:, j, :], si[:, 0:n], start=False, stop=(j % 4 == 3))
        for a in range(2):
            k.stt(yv[:, 0:n], uf[a][:, 0:n], dsk[:, a:a + 1], yp[a][:, 0:n], ALU.mult, ALU.add)
            k.tt(x2[:, 0:n], yv[:, 0:n], yv[:, 0:n], ALU.mult, eng="pool")
            k.ts(x2[:, 0:n], x2[:, 0:n], 0.044715, ALU.mult, 1.0, ALU.add, eng="pool")
            k.tt(x3[:, 0:n], x2[:, 0:n], yv[:, 0:n], ALU.mult, eng="pool")
            k.act(x3[:, 0:n], x3[:, 0:n], AF.Sigmoid, scale=GELU_C)
            o = go[a]
            k.tt(o[:, 0:n], x3[:, 0:n], yv[:, 0:n], ALU.mult)
            k.dma("act", gT[a * 128:(a + 1) * 128, t0:t1_], o[:, 0:n])


PERM = np.concatenate([np.arange(32, 64), np.arange(0, 32)])


def ktile(w):
    K, M = w.shape
    return np.ascontiguousarray(w.reshape(K // 128, 128, M).transpose(1, 0, 2))


def vec_tile(g):
    return np.ascontiguousarray(g.reshape(-1, 128).T)


def rope_tables(T):
    pos = np.arange(T, dtype=np.float32)
    inv_freq = (np.float32(10000.0) ** (-np.arange(0, 64, 2, dtype=np.float32) / np.float32(64))).astype(np.float32)
    ang = (pos[:, None] * inv_freq[None, :]).astype(np.float32)
    cos = np.cos(ang).astype(np.float32).T
    sin = np.sin(ang).astype(np.float32).T
    cos2 = np.ascontiguousarray(np.concatenate([cos, cos], 0))
    sin2s = np.ascontiguousarray(np.concatenate([-sin, sin], 0))
    return cos2, sin2s


def mla_weights(c, w_in, q_norm, w_uq, kv_norm, w_ukv):
    kr = w_in[:, 1536:1600]
    wkv = np.concatenate([w_in[:, 1024:1536], kr, kr[:, PERM], w_in[:, 1600 + c * 512:1600 + (c + 1) * 512]], 1)
    uq = []
    kn = []
    vv = []
    for h in range(4):
        b = (4 * c + h) * 192
        rope = w_uq[:, b + 128:b + 192]
        uq += [w_uq[:, b:b + 128], rope, rope[:, PERM]]
        b2 = (4 * c + h) * 256
        kn.append(w_ukv[:, b2:b2 + 128])
        vv.append(w_ukv[:, b2 + 128:b2 + 256])
    return dict(
        wq_in=ktile(w_in[:, 0:1024]),
        wkv_in=ktile(wkv),
        wuq=ktile(np.concatenate(uq, 1)),
        wukv=ktile(np.concatenate(kn + vv, 1)),
        gq=vec_tile(q_norm),
        gkv=vec_tile(kv_norm),
    )


def out_w_tile(W):
    K = W.shape[0]
    nkt = K // 128
    return np.ascontiguousarray(W.reshape(nkt, 128, 32, 128).transpose(2, 1, 0, 3).reshape(32, 128, nkt * 128))


def hn_layout(hnT):
    T = hnT.shape[1]
    nb = (T - 16) // 256
    v = hnT.reshape(32, 128, T)
    meta = np.ascontiguousarray(v[:, :, 0:16].transpose(1, 0, 2))
    own = np.ascontiguousarray(v[:, :, 16:].reshape(32, 128, nb, 256).transpose(2, 1, 0, 3))
    return dict(hn_meta=meta, hn_own=own)


def const_mats():
    U = np.triu(np.ones((128, 128), np.float32))
    ident = np.eye(128, dtype=np.float32)
    return U, ident


def hyb_weights(c, w_in, conv_w, conv_b, dt_bias, a_log, d_skip, norm_g,
                a_re, a_im, log_dt, b_re, b_im, c_re, c_im, s5_d):
    z = w_in[:, c * 512:(c + 1) * 512]
    x = w_in[:, 4096 + c * 512:4096 + (c + 1) * 512]
    B = w_in[:, 8192 + c * 128:8192 + (c + 1) * 128]
    C = w_in[:, 9216 + c * 128:9216 + (c + 1) * 128]
    dt = w_in[:, 10240 + c * 8:10240 + (c + 1) * 8]
    w_ssd = ktile(np.concatenate([z, x, B, C, dt], 1))
    u = w_in[:, 10304 + c * 256:10304 + (c + 1) * 256]
    gate = w_in[:, 12352 + c * 256:12352 + (c + 1) * 256]
    w_s5 = ktile(np.concatenate([u, gate], 1))
    chans = [np.arange(c * 512 + m * 128, c * 512 + (m + 1) * 128) for m in range(4)]
    chans.append(np.arange(4096 + c * 128, 4096 + (c + 1) * 128))
    chans.append(np.arange(5120 + c * 128, 5120 + (c + 1) * 128))
    convw = np.ascontiguousarray(np.stack([conv_w[:, ch].T for ch in chans], 1))
    convb = np.ascontiguousarray(np.stack([conv_b[ch] for ch in chans], 1))
    hs = slice(c * 8, (c + 1) * 8)
    dtb_bc = np.ascontiguousarray(np.broadcast_to(dt_bias[hs][None, :], (128, 8)))
    alog_bc = np.ascontiguousarray(np.broadcast_to(a_log[hs][None, :], (128, 8)))
    d_bc = np.ascontiguousarray(np.broadcast_to(np.repeat(d_skip[hs], 64)[None, :], (128, 512)))
    ng_bc = np.ascontiguousarray(np.broadcast_to(norm_g[c * 512:(c + 1) * 512][None, :], (128, 512)))
    bre = np.zeros((128, 8, 128), np.float32); bim = np.zeros((128, 8, 128), np.float32)
    cre = np.zeros((128, 8, 128), np.float32); cim = np.zeros((128, 8, 128), np.float32)
    are_l = np.zeros((128, 8), np.float32); aim_l = np.zeros((128, 8), np.float32); ldt_l = np.zeros((128, 8), np.float32)
    for j in range(8):
        a, q = j // 4, (j % 4) * 32
        for m in range(2):
            g = 16 * c + 2 * j + m
            bre[q + m * 16:q + (m + 1) * 16, j, m * 64:(m + 1) * 64] = b_re[g].T
            bim[q + m * 16:q + (m + 1) * 16, j, m * 64:(m + 1) * 64] = b_im[g].T
            cre[m * 64:(m + 1) * 64, j, q + m * 16:q + (m + 1) * 16] = c_re[g].T
            cim[m * 64:(m + 1) * 64, j, q + m * 16:q + (m + 1) * 16] = c_im[g].T
            are_l[m * 64:(m + 1) * 64, j] = a_re[g]
            aim_l[m * 64:(m + 1) * 64, j] = a_im[g]
            ldt_l[m * 64:(m + 1) * 64, j] = log_dt[g]
    d_l = np.ascontiguousarray(s5_d[c * 256:(c + 1) * 256].reshape(2, 128).T)
    return dict(w_ssd=w_ssd, w_s5=w_s5, convw=convw, convb=convb, dtb_bc=dtb_bc, alog_bc=alog_bc, d_bc=d_bc, ng_bc=ng_bc,
                bre=bre, bim=bim, cre=cre, cim=cim, are_l=are_l, aim_l=aim_l, ldt_l=ldt_l, d_l=d_l)


def glu_w_tile(W):
    return np.ascontiguousarray(W.reshape(16, 128, 16, 128).transpose(2, 1, 0, 3).reshape(16, 128, 2048))

from concourse.bass_utils import run_bass_kernel_spmd

BFNP = ml_dtypes.bfloat16
T_ALL = 16400
TC = 2064
NCORE = 8
_PROGS = {}

HYB_SHAPES = dict(w_ssd=[128, 32, 1288], w_s5=[128, 32, 512], convw=[128, 6, 4], convb=[128, 6], dtb_bc=[128, 8],
                  alog_bc=[128, 8], d_bc=[128, 512], ng_bc=[128, 512], bre=[128, 8, 128], bim=[128, 8, 128],
                  cre=[128, 8, 128], cim=[128, 8, 128], are_l=[128, 8], aim_l=[128, 8], ldt_l=[128, 8], d_l=[128, 2],
                  Umat=[128, 128], ident=[128, 128])


def _new():
    return bass.Bass("TRN2", target_bir_lowering=False)


def prog_norm():
    nc = _new()
    with contextlib.ExitStack() as st:
        k = KB(nc, st)
        hT = k.dram("hT", [D, TC], F32, kind="ExternalInput")
        g_l = k.dram("g_l", [128, 32], F32, kind="ExternalInput")
        hnT = k.dram("hnT", [D, TC], BF16, kind="ExternalOutput")
        stage_out(k, hT, None, None, g_l, None, hnT, TC, 0, True, BF16)
        k.final_wait("sp", [hnT])
        k.emit()
    return nc


def prog_out(hyb, final):
    nc = _new()
    nkt = 48 if hyb else 32
    with contextlib.ExitStack() as st:
        k = KB(nc, st)
        hT = k.dram("hT", [D, TC], F32, kind="ExternalInput")
        yT = k.dram("yT", [D, TC], BF16, kind="ExternalInput")
        wl = k.dram("wl", [32, 128, nkt * 128], F32, kind="ExternalInput")
        g_l = k.dram("g_l", [128, 32], F32, kind="ExternalInput")
        glu = None
        if hyb:
            g_all = k.dram("g_all", [2048, TC], BF16, kind="ExternalInput")
            sg_all = k.dram("sg_all", [2048, TC], BF16, kind="ExternalInput")
            wglu_l = k.dram("wglu_l", [16, 128, 2048], F32, kind="ExternalInput")
            glu = (g_all, sg_all, wglu_l)
        outs = []
        if not final:
            hT_new = k.dram("hT_new", [D, TC], F32, kind="ExternalOutput")
            outs.append(hT_new)
        else:
            hT_new = k.dram("hT_new", [D, TC], F32)
        hnT = k.dram("hnT", [D, TC], F32 if final else BF16, kind="ExternalOutput")
        outs.append(hnT)
        stage_out(k, hT, yT, wl, g_l, hT_new, hnT, TC, nkt, False, F32 if final else BF16, glu=glu)
        k.final_wait("sp", outs)
        k.emit()
    return nc


def prog_hyb():
    nc = _new()
    T = T_ALL
    with contextlib.ExitStack() as st:
        k = KB(nc, st)
        hn_meta = k.dram("hn_meta", [128, 32, 16], BF16, kind="ExternalInput")
        hn_own = k.dram("hn_own", [(T - 16) // 256, 128, 32, 256], BF16, kind="ExternalInput")
        d = {n: k.dram(n, s, F32, kind="ExternalInput") for n, s in HYB_SHAPES.items()}
        yT = k.dram("yT", [512, T], BF16, kind="ExternalOutput")
        gT = k.dram("gT", [256, T], BF16, kind="ExternalOutput")
        sgT = k.dram("sgT", [256, T], BF16, kind="ExternalOutput")
        with k.scope():
            hyb_ssd(k, T, hn_meta, hn_own, d["w_ssd"], d["convw"], d["convb"], d["dtb_bc"], d["alog_bc"], d["d_bc"],
                    d["ng_bc"], d["Umat"], d["ident"], yT)
        with k.scope():
            hyb_s5(k, T, hn_meta, hn_own, d["w_s5"], d["bre"], d["bim"], d["cre"], d["cim"], d["are_l"], d["aim_l"],
                   d["ldt_l"], d["d_l"], gT, sgT)
        k.final_wait("sp", [yT, gT, sgT])
        k.emit()
    return nc


def prog_mla():
    nc = _new()
    T = T_ALL
    with contextlib.ExitStack() as st:
        k = KB(nc, st)
        hn_meta = k.dram("hn_meta", [128, 32, 16], BF16, kind="ExternalInput")
        hn_own = k.dram("hn_own", [(T - 16) // 256, 128, 32, 256], BF16, kind="ExternalInput")
        wq_in = k.dram("wq_in", [128, 32, 1024], F32, kind="ExternalInput")
        wkv_in = k.dram("wkv_in", [128, 32, 1152], F32, kind="ExternalInput")
        wuq = k.dram("wuq", [128, 8, 1024], F32, kind="ExternalInput")
        wukv = k.dram("wukv", [128, 4, 1024], F32, kind="ExternalInput")
        gq = k.dram("gq", [128, 8], F32, kind="ExternalInput")
        gkv = k.dram("gkv", [128, 4], F32, kind="ExternalInput")
        cos2 = k.dram("cos2", [64, T], F32, kind="ExternalInput")
        sin2s = k.dram("sin2s", [64, T], F32, kind="ExternalInput")
        yT = k.dram("yT", [512, T], BF16, kind="ExternalOutput")
        stage_mla(k, T, hn_meta, hn_own, wq_in, wkv_in, wuq, wukv, gq, gkv, cos2, sin2s, yT)
        k.final_wait("sp", [yT])
        k.emit()
    return nc


def _get(name, fn, *a):
    if name not in _PROGS:
        _PROGS[name] = fn(*a)
    return _PROGS[name]


def _run(nc, in_maps):
    res = run_bass_kernel_spmd(nc, in_maps, core_ids=list(range(NCORE)))
    return res.results


def _tok_idx(c):
    return np.concatenate([np.arange(16), 16 + 2048 * c + np.arange(2048)])


def _gather_tokens(per_core):
    return np.concatenate([per_core[0][:, 0:16]] + [per_core[c][:, 16:] for c in range(NCORE)], axis=1)


def _split_tokens(full):
    return [np.ascontiguousarray(full[:, _tok_idx(c)]) for c in range(NCORE)]


def kernel(x, meta, hyb_norm, hyb_w_in, ssd_conv_w, ssd_conv_b, ssd_dt_bias, ssd_a_log, ssd_d, ssd_norm,
           s5_a_re, s5_a_im, s5_log_dt, s5_b_re, s5_b_im, s5_c_re, s5_c_im, s5_d, s5_w_glu, hyb_w_out,
           mla_norm, mla_w_in, mla_q_norm, mla_w_uq, mla_kv_norm, mla_w_ukv, mla_w_out, final_norm):
    f32 = lambda a: np.asarray(a, dtype=np.float32)
    x, meta = f32(x), f32(meta)
    h_full_T = np.ascontiguousarray(np.concatenate([meta, x[0]], axis=0).T)
    hT = _split_tokens(h_full_T)
    del h_full_T
    U, ident = const_mats()
    cos2, sin2s = rope_tables(T_ALL)

    g0 = vec_tile(f32(hyb_norm[0]))
    res = _run(_get("norm", prog_norm), [dict(hT=hT[c], g_l=g0) for c in range(NCORE)])
    hn = [np.asarray(r["hnT"]) for r in res]

    for layer in range(4):
        i = layer // 2
        hn_l = hn_layout(_gather_tokens(hn))
        last = (layer == 3)
        if layer % 2 == 0:
            ims = []
            for c in range(NCORE):
                im = hyb_weights(c, f32(hyb_w_in[i]), f32(ssd_conv_w[i]), f32(ssd_conv_b[i]), f32(ssd_dt_bias[i]),
                                 f32(ssd_a_log[i]), f32(ssd_d[i]), f32(ssd_norm[i]), f32(s5_a_re[i]), f32(s5_a_im[i]),
                                 f32(s5_log_dt[i]), f32(s5_b_re[i]), f32(s5_b_im[i]), f32(s5_c_re[i]), f32(s5_c_im[i]),
                                 f32(s5_d[i]))
                im.update(Umat=U, ident=ident, **hn_l)
                ims.append(im)
            res = _run(_get("hyb", prog_hyb), ims)
            del ims
            y_all = _split_tokens(np.concatenate([np.asarray(r["yT"]) for r in res], axis=0))
            g_all = _split_tokens(np.concatenate([np.asarray(r["gT"]) for r in res], axis=0))
            sg_all = _split_tokens(np.concatenate([np.asarray(r["sgT"]) for r in res], axis=0))
            wl = out_w_tile(f32(hyb_w_out[i]))
            wglu_l = glu_w_tile(f32(s5_w_glu[i]))
            gn = vec_tile(f32(mla_norm[i]))
            ims = [dict(hT=hT[c], yT=y_all[c], wl=wl, g_l=gn, g_all=g_all[c], sg_all=sg_all[c], wglu_l=wglu_l)
                   for c in range(NCORE)]
            res = _run(_get("out_hyb", prog_out, True, False), ims)
        else:
            mw = None
            ims = []
            for c in range(NCORE):
                im = mla_weights(c, f32(mla_w_in[i]), f32(mla_q_norm[i]), f32(mla_w_uq[i]), f32(mla_kv_norm[i]),
                                 f32(mla_w_ukv[i]))
                im.update(cos2=cos2, sin2s=sin2s, **hn_l)
                ims.append(im)
            res = _run(_get("mla", prog_mla), ims)
            del ims
            y_all = _split_tokens(np.concatenate([np.asarray(r["yT"]) for r in res], axis=0))
            wl = out_w_tile(f32(mla_w_out[i]))
            gn = vec_tile(f32(final_norm) if last else f32(hyb_norm[i + 1]))
            ims = [dict(hT=hT[c], yT=y_all[c], wl=wl, g_l=gn) for c in range(NCORE)]
            res = _run(_get("out_mla_final" if last else "out_mla", prog_out, False, last), ims)
        del ims
        if not last:
            hT = [np.asarray(r["hT_new"]) for r in res]
        hn = [np.asarray(r["hnT"]) for r in res]

    out = np.concatenate([hn[c][:, 16:].T for c in range(NCORE)], axis=0)
    return np.ascontiguousarray(out[None].astype(np.float32))
```

```python
import contextlib
import math
import os
import numpy as np
import ml_dtypes


import concourse.bass as bass
import concourse.mybir as mybir

F32 = mybir.dt.float32
BF16 = mybir.dt.bfloat16
I32 = mybir.dt.int32
AF = mybir.ActivationFunctionType
ALU = mybir.AluOpType
AX = mybir.AxisListType

COMPUTE = ("pe", "act", "dve", "pool")


class View:
    __slots__ = ("tl", "ap")

    def __init__(self, tl, ap):
        self.tl = tl
        self.ap = ap

    def __getitem__(self, idx):
        return View(self.tl, self.ap[idx])

    def rearrange(self, pat, **kw):
        return View(self.tl, self.ap.rearrange(pat, **kw))

    def broadcast_to(self, shape):
        return View(self.tl, self.ap.broadcast_to(list(shape)))

    def unsqueeze(self, ax):
        return View(self.tl, self.ap.unsqueeze(ax))

    def partition_broadcast(self, n):
        return View(self.tl, self.ap.partition_broadcast(n))

    def bitcast(self, dt):
        return View(self.tl, self.ap.bitcast(dt))

    @property
    def shape(self):
        return self.ap.shape


class Tl:
    __slots__ = ("t", "name", "lw", "rd", "dsem", "dcnt", "is_dram", "is_psum")

    def __init__(self, t, name, is_dram=False, is_psum=False):
        self.is_psum = is_psum
        self.t = t
        self.name = name
        self.lw = {}
        self.rd = {}
        self.dsem = None
        self.dcnt = 0
        self.is_dram = is_dram

    def __getitem__(self, idx):
        return View(self, self.t[idx])

    def rearrange(self, pat, **kw):
        return View(self, self.t.rearrange(pat, **kw))

    @property
    def v(self):
        return View(self, self.t[:])


def _is_view(x):
    return isinstance(x, View)


class KB:
    def __init__(self, nc, stack):
        self.nc = nc
        self.stack = stack
        self.root = stack
        self.lists = {e: [] for e in ("pe", "act", "dve", "pool", "sp")}
        self.psem = {}
        self.pcnt = {}
        for e in COMPUTE:
            self.psem[e] = stack.enter_context(nc.semaphore("prog_" + e))
            self.pcnt[e] = 0
        self.known = {e: {} for e in self.lists}
        self.cinst = {e: [] for e in COMPUTE}
        self.ntile = 0
        self.tiles = []
        self.sem_pool = []
        self.n_sem = 4
        self.n_inst = 0
        self.n_wait = 0

    def sb(self, shape, dt, name=None):
        self.ntile += 1
        name = name or f"t{self.ntile}"
        t = self.stack.enter_context(self.nc.sbuf_tensor(name, list(shape), dt))
        tl = Tl(t, name)
        self.tiles.append(tl)
        return tl

    def ps(self, shape, dt, name=None):
        self.ntile += 1
        name = name or f"p{self.ntile}"
        t = self.stack.enter_context(self.nc.psum_tensor(name, list(shape), dt))
        return Tl(t, name, is_psum=True)

    def dram(self, name, shape, dt, kind="Internal", **kw):
        t = self.nc.dram_tensor(name, list(shape), dt, kind=kind, **kw)
        tl = Tl(t.ap(), name, is_dram=True)
        self.tiles.append(tl)
        return tl

    def _need(self, eng, ev, waits):
        if ev is None:
            return
        if ev[0] == "c":
            _, src, idx = ev
            if src == "pe" and eng == "pe":
                return
            key = ("c", src)
        else:
            _, sem, idx = ev
            key = id(sem)
        kn = self.known[eng]
        if kn.get(key, 0) >= idx:
            return
        kn[key] = idx
        if ev[0] == "c":
            self.cinst[src][idx - 1][4] = True
        waits[key] = ev

    def _deps(self, eng, reads, writes):
        waits = {}
        for t in reads:
            for ev in t.lw.values():
                self._need(eng, ev, waits)
            if t.is_psum:
                for ev in t.rd.values():
                    if not (ev[0] == "c" and ev[1] == eng):
                        self._need(eng, ev, waits)
        for t in writes:
            for ev in t.lw.values():
                self._need(eng, ev, waits)
            for ev in t.rd.values():
                self._need(eng, ev, waits)
        return list(waits.values())

    @staticmethod
    def _evkey(ev):
        return ("c", ev[1]) if ev[0] == "c" else id(ev[1])

    def _record(self, ev, reads, writes):
        key = self._evkey(ev)
        for t in reads:
            t.rd[key] = ev
        for t in writes:
            if t.is_dram and ev[0] == "d":
                t.lw[key] = ev
            else:
                t.lw = {key: ev}
            t.rd = {}

    def op(self, eng, fn, reads=(), writes=(), inc=True):
        reads = [r.tl if _is_view(r) else r for r in reads]
        writes = [w.tl if _is_view(w) else w for w in writes]
        waits = self._deps(eng, reads, writes)
        ent = [waits, fn, "c", eng, False]
        self.cinst[eng].append(ent)
        ev = ("c", eng, len(self.cinst[eng]))
        self.lists[eng].append(ent)
        self._record(ev, reads, writes)
        self.n_inst += 1
        self.n_wait += len(waits)

    def dma(self, q, out, in_, sem_tile=None, **kw):
        reads = [in_.tl]
        writes = [out.tl]
        waits = self._deps(q, reads, writes)
        st = sem_tile or (in_.tl if out.tl.is_dram and not in_.tl.is_dram else out.tl)
        if st.dsem is None:
            st.dsem, st.dcnt = self.get_sem("d_" + st.name)
        st.dcnt += 16
        ev = ("d", st.dsem, st.dcnt)
        oap, iap = out.ap, in_.ap
        self.lists[q].append([waits, lambda e: e.dma_start(out=oap, in_=iap, **kw), "d", st.dsem, True])
        self._record(ev, reads, writes)
        self.n_inst += 1
        self.n_wait += len(waits)

    def collective(self, kind, out, in_, op=None):
        reads = [in_.tl]
        writes = [out.tl]
        waits = self._deps("pool", reads, writes)
        st = out.tl
        if st.dsem is None:
            st.dsem, st.dcnt = self.get_sem("c_" + st.name)
        st.dcnt += 16
        ev = ("d", st.dsem, st.dcnt)
        oap, iap = out.ap, in_.ap
        aop = op if op is not None else ALU.bypass
        groups = [list(range(8))]
        self.lists["pool"].append([waits, lambda e: e.collective_compute(kind, aop, replica_groups=groups, ins=[iap], outs=[oap]), "d", st.dsem, True])
        self._record(ev, reads, writes)
        self.n_inst += 1

    def get_sem(self, name):
        if self.sem_pool:
            return self.sem_pool.pop()
        self.n_sem += 1
        return self.root.enter_context(self.nc.semaphore(name)), 0

    @contextlib.contextmanager
    def scope(self):
        old_stack, old_tiles = self.stack, self.tiles
        with contextlib.ExitStack() as st:
            self.stack = st
            self.tiles = []
            yield
            self.tiles = old_tiles + self.tiles
            self.barrier()
            new = self.tiles[len(old_tiles):]
            self.release([t for t in new if not t.is_dram])
            self.tiles = old_tiles + [t for t in new if t.is_dram]
            self.stack = old_stack

    def release(self, tiles):
        for t in tiles:
            if t.dsem is not None:
                self.sem_pool.append((t.dsem, t.dcnt))
                t.dsem = None

    def barrier(self):
        evs = [("c", e, len(self.cinst[e])) for e in COMPUTE if self.cinst[e]]
        evs += [("d", s, c) for (s, c) in self.all_dsems() if c > 0]
        for eng in self.lists:
            waits = {}
            for ev in evs:
                if ev[0] == "c" and ev[1] == eng:
                    continue
                saved = None
                if ev[0] == "c" and ev[1] == "pe" and eng == "pe":
                    continue
                self._need(eng, ev, waits)
            if waits:
                self.lists[eng].append([list(waits.values()), None, None, None, False])

    def all_dsems(self):
        out = [(t.dsem, t.dcnt) for t in self.tiles if t.dsem is not None]
        out += list(self.sem_pool)
        return out

    def final_wait(self, eng, tiles):
        waits = {}
        for t in tiles:
            for ev in t.lw.values():
                self._need(eng, ev, waits)
        self.lists[eng].append([list(waits.values()), None, None, None, False])

    def matmul(self, out, lhsT, rhs, start=True, stop=True):
        o, l, r = out.ap, lhsT.ap, rhs.ap
        self.op("pe", lambda e: e.matmul(o, lhsT=l, rhs=r, start=start, stop=stop), [lhsT, rhs], [out], inc=bool(stop))

    def transpose(self, out, in_, ident):
        o, i, d = out.ap, in_.ap, ident.ap
        self.op("pe", lambda e: e.transpose(o, i, d), [in_, ident], [out])

    def act(self, out, in_, func, bias=None, scale=None, accum_out=None):
        o, i = out.ap, in_.ap
        kw = {}
        rd = [in_]
        wr = [out]
        if bias is not None:
            if _is_view(bias):
                rd.append(bias)
                kw["bias"] = bias.ap
            else:
                kw["bias"] = bias
        if scale is not None:
            if _is_view(scale):
                rd.append(scale)
                kw["scale"] = scale.ap
            else:
                kw["scale"] = scale
        if accum_out is not None:
            wr.append(accum_out)
            kw["accum_out"] = accum_out.ap
        self.op("act", lambda e: e.activation(out=o, in_=i, func=func, **kw), rd, wr)

    def tt(self, out, in0, in1, op, eng="dve"):
        o, a, b = out.ap, in0.ap, in1.ap
        self.op(eng, lambda e: e.tensor_tensor(out=o, in0=a, in1=b, op=op), [in0, in1], [out])

    def ts(self, out, in0, s1, op0, s2=None, op1=None, eng="dve", accum_out=None):
        o, a = out.ap, in0.ap
        rd = [in0]
        wr = [out]
        a1 = s1
        a2 = s2
        if _is_view(s1):
            rd.append(s1)
            a1 = s1.ap
        if _is_view(s2):
            rd.append(s2)
            a2 = s2.ap
        kw = {}
        if op1 is not None:
            kw["op1"] = op1
        if accum_out is not None:
            wr.append(accum_out)
            kw["accum_out"] = accum_out.ap
        self.op(eng, lambda e: e.tensor_scalar(out=o, in0=a, scalar1=a1, scalar2=a2, op0=op0, **kw), rd, wr)

    def stt(self, out, in0, scalar, in1, op0, op1):
        o, a, b = out.ap, in0.ap, in1.ap
        rd = [in0, in1]
        s = scalar
        if _is_view(scalar):
            rd.append(scalar)
            s = scalar.ap
        self.op("dve", lambda e: e.scalar_tensor_tensor(out=o, in0=a, scalar=s, in1=b, op0=op0, op1=op1), rd, [out])

    def copy(self, out, in_, eng="dve"):
        o, i = out.ap, in_.ap
        if eng == "act":
            self.op("act", lambda e: e.copy(out=o, in_=i), [in_], [out])
        else:
            self.op(eng, lambda e: e.tensor_copy(out=o, in_=i), [in_], [out])

    def memset(self, out, val, eng="pool"):
        o = out.ap
        self.op(eng, lambda e: e.memset(o, val), [], [out])

    def recip(self, out, in_):
        o, i = out.ap, in_.ap
        self.op("dve", lambda e: e.reciprocal(out=o, in_=i), [in_], [out])

    def scan(self, out, d0, d1, initial, op0=ALU.mult, op1=ALU.add):
        o, a, b = out.ap, d0.ap, d1.ap
        rd = [d0, d1]
        ini = initial
        if _is_view(initial):
            rd.append(initial)
            ini = initial.ap
        self.op("dve", lambda e: e.tensor_tensor_scan(out=o, data0=a, data1=b, initial=ini, op0=op0, op1=op1), rd, [out])

    def reduce(self, out, in_, op, axis=AX.X):
        o, i = out.ap, in_.ap
        self.op("dve", lambda e: e.tensor_reduce(out=o, in_=i, axis=axis, op=op), [in_], [out])

    def emit(self):
        nc = self.nc
        lists = self.lists
        cum = {}
        for eng in COMPUTE:
            c = 0
            arr = []
            for ent in self.cinst[eng]:
                if ent[4]:
                    c += 1
                arr.append(c)
            cum[eng] = arr
        self.n_marked = {e: (cum[e][-1] if cum[e] else 0) for e in COMPUTE}
        psem = self.psem

        def run(e, items):
            for ent in items:
                waits, fn, kind, who, mark = ent
                for ev in waits:
                    if ev[0] == "c":
                        e.wait_ge(psem[ev[1]], cum[ev[1]][ev[2] - 1])
                    else:
                        e.wait_ge(ev[1], ev[2])
                if fn is None:
                    continue
                if kind == "d":
                    fn(e).then_inc(who, 16)
                elif mark:
                    fn(e).then_inc(psem[who], 1)
                else:
                    fn(e)

        with nc.Block() as block:
            @block.tensor
            def _(e):
                run(e, lists["pe"])

            @block.scalar
            def _(e):
                run(e, lists["act"])

            @block.vector
            def _(e):
                run(e, lists["dve"])

            @block.gpsimd
            def _(e):
                run(e, lists["pool"])

            @block.sync
            def _(e):
                run(e, lists["sp"])


EPS = 1e-6
D = 4096
NDT = 32


def col_groups(Tc, gmax=1024):
    groups = []
    s = 0
    while s < Tc:
        e = min(s + gmax, Tc)
        if 0 < Tc - e < 64:
            e = Tc
        groups.append((s, e))
        s = e
    return groups


def stage_out(k, hT, yT, wl, g_l, hT_new, hnT, Tc, nkt, first, out_dt, pfx="o", glu=None):
    ones = k.sb([128, 128], F32, pfx + "ones")
    k.memset(ones.v, 1.0)
    gt = k.sb([128, NDT], F32, pfx + "g")
    k.dma("sp", gt.v, g_l.v)
    GW = 528 if glu is not None else 1040
    GMAX = 512 if glu is not None else 1024
    if not first:
        yb = k.sb([128, nkt, GW], BF16, pfx + "yb")
        KH = nkt // 2
        NST = 3 if glu is not None else 4
        wst = [k.sb([128, KH * 128], F32, pfx + f"wst{i}") for i in range(NST)]
        wbf = [k.sb([128, nkt * 128], BF16, pfx + f"wbf{i}") for i in range(2)]
        acc = [k.ps([128, 512], F32, pfx + f"acc{i}") for i in range(4)]
        yTv = yT.rearrange("(kt p) t -> p kt t", p=128)
    ssq = [k.ps([128, 512], F32, pfx + f"ssq{i}") for i in range(3)]
    hin = [k.sb([128, 512], F32, pfx + f"hin{i}") for i in range(3)]
    hnw = [k.sb([128, 512], F32, pfx + f"hnw{i}") for i in range(3)]
    sq = [k.sb([128, 512], F32, pfx + f"sq{i}") for i in range(2)]
    rstd = k.sb([128, GW], F32, pfx + "rstd")
    hno = [k.sb([128, 512], out_dt, pfx + f"hno{i}") for i in range(3)]
    hsrc = hT if first else hT_new
    u = 0
    wcnt = 0
    if glu is not None:
        g_all, sg_all, wglu_l = glu
        gb = k.sb([128, 16, GW], BF16, pfx + "gb")
        sgb = k.sb([128, 16, GW], BF16, pfx + "sgb")
        gst = [k.sb([128, 2048], F32, pfx + f"gst{i}") for i in range(2)]
        gwb = [k.sb([128, 2048], BF16, pfx + f"gwb{i}") for i in range(2)]
        sig = [k.sb([128, 512], F32, pfx + f"sig{i}") for i in range(2)]
        gv = g_all.rearrange("(kt p) t -> p kt t", p=128)
        sgv = sg_all.rearrange("(kt p) t -> p kt t", p=128)
        gcnt = 0
    for (c0, c1) in col_groups(Tc, GMAX):
        gw = c1 - c0
        chunks = [(s, min(s + 512, c1)) for s in range(c0, c1, 512)]
        assert len(chunks) <= 3 and gw <= GW
        if not first:
            nkt_y = nkt - 16 if glu is not None else nkt
            for kt0 in range(0, nkt_y, 8):
                k.dma("sp", yb[:, kt0:kt0 + 8, 0:gw], yTv[:, kt0:kt0 + 8, c0:c1])
        if glu is not None:
            for kt0 in range(0, 16, 8):
                k.dma("sp", gb[:, kt0:kt0 + 8, 0:gw], gv[:, kt0:kt0 + 8, c0:c1])
                k.dma("sp", sgb[:, kt0:kt0 + 8, 0:gw], sgv[:, kt0:kt0 + 8, c0:c1])
            for mt in range(16):
                gs, gw_ = gst[gcnt % 2], gwb[gcnt % 2]
                k.dma("act", gs.v, wglu_l[mt])
                k.copy(gw_.v, gs.v, eng="pool")
                gcnt += 1
                for ci, (s0, s1) in enumerate(chunks):
                    n = s1 - s0
                    ac = acc[(mt * 2 + ci) % 4]
                    for kt in range(16):
                        k.matmul(ac[:, 0:n], gw_[:, kt * 128:(kt + 1) * 128], gb[:, kt, s0 - c0:s1 - c0], start=(kt == 0), stop=(kt == 15))
                    sg_ = sig[(mt * 2 + ci) % 2]
                    k.act(sg_[:, 0:n], ac[:, 0:n], AF.Sigmoid)
                    k.tt(sg_[:, 0:n], sg_[:, 0:n], gb[:, mt, s0 - c0:s1 - c0], ALU.mult)
                    k.tt(yb[:, 32 + mt, s0 - c0:s1 - c0], sg_[:, 0:n], sgb[:, mt, s0 - c0:s1 - c0], ALU.mult, eng="pool")
        for d in range(NDT):
            if not first:
                wb = wbf[wcnt % 2]
                for hh in range(2):
                    ws = wst[(2 * wcnt + hh) % NST]
                    k.dma("act" if hh == 0 else "sp", ws.v, wl[d, :, hh * KH * 128:(hh + 1) * KH * 128])
                    k.copy(wb[:, hh * KH * 128:(hh + 1) * KH * 128], ws.v, eng="pool")
                wcnt += 1
            for ci, (s0, s1) in enumerate(chunks):
                n = s1 - s0
                hi = hin[u % 3]
                hw = hnw[u % 3]
                sqt = sq[u % 2]
                k.dma("sp", hi[:, 0:n], hT[d * 128:(d + 1) * 128, s0:s1])
                if not first:
                    ac = acc[u % 4]
                    for kt in range(nkt):
                        k.matmul(ac[:, 0:n], wb[:, kt * 128:(kt + 1) * 128], yb[:, kt, s0 - c0:s1 - c0],
                                 start=(kt == 0), stop=(kt == nkt - 1))
                    k.tt(hw[:, 0:n], ac[:, 0:n], hi[:, 0:n], ALU.add)
                    k.dma("sp", hT_new[d * 128:(d + 1) * 128, s0:s1], hw[:, 0:n])
                    src = hw
                else:
                    src = hi
                k.act(sqt[:, 0:n], src[:, 0:n], AF.Square)
                k.matmul(ssq[ci][:, 0:n], ones.v, sqt[:, 0:n], start=(d == 0), stop=(d == NDT - 1))
                u += 1
        for ci, (s0, s1) in enumerate(chunks):
            n = s1 - s0
            k.ts(rstd[:, s0 - c0:s1 - c0], ssq[ci][:, 0:n], 1.0 / D, ALU.mult, EPS, ALU.add)
            k.act(rstd[:, s0 - c0:s1 - c0], rstd[:, s0 - c0:s1 - c0], AF.Sqrt)
            k.recip(rstd[:, s0 - c0:s1 - c0], rstd[:, s0 - c0:s1 - c0])
        for d in range(NDT):
            for ci, (s0, s1) in enumerate(chunks):
                n = s1 - s0
                hi = hin[u % 3]
                ho = hno[u % 3]
                k.dma("sp", hi[:, 0:n], hsrc[d * 128:(d + 1) * 128, s0:s1])
                k.stt(ho[:, 0:n], hi[:, 0:n], gt[:, d:d + 1], rstd[:, s0 - c0:s1 - c0], ALU.mult, ALU.mult)
                k.dma("act", hnT[d * 128:(d + 1) * 128, s0:s1], ho[:, 0:n])
                u += 1

DBG_NB = int(os.environ.get('DBG_NB', '0'))
DBG_SKIP = os.environ.get('DBG_SKIP', '')
DBG_START = int(os.environ.get('DBG_START', '0'))

EPS = 1e-6
NH = 4
QSCALE = 192 ** -0.5


def tok_blocks(T, bs=512):
    assert (T - 16) % bs == 0
    return [(0, 16)] + [(s, s + bs) for s in range(16, T, bs)]


def load_cast(k, dst, src_dram, nkt, ncols, stg, q="act", ceng="pool"):
    if 'lc' in DBG_SKIP:
        k.memset(dst.v, 0.01)
        return
    for kt in range(nkt):
        s = stg[kt % len(stg)]
        k.dma(q, s[:, 0:ncols], src_dram[:, kt, :])
        k.copy(dst[:, kt, :], s[:, 0:ncols], eng=ceng)


def rstd_from_ssq(k, rstd, ssq, n, dim):
    k.ts(rstd[:, 0:n], ssq[:, 0:n], 1.0 / dim, ALU.mult, EPS, ALU.add)
    k.act(rstd[:, 0:n], rstd[:, 0:n], AF.Sqrt)
    k.recip(rstd[:, 0:n], rstd[:, 0:n])


BS_A = 256


def _a_common(k, pfx):
    ones = k.sb([128, 128], BF16, pfx + "ones")
    k.memset(ones.v, 1.0)
    stg = [k.sb([128, 1152], F32, pfx + f"stg{i}") for i in range(2)]
    hb = [k.sb([128, 32, BS_A], BF16, pfx + f"hb{i}") for i in range(2)]
    if os.environ.get("HB1"): hb = [hb[0], hb[0]]
    cst = [k.sb([128, BS_A], F32, pfx + f"cos{i}") for i in range(2)]
    snt = [k.sb([128, BS_A], F32, pfx + f"sin{i}") for i in range(2)]
    rstd = k.sb([128, BS_A], F32, pfx + "rstd")
    acc = [k.ps([128, 512], F32, pfx + f"acc{i}") for i in range(3)]
    ssq = k.ps([128, 512], F32, pfx + "ssq")
    up = [k.ps([128, 512], F32, pfx + f"up{i}") for i in range(3)]
    sqb = [k.sb([128, BS_A], BF16, pfx + f"sqb{i}") for i in range(2)]
    ra = [k.sb([128, BS_A], F32, pfx + f"ra{i}") for i in range(2)]
    rb = [k.sb([128, BS_A], F32, pfx + f"rb{i}") for i in range(2)]
    ob = [k.sb([128, 512], BF16, pfx + f"ob{i}") for i in range(4)]
    return ones, stg, hb, cst, snt, rstd, acc, ssq, up, sqb, ra, rb, ob


def mla_a1(k, T, hn_meta, hn_own, wq_in, wuq, gq, cos2, sin2s, qnT, qrT, pfx="m1"):
    blocks = tok_blocks(T, BS_A)
    hbm = k.sb([128, 32, 16], BF16, pfx + "hbm")
    ones, stg, hb, cst, snt, rstd, acc, ssq, up, sqb, ra, rb, ob = _a_common(k, pfx)
    oc = [0]

    def nob():
        oc[0] += 1
        return ob[oc[0] % 4]

    w1 = k.sb([128, 32, 1024], BF16, pfx + "w1")
    load_cast(k, w1, wq_in, 32, 1024, stg)
    wu = k.sb([128, 8, NH * 256], BF16, pfx + "wu")
    load_cast(k, wu, wuq, 8, NH * 256, stg)
    gqt = k.sb([128, 8], F32, pfx + "gq")
    if 'gq' not in DBG_SKIP:
        k.dma("sp", gqt.v, gq.v)
    cq = k.sb([128, 8, BS_A], F32, pfx + "cq")
    cqn = k.sb([128, 8, BS_A], BF16, pfx + "cqn")
    for bi, (t0, t1) in enumerate(blocks):
        if DBG_NB and bi >= DBG_NB:
            break
        if bi < DBG_START:
            continue
        n = t1 - t0
        if bi == 0:
            h = hbm
            k.dma("sp", h.v, hn_meta.v)
        else:
            h = hb[bi % 2]
            k.dma(os.environ.get("HQ", "sp"), h.v, hn_own[bi - 1])
        ct, sn = cst[bi % 2], snt[bi % 2]
        if 'cs' not in DBG_SKIP:
            _q = os.environ.get("CSQ", "sp")
            _o = 0 if os.environ.get("CS0") else t0
            k.dma(_q, ct[0:64, 0:n], cos2[:, _o:_o + n])
            k.dma(_q, sn[0:64, 0:n], sin2s[:, _o:_o + n])
        for m in range(8):
            a = acc[m % 3]
            for kt in range(32):
                k.matmul(a[:, 0:n], w1[:, kt, m * 128:(m + 1) * 128], h[:, kt, 0:n], start=(kt == 0), stop=(kt == 31))
            sq = sqb[m % 2]
            if 'sq' not in DBG_SKIP:
                k.act(sq[:, 0:n], a[:, 0:n], AF.Square)
            if 'cp' not in DBG_SKIP:
                k.copy(cq[:, m, 0:n], a[:, 0:n], eng=os.environ.get("CPENG","dve"))
            if 'ssq' not in DBG_SKIP:
                k.matmul(ssq[:, 0:n], ones.v, sq[:, 0:n], start=(m == 0), stop=(m == 7))
        if 'rstd' not in DBG_SKIP:
            rstd_from_ssq(k, rstd, ssq, n, 1024)
        for m in range(8):
            if 'stt' in DBG_SKIP:
                break
            k.stt(cqn[:, m, 0:n], cq[:, m, 0:n], gqt[:, m:m + 1], rstd[:, 0:n], ALU.mult, ALU.mult)
        for hd in range(NH):
            if 'up' in DBG_SKIP:
                break
            c0 = hd * 256
            u0 = up[0]
            for kt in range(8):
                k.matmul(u0[:, 0:n], wu[:, kt, c0:c0 + 128], cqn[:, kt, 0:n], start=(kt == 0), stop=(kt == 7))
            o = nob()
            k.act(o[:, 0:n], u0[:, 0:n], AF.Copy, scale=QSCALE)
            k.dma("act", qnT[hd, :, t0:t1], o[:, 0:n])
            u1, u2 = up[1], up[2]
            for kt in range(8):
                k.matmul(u1[0:64, 0:n], wu[:, kt, c0 + 128:c0 + 192], cqn[:, kt, 0:n], start=(kt == 0), stop=(kt == 7))
            for kt in range(8):
                k.matmul(u2[0:64, 0:n], wu[:, kt, c0 + 192:c0 + 256], cqn[:, kt, 0:n], start=(kt == 0), stop=(kt == 7))
            a_, b_ = ra[hd % 2], rb[hd % 2]
            k.tt(a_[0:64, 0:n], u1[0:64, 0:n], ct[0:64, 0:n], ALU.mult)
            k.tt(b_[0:64, 0:n], u2[0:64, 0:n], sn[0:64, 0:n], ALU.mult)
            o = nob()
            k.tt(a_[0:64, 0:n], a_[0:64, 0:n], b_[0:64, 0:n], ALU.add, eng="pool")
            k.act(o[0:64, 0:n], a_[0:64, 0:n], AF.Copy, scale=QSCALE)
            k.dma("act", qrT[hd, :, t0:t1], o[0:64, 0:n])


def mla_a2(k, T, hn_meta, hn_own, wkv_in, wukv, gkv, cos2, sin2s, knT, krT, vtok, gT, pfx="m2"):
    blocks = tok_blocks(T, BS_A)
    hbm = k.sb([128, 32, 16], BF16, pfx + "hbm")
    ones, stg, hb, cst, snt, rstd, acc, ssq, up, sqb, ra, rb, ob = _a_common(k, pfx)
    oc = [0]

    def nob():
        oc[0] += 1
        return ob[oc[0] % 4]

    w2 = k.sb([128, 32, 1152], BF16, pfx + "w2")
    load_cast(k, w2, wkv_in, 32, 1152, stg)
    wk = k.sb([128, 4, 1024], BF16, pfx + "wk")
    load_cast(k, wk, wukv, 4, 1024, stg)
    gkt = k.sb([128, 4], F32, pfx + "gk")
    k.dma("sp", gkt.v, gkv.v)
    ckv = k.sb([128, 4, BS_A], F32, pfx + "ckv")
    ckn = k.sb([128, 4, BS_A], BF16, pfx + "ckn")
    for bi, (t0, t1) in enumerate(blocks):
        n = t1 - t0
        if bi == 0:
            h = hbm
            k.dma("sp", h.v, hn_meta.v)
        else:
            h = hb[bi % 2]
            k.dma(os.environ.get("HQ", "sp"), h.v, hn_own[bi - 1])
        ct, sn = cst[bi % 2], snt[bi % 2]
        k.dma("sp", ct[0:64, 0:n], cos2[:, t0:t1])
        k.dma("sp", sn[0:64, 0:n], sin2s[:, t0:t1])
        for m in range(4):
            a = acc[m % 3]
            for kt in range(32):
                k.matmul(a[:, 0:n], w2[:, kt, m * 128:(m + 1) * 128], h[:, kt, 0:n], start=(kt == 0), stop=(kt == 31))
            sq = sqb[m % 2]
            k.act(sq[:, 0:n], a[:, 0:n], AF.Square)
            k.copy(ckv[:, m, 0:n], a[:, 0:n], eng="dve")
            k.matmul(ssq[:, 0:n], ones.v, sq[:, 0:n], start=(m == 0), stop=(m == 3))
        rstd_from_ssq(k, rstd, ssq, n, 512)
        for m in range(4):
            k.stt(ckn[:, m, 0:n], ckv[:, m, 0:n], gkt[:, m:m + 1], rstd[:, 0:n], ALU.mult, ALU.mult)
        u1, u2 = up[1], up[2]
        for kt in range(32):
            k.matmul(u1[0:64, 0:n], w2[:, kt, 512:576], h[:, kt, 0:n], start=(kt == 0), stop=(kt == 31))
        for kt in range(32):
            k.matmul(u2[0:64, 0:n], w2[:, kt, 576:640], h[:, kt, 0:n], start=(kt == 0), stop=(kt == 31))
        a_, b_ = ra[0], rb[0]
        k.tt(a_[0:64, 0:n], u1[0:64, 0:n], ct[0:64, 0:n], ALU.mult)
        k.tt(b_[0:64, 0:n], u2[0:64, 0:n], sn[0:64, 0:n], ALU.mult)
        o = nob()
        k.tt(o[0:64, 0:n], a_[0:64, 0:n], b_[0:64, 0:n], ALU.add)
        k.dma("act", krT[:, t0:t1], o[0:64, 0:n])
        for m in range(4):
            a = acc[m % 3]
            for kt in range(32):
                k.matmul(a[:, 0:n], w2[:, kt, 640 + m * 128:640 + (m + 1) * 128], h[:, kt, 0:n], start=(kt == 0), stop=(kt == 31))
            o = nob()
            k.act(o[:, 0:n], a[:, 0:n], AF.Silu)
            k.dma("act", gT[m * 128:(m + 1) * 128, t0:t1], o[:, 0:n])
        for hd in range(NH):
            u0 = up[0]
            for kt in range(4):
                k.matmul(u0[:, 0:n], wk[:, kt, hd * 128:(hd + 1) * 128], ckn[:, kt, 0:n], start=(kt == 0), stop=(kt == 3))
            o = nob()
            k.copy(o[:, 0:n], u0[:, 0:n], eng="act")
            k.dma("act", knT[hd, :, t0:t1], o[:, 0:n])
        for s0 in range(0, n, 128):
            ns = min(128, n - s0)
            a = acc[(s0 // 128) % 3]
            for kt in range(4):
                k.matmul(a[0:ns, :], ckn[:, kt, s0:s0 + ns], wk[:, kt, 512:1024], start=(kt == 0), stop=(kt == 3))
            o = nob()
            k.copy(o[0:ns, :], a[0:ns, :], eng="dve")
            kb = 0 if bi == 0 else 1 + (t0 + s0 - 16) // 128
            for hd in range(NH):
                k.dma("act", vtok[hd, 0:ns, kb, :], o[0:ns, hd * 128:(hd + 1) * 128])


def mla_phase_b(k, T, qnT, qrT, knT, krT, vtok, gT, yT, pfx="mb"):
    NB = (T - 16) // 512
    NKB = (T - 16) // 128
    onesf = k.sb([128, 128], F32, pfx + "onesf")
    k.memset(onesf.v, 1.0)
    kr = k.sb([128, T], BF16, pfx + "kr")
    k.dma("sp", kr[0:64, :], krT.v)
    kn = k.sb([128, T], BF16, pfx + "kn")
    vv = k.sb([128, NKB + 1, 128], BF16, pfx + "vv")
    qn = [k.sb([128, 512], BF16, pfx + f"qn{i}") for i in range(2)]
    qr = [k.sb([128, 512], BF16, pfx + f"qr{i}") for i in range(2)]
    gt = [k.sb([128, 512], BF16, pfx + f"gt{i}") for i in range(2)]
    pt = [k.sb([128, 512], BF16, pfx + f"pt{i}") for i in range(4)]
    sc = [k.ps([128, 512], F32, pfx + f"sc{i}") for i in range(4)]
    oT = [k.ps([128, 512], F32, pfx + f"oT{i}") for i in range(2)]
    dn = k.ps([128, 512], F32, pfx + "dn")
    dacc = [k.sb([128, 512], F32, pfx + f"dacc{i}") for i in range(2)]
    rden = [k.sb([128, 512], F32, pfx + f"rden{i}") for i in range(2)]
    yo = [k.sb([128, 512], F32, pfx + f"yo{i}") for i in range(2)]
    yb = [k.sb([128, 512], BF16, pfx + f"yb{i}") for i in range(2)]
    u = 0
    g = 0
    for hd in range(NH):
        k.dma("sp", kn.v, knT[hd, :, :])
        k.dma("act", vv.v, vtok[hd])
        groups = [(0, 16, -1)] + [(16 + 512 * i, 16 + 512 * (i + 1), i) for i in range(NB)]
        for (q0, q1, gi) in groups:
            n = q1 - q0
            qnt, qrt, gtt = qn[g % 2], qr[g % 2], gt[g % 2]
            o_ = oT[g % 2]
            k.dma("sp", qnt[:, 0:n], qnT[hd, :, q0:q1])
            k.dma("sp", qrt[0:64, 0:n], qrT[hd, :, q0:q1])
            k.dma("sp", gtt[:, 0:n], gT[hd * 128:(hd + 1) * 128, q0:q1])
            k.memset(dacc[0][:, 0:n], 0.0, eng="dve")
            k.memset(dacc[1][:, 0:n], 0.0, eng="pool")
            kbs = [(0, 16, 0, 0, False)]
            if gi >= 0:
                for j in range(4 * gi):
                    kbs.append((16 + 128 * j, 128, 1 + j, 0, False))
                for dgi in range(4):
                    j = 4 * gi + dgi
                    kbs.append((16 + 128 * j, 128, 1 + j, 128 * dgi, True))

            def scores(idx, uu):
                kc, nk, vb, qs, diag = kbs[idx]
                s_ = sc[uu % 4]
                k.matmul(s_[0:nk, qs:n], kn[:, kc:kc + nk], qnt[:, qs:n], start=True, stop=False)
                k.matmul(s_[0:nk, qs:n], kr[0:64, kc:kc + nk], qrt[0:64, qs:n], start=False, stop=True)

            scores(0, u)
            for idx, (kc, nk, vb, qs, diag) in enumerate(kbs):
                if idx + 1 < len(kbs):
                    scores(idx + 1, u + 1)
                s_ = sc[u % 4]
                p_ = pt[u % 4]
                k.act(p_[0:nk, qs:n], s_[0:nk, qs:n], AF.Exp)
                if diag:
                    k.memset(p_[64:128, qs:qs + 64], 0.0, eng="pool")
                last = (idx == len(kbs) - 1)
                k.matmul(o_[:, qs:n], vv[0:nk, vb, :], p_[0:nk, qs:n], start=(idx == 0), stop=last)
                da = dacc[u % 2]
                k.tt(da[0:nk, qs:n], da[0:nk, qs:n], p_[0:nk, qs:n], ALU.add, eng=("dve" if u % 2 == 0 else "pool"))
                u += 1
            k.matmul(dn[:, 0:n], onesf.v, dacc[0][:, 0:n], start=True, stop=False)
            k.matmul(dn[:, 0:n], onesf.v, dacc[1][:, 0:n], start=False, stop=True)
            rd, y1, y2 = rden[g % 2], yo[g % 2], yb[g % 2]
            k.recip(rd[:, 0:n], dn[:, 0:n])
            k.tt(y1[:, 0:n], o_[:, 0:n], rd[:, 0:n], ALU.mult)
            k.tt(y2[:, 0:n], y1[:, 0:n], gtt[:, 0:n], ALU.mult, eng="pool")
            k.dma("act", yT[hd * 128:(hd + 1) * 128, q0:q1], y2[:, 0:n])
            g += 1


def stage_mla(k, T, hn_meta, hn_own, wq_in, wkv_in, wuq, wukv, gq, gkv, cos2, sin2s, yT, pfx="ml", kind="Internal", phases="12b"):
    qnT = k.dram(pfx + "_qnT", [NH, 128, T], BF16, kind=kind)
    qrT = k.dram(pfx + "_qrT", [NH, 64, T], BF16, kind=kind)
    knT = k.dram(pfx + "_knT", [NH, 128, T], BF16, kind=kind)
    krT = k.dram(pfx + "_krT", [64, T], BF16, kind=kind)
    vtok = k.dram(pfx + "_vtok", [NH, 128, (T - 16) // 128 + 1, 128], BF16, kind=kind)
    gT = k.dram(pfx + "_gT", [512, T], BF16, kind=kind)
    if "1" in phases:
      with (contextlib.nullcontext() if os.environ.get("NOSCOPE") else k.scope()):
        mla_a1(k, T, hn_meta, hn_own, wq_in, wuq, gq, cos2, sin2s, qnT, qrT, pfx + "1")
    if "2" in phases:
      with k.scope():
        mla_a2(k, T, hn_meta, hn_own, wkv_in, wukv, gkv, cos2, sin2s, knT, krT, vtok, gT, pfx + "2")
    if "b" in phases:
      with k.scope():
        mla_phase_b(k, T, qnT, qrT, knT, krT, vtok, gT, yT, pfx + "b")
    return dict(qnT=qnT, qrT=qrT, knT=knT, krT=krT, vtok=vtok, gT=gT)


EPS = 1e-6
GELU_C = 1.5957691216057308


def hyb_ssd(k, T, hn_meta, hn_own, w_ssd, convw, convb, dtb_bc, alog_bc, d_bc, ng_bc, Umat, ident, yT, pfx="hs"):
    blocks = tok_blocks(T, BS_A)
    NW = 1288
    stg = [k.sb([128, NW], F32, pfx + f"stg{i}") for i in range(2)]
    w = k.sb([128, 32, NW], BF16, pfx + "w")
    load_cast(k, w, w_ssd, 32, NW, stg)
    hbm = k.sb([128, 32, 16], BF16, pfx + "hbm")
    hb = [k.sb([128, 32, BS_A], BF16, pfx + f"hb{i}") for i in range(2)]
    cw = k.sb([128, 6, 4], F32, pfx + "cw")
    cb = k.sb([128, 6], F32, pfx + "cb")
    k.dma("sp", cw.v, convw.v)
    k.dma("sp", cb.v, convb.v)
    dtb = k.sb([128, 8], F32, pfx + "dtb")
    aneg = k.sb([128, 8], F32, pfx + "aneg")
    dbc = k.sb([128, 512], F32, pfx + "dbc")
    ngb = k.sb([128, 512], F32, pfx + "ngb")
    U = k.sb([128, 128], F32, pfx + "U")
    idb = k.sb([128, 128], BF16, pfx + "idb")
    idf = k.sb([128, 128], F32, pfx + "idf")
    k.dma("sp", dtb.v, dtb_bc.v)
    k.dma("sp", aneg.v, alog_bc.v)
    k.dma("sp", dbc.v, d_bc.v)
    k.dma("sp", ngb.v, ng_bc.v)
    k.dma("sp", U.v, Umat.v)
    k.dma("sp", idf.v, ident.v)
    k.copy(idb.v, idf.v, eng="pool")
    k.act(aneg.v, aneg.v, AF.Exp)
    k.ts(aneg.v, aneg.v, -1.0, ALU.mult)
    ones = k.sb([128, 128], F32, pfx + "ones")
    k.memset(ones.v, 1.0)

    cin = [k.sb([128, 3 + BS_A], F32, pfx + f"cin{m}") for m in range(6)]
    for m in range(6):
        k.memset(cin[m].v, 0.0)
    cacc = [k.sb([128, BS_A], F32, pfx + f"cacc{i}") for i in range(2)]
    fT = [k.sb([128, BS_A], BF16, pfx + f"fT{m}") for m in range(6)]
    zs = k.sb([128, 512], F32, pfx + "zs")
    dt = k.sb([128, 8], F32, pfx + "dt")
    da = k.sb([128, 8], F32, pfx + "da")
    dab = k.sb([128, 8, 128], F32, pfx + "dab")
    acum = k.sb([128, 8], F32, pfx + "acum")
    nacum = k.sb([128, 8], F32, pfx + "nacum")
    aend = k.sb([128, 8], F32, pfx + "aend")
    eend = k.sb([128, 8], F32, pfx + "eend")
    eac = k.sb([128, 8], F32, pfx + "eac")
    dte = k.sb([128, 8], F32, pfx + "dte")
    xtok = k.sb([128, 512], BF16, pfx + "xtok")
    btok = k.sb([128, 128], BF16, pfx + "btok")
    xdt = k.sb([128, 512], BF16, pfx + "xdt")
    xw = k.sb([128, 512], BF16, pfx + "xw")
    segc = k.sb([128, 8, 128], F32, pfx + "segc")
    cbm = k.sb([128, 128], F32, pfx + "cbm")
    MT = k.sb([128, 8, 128], BF16, pfx + "MT")
    S = k.sb([128, 512], F32, pfx + "S")
    Sb = k.sb([128, 512], BF16, pfx + "Sb")
    k.memset(S.v, 0.0)
    k.memset(Sb.v, 0.0)
    t1 = k.sb([128, 512], F32, pfx + "t1")
    t2 = k.sb([128, 512], F32, pfx + "t2")
    ssq = k.sb([128, 1], F32, pfx + "ssq")
    yn = k.sb([128, 512], BF16, pfx + "yn")
    yTs = [k.sb([128, 128], BF16, pfx + f"yTs{i}") for i in range(4)]

    accA = k.ps([128, 512], F32, pfx + "accA")
    accB = k.ps([128, 512], F32, pfx + "accB")
    misc = k.ps([128, 512], F32, pfx + "misc")
    AB = k.ps([128, 8, 128], F32, pfx + "AB")
    ydg = k.ps([128, 512], F32, pfx + "ydg")
    yof = k.ps([128, 512], F32, pfx + "yof")
    tr = k.ps([128, 512], BF16, pfx + "tr")
    accs = [accA, accB]

    for bi, (t0, t1_) in enumerate(blocks):
        n = t1_ - t0
        if bi == 0:
            h = hbm
            k.dma("sp", h.v, hn_meta.v)
        else:
            h = hb[bi % 2]
            k.dma("sp", h.v, hn_own[bi - 1])
        for m in range(6):
            a = accs[m % 2]
            c0 = 512 + m * 128
            for kt in range(32):
                k.matmul(a[:, 0:n], w[:, kt, c0:c0 + 128], h[:, kt, 0:n], start=(kt == 0), stop=(kt == 31))
            ci = cin[m]
            k.copy(ci[:, 3:3 + n], a[:, 0:n], eng="act")
            ca = cacc[m % 2]
            k.ts(ca[:, 0:n], ci[:, 0:n], cw[:, m, 0:1], ALU.mult, cb[:, m:m + 1], ALU.add)
            for j in range(1, 4):
                k.stt(ca[:, 0:n], ci[:, j:j + n], cw[:, m, j:j + 1], ca[:, 0:n], ALU.mult, ALU.add)
            k.act(fT[m][:, 0:n], ca[:, 0:n], AF.Silu)
            k.copy(ci[:, 0:3], ci[:, n:n + 3], eng="pool")
        for s0 in range(0, n, 128):
            cl = min(128, n - s0)
            cs = slice(s0, s0 + cl)
            for kt in range(32):
                k.matmul(accA[0:cl, :], h[:, kt, cs], w[:, kt, 0:512], start=(kt == 0), stop=(kt == 31))
            k.act(zs[0:cl, :], accA[0:cl, :], AF.Silu)
            for kt in range(32):
                k.matmul(misc[0:cl, 0:8], h[:, kt, cs], w[:, kt, 1280:1288], start=(kt == 0), stop=(kt == 31))
            k.tt(dt[0:cl, :], misc[0:cl, 0:8], dtb[0:cl, :], ALU.add)
            k.act(dt[0:cl, :], dt[0:cl, :], AF.Exp)
            k.act(dt[0:cl, :], dt[0:cl, :], AF.Ln, bias=1.0)
            k.tt(da[0:cl, :], dt[0:cl, :], aneg[0:cl, :], ALU.mult)
            for m in range(4):
                k.transpose(tr[0:cl, m * 128:(m + 1) * 128], fT[m][:, cs], idb.v)
            k.copy(xtok[0:cl, :], tr[0:cl, :], eng="act")
            k.transpose(tr[0:cl, 0:128], fT[4][:, cs], idb.v)
            k.copy(btok[0:cl, :], tr[0:cl, 0:128], eng="act")
            k.tt(xdt[0:cl, :].rearrange("p (h d) -> p h d", h=8), xtok[0:cl, :].rearrange("p (h d) -> p h d", h=8),
                 dt[0:cl, :].unsqueeze(2).broadcast_to([cl, 8, 64]), ALU.mult)
            k.matmul(misc[0:cl, 8:16], U[0:cl, 0:cl], da[0:cl, :])
            k.copy(acum[0:cl, :], misc[0:cl, 8:16], eng="dve")
            k.ts(nacum[0:cl, :], acum[0:cl, :], -1.0, ALU.mult)
            k.act(eac[0:cl, :], acum[0:cl, :], AF.Exp)
            k.tt(dab[0:cl, :, 0:cl], ones[0:cl, 0:cl].unsqueeze(1).broadcast_to([cl, 8, cl]),
                 da[0:cl, :].unsqueeze(2).broadcast_to([cl, 8, cl]), ALU.mult, eng="pool")
            for hh in range(8):
                k.matmul(AB[0:cl, hh, 0:cl], dab[0:cl, hh, 0:cl], U[0:cl, 0:cl])
            k.tt(segc[0:cl, :, 0:cl], AB[0:cl, :, 0:cl], nacum[0:cl, :].unsqueeze(2).broadcast_to([cl, 8, cl]), ALU.add)
            k.copy(aend[0:cl, :], AB[0:cl, :, cl - 1], eng="dve")
            k.ts(segc[0:cl, :, 0:cl], segc[0:cl, :, 0:cl], 0.0, ALU.min, eng="pool")
            k.act(segc[0:cl, :, 0:cl], segc[0:cl, :, 0:cl], AF.Exp)
            k.matmul(misc[0:cl, 128:128 + cl], fT[4][:, cs], fT[5][:, cs])
            k.tt(cbm[0:cl, 0:cl], misc[0:cl, 128:128 + cl], U[0:cl, 0:cl], ALU.mult)
            k.tt(MT[0:cl, :, 0:cl], segc[0:cl, :, 0:cl], cbm[0:cl, 0:cl].unsqueeze(1).broadcast_to([cl, 8, cl]), ALU.mult, eng="pool")
            for hh in range(8):
                k.matmul(ydg[0:cl, hh * 64:(hh + 1) * 64], MT[0:cl, hh, 0:cl], xdt[0:cl, hh * 64:(hh + 1) * 64])
            k.matmul(yof[0:cl, :], fT[5][:, cs], Sb.v)
            k.tt(t1[0:cl, :].rearrange("p (h d) -> p h d", h=8), yof[0:cl, :].rearrange("p (h d) -> p h d", h=8),
                 eac[0:cl, :].unsqueeze(2).broadcast_to([cl, 8, 64]), ALU.mult)
            k.tt(t1[0:cl, :], t1[0:cl, :], ydg[0:cl, :], ALU.add)
            k.tt(t2[0:cl, :], xtok[0:cl, :], dbc[0:cl, :], ALU.mult, eng="pool")
            k.tt(t1[0:cl, :], t1[0:cl, :], t2[0:cl, :], ALU.add)
            k.tt(t1[0:cl, :], t1[0:cl, :], zs[0:cl, :], ALU.mult)
            k.act(t2[0:cl, :], t1[0:cl, :], AF.Square, accum_out=ssq[0:cl, :])
            k.ts(ssq[0:cl, :], ssq[0:cl, :], 1.0 / 512, ALU.mult, EPS, ALU.add)
            k.act(ssq[0:cl, :], ssq[0:cl, :], AF.Sqrt)
            k.recip(ssq[0:cl, :], ssq[0:cl, :])
            k.stt(yn[0:cl, :], t1[0:cl, :], ssq[0:cl, 0:1], ngb[0:cl, :], ALU.mult, ALU.mult)
            for m in range(4):
                k.transpose(tr[:, m * 128:m * 128 + cl], yn[0:cl, m * 128:(m + 1) * 128], idb[0:cl, 0:cl])
            for m in range(4):
                k.copy(yTs[m][:, 0:cl], tr[:, m * 128:m * 128 + cl], eng=("act" if m % 2 else "dve"))
                k.dma("act", yT[m * 128:(m + 1) * 128, t0 + s0:t0 + s0 + cl], yTs[m][:, 0:cl])
            k.ts(dte[0:cl, :], aend[0:cl, :], 1.0 / cl, ALU.mult)
            k.matmul(misc[:, 16:24], ones[0:cl, :], dte[0:cl, :])
            k.act(eend.v, misc[:, 16:24], AF.Exp)
            k.tt(dte[0:cl, :], aend[0:cl, :], acum[0:cl, :], ALU.subtract)
            k.act(dte[0:cl, :], dte[0:cl, :], AF.Exp)
            k.tt(xw[0:cl, :].rearrange("p (h d) -> p h d", h=8), xdt[0:cl, :].rearrange("p (h d) -> p h d", h=8),
                 dte[0:cl, :].unsqueeze(2).broadcast_to([cl, 8, 64]), ALU.mult)
            k.matmul(yof.v, btok[0:cl, :], xw[0:cl, :])
            k.tt(S.v.rearrange("p (h d) -> p h d", h=8), S.v.rearrange("p (h d) -> p h d", h=8),
                 eend.v.unsqueeze(2).broadcast_to([128, 8, 64]), ALU.mult)
            k.tt(S.v, S.v, yof.v, ALU.add)
            k.copy(Sb.v, S.v, eng="pool")


def hyb_s5(k, T, hn_meta, hn_own, w_s5, bre, bim, cre, cim, are_l, aim_l, ldt_l, d_l, gT, sgT, pfx="h5"):
    blocks = tok_blocks(T, BS_A)
    L = BS_A
    stg = [k.sb([128, 512], F32, pfx + f"stg{i}") for i in range(2)]
    w = k.sb([128, 32, 512], BF16, pfx + "w")
    load_cast(k, w, w_s5, 32, 512, stg)
    hbm = k.sb([128, 32, 16], BF16, pfx + "hbm")
    hb = [k.sb([128, 32, BS_A], BF16, pfx + f"hb{i}") for i in range(2)]
    f_bre = k.sb([128, 8, 128], F32, pfx + "fbre"); f_bim = k.sb([128, 8, 128], F32, pfx + "fbim")
    f_cre = k.sb([128, 8, 128], F32, pfx + "fcre"); f_cim = k.sb([128, 8, 128], F32, pfx + "fcim")
    Bre = k.sb([128, 8, 128], BF16, pfx + "Bre"); Bim = k.sb([128, 8, 128], BF16, pfx + "Bim")
    Cre = k.sb([128, 8, 128], BF16, pfx + "Cre"); Cim = k.sb([128, 8, 128], BF16, pfx + "Cim")
    for (dst, f, src) in ((Bre, f_bre, bre), (Bim, f_bim, bim), (Cre, f_cre, cre), (Cim, f_cim, cim)):
        k.dma("sp", f.v, src.v)
        k.copy(dst.v, f.v, eng="pool")
    are = k.sb([128, 8], F32, pfx + "are"); aim = k.sb([128, 8], F32, pfx + "aim"); dtt = k.sb([128, 8], F32, pfx + "dtt")
    dsk = k.sb([128, 2], F32, pfx + "dsk")
    k.dma("sp", are.v, are_l.v); k.dma("sp", aim.v, aim_l.v); k.dma("sp", dtt.v, ldt_l.v); k.dma("sp", dsk.v, d_l.v)
    k.act(dtt.v, dtt.v, AF.Exp)
    th = k.sb([128, 8], F32, pfx + "th"); rho = k.sb([128, 8], F32, pfx + "rho")
    k.tt(th.v, dtt.v, aim.v, ALU.mult)
    k.tt(rho.v, dtt.v, are.v, ALU.mult)
    k.act(rho.v, rho.v, AF.Exp)
    ki = k.sb([128, 8], I32, pfx + "ki"); kf = k.sb([128, 8], F32, pfx + "kf")
    hh_ = k.sb([128, 8], F32, pfx + "hh"); sh = k.sb([128, 8], F32, pfx + "sh"); ch = k.sb([128, 8], F32, pfx + "ch")
    k.ts(kf.v, th.v, 1.0 / (2 * math.pi), ALU.mult)
    k.copy(ki.v, kf.v, eng="dve")
    k.copy(kf.v, ki.v, eng="dve")
    k.stt(hh_.v, kf.v, -2 * math.pi, th.v, ALU.mult, ALU.add)
    k.ts(hh_.v, hh_.v, 0.5, ALU.mult)
    k.act(sh.v, hh_.v, AF.Sin)
    q4 = k.sb([128, 8], F32, pfx + "q4")
    k.act(q4.v, hh_.v, AF.Sin, scale=0.5)
    k.tt(q4.v, q4.v, q4.v, ALU.mult)
    k.ts(ch.v, q4.v, -2.0, ALU.mult, 1.0, ALU.add)
    zr = k.sb([128, 8, 9], F32, pfx + "zr"); zi = k.sb([128, 8, 9], F32, pfx + "zi"); nzi = k.sb([128, 8, 9], F32, pfx + "nzi")
    tmp8 = k.sb([128, 8], F32, pfx + "tmp8"); tmp8b = k.sb([128, 8], F32, pfx + "tmp8b")
    k.tt(tmp8.v, sh.v, sh.v, ALU.mult)
    k.ts(zr[:, :, 0], tmp8.v, -2.0, ALU.mult, 1.0, ALU.add)
    k.tt(tmp8.v, sh.v, ch.v, ALU.mult)
    k.ts(zi[:, :, 0], tmp8.v, 2.0, ALU.mult)
    for m_ in range(8):
        k.tt(tmp8.v, zr[:, :, m_], zr[:, :, m_], ALU.mult)
        k.tt(tmp8b.v, zi[:, :, m_], zi[:, :, m_], ALU.mult)
        k.tt(zr[:, :, m_ + 1], tmp8.v, tmp8b.v, ALU.subtract)
        k.tt(tmp8.v, zr[:, :, m_], zi[:, :, m_], ALU.mult)
        k.ts(zi[:, :, m_ + 1], tmp8.v, 2.0, ALU.mult)
    k.ts(nzi.v, zi.v, -1.0, ALU.mult)
    abr = k.sb([128, 8], F32, pfx + "abr"); abi = k.sb([128, 8], F32, pfx + "abi"); den = k.sb([128, 8], F32, pfx + "den")
    kre = k.sb([128, 8], F32, pfx + "kre"); kim = k.sb([128, 8], F32, pfx + "kim"); nkre = k.sb([128, 8], F32, pfx + "nkre")
    k.tt(abr.v, rho.v, zr[:, :, 0], ALU.mult)
    k.ts(abr.v, abr.v, -1.0, ALU.add)
    k.tt(abi.v, rho.v, zi[:, :, 0], ALU.mult)
    k.tt(den.v, are.v, are.v, ALU.mult)
    k.tt(tmp8.v, aim.v, aim.v, ALU.mult)
    k.tt(den.v, den.v, tmp8.v, ALU.add)
    k.recip(den.v, den.v)
    k.tt(kre.v, abr.v, are.v, ALU.mult)
    k.tt(tmp8.v, abi.v, aim.v, ALU.mult)
    k.tt(kre.v, kre.v, tmp8.v, ALU.add)
    k.tt(kre.v, kre.v, den.v, ALU.mult)
    k.tt(kim.v, abi.v, are.v, ALU.mult)
    k.tt(tmp8.v, abr.v, aim.v, ALU.mult)
    k.tt(kim.v, kim.v, tmp8.v, ALU.subtract)
    k.tt(kim.v, kim.v, den.v, ALU.mult)
    k.ts(nkre.v, kre.v, -1.0, ALU.mult)
    Fc = k.sb([128, 8, L], F32, pfx + "Fc"); Fs = k.sb([128, 8, L], F32, pfx + "Fs")
    Ere = k.sb([128, 8, L], F32, pfx + "Ere"); Eim = k.sb([128, 8, L], F32, pfx + "Eim")
    rhoT = k.sb([128, 8, L], F32, pfx + "rhoT")
    tl = k.sb([128, L], F32, pfx + "tl")
    k.memset(Fc.v, 1.0)
    k.memset(Fs.v, 0.0)
    k.memset(rhoT.v, 1.0)
    for j in range(8):
        for m_ in range(8):
            lo = slice(0, 2 ** m_)
            hi = slice(2 ** m_, 2 ** (m_ + 1))
            w_ = 2 ** m_
            k.ts(tl[:, 0:w_], Fs[:, j, lo], zi[:, j, m_:m_ + 1], ALU.mult)
            k.stt(Fc[:, j, hi], Fc[:, j, lo], zr[:, j, m_:m_ + 1], tl[:, 0:w_], ALU.mult, ALU.subtract)
            k.ts(tl[:, 0:w_], Fc[:, j, lo], zi[:, j, m_:m_ + 1], ALU.mult)
            k.stt(Fs[:, j, hi], Fs[:, j, lo], zr[:, j, m_:m_ + 1], tl[:, 0:w_], ALU.mult, ALU.add)
        k.ts(tl.v, Fs[:, j, :], kim[:, j:j + 1], ALU.mult)
        k.stt(Ere[:, j, :], Fc[:, j, :], kre[:, j:j + 1], tl.v, ALU.mult, ALU.add)
        k.ts(tl.v, Fs[:, j, :], nkre[:, j:j + 1], ALU.mult)
        k.stt(Eim[:, j, :], Fc[:, j, :], kim[:, j:j + 1], tl.v, ALU.mult, ALU.add)
        k.ts(rhoT[:, j, :], rhoT[:, j, :], rho[:, j:j + 1], ALU.mult)
    uf = [k.sb([128, BS_A], F32, pfx + f"uf{a}") for a in range(2)]
    ub = [k.sb([128, BS_A], BF16, pfx + f"ub{a}") for a in range(2)]
    go = [k.sb([128, BS_A], BF16, pfx + f"go{i}") for i in range(2)]
    vre = k.sb([128, BS_A], F32, pfx + "vre"); vim = k.sb([128, BS_A], F32, pfx + "vim")
    p1 = k.sb([128, BS_A], F32, pfx + "p1"); p2 = k.sb([128, BS_A], F32, pfx + "p2")
    p3 = k.sb([128, BS_A], F32, pfx + "p3"); p4 = k.sb([128, BS_A], F32, pfx + "p4")
    wre = k.sb([128, BS_A], F32, pfx + "wre"); wim = k.sb([128, BS_A], F32, pfx + "wim")
    sre = [k.sb([128, BS_A], BF16, pfx + f"sre{i}") for i in range(2)]
    sim_ = [k.sb([128, BS_A], BF16, pfx + f"sim{i}") for i in range(2)]
    wir = k.sb([128, 8], F32, pfx + "wir"); wii = k.sb([128, 8], F32, pfx + "wii")
    k.memset(wir.v, 0.0)
    k.memset(wii.v, 0.0)
    c1 = k.sb([128, 1], F32, pfx + "c1")
    x2 = k.sb([128, BS_A], F32, pfx + "x2"); x3 = k.sb([128, BS_A], F32, pfx + "x3"); yv = k.sb([128, BS_A], F32, pfx + "yv")
    acc = [k.ps([128, 512], F32, pfx + f"acc{i}") for i in range(2)]
    Pp = k.ps([128, 512], F32, pfx + "Pp"); Qp = k.ps([128, 512], F32, pfx + "Qp")
    yp = [k.ps([128, 512], F32, pfx + f"yp{a}") for a in range(2)]
    for bi, (t0, t1_) in enumerate(blocks):
        n = t1_ - t0
        if bi == 0:
            h = hbm
            k.dma("sp", h.v, hn_meta.v)
        else:
            h = hb[bi % 2]
            k.dma("sp", h.v, hn_own[bi - 1])
        lev = 4 if bi == 0 else 8
        for a in range(2):
            for kt in range(32):
                k.matmul(acc[0][:, 0:n], w[:, kt, a * 128:(a + 1) * 128], h[:, kt, 0:n], start=(kt == 0), stop=(kt == 31))
            k.copy(uf[a][:, 0:n], acc[0][:, 0:n], eng="dve")
            k.copy(ub[a][:, 0:n], uf[a][:, 0:n], eng="pool")
            for kt in range(32):
                k.matmul(acc[1][:, 0:n], w[:, kt, 256 + a * 128:256 + (a + 1) * 128], h[:, kt, 0:n], start=(kt == 0), stop=(kt == 31))
            o = go[a]
            k.act(o[:, 0:n], acc[1][:, 0:n], AF.Silu)
            k.dma("act", sgT[a * 128:(a + 1) * 128, t0:t1_], o[:, 0:n])
        for j in range(8):
            a, q = j // 4, (j % 4) * 32
            k.matmul(Pp[:, 0:n], Bre[:, j, :], ub[a][:, 0:n])
            k.matmul(Qp[:, 0:n], Bim[:, j, :], ub[a][:, 0:n])
            k.tt(p1[:, 0:n], Pp[:, 0:n], Ere[:, j, 0:n], ALU.mult)
            k.tt(p2[:, 0:n], Qp[:, 0:n], Eim[:, j, 0:n], ALU.mult)
            k.tt(p3[:, 0:n], Qp[:, 0:n], Ere[:, j, 0:n], ALU.mult)
            k.tt(p4[:, 0:n], Pp[:, 0:n], Eim[:, j, 0:n], ALU.mult)
            k.tt(vre[:, 0:n], p1[:, 0:n], p2[:, 0:n], ALU.subtract, eng="pool")
            k.tt(vim[:, 0:n], p3[:, 0:n], p4[:, 0:n], ALU.add, eng="pool")
            k.scan(wre[:, 0:n], rhoT[:, j, 0:n], vre[:, 0:n], wir[:, j:j + 1])
            k.scan(wim[:, 0:n], rhoT[:, j, 0:n], vim[:, 0:n], wii[:, j:j + 1])
            k.ts(c1.v, wim[:, n - 1:n], nzi[:, j, lev:lev + 1], ALU.mult)
            k.stt(wir[:, j:j + 1], wre[:, n - 1:n], zr[:, j, lev:lev + 1], c1.v, ALU.mult, ALU.add)
            k.ts(c1.v, wre[:, n - 1:n], zi[:, j, lev:lev + 1], ALU.mult)
            k.stt(wii[:, j:j + 1], wim[:, n - 1:n], zr[:, j, lev:lev + 1], c1.v, ALU.mult, ALU.add)
            sr, si = sre[j % 2], sim_[j % 2]
            k.tt(p1[:, 0:n], wre[:, 0:n], Fc[:, j, 0:n], ALU.mult, eng="pool")
            k.tt(p2[:, 0:n], wim[:, 0:n], Fs[:, j, 0:n], ALU.mult, eng="pool")
            k.tt(sr[:, 0:n], p1[:, 0:n], p2[:, 0:n], ALU.subtract, eng="pool")
            k.tt(p3[:, 0:n], wre[:, 0:n], Fs[:, j, 0:n], ALU.mult, eng="pool")
            k.tt(p4[:, 0:n], wim[:, 0:n], Fc[:, j, 0:n], ALU.mult, eng="pool")
            k.stt(si[:, 0:n], p3[:, 0:n], -1.0, p4[:, 0:n], ALU.mult, ALU.subtract)
            k.matmul(yp[a][:, 0:n], Cre[:, j, :], sr[:, 0:n], start=(j % 4 == 0), stop=False)
            k.matmul(yp[a][:, 0:n], Cim[:, j, :], si[:, 0:n], start=False, stop=(j % 4 == 3))
        for a in range(2):
            k.stt(yv[:, 0:n], uf[a][:, 0:n], dsk[:, a:a + 1], yp[a][:, 0:n], ALU.mult, ALU.add)
            k.tt(x2[:, 0:n], yv[:, 0:n], yv[:, 0:n], ALU.mult, eng="pool")
            k.ts(x2[:, 0:n], x2[:, 0:n], 0.044715, ALU.mult, 1.0, ALU.add, eng="pool")
            k.tt(x3[:, 0:n], x2[:, 0:n], yv[:, 0:n], ALU.mult, eng="pool")
            k.act(x3[:, 0:n], x3[:, 0:n], AF.Sigmoid, scale=GELU_C)
            o = go[a]
            k.tt(o[:, 0:n], x3[:, 0:n], yv[:, 0:n], ALU.mult)
            k.dma("act", gT[a * 128:(a + 1) * 128, t0:t1_], o[:, 0:n])


PERM = np.concatenate([np.arange(32, 64), np.arange(0, 32)])


def ktile(w):
    K, M = w.shape
    return np.ascontiguousarray(w.reshape(K // 128, 128, M).transpose(1, 0, 2))


def vec_tile(g):
    return np.ascontiguousarray(g.reshape(-1, 128).T)


def rope_tables(T):
    pos = np.arange(T, dtype=np.float32)
    inv_freq = (np.float32(10000.0) ** (-np.arange(0, 64, 2, dtype=np.float32) / np.float32(64))).astype(np.float32)
    ang = (pos[:, None] * inv_freq[None, :]).astype(np.float32)
    cos = np.cos(ang).astype(np.float32).T
    sin = np.sin(ang).astype(np.float32).T
    cos2 = np.ascontiguousarray(np.concatenate([cos, cos], 0))
    sin2s = np.ascontiguousarray(np.concatenate([-sin, sin], 0))
    return cos2, sin2s


def mla_weights(c, w_in, q_norm, w_uq, kv_norm, w_ukv):
    kr = w_in[:, 1536:1600]
    wkv = np.concatenate([w_in[:, 1024:1536], kr, kr[:, PERM], w_in[:, 1600 + c * 512:1600 + (c + 1) * 512]], 1)
    uq = []
    kn = []
    vv = []
    for h in range(4):
        b = (4 * c + h) * 192
        rope = w_uq[:, b + 128:b + 192]
        uq += [w_uq[:, b:b + 128], rope, rope[:, PERM]]
        b2 = (4 * c + h) * 256
        kn.append(w_ukv[:, b2:b2 + 128])
        vv.append(w_ukv[:, b2 + 128:b2 + 256])
    return dict(
        wq_in=ktile(w_in[:, 0:1024]),
        wkv_in=ktile(wkv),
        wuq=ktile(np.concatenate(uq, 1)),
        wukv=ktile(np.concatenate(kn + vv, 1)),
        gq=vec_tile(q_norm),
        gkv=vec_tile(kv_norm),
    )


def out_w_tile(W):
    K = W.shape[0]
    nkt = K // 128
    return np.ascontiguousarray(W.reshape(nkt, 128, 32, 128).transpose(2, 1, 0, 3).reshape(32, 128, nkt * 128))


def hn_layout(hnT):
    T = hnT.shape[1]
    nb = (T - 16) // 256
    v = hnT.reshape(32, 128, T)
    meta = np.ascontiguousarray(v[:, :, 0:16].transpose(1, 0, 2))
    own = np.ascontiguousarray(v[:, :, 16:].reshape(32, 128, nb, 256).transpose(2, 1, 0, 3))
    return dict(hn_meta=meta, hn_own=own)


def const_mats():
    U = np.triu(np.ones((128, 128), np.float32))
    ident = np.eye(128, dtype=np.float32)
    return U, ident


def hyb_weights(c, w_in, conv_w, conv_b, dt_bias, a_log, d_skip, norm_g,
                a_re, a_im, log_dt, b_re, b_im, c_re, c_im, s5_d):
    z = w_in[:, c * 512:(c + 1) * 512]
    x = w_in[:, 4096 + c * 512:4096 + (c + 1) * 512]
    B = w_in[:, 8192 + c * 128:8192 + (c + 1) * 128]
    C = w_in[:, 9216 + c * 128:9216 + (c + 1) * 128]
    dt = w_in[:, 10240 + c * 8:10240 + (c + 1) * 8]
    w_ssd = ktile(np.concatenate([z, x, B, C, dt], 1))
    u = w_in[:, 10304 + c * 256:10304 + (c + 1) * 256]
    gate = w_in[:, 12352 + c * 256:12352 + (c + 1) * 256]
    w_s5 = ktile(np.concatenate([u, gate], 1))
    chans = [np.arange(c * 512 + m * 128, c * 512 + (m + 1) * 128) for m in range(4)]
    chans.append(np.arange(4096 + c * 128, 4096 + (c + 1) * 128))
    chans.append(np.arange(5120 + c * 128, 5120 + (c + 1) * 128))
    convw = np.ascontiguousarray(np.stack([conv_w[:, ch].T for ch in chans], 1))
    convb = np.ascontiguousarray(np.stack([conv_b[ch] for ch in chans], 1))
    hs = slice(c * 8, (c + 1) * 8)
    dtb_bc = np.ascontiguousarray(np.broadcast_to(dt_bias[hs][None, :], (128, 8)))
    alog_bc = np.ascontiguousarray(np.broadcast_to(a_log[hs][None, :], (128, 8)))
    d_bc = np.ascontiguousarray(np.broadcast_to(np.repeat(d_skip[hs], 64)[None, :], (128, 512)))
    ng_bc = np.ascontiguousarray(np.broadcast_to(norm_g[c * 512:(c + 1) * 512][None, :], (128, 512)))
    bre = np.zeros((128, 8, 128), np.float32); bim = np.zeros((128, 8, 128), np.float32)
    cre = np.zeros((128, 8, 128), np.float32); cim = np.zeros((128, 8, 128), np.float32)
    are_l = np.zeros((128, 8), np.float32); aim_l = np.zeros((128, 8), np.float32); ldt_l = np.zeros((128, 8), np.float32)
    for j in range(8):
        a, q = j // 4, (j % 4) * 32
        for m in range(2):
            g = 16 * c + 2 * j + m
            bre[q + m * 16:q + (m + 1) * 16, j, m * 64:(m + 1) * 64] = b_re[g].T
            bim[q + m * 16:q + (m + 1) * 16, j, m * 64:(m + 1) * 64] = b_im[g].T
            cre[m * 64:(m + 1) * 64, j, q + m * 16:q + (m + 1) * 16] = c_re[g].T
            cim[m * 64:(m + 1) * 64, j, q + m * 16:q + (m + 1) * 16] = c_im[g].T
            are_l[m * 64:(m + 1) * 64, j] = a_re[g]
            aim_l[m * 64:(m + 1) * 64, j] = a_im[g]
            ldt_l[m * 64:(m + 1) * 64, j] = log_dt[g]
    d_l = np.ascontiguousarray(s5_d[c * 256:(c + 1) * 256].reshape(2, 128).T)
    return dict(w_ssd=w_ssd, w_s5=w_s5, convw=convw, convb=convb, dtb_bc=dtb_bc, alog_bc=alog_bc, d_bc=d_bc, ng_bc=ng_bc,
                bre=bre, bim=bim, cre=cre, cim=cim, are_l=are_l, aim_l=aim_l, ldt_l=ldt_l, d_l=d_l)


def glu_w_tile(W):
    return np.ascontiguousarray(W.reshape(16, 128, 16, 128).transpose(2, 1, 0, 3).reshape(16, 128, 2048))

from concourse.bass_utils import run_bass_kernel_spmd

BFNP = ml_dtypes.bfloat16
T_ALL = 16400
TC = 2064
NCORE = 8
_PROGS = {}

HYB_SHAPES = dict(w_ssd=[128, 32, 1288], w_s5=[128, 32, 512], convw=[128, 6, 4], convb=[128, 6], dtb_bc=[128, 8],
                  alog_bc=[128, 8], d_bc=[128, 512], ng_bc=[128, 512], bre=[128, 8, 128], bim=[128, 8, 128],
                  cre=[128, 8, 128], cim=[128, 8, 128], are_l=[128, 8], aim_l=[128, 8], ldt_l=[128, 8], d_l=[128, 2],
                  Umat=[128, 128], ident=[128, 128])


def _new():
    return bass.Bass("TRN2", target_bir_lowering=False)


def prog_norm():
    nc = _new()
    with contextlib.ExitStack() as st:
        k = KB(nc, st)
        hT = k.dram("hT", [D, TC], F32, kind="ExternalInput")
        g_l = k.dram("g_l", [128, 32], F32, kind="ExternalInput")
        hnT = k.dram("hnT", [D, TC], BF16, kind="ExternalOutput")
        stage_out(k, hT, None, None, g_l, None, hnT, TC, 0, True, BF16)
        k.final_wait("sp", [hnT])
        k.emit()
    return nc


def prog_out(hyb, final):
    nc = _new()
    nkt = 48 if hyb else 32
    with contextlib.ExitStack() as st:
        k = KB(nc, st)
        hT = k.dram("hT", [D, TC], F32, kind="ExternalInput")
        yT = k.dram("yT", [D, TC], BF16, kind="ExternalInput")
        wl = k.dram("wl", [32, 128, nkt * 128], F32, kind="ExternalInput")
        g_l = k.dram("g_l", [128, 32], F32, kind="ExternalInput")
        glu = None
        if hyb:
            g_all = k.dram("g_all", [2048, TC], BF16, kind="ExternalInput")
            sg_all = k.dram("sg_all", [2048, TC], BF16, kind="ExternalInput")
            wglu_l = k.dram("wglu_l", [16, 128, 2048], F32, kind="ExternalInput")
            glu = (g_all, sg_all, wglu_l)
        outs = []
        if not final:
            hT_new = k.dram("hT_new", [D, TC], F32, kind="ExternalOutput")
            outs.append(hT_new)
        else:
            hT_new = k.dram("hT_new", [D, TC], F32)
        hnT = k.dram("hnT", [D, TC], F32 if final else BF16, kind="ExternalOutput")
        outs.append(hnT)
        stage_out(k, hT, yT, wl, g_l, hT_new, hnT, TC, nkt, False, F32 if final else BF16, glu=glu)
        k.final_wait("sp", outs)
        k.emit()
    return nc


def prog_hyb():
    nc = _new()
    T = T_ALL
    with contextlib.ExitStack() as st:
        k = KB(nc, st)
        hn_meta = k.dram("hn_meta", [128, 32, 16], BF16, kind="ExternalInput")
        hn_own = k.dram("hn_own", [(T - 16) // 256, 128, 32, 256], BF16, kind="ExternalInput")
        d = {n: k.dram(n, s, F32, kind="ExternalInput") for n, s in HYB_SHAPES.items()}
        yT = k.dram("yT", [512, T], BF16, kind="ExternalOutput")
        gT = k.dram("gT", [256, T], BF16, kind="ExternalOutput")
        sgT = k.dram("sgT", [256, T], BF16, kind="ExternalOutput")
        with k.scope():
            hyb_ssd(k, T, hn_meta, hn_own, d["w_ssd"], d["convw"], d["convb"], d["dtb_bc"], d["alog_bc"], d["d_bc"],
                    d["ng_bc"], d["Umat"], d["ident"], yT)
        with k.scope():
            hyb_s5(k, T, hn_meta, hn_own, d["w_s5"], d["bre"], d["bim"], d["cre"], d["cim"], d["are_l"], d["aim_l"],
                   d["ldt_l"], d["d_l"], gT, sgT)
        k.final_wait("sp", [yT, gT, sgT])
        k.emit()
    return nc


def prog_mla():
    nc = _new()
    T = T_ALL
    with contextlib.ExitStack() as st:
        k = KB(nc, st)
        hn_meta = k.dram("hn_meta", [128, 32, 16], BF16, kind="ExternalInput")
        hn_own = k.dram("hn_own", [(T - 16) // 256, 128, 32, 256], BF16, kind="ExternalInput")
        wq_in = k.dram("wq_in", [128, 32, 1024], F32, kind="ExternalInput")
        wkv_in = k.dram("wkv_in", [128, 32, 1152], F32, kind="ExternalInput")
        wuq = k.dram("wuq", [128, 8, 1024], F32, kind="ExternalInput")
        wukv = k.dram("wukv", [128, 4, 1024], F32, kind="ExternalInput")
        gq = k.dram("gq", [128, 8], F32, kind="ExternalInput")
        gkv = k.dram("gkv", [128, 4], F32, kind="ExternalInput")
        cos2 = k.dram("cos2", [64, T], F32, kind="ExternalInput")
        sin2s = k.dram("sin2s", [64, T], F32, kind="ExternalInput")
        yT = k.dram("yT", [512, T], BF16, kind="ExternalOutput")
        stage_mla(k, T, hn_meta, hn_own, wq_in, wkv_in, wuq, wukv, gq, gkv, cos2, sin2s, yT)
        k.final_wait("sp", [yT])
        k.emit()
    return nc


def _get(name, fn, *a):
    if name not in _PROGS:
        _PROGS[name] = fn(*a)
    return _PROGS[name]


def _run(nc, in_maps):
    res = run_bass_kernel_spmd(nc, in_maps, core_ids=list(range(NCORE)))
    return res.results


def _tok_idx(c):
    return np.concatenate([np.arange(16), 16 + 2048 * c + np.arange(2048)])


def _gather_tokens(per_core):
    return np.concatenate([per_core[0][:, 0:16]] + [per_core[c][:, 16:] for c in range(NCORE)], axis=1)


def _split_tokens(full):
    return [np.ascontiguousarray(full[:, _tok_idx(c)]) for c in range(NCORE)]


def kernel(x, meta, hyb_norm, hyb_w_in, ssd_conv_w, ssd_conv_b, ssd_dt_bias, ssd_a_log, ssd_d, ssd_norm,
           s5_a_re, s5_a_im, s5_log_dt, s5_b_re, s5_b_im, s5_c_re, s5_c_im, s5_d, s5_w_glu, hyb_w_out,
           mla_norm, mla_w_in, mla_q_norm, mla_w_uq, mla_kv_norm, mla_w_ukv, mla_w_out, final_norm):
    f32 = lambda a: np.asarray(a, dtype=np.float32)
    x, meta = f32(x), f32(meta)
    h_full_T = np.ascontiguousarray(np.concatenate([meta, x[0]], axis=0).T)
    hT = _split_tokens(h_full_T)
    del h_full_T
    U, ident = const_mats()
    cos2, sin2s = rope_tables(T_ALL)

    g0 = vec_tile(f32(hyb_norm[0]))
    res = _run(_get("norm", prog_norm), [dict(hT=hT[c], g_l=g0) for c in range(NCORE)])
    hn = [np.asarray(r["hnT"]) for r in res]

    for layer in range(4):
        i = layer // 2
        hn_l = hn_layout(_gather_tokens(hn))
        last = (layer == 3)
        if layer % 2 == 0:
            ims = []
            for c in range(NCORE):
                im = hyb_weights(c, f32(hyb_w_in[i]), f32(ssd_conv_w[i]), f32(ssd_conv_b[i]), f32(ssd_dt_bias[i]),
                                 f32(ssd_a_log[i]), f32(ssd_d[i]), f32(ssd_norm[i]), f32(s5_a_re[i]), f32(s5_a_im[i]),
                                 f32(s5_log_dt[i]), f32(s5_b_re[i]), f32(s5_b_im[i]), f32(s5_c_re[i]), f32(s5_c_im[i]),
                                 f32(s5_d[i]))
                im.update(Umat=U, ident=ident, **hn_l)
                ims.append(im)
            res = _run(_get("hyb", prog_hyb), ims)
            del ims
            y_all = _split_tokens(np.concatenate([np.asarray(r["yT"]) for r in res], axis=0))
            g_all = _split_tokens(np.concatenate([np.asarray(r["gT"]) for r in res], axis=0))
            sg_all = _split_tokens(np.concatenate([np.asarray(r["sgT"]) for r in res], axis=0))
            wl = out_w_tile(f32(hyb_w_out[i]))
            wglu_l = glu_w_tile(f32(s5_w_glu[i]))
            gn = vec_tile(f32(mla_norm[i]))
            ims = [dict(hT=hT[c], yT=y_all[c], wl=wl, g_l=gn, g_all=g_all[c], sg_all=sg_all[c], wglu_l=wglu_l)
                   for c in range(NCORE)]
            res = _run(_get("out_hyb", prog_out, True, False), ims)
        else:
            mw = None
            ims = []
            for c in range(NCORE):
                im = mla_weights(c, f32(mla_w_in[i]), f32(mla_q_norm[i]), f32(mla_w_uq[i]), f32(mla_kv_norm[i]),
                                 f32(mla_w_ukv[i]))
                im.update(cos2=cos2, sin2s=sin2s, **hn_l)
                ims.append(im)
            res = _run(_get("mla", prog_mla), ims)
            del ims
            y_all = _split_tokens(np.concatenate([np.asarray(r["yT"]) for r in res], axis=0))
            wl = out_w_tile(f32(mla_w_out[i]))
            gn = vec_tile(f32(final_norm) if last else f32(hyb_norm[i + 1]))
            ims = [dict(hT=hT[c], yT=y_all[c], wl=wl, g_l=gn) for c in range(NCORE)]
            res = _run(_get("out_mla_final" if last else "out_mla", prog_out, False, last), ims)
        del ims
        if not last:
            hT = [np.asarray(r["hT_new"]) for r in res]
        hn = [np.asarray(r["hnT"]) for r in res]

    out = np.concatenate([hn[c][:, 16:].T for c in range(NCORE)], axis=0)
    return np.ascontiguousarray(out[None].astype(np.float32))
```

```python
import contextlib
import math
import os
import numpy as np
import ml_dtypes


import concourse.bass as bass
import concourse.mybir as mybir

F32 = mybir.dt.float32
BF16 = mybir.dt.bfloat16
I32 = mybir.dt.int32
AF = mybir.ActivationFunctionType
ALU = mybir.AluOpType
AX = mybir.AxisListType

COMPUTE = ("pe", "act", "dve", "pool")


class View:
    __slots__ = ("tl", "ap")

    def __init__(self, tl, ap):
        self.tl = tl
        self.ap = ap

    def __getitem__(self, idx):
        return View(self.tl, self.ap[idx])

    def rearrange(self, pat, **kw):
        return View(self.tl, self.ap.rearrange(pat, **kw))

    def broadcast_to(self, shape):
        return View(self.tl, self.ap.broadcast_to(list(shape)))

    def unsqueeze(self, ax):
        return View(self.tl, self.ap.unsqueeze(ax))

    def partition_broadcast(self, n):
        return View(self.tl, self.ap.partition_broadcast(n))

    def bitcast(self, dt):
        return View(self.tl, self.ap.bitcast(dt))

    @property
    def shape(self):
        return self.ap.shape


class Tl:
    __slots__ = ("t", "name", "lw", "rd", "dsem", "dcnt", "is_dram", "is_psum")

    def __init__(self, t, name, is_dram=False, is_psum=False):
        self.is_psum = is_psum
        self.t = t
        self.name = name
        self.lw = {}
        self.rd = {}
        self.dsem = None
        self.dcnt = 0
        self.is_dram = is_dram

    def __getitem__(self, idx):
        return View(self, self.t[idx])

    def rearrange(self, pat, **kw):
        return View(self, self.t.rearrange(pat, **kw))

    @property
    def v(self):
        return View(self, self.t[:])


def _is_view(x):
    return isinstance(x, View)


class KB:
    def __init__(self, nc, stack):
        self.nc = nc
        self.stack = stack
        self.root = stack
        self.lists = {e: [] for e in ("pe", "act", "dve", "pool", "sp")}
        self.psem = {}
        self.pcnt = {}
        for e in COMPUTE:
            self.psem[e] = stack.enter_context(nc.semaphore("prog_" + e))
            self.pcnt[e] = 0
        self.known = {e: {} for e in self.lists}
        self.cinst = {e: [] for e in COMPUTE}
        self.ntile = 0
        self.tiles = []
        self.sem_pool = []
        self.n_sem = 4
        self.n_inst = 0
        self.n_wait = 0

    def sb(self, shape, dt, name=None):
        self.ntile += 1
        name = name or f"t{self.ntile}"
        t = self.stack.enter_context(self.nc.sbuf_tensor(name, list(shape), dt))
        tl = Tl(t, name)
        self.tiles.append(tl)
        return tl

    def ps(self, shape, dt, name=None):
        self.ntile += 1
        name = name or f"p{self.ntile}"
        t = self.stack.enter_context(self.nc.psum_tensor(name, list(shape), dt))
        return Tl(t, name, is_psum=True)

    def dram(self, name, shape, dt, kind="Internal", **kw):
        t = self.nc.dram_tensor(name, list(shape), dt, kind=kind, **kw)
        tl = Tl(t.ap(), name, is_dram=True)
        self.tiles.append(tl)
        return tl

    def _need(self, eng, ev, waits):
        if ev is None:
            return
        if ev[0] == "c":
            _, src, idx = ev
            if src == "pe" and eng == "pe":
                return
            key = ("c", src)
        else:
            _, sem, idx = ev
            key = id(sem)
        kn = self.known[eng]
        if kn.get(key, 0) >= idx:
            return
        kn[key] = idx
        if ev[0] == "c":
            self.cinst[src][idx - 1][4] = True
        waits[key] = ev

    def _deps(self, eng, reads, writes):
        waits = {}
        for t in reads:
            for ev in t.lw.values():
                self._need(eng, ev, waits)
            if t.is_psum:
                for ev in t.rd.values():
                    if not (ev[0] == "c" and ev[1] == eng):
                        self._need(eng, ev, waits)
        for t in writes:
            for ev in t.lw.values():
                self._need(eng, ev, waits)
            for ev in t.rd.values():
                self._need(eng, ev, waits)
        return list(waits.values())

    @staticmethod
    def _evkey(ev):
        return ("c", ev[1]) if ev[0] == "c" else id(ev[1])

    def _record(self, ev, reads, writes):
        key = self._evkey(ev)
        for t in reads:
            t.rd[key] = ev
        for t in writes:
            if t.is_dram and ev[0] == "d":
                t.lw[key] = ev
            else:
                t.lw = {key: ev}
            t.rd = {}

    def op(self, eng, fn, reads=(), writes=(), inc=True):
        reads = [r.tl if _is_view(r) else r for r in reads]
        writes = [w.tl if _is_view(w) else w for w in writes]
        waits = self._deps(eng, reads, writes)
        ent = [waits, fn, "c", eng, False]
        self.cinst[eng].append(ent)
        ev = ("c", eng, len(self.cinst[eng]))
        self.lists[eng].append(ent)
        self._record(ev, reads, writes)
        self.n_inst += 1
        self.n_wait += len(waits)

    def dma(self, q, out, in_, sem_tile=None, **kw):
        reads = [in_.tl]
        writes = [out.tl]
        waits = self._deps(q, reads, writes)
        st = sem_tile or (in_.tl if out.tl.is_dram and not in_.tl.is_dram else out.tl)
        if st.dsem is None:
            st.dsem, st.dcnt = self.get_sem("d_" + st.name)
        st.dcnt += 16
        ev = ("d", st.dsem, st.dcnt)
        oap, iap = out.ap, in_.ap
        self.lists[q].append([waits, lambda e: e.dma_start(out=oap, in_=iap, **kw), "d", st.dsem, True])
        self._record(ev, reads, writes)
        self.n_inst += 1
        self.n_wait += len(waits)

    def collective(self, kind, out, in_, op=None):
        reads = [in_.tl]
        writes = [out.tl]
        waits = self._deps("pool", reads, writes)
        st = out.tl
        if st.dsem is None:
            st.dsem, st.dcnt = self.get_sem("c_" + st.name)
        st.dcnt += 16
        ev = ("d", st.dsem, st.dcnt)
        oap, iap = out.ap, in_.ap
        aop = op if op is not None else ALU.bypass
        groups = [list(range(8))]
        self.lists["pool"].append([waits, lambda e: e.collective_compute(kind, aop, replica_groups=groups, ins=[iap], outs=[oap]), "d", st.dsem, True])
        self._record(ev, reads, writes)
        self.n_inst += 1

    def get_sem(self, name):
        if self.sem_pool:
            return self.sem_pool.pop()
        self.n_sem += 1
        return self.root.enter_context(self.nc.semaphore(name)), 0

    @contextlib.contextmanager
    def scope(self):
        old_stack, old_tiles = self.stack, self.tiles
        with contextlib.ExitStack() as st:
            self.stack = st
            self.tiles = []
            yield
            self.tiles = old_tiles + self.tiles
            self.barrier()
            new = self.tiles[len(old_tiles):]
            self.release([t for t in new if not t.is_dram])
            self.tiles = old_tiles + [t for t in new if t.is_dram]
            self.stack = old_stack

    def release(self, tiles):
        for t in tiles:
            if t.dsem is not None:
                self.sem_pool.append((t.dsem, t.dcnt))
                t.dsem = None

    def barrier(self):
        evs = [("c", e, len(self.cinst[e])) for e in COMPUTE if self.cinst[e]]
        evs += [("d", s, c) for (s, c) in self.all_dsems() if c > 0]
        for eng in self.lists:
            waits = {}
            for ev in evs:
                if ev[0] == "c" and ev[1] == eng:
                    continue
                saved = None
                if ev[0] == "c" and ev[1] == "pe" and eng == "pe":
                    continue
                self._need(eng, ev, waits)
            if waits:
                self.lists[eng].append([list(waits.values()), None, None, None, False])

    def all_dsems(self):
        out = [(t.dsem, t.dcnt) for t in self.tiles if t.dsem is not None]
        out += list(self.sem_pool)
        return out

    def final_wait(self, eng, tiles):
        waits = {}
        for t in tiles:
            for ev in t.lw.values():
                self._need(eng, ev, waits)
        self.lists[eng].append([list(waits.values()), None, None, None, False])

    def matmul(self, out, lhsT, rhs, start=True, stop=True):
        o, l, r = out.ap, lhsT.ap, rhs.ap
        self.op("pe", lambda e: e.matmul(o, lhsT=l, rhs=r, start=start, stop=stop), [lhsT, rhs], [out], inc=bool(stop))

    def transpose(self, out, in_, ident):
        o, i, d = out.ap, in_.ap, ident.ap
        self.op("pe", lambda e: e.transpose(o, i, d), [in_, ident], [out])

    def act(self, out, in_, func, bias=None, scale=None, accum_out=None):
        o, i = out.ap, in_.ap
        kw = {}
        rd = [in_]
        wr = [out]
        if bias is not None:
            if _is_view(bias):
                rd.append(bias)
                kw["bias"] = bias.ap
            else:
                kw["bias"] = bias
        if scale is not None:
            if _is_view(scale):
                rd.append(scale)
                kw["scale"] = scale.ap
            else:
                kw["scale"] = scale
        if accum_out is not None:
            wr.append(accum_out)
            kw["accum_out"] = accum_out.ap
        self.op("act", lambda e: e.activation(out=o, in_=i, func=func, **kw), rd, wr)

    def tt(self, out, in0, in1, op, eng="dve"):
        o, a, b = out.ap, in0.ap, in1.ap
        self.op(eng, lambda e: e.tensor_tensor(out=o, in0=a, in1=b, op=op), [in0, in1], [out])

    def ts(self, out, in0, s1, op0, s2=None, op1=None, eng="dve", accum_out=None):
        o, a = out.ap, in0.ap
        rd = [in0]
        wr = [out]
        a1 = s1
        a2 = s2
        if _is_view(s1):
            rd.append(s1)
            a1 = s1.ap
        if _is_view(s2):
            rd.append(s2)
            a2 = s2.ap
        kw = {}
        if op1 is not None:
            kw["op1"] = op1
        if accum_out is not None:
            wr.append(accum_out)
            kw["accum_out"] = accum_out.ap
        self.op(eng, lambda e: e.tensor_scalar(out=o, in0=a, scalar1=a1, scalar2=a2, op0=op0, **kw), rd, wr)

    def stt(self, out, in0, scalar, in1, op0, op1):
        o, a, b = out.ap, in0.ap, in1.ap
        rd = [in0, in1]
        s = scalar
        if _is_view(scalar):
            rd.append(scalar)
            s = scalar.ap
        self.op("dve", lambda e: e.scalar_tensor_tensor(out=o, in0=a, scalar=s, in1=b, op0=op0, op1=op1), rd, [out])

    def copy(self, out, in_, eng="dve"):
        o, i = out.ap, in_.ap
        if eng == "act":
            self.op("act", lambda e: e.copy(out=o, in_=i), [in_], [out])
        else:
            self.op(eng, lambda e: e.tensor_copy(out=o, in_=i), [in_], [out])

    def memset(self, out, val, eng="pool"):
        o = out.ap
        self.op(eng, lambda e: e.memset(o, val), [], [out])

    def recip(self, out, in_):
        o, i = out.ap, in_.ap
        self.op("dve", lambda e: e.reciprocal(out=o, in_=i), [in_], [out])

    def scan(self, out, d0, d1, initial, op0=ALU.mult, op1=ALU.add):
        o, a, b = out.ap, d0.ap, d1.ap
        rd = [d0, d1]
        ini = initial
        if _is_view(initial):
            rd.append(initial)
            ini = initial.ap
        self.op("dve", lambda e: e.tensor_tensor_scan(out=o, data0=a, data1=b, initial=ini, op0=op0, op1=op1), rd, [out])

    def reduce(self, out, in_, op, axis=AX.X):
        o, i = out.ap, in_.ap
        self.op("dve", lambda e: e.tensor_reduce(out=o, in_=i, axis=axis, op=op), [in_], [out])

    def emit(self):
        nc = self.nc
        lists = self.lists
        cum = {}
        for eng in COMPUTE:
            c = 0
            arr = []
            for ent in self.cinst[eng]:
                if ent[4]:
                    c += 1
                arr.append(c)
            cum[eng] = arr
        self.n_marked = {e: (cum[e][-1] if cum[e] else 0) for e in COMPUTE}
        psem = self.psem

        def run(e, items):
            for ent in items:
                waits, fn, kind, who, mark = ent
                for ev in waits:
                    if ev[0] == "c":
                        e.wait_ge(psem[ev[1]], cum[ev[1]][ev[2] - 1])
                    else:
                        e.wait_ge(ev[1], ev[2])
                if fn is None:
                    continue
                if kind == "d":
                    fn(e).then_inc(who, 16)
                elif mark:
                    fn(e).then_inc(psem[who], 1)
                else:
                    fn(e)

        with nc.Block() as block:
            @block.tensor
            def _(e):
                run(e, lists["pe"])

            @block.scalar
            def _(e):
                run(e, lists["act"])

            @block.vector
            def _(e):
                run(e, lists["dve"])

            @block.gpsimd
            def _(e):
                run(e, lists["pool"])

            @block.sync
            def _(e):
                run(e, lists["sp"])


EPS = 1e-6
D = 4096
NDT = 32


def col_groups(Tc, gmax=1024):
    groups = []
    s = 0
    while s < Tc:
        e = min(s + gmax, Tc)
        if 0 < Tc - e < 64:
            e = Tc
        groups.append((s, e))
        s = e
    return groups


def stage_glu(k, g_all, sg_all, wglu_l, ybT, Tc, pfx="gl"):
    GW = 1040
    gb = k.sb([128, 16, GW], BF16, pfx + "gb")
    sgb = k.sb([128, 16, GW], BF16, pfx + "sgb")
    gst = [k.sb([128, 2048], F32, pfx + f"gst{i}") for i in range(3)]
    gwb = [k.sb([128, 2048], BF16, pfx + f"gwb{i}") for i in range(2)]
    sig = [k.sb([128, 512], F32, pfx + f"sig{i}") for i in range(3)]
    yo = [k.sb([128, 512], BF16, pfx + f"yo{i}") for i in range(3)]
    acc = [k.ps([128, 512], F32, pfx + f"acc{i}") for i in range(4)]
    gv = g_all.rearrange("(kt p) t -> p kt t", p=128)
    sgv = sg_all.rearrange("(kt p) t -> p kt t", p=128)
    gcnt = 0
    u = 0
    for (c0, c1) in col_groups(Tc, 1024):
        gw = c1 - c0
        chunks = [(s, min(s + 512, c1)) for s in range(c0, c1, 512)]
        for kt0 in range(0, 16, 8):
            k.dma("sp", gb[:, kt0:kt0 + 8, 0:gw], gv[:, kt0:kt0 + 8, c0:c1])
            k.dma("sp", sgb[:, kt0:kt0 + 8, 0:gw], sgv[:, kt0:kt0 + 8, c0:c1])
        for mt in range(16):
            gs, gw_ = gst[gcnt % 3], gwb[gcnt % 2]
            k.dma("act", gs.v, wglu_l[mt])
            k.copy(gw_.v, gs.v, eng="pool")
            gcnt += 1
            for ci, (s0, s1) in enumerate(chunks):
                n = s1 - s0
                ac = acc[u % 4]
                for kt in range(16):
                    k.matmul(ac[:, 0:n], gw_[:, kt * 128:(kt + 1) * 128], gb[:, kt, s0 - c0:s1 - c0], start=(kt == 0), stop=(kt == 15))
                sg_ = sig[u % 3]
                y_ = yo[u % 3]
                k.act(sg_[:, 0:n], ac[:, 0:n], AF.Sigmoid)
                k.tt(sg_[:, 0:n], sg_[:, 0:n], gb[:, mt, s0 - c0:s1 - c0], ALU.mult)
                k.tt(y_[:, 0:n], sg_[:, 0:n], sgb[:, mt, s0 - c0:s1 - c0], ALU.mult, eng="pool")
                k.dma("sp", ybT[mt * 128:(mt + 1) * 128, s0:s1], y_[:, 0:n])
                u += 1


def stage_out(k, hT, yT, wl, g_l, hT_new, hnT, Tc, nkt, first, out_dt, pfx="o", glu=None):
    ones = k.sb([128, 128], F32, pfx + "ones")
    k.memset(ones.v, 1.0)
    gt = k.sb([128, NDT], F32, pfx + "g")
    k.dma("sp", gt.v, g_l.v)
    GW = 1040
    GMAX = 1024
    if not first:
        yb = k.sb([128, nkt, GW], BF16, pfx + "yb")
        KH = nkt // 2
        NST = 3 if nkt > 32 else 4
        wst = [k.sb([128, KH * 128], F32, pfx + f"wst{i}") for i in range(NST)]
        wbf = [k.sb([128, nkt * 128], BF16, pfx + f"wbf{i}") for i in range(2)]
        acc = [k.ps([128, 512], F32, pfx + f"acc{i}") for i in range(4)]
        yTv = yT.rearrange("(kt p) t -> p kt t", p=128)
    ssq = [k.ps([128, 512], F32, pfx + f"ssq{i}") for i in range(3)]
    hin = [k.sb([128, 512], F32, pfx + f"hin{i}") for i in range(3)]
    hnw = [k.sb([128, 512], F32, pfx + f"hnw{i}") for i in range(3)]
    sq = [k.sb([128, 512], F32, pfx + f"sq{i}") for i in range(3)]
    rstd = k.sb([128, GW], F32, pfx + "rstd")
    hno = [k.sb([128, 512], out_dt, pfx + f"hno{i}") for i in range(3)]
    hsrc = hT if first else hT_new
    u = 0
    wcnt = 0
    if glu is not None:
        ybv = glu.rearrange("(kt p) t -> p kt t", p=128)
    for (c0, c1) in col_groups(Tc, GMAX):
        gw = c1 - c0
        chunks = [(s, min(s + 512, c1)) for s in range(c0, c1, 512)]
        assert len(chunks) <= 3 and gw <= GW
        if not first:
            nkt_y = nkt - 16 if glu is not None else nkt
            for kt0 in range(0, nkt_y, 8):
                k.dma("sp", yb[:, kt0:kt0 + 8, 0:gw], yTv[:, kt0:kt0 + 8, c0:c1])
        if glu is not None:
            for kt0 in range(0, 16, 8):
                k.dma("sp", yb[:, 32 + kt0:32 + kt0 + 8, 0:gw], ybv[:, kt0:kt0 + 8, c0:c1])
        pend = None
        for d in range(NDT):
            if not first:
                wb = wbf[wcnt % 2]
                for hh in range(2):
                    ws = wst[(2 * wcnt + hh) % NST]
                    k.dma("act" if hh == 0 else "sp", ws.v, wl[d, :, hh * KH * 128:(hh + 1) * KH * 128])
                    k.copy(wb[:, hh * KH * 128:(hh + 1) * KH * 128], ws.v, eng="pool")
                wcnt += 1
            for ci, (s0, s1) in enumerate(chunks):
                n = s1 - s0
                hi = hin[u % 3]
                hw = hnw[u % 3]
                sqt = sq[u % 3]
                k.dma("sp", hi[:, 0:n], hT[d * 128:(d + 1) * 128, s0:s1])
                if not first:
                    ac = acc[u % 4]
                    for kt in range(nkt):
                        k.matmul(ac[:, 0:n], wb[:, kt * 128:(kt + 1) * 128], yb[:, kt, s0 - c0:s1 - c0],
                                 start=(kt == 0), stop=(kt == nkt - 1))
                    k.tt(hw[:, 0:n], ac[:, 0:n], hi[:, 0:n], ALU.add)
                    k.dma("sp", hT_new[d * 128:(d + 1) * 128, s0:s1], hw[:, 0:n])
                    src = hw
                else:
                    src = hi
                k.act(sqt[:, 0:n], src[:, 0:n], AF.Square)
                if pend is not None:
                    k.matmul(*pend[0], **pend[1])
                pend = ((ssq[ci][:, 0:n], ones.v, sqt[:, 0:n]), dict(start=(d == 0), stop=(d == NDT - 1)))
                u += 1
        if pend is not None:
            k.matmul(*pend[0], **pend[1])
            pend = None
        for ci, (s0, s1) in enumerate(chunks):
            n = s1 - s0
            k.ts(rstd[:, s0 - c0:s1 - c0], ssq[ci][:, 0:n], 1.0 / D, ALU.mult, EPS, ALU.add)
            k.act(rstd[:, s0 - c0:s1 - c0], rstd[:, s0 - c0:s1 - c0], AF.Sqrt)
            k.recip(rstd[:, s0 - c0:s1 - c0], rstd[:, s0 - c0:s1 - c0])
        for d in range(NDT):
            for ci, (s0, s1) in enumerate(chunks):
                n = s1 - s0
                hi = hin[u % 3]
                ho = hno[u % 3]
                k.dma("sp", hi[:, 0:n], hsrc[d * 128:(d + 1) * 128, s0:s1])
                k.stt(ho[:, 0:n], hi[:, 0:n], gt[:, d:d + 1], rstd[:, s0 - c0:s1 - c0], ALU.mult, ALU.mult)
                k.dma("act", hnT[d * 128:(d + 1) * 128, s0:s1], ho[:, 0:n])
                u += 1

DBG_NB = int(os.environ.get('DBG_NB', '0'))
DBG_SKIP = os.environ.get('DBG_SKIP', '')
DBG_START = int(os.environ.get('DBG_START', '0'))

EPS = 1e-6
NH = 4
QSCALE = 192 ** -0.5


def tok_blocks(T, bs=512):
    assert (T - 16) % bs == 0
    return [(0, 16)] + [(s, s + bs) for s in range(16, T, bs)]


def load_cast(k, dst, src_dram, nkt, ncols, stg, q="act", ceng="pool"):
    if 'lc' in DBG_SKIP:
        k.memset(dst.v, 0.01)
        return
    for kt in range(nkt):
        s = stg[kt % len(stg)]
        k.dma(q, s[:, 0:ncols], src_dram[:, kt, :])
        k.copy(dst[:, kt, :], s[:, 0:ncols], eng=ceng)


def rstd_from_ssq(k, rstd, ssq, n, dim):
    k.ts(rstd[:, 0:n], ssq[:, 0:n], 1.0 / dim, ALU.mult, EPS, ALU.add)
    k.act(rstd[:, 0:n], rstd[:, 0:n], AF.Sqrt)
    k.recip(rstd[:, 0:n], rstd[:, 0:n])


BS_A = 256


def _a_common(k, pfx):
    ones = k.sb([128, 128], BF16, pfx + "ones")
    k.memset(ones.v, 1.0)
    stg = [k.sb([128, 1152], F32, pfx + f"stg{i}") for i in range(2)]
    hb = [k.sb([128, 32, BS_A], BF16, pfx + f"hb{i}") for i in range(2)]
    if os.environ.get("HB1"): hb = [hb[0], hb[0]]
    cst = [k.sb([128, BS_A], F32, pfx + f"cos{i}") for i in range(2)]
    snt = [k.sb([128, BS_A], F32, pfx + f"sin{i}") for i in range(2)]
    rstd = k.sb([128, BS_A], F32, pfx + "rstd")
    acc = [k.ps([128, 512], F32, pfx + f"acc{i}") for i in range(3)]
    ssq = k.ps([128, 512], F32, pfx + "ssq")
    up = [k.ps([128, 512], F32, pfx + f"up{i}") for i in range(3)]
    sqb = [k.sb([128, BS_A], BF16, pfx + f"sqb{i}") for i in range(2)]
    ra = [k.sb([128, BS_A], F32, pfx + f"ra{i}") for i in range(2)]
    rb = [k.sb([128, BS_A], F32, pfx + f"rb{i}") for i in range(2)]
    ob = [k.sb([128, 512], BF16, pfx + f"ob{i}") for i in range(4)]
    return ones, stg, hb, cst, snt, rstd, acc, ssq, up, sqb, ra, rb, ob


def mla_a1(k, T, hn_meta, hn_own, wq_in, wuq, gq, cos2, sin2s, qnT, qrT, pfx="m1"):
    blocks = tok_blocks(T, BS_A)
    hbm = k.sb([128, 32, 16], BF16, pfx + "hbm")
    ones, stg, hb, cst, snt, rstd, acc, ssq, up, sqb, ra, rb, ob = _a_common(k, pfx)
    oc = [0]

    def nob():
        oc[0] += 1
        return ob[oc[0] % 4]

    w1 = k.sb([128, 32, 1024], BF16, pfx + "w1")
    load_cast(k, w1, wq_in, 32, 1024, stg)
    wu = k.sb([128, 8, NH * 256], BF16, pfx + "wu")
    load_cast(k, wu, wuq, 8, NH * 256, stg)
    gqt = k.sb([128, 8], F32, pfx + "gq")
    if 'gq' not in DBG_SKIP:
        k.dma("sp", gqt.v, gq.v)
    cq = k.sb([128, 8, BS_A], F32, pfx + "cq")
    cqn = k.sb([128, 8, BS_A], BF16, pfx + "cqn")
    for bi, (t0, t1) in enumerate(blocks):
        if DBG_NB and bi >= DBG_NB:
            break
        if bi < DBG_START:
            continue
        n = t1 - t0
        if bi == 0:
            h = hbm
            k.dma("sp", h.v, hn_meta.v)
        else:
            h = hb[bi % 2]
            k.dma(os.environ.get("HQ", "sp"), h.v, hn_own[bi - 1])
        ct, sn = cst[bi % 2], snt[bi % 2]
        if 'cs' not in DBG_SKIP:
            _q = os.environ.get("CSQ", "sp")
            _o = 0 if os.environ.get("CS0") else t0
            k.dma(_q, ct[0:64, 0:n], cos2[:, _o:_o + n])
            k.dma(_q, sn[0:64, 0:n], sin2s[:, _o:_o + n])
        for m in range(8):
            a = acc[m % 3]
            for kt in range(32):
                k.matmul(a[:, 0:n], w1[:, kt, m * 128:(m + 1) * 128], h[:, kt, 0:n], start=(kt == 0), stop=(kt == 31))
            sq = sqb[m % 2]
            if 'sq' not in DBG_SKIP:
                k.act(sq[:, 0:n], a[:, 0:n], AF.Square)
            if 'cp' not in DBG_SKIP:
                k.copy(cq[:, m, 0:n], a[:, 0:n], eng=os.environ.get("CPENG","dve"))
            if 'ssq' not in DBG_SKIP:
                k.matmul(ssq[:, 0:n], ones.v, sq[:, 0:n], start=(m == 0), stop=(m == 7))
        if 'rstd' not in DBG_SKIP:
            rstd_from_ssq(k, rstd, ssq, n, 1024)
        for m in range(8):
            if 'stt' in DBG_SKIP:
                break
            k.stt(cqn[:, m, 0:n], cq[:, m, 0:n], gqt[:, m:m + 1], rstd[:, 0:n], ALU.mult, ALU.mult)
        for hd in range(NH):
            if 'up' in DBG_SKIP:
                break
            c0 = hd * 256
            u0 = up[0]
            for kt in range(8):
                k.matmul(u0[:, 0:n], wu[:, kt, c0:c0 + 128], cqn[:, kt, 0:n], start=(kt == 0), stop=(kt == 7))
            o = nob()
            k.act(o[:, 0:n], u0[:, 0:n], AF.Copy, scale=QSCALE)
            k.dma("act", qnT[hd, :, t0:t1], o[:, 0:n])
            u1, u2 = up[1], up[2]
            for kt in range(8):
                k.matmul(u1[0:64, 0:n], wu[:, kt, c0 + 128:c0 + 192], cqn[:, kt, 0:n], start=(kt == 0), stop=(kt == 7))
            for kt in range(8):
                k.matmul(u2[0:64, 0:n], wu[:, kt, c0 + 192:c0 + 256], cqn[:, kt, 0:n], start=(kt == 0), stop=(kt == 7))
            a_, b_ = ra[hd % 2], rb[hd % 2]
            k.tt(a_[0:64, 0:n], u1[0:64, 0:n], ct[0:64, 0:n], ALU.mult)
            k.tt(b_[0:64, 0:n], u2[0:64, 0:n], sn[0:64, 0:n], ALU.mult)
            o = nob()
            k.tt(a_[0:64, 0:n], a_[0:64, 0:n], b_[0:64, 0:n], ALU.add, eng="pool")
            k.act(o[0:64, 0:n], a_[0:64, 0:n], AF.Copy, scale=QSCALE)
            k.dma("act", qrT[hd, :, t0:t1], o[0:64, 0:n])


def mla_a2(k, T, hn_meta, hn_own, wkv_in, wukv, gkv, cos2, sin2s, knT, krT, vtok, gT, pfx="m2"):
    blocks = tok_blocks(T, BS_A)
    hbm = k.sb([128, 32, 16], BF16, pfx + "hbm")
    ones, stg, hb, cst, snt, rstd, acc, ssq, up, sqb, ra, rb, ob = _a_common(k, pfx)
    oc = [0]

    def nob():
        oc[0] += 1
        return ob[oc[0] % 4]

    w2 = k.sb([128, 32, 1152], BF16, pfx + "w2")
    load_cast(k, w2, wkv_in, 32, 1152, stg)
    wk = k.sb([128, 4, 1024], BF16, pfx + "wk")
    load_cast(k, wk, wukv, 4, 1024, stg)
    gkt = k.sb([128, 4], F32, pfx + "gk")
    k.dma("sp", gkt.v, gkv.v)
    ckv = k.sb([128, 4, BS_A], F32, pfx + "ckv")
    ckn = k.sb([128, 4, BS_A], BF16, pfx + "ckn")
    for bi, (t0, t1) in enumerate(blocks):
        n = t1 - t0
        if bi == 0:
            h = hbm
            k.dma("sp", h.v, hn_meta.v)
        else:
            h = hb[bi % 2]
            k.dma(os.environ.get("HQ", "sp"), h.v, hn_own[bi - 1])
        ct, sn = cst[bi % 2], snt[bi % 2]
        k.dma("sp", ct[0:64, 0:n], cos2[:, t0:t1])
        k.dma("sp", sn[0:64, 0:n], sin2s[:, t0:t1])
        for m in range(4):
            a = acc[m % 3]
            for kt in range(32):
                k.matmul(a[:, 0:n], w2[:, kt, m * 128:(m + 1) * 128], h[:, kt, 0:n], start=(kt == 0), stop=(kt == 31))
            sq = sqb[m % 2]
            k.act(sq[:, 0:n], a[:, 0:n], AF.Square)
            k.copy(ckv[:, m, 0:n], a[:, 0:n], eng="dve")
            k.matmul(ssq[:, 0:n], ones.v, sq[:, 0:n], start=(m == 0), stop=(m == 3))
        rstd_from_ssq(k, rstd, ssq, n, 512)
        for m in range(4):
            k.stt(ckn[:, m, 0:n], ckv[:, m, 0:n], gkt[:, m:m + 1], rstd[:, 0:n], ALU.mult, ALU.mult)
        u1, u2 = up[1], up[2]
        for kt in range(32):
            k.matmul(u1[0:64, 0:n], w2[:, kt, 512:576], h[:, kt, 0:n], start=(kt == 0), stop=(kt == 31))
        for kt in range(32):
            k.matmul(u2[0:64, 0:n], w2[:, kt, 576:640], h[:, kt, 0:n], start=(kt == 0), stop=(kt == 31))
        a_, b_ = ra[0], rb[0]
        k.tt(a_[0:64, 0:n], u1[0:64, 0:n], ct[0:64, 0:n], ALU.mult)
        k.tt(b_[0:64, 0:n], u2[0:64, 0:n], sn[0:64, 0:n], ALU.mult)
        o = nob()
        k.tt(o[0:64, 0:n], a_[0:64, 0:n], b_[0:64, 0:n], ALU.add)
        k.dma("act", krT[:, t0:t1], o[0:64, 0:n])
        for m in range(4):
            a = acc[m % 3]
            for kt in range(32):
                k.matmul(a[:, 0:n], w2[:, kt, 640 + m * 128:640 + (m + 1) * 128], h[:, kt, 0:n], start=(kt == 0), stop=(kt == 31))
            o = nob()
            k.act(o[:, 0:n], a[:, 0:n], AF.Silu)
            k.dma("act", gT[m * 128:(m + 1) * 128, t0:t1], o[:, 0:n])
        for hd in range(NH):
            u0 = up[0]
            for kt in range(4):
                k.matmul(u0[:, 0:n], wk[:, kt, hd * 128:(hd + 1) * 128], ckn[:, kt, 0:n], start=(kt == 0), stop=(kt == 3))
            o = nob()
            k.copy(o[:, 0:n], u0[:, 0:n], eng="act")
            k.dma("act", knT[hd, :, t0:t1], o[:, 0:n])
        for s0 in range(0, n, 128):
            ns = min(128, n - s0)
            a = acc[(s0 // 128) % 3]
            for kt in range(4):
                k.matmul(a[0:ns, :], ckn[:, kt, s0:s0 + ns], wk[:, kt, 512:1024], start=(kt == 0), stop=(kt == 3))
            o = nob()
            k.copy(o[0:ns, :], a[0:ns, :], eng="dve")
            kb = 0 if bi == 0 else 1 + (t0 + s0 - 16) // 128
            for hd in range(NH):
                k.dma("act", vtok[hd, 0:ns, kb, :], o[0:ns, hd * 128:(hd + 1) * 128])


def mla_phase_b(k, T, qnT, qrT, knT, krT, vtok, gT, yT, pfx="mb"):
    NB = (T - 16) // 512
    NKB = (T - 16) // 128
    onesf = k.sb([128, 128], F32, pfx + "onesf")
    k.memset(onesf.v, 1.0)
    kr = k.sb([128, T], BF16, pfx + "kr")
    k.dma("sp", kr[0:64, :], krT.v)
    kn = k.sb([128, T], BF16, pfx + "kn")
    vv = k.sb([128, NKB + 1, 128], BF16, pfx + "vv")
    qn = [k.sb([128, 512], BF16, pfx + f"qn{i}") for i in range(2)]
    qr = [k.sb([128, 512], BF16, pfx + f"qr{i}") for i in range(2)]
    gt = [k.sb([128, 512], BF16, pfx + f"gt{i}") for i in range(2)]
    pt = [k.sb([128, 512], BF16, pfx + f"pt{i}") for i in range(4)]
    sc = [k.ps([128, 512], F32, pfx + f"sc{i}") for i in range(4)]
    oT = [k.ps([128, 512], F32, pfx + f"oT{i}") for i in range(2)]
    dn = k.ps([128, 512], F32, pfx + "dn")
    dacc = [k.sb([128, 512], F32, pfx + f"dacc{i}") for i in range(2)]
    rden = [k.sb([128, 512], F32, pfx + f"rden{i}") for i in range(2)]
    yo = [k.sb([128, 512], F32, pfx + f"yo{i}") for i in range(2)]
    yb = [k.sb([128, 512], BF16, pfx + f"yb{i}") for i in range(2)]
    u = 0
    g = 0
    for hd in range(NH):
        k.dma("sp", kn.v, knT[hd, :, :])
        k.dma("act", vv.v, vtok[hd])
        groups = [(0, 16, -1)] + [(16 + 512 * i, 16 + 512 * (i + 1), i) for i in range(NB)]
        for (q0, q1, gi) in groups:
            n = q1 - q0
            qnt, qrt, gtt = qn[g % 2], qr[g % 2], gt[g % 2]
            o_ = oT[g % 2]
            k.dma("sp", qnt[:, 0:n], qnT[hd, :, q0:q1])
            k.dma("sp", qrt[0:64, 0:n], qrT[hd, :, q0:q1])
            k.dma("sp", gtt[:, 0:n], gT[hd * 128:(hd + 1) * 128, q0:q1])
            k.memset(dacc[0][:, 0:n], 0.0, eng="dve")
            k.memset(dacc[1][:, 0:n], 0.0, eng="pool")
            kbs = [(0, 16, 0, 0, False)]
            if gi >= 0:
                for j in range(4 * gi):
                    kbs.append((16 + 128 * j, 128, 1 + j, 0, False))
                for dgi in range(4):
                    j = 4 * gi + dgi
                    kbs.append((16 + 128 * j, 128, 1 + j, 128 * dgi, True))

            def scores(idx, uu):
                kc, nk, vb, qs, diag = kbs[idx]
                s_ = sc[uu % 4]
                k.matmul(s_[0:nk, qs:n], kn[:, kc:kc + nk], qnt[:, qs:n], start=True, stop=False)
                k.matmul(s_[0:nk, qs:n], kr[0:64, kc:kc + nk], qrt[0:64, qs:n], start=False, stop=True)

            scores(0, u)
            for idx, (kc, nk, vb, qs, diag) in enumerate(kbs):
                if idx + 1 < len(kbs):
                    scores(idx + 1, u + 1)
                s_ = sc[u % 4]
                p_ = pt[u % 4]
                k.act(p_[0:nk, qs:n], s_[0:nk, qs:n], AF.Exp)
                if diag:
                    k.memset(p_[64:128, qs:qs + 64], 0.0, eng="pool")
                last = (idx == len(kbs) - 1)
                k.matmul(o_[:, qs:n], vv[0:nk, vb, :], p_[0:nk, qs:n], start=(idx == 0), stop=last)
                da = dacc[u % 2]
                k.tt(da[0:nk, qs:n], da[0:nk, qs:n], p_[0:nk, qs:n], ALU.add, eng=("dve" if u % 2 == 0 else "pool"))
                u += 1
            k.matmul(dn[:, 0:n], onesf.v, dacc[0][:, 0:n], start=True, stop=False)
            k.matmul(dn[:, 0:n], onesf.v, dacc[1][:, 0:n], start=False, stop=True)
            rd, y1, y2 = rden[g % 2], yo[g % 2], yb[g % 2]
            k.recip(rd[:, 0:n], dn[:, 0:n])
            k.tt(y1[:, 0:n], o_[:, 0:n], rd[:, 0:n], ALU.mult)
            k.tt(y2[:, 0:n], y1[:, 0:n], gtt[:, 0:n], ALU.mult, eng="pool")
            k.dma("act", yT[hd * 128:(hd + 1) * 128, q0:q1], y2[:, 0:n])
            g += 1


def stage_mla(k, T, hn_meta, hn_own, wq_in, wkv_in, wuq, wukv, gq, gkv, cos2, sin2s, yT, pfx="ml", kind="Internal", phases="12b"):
    qnT = k.dram(pfx + "_qnT", [NH, 128, T], BF16, kind=kind)
    qrT = k.dram(pfx + "_qrT", [NH, 64, T], BF16, kind=kind)
    knT = k.dram(pfx + "_knT", [NH, 128, T], BF16, kind=kind)
    krT = k.dram(pfx + "_krT", [64, T], BF16, kind=kind)
    vtok = k.dram(pfx + "_vtok", [NH, 128, (T - 16) // 128 + 1, 128], BF16, kind=kind)
    gT = k.dram(pfx + "_gT", [512, T], BF16, kind=kind)
    if "1" in phases:
      with (contextlib.nullcontext() if os.environ.get("NOSCOPE") else k.scope()):
        mla_a1(k, T, hn_meta, hn_own, wq_in, wuq, gq, cos2, sin2s, qnT, qrT, pfx + "1")
    if "2" in phases:
      with k.scope():
        mla_a2(k, T, hn_meta, hn_own, wkv_in, wukv, gkv, cos2, sin2s, knT, krT, vtok, gT, pfx + "2")
    if "b" in phases:
      with k.scope():
        mla_phase_b(k, T, qnT, qrT, knT, krT, vtok, gT, yT, pfx + "b")
    return dict(qnT=qnT, qrT=qrT, knT=knT, krT=krT, vtok=vtok, gT=gT)


EPS = 1e-6
GELU_C = 1.5957691216057308


def hyb_ssd(k, T, hn_meta, hn_own, w_ssd, convw, convb, dtb_bc, alog_bc, d_bc, ng_bc, Umat, ident, yT, pfx="hs"):
    blocks = tok_blocks(T, BS_A)
    NW = 1288
    stg = [k.sb([128, NW], F32, pfx + f"stg{i}") for i in range(2)]
    w = k.sb([128, 32, NW], BF16, pfx + "w")
    load_cast(k, w, w_ssd, 32, NW, stg)
    hbm = k.sb([128, 32, 16], BF16, pfx + "hbm")
    hb = [k.sb([128, 32, BS_A], BF16, pfx + f"hb{i}") for i in range(2)]
    cw = k.sb([128, 6, 4], F32, pfx + "cw")
    cb = k.sb([128, 6], F32, pfx + "cb")
    k.dma("sp", cw.v, convw.v)
    k.dma("sp", cb.v, convb.v)
    dtb = k.sb([128, 8], F32, pfx + "dtb")
    aneg = k.sb([128, 8], F32, pfx + "aneg")
    dbc = k.sb([128, 512], F32, pfx + "dbc")
    ngb = k.sb([128, 512], F32, pfx + "ngb")
    U = k.sb([128, 128], F32, pfx + "U")
    idb = k.sb([128, 128], BF16, pfx + "idb")
    idf = k.sb([128, 128], F32, pfx + "idf")
    k.dma("sp", dtb.v, dtb_bc.v)
    k.dma("sp", aneg.v, alog_bc.v)
    k.dma("sp", dbc.v, d_bc.v)
    k.dma("sp", ngb.v, ng_bc.v)
    k.dma("sp", U.v, Umat.v)
    k.dma("sp", idf.v, ident.v)
    k.copy(idb.v, idf.v, eng="pool")
    k.act(aneg.v, aneg.v, AF.Exp)
    k.ts(aneg.v, aneg.v, -1.0, ALU.mult)
    ones = k.sb([128, 128], F32, pfx + "ones")
    k.memset(ones.v, 1.0)

    cin = [k.sb([128, 3 + BS_A], F32, pfx + f"cin{m}") for m in range(6)]
    for m in range(6):
        k.memset(cin[m].v, 0.0)
    cacc = [k.sb([128, BS_A], F32, pfx + f"cacc{i}") for i in range(2)]
    fT = [k.sb([128, BS_A], BF16, pfx + f"fT{m}") for m in range(6)]
    L_zs = [k.sb([128, 512], F32, pfx + f"zs{i}") for i in range(2)]
    L_ctk = [k.sb([128, 128], BF16, pfx + f"ctk{i}") for i in range(2)]
    L_dt = [k.sb([128, 8], F32, pfx + f"dt{i}") for i in range(2)]
    L_da = [k.sb([128, 8], F32, pfx + f"da{i}") for i in range(2)]
    L_dab = [k.sb([128, 8, 128], F32, pfx + f"dab{i}") for i in range(2)]
    L_acum = [k.sb([128, 8], F32, pfx + f"acum{i}") for i in range(2)]
    L_nacum = [k.sb([128, 8], F32, pfx + f"nacum{i}") for i in range(2)]
    L_aend = [k.sb([128, 8], F32, pfx + f"aend{i}") for i in range(2)]
    L_eend = [k.sb([128, 8], F32, pfx + f"eend{i}") for i in range(2)]
    L_eac = [k.sb([128, 8], F32, pfx + f"eac{i}") for i in range(2)]
    L_dte = [k.sb([128, 8], F32, pfx + f"dte{i}") for i in range(2)]
    L_xtok = [k.sb([128, 512], BF16, pfx + f"xtok{i}") for i in range(2)]
    L_btok = [k.sb([128, 128], BF16, pfx + f"btok{i}") for i in range(2)]
    L_xdt = [k.sb([128, 512], BF16, pfx + f"xdt{i}") for i in range(2)]
    L_xw = [k.sb([128, 512], BF16, pfx + f"xw{i}") for i in range(2)]
    L_segc = [k.sb([128, 8, 128], F32, pfx + f"segc{i}") for i in range(2)]
    L_cbm = [k.sb([128, 128], F32, pfx + f"cbm{i}") for i in range(2)]
    L_MT = [k.sb([128, 8, 128], BF16, pfx + f"MT{i}") for i in range(2)]
    S = k.sb([128, 512], F32, pfx + "S")
    Sb = k.sb([128, 512], BF16, pfx + "Sb")
    k.memset(S.v, 0.0)
    k.memset(Sb.v, 0.0)
    L_t1 = [k.sb([128, 512], F32, pfx + f"t1{i}") for i in range(2)]
    L_t2 = [k.sb([128, 512], F32, pfx + f"t2{i}") for i in range(2)]
    L_ssq = [k.sb([128, 1], F32, pfx + f"ssq{i}") for i in range(2)]
    L_yn = [k.sb([128, 512], BF16, pfx + f"yn{i}") for i in range(2)]
    yTs = [k.sb([128, 128], BF16, pfx + f"yTs{i}") for i in range(4)]

    accA = k.ps([128, 512], F32, pfx + "accA")
    accB = k.ps([128, 512], F32, pfx + "accB")
    misc = k.ps([128, 512], F32, pfx + "misc")
    AB = k.ps([128, 8, 128], F32, pfx + "AB")
    ydg = k.ps([128, 512], F32, pfx + "ydg")
    yof = k.ps([128, 512], F32, pfx + "yof")
    tr = k.ps([128, 512], BF16, pfx + "tr")
    accs = [accA, accB]

    def stageA(c):
        cl, cs, h, par = c['cl'], c['cs'], c['h'], c['par']
        zs = L_zs[par]
        dt = L_dt[par]
        da = L_da[par]
        dab = L_dab[par]
        acum = L_acum[par]
        nacum = L_nacum[par]
        aend = L_aend[par]
        eend = L_eend[par]
        eac = L_eac[par]
        dte = L_dte[par]
        xtok = L_xtok[par]
        btok = L_btok[par]
        xdt = L_xdt[par]
        xw = L_xw[par]
        segc = L_segc[par]
        cbm = L_cbm[par]
        MT = L_MT[par]
        t1 = L_t1[par]
        t2 = L_t2[par]
        ssq = L_ssq[par]
        yn = L_yn[par]
        ctk = L_ctk[par]
        for kt in range(32):
            k.matmul(accA[0:cl, :], h[:, kt, cs], w[:, kt, 0:512], start=(kt == 0), stop=(kt == 31))
        k.act(zs[0:cl, :], accA[0:cl, :], AF.Silu)
        for kt in range(32):
            k.matmul(misc[0:cl, 0:8], h[:, kt, cs], w[:, kt, 1280:1288], start=(kt == 0), stop=(kt == 31))
        k.tt(dt[0:cl, :], misc[0:cl, 0:8], dtb[0:cl, :], ALU.add)
        k.act(dt[0:cl, :], dt[0:cl, :], AF.Exp)
        k.act(dt[0:cl, :], dt[0:cl, :], AF.Ln, bias=1.0)
        k.tt(da[0:cl, :], dt[0:cl, :], aneg[0:cl, :], ALU.mult)
        for m in range(4):
            k.transpose(tr[0:cl, m * 128:(m + 1) * 128], fT[m][:, cs], idb.v)
        k.copy(xtok[0:cl, :], tr[0:cl, :], eng="act")
        k.transpose(tr[0:cl, 0:128], fT[4][:, cs], idb.v)
        k.copy(btok[0:cl, :], tr[0:cl, 0:128], eng="act")
        k.tt(xdt[0:cl, :].rearrange("p (h d) -> p h d", h=8), xtok[0:cl, :].rearrange("p (h d) -> p h d", h=8),
             dt[0:cl, :].unsqueeze(2).broadcast_to([cl, 8, 64]), ALU.mult)
        k.matmul(misc[0:cl, 8:16], U[0:cl, 0:cl], da[0:cl, :])
        k.copy(acum[0:cl, :], misc[0:cl, 8:16], eng="dve")
        k.ts(nacum[0:cl, :], acum[0:cl, :], -1.0, ALU.mult)
        k.act(eac[0:cl, :], acum[0:cl, :], AF.Exp)
        k.tt(dab[0:cl, :, 0:cl], ones[0:cl, 0:cl].unsqueeze(1).broadcast_to([cl, 8, cl]),
             da[0:cl, :].unsqueeze(2).broadcast_to([cl, 8, cl]), ALU.mult, eng="pool")
        for hh in range(8):
            k.matmul(AB[0:cl, hh, 0:cl], dab[0:cl, hh, 0:cl], U[0:cl, 0:cl])
        k.tt(segc[0:cl, :, 0:cl], AB[0:cl, :, 0:cl], nacum[0:cl, :].unsqueeze(2).broadcast_to([cl, 8, cl]), ALU.add)
        k.copy(aend[0:cl, :], AB[0:cl, :, cl - 1], eng="dve")
        k.ts(segc[0:cl, :, 0:cl], segc[0:cl, :, 0:cl], 0.0, ALU.min, eng="pool")
        k.act(segc[0:cl, :, 0:cl], segc[0:cl, :, 0:cl], AF.Exp)
        k.matmul(misc[0:cl, 128:128 + cl], fT[4][:, cs], fT[5][:, cs])
        k.tt(cbm[0:cl, 0:cl], misc[0:cl, 128:128 + cl], U[0:cl, 0:cl], ALU.mult)
        k.tt(MT[0:cl, :, 0:cl], segc[0:cl, :, 0:cl], cbm[0:cl, 0:cl].unsqueeze(1).broadcast_to([cl, 8, cl]), ALU.mult, eng="pool")
        k.copy(ctk[:, 0:cl], fT[5][:, cs], eng="pool")

    def stageB(c):
        cl, cs, par, tok0 = c['cl'], c['cs'], c['par'], c['tok0']
        zs = L_zs[par]
        dt = L_dt[par]
        da = L_da[par]
        dab = L_dab[par]
        acum = L_acum[par]
        nacum = L_nacum[par]
        aend = L_aend[par]
        eend = L_eend[par]
        eac = L_eac[par]
        dte = L_dte[par]
        xtok = L_xtok[par]
        btok = L_btok[par]
        xdt = L_xdt[par]
        xw = L_xw[par]
        segc = L_segc[par]
        cbm = L_cbm[par]
        MT = L_MT[par]
        t1 = L_t1[par]
        t2 = L_t2[par]
        ssq = L_ssq[par]
        yn = L_yn[par]
        ctk = L_ctk[par]
        for hh in range(8):
            k.matmul(ydg[0:cl, hh * 64:(hh + 1) * 64], MT[0:cl, hh, 0:cl], xdt[0:cl, hh * 64:(hh + 1) * 64])
        k.matmul(yof[0:cl, :], ctk[:, 0:cl], Sb.v)
        k.tt(t1[0:cl, :].rearrange("p (h d) -> p h d", h=8), yof[0:cl, :].rearrange("p (h d) -> p h d", h=8),
             eac[0:cl, :].unsqueeze(2).broadcast_to([cl, 8, 64]), ALU.mult)
        k.tt(t1[0:cl, :], t1[0:cl, :], ydg[0:cl, :], ALU.add)
        k.tt(t2[0:cl, :], xtok[0:cl, :], dbc[0:cl, :], ALU.mult, eng="pool")
        k.tt(t1[0:cl, :], t1[0:cl, :], t2[0:cl, :], ALU.add)
        k.tt(t1[0:cl, :], t1[0:cl, :], zs[0:cl, :], ALU.mult)
        k.act(t2[0:cl, :], t1[0:cl, :], AF.Square, accum_out=ssq[0:cl, :])
        k.ts(ssq[0:cl, :], ssq[0:cl, :], 1.0 / 512, ALU.mult, EPS, ALU.add)
        k.act(ssq[0:cl, :], ssq[0:cl, :], AF.Sqrt)
        k.recip(ssq[0:cl, :], ssq[0:cl, :])
        k.stt(yn[0:cl, :], t1[0:cl, :], ssq[0:cl, 0:1], ngb[0:cl, :], ALU.mult, ALU.mult)
        for m in range(4):
            k.transpose(tr[:, m * 128:m * 128 + cl], yn[0:cl, m * 128:(m + 1) * 128], idb[0:cl, 0:cl])
        for m in range(4):
            k.copy(yTs[m][:, 0:cl], tr[:, m * 128:m * 128 + cl], eng=("act" if m % 2 else "dve"))
            k.dma("act", yT[m * 128:(m + 1) * 128, tok0:tok0 + cl], yTs[m][:, 0:cl])
        k.ts(dte[0:cl, :], aend[0:cl, :], 1.0 / cl, ALU.mult)
        k.matmul(misc[:, 16:24], ones[0:cl, :], dte[0:cl, :])
        k.act(eend.v, misc[:, 16:24], AF.Exp)
        k.tt(dte[0:cl, :], aend[0:cl, :], acum[0:cl, :], ALU.subtract)
        k.act(dte[0:cl, :], dte[0:cl, :], AF.Exp)
        k.tt(xw[0:cl, :].rearrange("p (h d) -> p h d", h=8), xdt[0:cl, :].rearrange("p (h d) -> p h d", h=8),
             dte[0:cl, :].unsqueeze(2).broadcast_to([cl, 8, 64]), ALU.mult)
        k.matmul(yof.v, btok[0:cl, :], xw[0:cl, :])
        k.tt(S.v.rearrange("p (h d) -> p h d", h=8), S.v.rearrange("p (h d) -> p h d", h=8),
             eend.v.unsqueeze(2).broadcast_to([128, 8, 64]), ALU.mult)
        k.tt(S.v, S.v, yof.v, ALU.add)
        k.copy(Sb.v, S.v, eng="pool")


    pendB = None
    nchunk = 0
    for bi, (t0, t1_) in enumerate(blocks):
        n = t1_ - t0
        if bi == 0:
            h = hbm
            k.dma("sp", h.v, hn_meta.v)
        else:
            h = hb[bi % 2]
            k.dma("sp", h.v, hn_own[bi - 1])
        for m in range(6):
            a = accs[m % 2]
            c0 = 512 + m * 128
            for kt in range(32):
                k.matmul(a[:, 0:n], w[:, kt, c0:c0 + 128], h[:, kt, 0:n], start=(kt == 0), stop=(kt == 31))
            ci = cin[m]
            k.copy(ci[:, 3:3 + n], a[:, 0:n], eng="act")
            ca = cacc[m % 2]
            k.ts(ca[:, 0:n], ci[:, 0:n], cw[:, m, 0:1], ALU.mult, cb[:, m:m + 1], ALU.add)
            for j in range(1, 4):
                k.stt(ca[:, 0:n], ci[:, j:j + n], cw[:, m, j:j + 1], ca[:, 0:n], ALU.mult, ALU.add)
            k.act(fT[m][:, 0:n], ca[:, 0:n], AF.Silu)
            k.copy(ci[:, 0:3], ci[:, n:n + 3], eng="pool")
        for s0 in range(0, n, 128):
            cl = min(128, n - s0)
            ctx = dict(cl=cl, cs=slice(s0, s0 + cl), tok0=t0 + s0, h=h, par=nchunk % 2)
            stageA(ctx)
            if pendB is not None:
                stageB(pendB)
            pendB = ctx
            nchunk += 1
    if pendB is not None:
        stageB(pendB)


def hyb_s5(k, T, hn_meta, hn_own, w_s5, bre, bim, cre, cim, are_l, aim_l, ldt_l, d_l, gT, sgT, pfx="h5"):
    blocks = tok_blocks(T, BS_A)
    L = BS_A
    stg = [k.sb([128, 512], F32, pfx + f"stg{i}") for i in range(2)]
    w = k.sb([128, 32, 512], BF16, pfx + "w")
    load_cast(k, w, w_s5, 32, 512, stg)
    hbm = k.sb([128, 32, 16], BF16, pfx + "hbm")
    hb = [k.sb([128, 32, BS_A], BF16, pfx + f"hb{i}") for i in range(2)]
    f_bre = k.sb([128, 8, 128], F32, pfx + "fbre"); f_bim = k.sb([128, 8, 128], F32, pfx + "fbim")
    f_cre = k.sb([128, 8, 128], F32, pfx + "fcre"); f_cim = k.sb([128, 8, 128], F32, pfx + "fcim")
    Bre = k.sb([128, 8, 128], BF16, pfx + "Bre"); Bim = k.sb([128, 8, 128], BF16, pfx + "Bim")
    Cre = k.sb([128, 8, 128], BF16, pfx + "Cre"); Cim = k.sb([128, 8, 128], BF16, pfx + "Cim")
    for (dst, f, src) in ((Bre, f_bre, bre), (Bim, f_bim, bim), (Cre, f_cre, cre), (Cim, f_cim, cim)):
        k.dma("sp", f.v, src.v)
        k.copy(dst.v, f.v, eng="pool")
    are = k.sb([128, 8], F32, pfx + "are"); aim = k.sb([128, 8], F32, pfx + "aim"); dtt = k.sb([128, 8], F32, pfx + "dtt")
    dsk = k.sb([128, 2], F32, pfx + "dsk")
    k.dma("sp", are.v, are_l.v); k.dma("sp", aim.v, aim_l.v); k.dma("sp", dtt.v, ldt_l.v); k.dma("sp", dsk.v, d_l.v)
    k.act(dtt.v, dtt.v, AF.Exp)
    th = k.sb([128, 8], F32, pfx + "th"); rho = k.sb([128, 8], F32, pfx + "rho")
    k.tt(th.v, dtt.v, aim.v, ALU.mult)
    k.tt(rho.v, dtt.v, are.v, ALU.mult)
    k.act(rho.v, rho.v, AF.Exp)
    ki = k.sb([128, 8], I32, pfx + "ki"); kf = k.sb([128, 8], F32, pfx + "kf")
    hh_ = k.sb([128, 8], F32, pfx + "hh"); sh = k.sb([128, 8], F32, pfx + "sh"); ch = k.sb([128, 8], F32, pfx + "ch")
    k.ts(kf.v, th.v, 1.0 / (2 * math.pi), ALU.mult)
    k.copy(ki.v, kf.v, eng="dve")
    k.copy(kf.v, ki.v, eng="dve")
    k.stt(hh_.v, kf.v, -2 * math.pi, th.v, ALU.mult, ALU.add)
    k.ts(hh_.v, hh_.v, 0.5, ALU.mult)
    k.act(sh.v, hh_.v, AF.Sin)
    q4 = k.sb([128, 8], F32, pfx + "q4")
    k.act(q4.v, hh_.v, AF.Sin, scale=0.5)
    k.tt(q4.v, q4.v, q4.v, ALU.mult)
    k.ts(ch.v, q4.v, -2.0, ALU.mult, 1.0, ALU.add)
    zr = k.sb([128, 8, 9], F32, pfx + "zr"); zi = k.sb([128, 8, 9], F32, pfx + "zi"); nzi = k.sb([128, 8, 9], F32, pfx + "nzi")
    tmp8 = k.sb([128, 8], F32, pfx + "tmp8"); tmp8b = k.sb([128, 8], F32, pfx + "tmp8b")
    k.tt(tmp8.v, sh.v, sh.v, ALU.mult)
    k.ts(zr[:, :, 0], tmp8.v, -2.0, ALU.mult, 1.0, ALU.add)
    k.tt(tmp8.v, sh.v, ch.v, ALU.mult)
    k.ts(zi[:, :, 0], tmp8.v, 2.0, ALU.mult)
    for m_ in range(8):
        k.tt(tmp8.v, zr[:, :, m_], zr[:, :, m_], ALU.mult)
        k.tt(tmp8b.v, zi[:, :, m_], zi[:, :, m_], ALU.mult)
        k.tt(zr[:, :, m_ + 1], tmp8.v, tmp8b.v, ALU.subtract)
        k.tt(tmp8.v, zr[:, :, m_], zi[:, :, m_], ALU.mult)
        k.ts(zi[:, :, m_ + 1], tmp8.v, 2.0, ALU.mult)
    k.ts(nzi.v, zi.v, -1.0, ALU.mult)
    abr = k.sb([128, 8], F32, pfx + "abr"); abi = k.sb([128, 8], F32, pfx + "abi"); den = k.sb([128, 8], F32, pfx + "den")
    kre = k.sb([128, 8], F32, pfx + "kre"); kim = k.sb([128, 8], F32, pfx + "kim"); nkre = k.sb([128, 8], F32, pfx + "nkre")
    k.tt(abr.v, rho.v, zr[:, :, 0], ALU.mult)
    k.ts(abr.v, abr.v, -1.0, ALU.add)
    k.tt(abi.v, rho.v, zi[:, :, 0], ALU.mult)
    k.tt(den.v, are.v, are.v, ALU.mult)
    k.tt(tmp8.v, aim.v, aim.v, ALU.mult)
    k.tt(den.v, den.v, tmp8.v, ALU.add)
    k.recip(den.v, den.v)
    k.tt(kre.v, abr.v, are.v, ALU.mult)
    k.tt(tmp8.v, abi.v, aim.v, ALU.mult)
    k.tt(kre.v, kre.v, tmp8.v, ALU.add)
    k.tt(kre.v, kre.v, den.v, ALU.mult)
    k.tt(kim.v, abi.v, are.v, ALU.mult)
    k.tt(tmp8.v, abr.v, aim.v, ALU.mult)
    k.tt(kim.v, kim.v, tmp8.v, ALU.subtract)
    k.tt(kim.v, kim.v, den.v, ALU.mult)
    k.ts(nkre.v, kre.v, -1.0, ALU.mult)
    Fc = k.sb([128, 8, L], F32, pfx + "Fc"); Fs = k.sb([128, 8, L], F32, pfx + "Fs")
    Ere = k.sb([128, 8, L], F32, pfx + "Ere"); Eim = k.sb([128, 8, L], F32, pfx + "Eim")
    rhoT = k.sb([128, 8, L], F32, pfx + "rhoT")
    tl = k.sb([128, L], F32, pfx + "tl")
    k.memset(Fc.v, 1.0)
    k.memset(Fs.v, 0.0)
    k.memset(rhoT.v, 1.0)
    for j in range(8):
        for m_ in range(8):
            lo = slice(0, 2 ** m_)
            hi = slice(2 ** m_, 2 ** (m_ + 1))
            w_ = 2 ** m_
            k.ts(tl[:, 0:w_], Fs[:, j, lo], zi[:, j, m_:m_ + 1], ALU.mult)
            k.stt(Fc[:, j, hi], Fc[:, j, lo], zr[:, j, m_:m_ + 1], tl[:, 0:w_], ALU.mult, ALU.subtract)
            k.ts(tl[:, 0:w_], Fc[:, j, lo], zi[:, j, m_:m_ + 1], ALU.mult)
            k.stt(Fs[:, j, hi], Fs[:, j, lo], zr[:, j, m_:m_ + 1], tl[:, 0:w_], ALU.mult, ALU.add)
        k.ts(tl.v, Fs[:, j, :], kim[:, j:j + 1], ALU.mult)
        k.stt(Ere[:, j, :], Fc[:, j, :], kre[:, j:j + 1], tl.v, ALU.mult, ALU.add)
        k.ts(tl.v, Fs[:, j, :], nkre[:, j:j + 1], ALU.mult)
        k.stt(Eim[:, j, :], Fc[:, j, :], kim[:, j:j + 1], tl.v, ALU.mult, ALU.add)
        k.ts(rhoT[:, j, :], rhoT[:, j, :], rho[:, j:j + 1], ALU.mult)
    uf = [k.sb([128, BS_A], F32, pfx + f"uf{a}") for a in range(2)]
    ub = [k.sb([128, BS_A], BF16, pfx + f"ub{a}") for a in range(2)]
    go = [k.sb([128, BS_A], BF16, pfx + f"go{i}") for i in range(2)]
    L5 = {nm: [k.sb([128, BS_A], F32, pfx + f"{nm}{i}") for i in range(2)]
          for nm in ("vre", "vim", "p1", "p2", "p3", "p4", "wre", "wim", "q1", "q2", "q3", "q4")}
    sre = [k.sb([128, BS_A], BF16, pfx + f"sre{i}") for i in range(2)]
    sim_ = [k.sb([128, BS_A], BF16, pfx + f"sim{i}") for i in range(2)]
    wir = k.sb([128, 8], F32, pfx + "wir"); wii = k.sb([128, 8], F32, pfx + "wii")
    k.memset(wir.v, 0.0)
    k.memset(wii.v, 0.0)
    c1 = k.sb([128, 1], F32, pfx + "c1")
    x2 = k.sb([128, BS_A], F32, pfx + "x2"); x3 = k.sb([128, BS_A], F32, pfx + "x3"); yv = k.sb([128, BS_A], F32, pfx + "yv")
    acc = [k.ps([128, 512], F32, pfx + f"acc{i}") for i in range(2)]
    Pp = k.ps([128, 512], F32, pfx + "Pp"); Qp = k.ps([128, 512], F32, pfx + "Qp")
    yp = [k.ps([128, 512], F32, pfx + f"yp{a}") for a in range(2)]
    for bi, (t0, t1_) in enumerate(blocks):
        n = t1_ - t0
        if bi == 0:
            h = hbm
            k.dma("sp", h.v, hn_meta.v)
        else:
            h = hb[bi % 2]
            k.dma("sp", h.v, hn_own[bi - 1])
        lev = 4 if bi == 0 else 8
        for a in range(2):
            for kt in range(32):
                k.matmul(acc[0][:, 0:n], w[:, kt, a * 128:(a + 1) * 128], h[:, kt, 0:n], start=(kt == 0), stop=(kt == 31))
            k.copy(uf[a][:, 0:n], acc[0][:, 0:n], eng="dve")
            k.copy(ub[a][:, 0:n], uf[a][:, 0:n], eng="pool")
            for kt in range(32):
                k.matmul(acc[1][:, 0:n], w[:, kt, 256 + a * 128:256 + (a + 1) * 128], h[:, kt, 0:n], start=(kt == 0), stop=(kt == 31))
            o = go[a]
            k.act(o[:, 0:n], acc[1][:, 0:n], AF.Silu)
            k.dma("act", sgT[a * 128:(a + 1) * 128, t0:t1_], o[:, 0:n])
        def pq(j):
            a = j // 4
            k.matmul(Pp[:, 0:n], Bre[:, j, :], ub[a][:, 0:n])
            k.matmul(Qp[:, 0:n], Bim[:, j, :], ub[a][:, 0:n])

        pq(0)
        for j in range(8):
            a = j // 4
            vre, vim, p1, p2, p3, p4, wre, wim, q1, q2, q3, q4 = (L5[nm][j % 2] for nm in
                ("vre", "vim", "p1", "p2", "p3", "p4", "wre", "wim", "q1", "q2", "q3", "q4"))
            k.tt(p1[:, 0:n], Pp[:, 0:n], Ere[:, j, 0:n], ALU.mult)
            k.tt(p2[:, 0:n], Qp[:, 0:n], Eim[:, j, 0:n], ALU.mult)
            k.tt(p3[:, 0:n], Qp[:, 0:n], Ere[:, j, 0:n], ALU.mult)
            k.tt(p4[:, 0:n], Pp[:, 0:n], Eim[:, j, 0:n], ALU.mult)
            if j + 1 < 8:
                pq(j + 1)
            k.tt(vre[:, 0:n], p1[:, 0:n], p2[:, 0:n], ALU.subtract, eng="pool")
            k.tt(vim[:, 0:n], p3[:, 0:n], p4[:, 0:n], ALU.add, eng="pool")
            k.scan(wre[:, 0:n], rhoT[:, j, 0:n], vre[:, 0:n], wir[:, j:j + 1])
            k.scan(wim[:, 0:n], rhoT[:, j, 0:n], vim[:, 0:n], wii[:, j:j + 1])
            k.ts(c1.v, wim[:, n - 1:n], nzi[:, j, lev:lev + 1], ALU.mult)
            k.stt(wir[:, j:j + 1], wre[:, n - 1:n], zr[:, j, lev:lev + 1], c1.v, ALU.mult, ALU.add)
            k.ts(c1.v, wre[:, n - 1:n], zi[:, j, lev:lev + 1], ALU.mult)
            k.stt(wii[:, j:j + 1], wim[:, n - 1:n], zr[:, j, lev:lev + 1], c1.v, ALU.mult, ALU.add)
            sr, si = sre[j % 2], sim_[j % 2]
            k.tt(q1[:, 0:n], wre[:, 0:n], Fc[:, j, 0:n], ALU.mult, eng="pool")
            k.tt(q2[:, 0:n], wim[:, 0:n], Fs[:, j, 0:n], ALU.mult, eng="pool")
            k.tt(sr[:, 0:n], q1[:, 0:n], q2[:, 0:n], ALU.subtract, eng="pool")
            k.tt(q3[:, 0:n], wre[:, 0:n], Fs[:, j, 0:n], ALU.mult, eng="pool")
            k.tt(q4[:, 0:n], wim[:, 0:n], Fc[:, j, 0:n], ALU.mult, eng="pool")
            k.stt(si[:, 0:n], q3[:, 0:n], -1.0, q4[:, 0:n], ALU.mult, ALU.subtract)
            k.matmul(yp[a][:, 0:n], Cre[:, j, :], sr[:, 0:n], start=(j % 4 == 0), stop=False)
            k.matmul(yp[a][:, 0:n], Cim[:, j, :], si[:, 0:n], start=False, stop=(j % 4 == 3))
        for a in range(2):
            k.stt(yv[:, 0:n], uf[a][:, 0:n], dsk[:, a:a + 1], yp[a][:, 0:n], ALU.mult, ALU.add)
            k.tt(x2[:, 0:n], yv[:, 0:n], yv[:, 0:n], ALU.mult, eng="pool")
            k.ts(x2[:, 0:n], x2[:, 0:n], 0.044715, ALU.mult, 1.0, ALU.add, eng="pool")
            k.tt(x3[:, 0:n], x2[:, 0:n], yv[:, 0:n], ALU.mult, eng="pool")
            k.act(x3[:, 0:n], x3[:, 0:n], AF.Sigmoid, scale=GELU_C)
            o = go[a]
            k.tt(o[:, 0:n], x3[:, 0:n], yv[:, 0:n], ALU.mult)
            k.dma("act", gT[a * 128:(a + 1) * 128, t0:t1_], o[:, 0:n])


PERM = np.concatenate([np.arange(32, 64), np.arange(0, 32)])


def ktile(w):
    K, M = w.shape
    return np.ascontiguousarray(w.reshape(K // 128, 128, M).transpose(1, 0, 2))


def vec_tile(g):
    return np.ascontiguousarray(g.reshape(-1, 128).T)


def rope_tables(T):
    pos = np.arange(T, dtype=np.float32)
    inv_freq = (np.float32(10000.0) ** (-np.arange(0, 64, 2, dtype=np.float32) / np.float32(64))).astype(np.float32)
    ang = (pos[:, None] * inv_freq[None, :]).astype(np.float32)
    cos = np.cos(ang).astype(np.float32).T
    sin = np.sin(ang).astype(np.float32).T
    cos2 = np.ascontiguousarray(np.concatenate([cos, cos], 0))
    sin2s = np.ascontiguousarray(np.concatenate([-sin, sin], 0))
    return cos2, sin2s


def mla_weights(c, w_in, q_norm, w_uq, kv_norm, w_ukv):
    kr = w_in[:, 1536:1600]
    wkv = np.concatenate([w_in[:, 1024:1536], kr, kr[:, PERM], w_in[:, 1600 + c * 512:1600 + (c + 1) * 512]], 1)
    uq = []
    kn = []
    vv = []
    for h in range(4):
        b = (4 * c + h) * 192
        rope = w_uq[:, b + 128:b + 192]
        uq += [w_uq[:, b:b + 128], rope, rope[:, PERM]]
        b2 = (4 * c + h) * 256
        kn.append(w_ukv[:, b2:b2 + 128])
        vv.append(w_ukv[:, b2 + 128:b2 + 256])
    return dict(
        wq_in=ktile(w_in[:, 0:1024]),
        wkv_in=ktile(wkv),
        wuq=ktile(np.concatenate(uq, 1)),
        wukv=ktile(np.concatenate(kn + vv, 1)),
        gq=vec_tile(q_norm),
        gkv=vec_tile(kv_norm),
    )


def out_w_tile(W):
    K = W.shape[0]
    nkt = K // 128
    return np.ascontiguousarray(W.reshape(nkt, 128, 32, 128).transpose(2, 1, 0, 3).reshape(32, 128, nkt * 128))


def hn_layout(hnT):
    T = hnT.shape[1]
    nb = (T - 16) // 256
    v = hnT.reshape(32, 128, T)
    meta = np.ascontiguousarray(v[:, :, 0:16].transpose(1, 0, 2))
    own = np.ascontiguousarray(v[:, :, 16:].reshape(32, 128, nb, 256).transpose(2, 1, 0, 3))
    return dict(hn_meta=meta, hn_own=own)


def const_mats():
    U = np.triu(np.ones((128, 128), np.float32))
    ident = np.eye(128, dtype=np.float32)
    return U, ident


def hyb_weights(c, w_in, conv_w, conv_b, dt_bias, a_log, d_skip, norm_g,
                a_re, a_im, log_dt, b_re, b_im, c_re, c_im, s5_d):
    z = w_in[:, c * 512:(c + 1) * 512]
    x = w_in[:, 4096 + c * 512:4096 + (c + 1) * 512]
    B = w_in[:, 8192 + c * 128:8192 + (c + 1) * 128]
    C = w_in[:, 9216 + c * 128:9216 + (c + 1) * 128]
    dt = w_in[:, 10240 + c * 8:10240 + (c + 1) * 8]
    w_ssd = ktile(np.concatenate([z, x, B, C, dt], 1))
    u = w_in[:, 10304 + c * 256:10304 + (c + 1) * 256]
    gate = w_in[:, 12352 + c * 256:12352 + (c + 1) * 256]
    w_s5 = ktile(np.concatenate([u, gate], 1))
    chans = [np.arange(c * 512 + m * 128, c * 512 + (m + 1) * 128) for m in range(4)]
    chans.append(np.arange(4096 + c * 128, 4096 + (c + 1) * 128))
    chans.append(np.arange(5120 + c * 128, 5120 + (c + 1) * 128))
    convw = np.ascontiguousarray(np.stack([conv_w[:, ch].T for ch in chans], 1))
    convb = np.ascontiguousarray(np.stack([conv_b[ch] for ch in chans], 1))
    hs = slice(c * 8, (c + 1) * 8)
    dtb_bc = np.ascontiguousarray(np.broadcast_to(dt_bias[hs][None, :], (128, 8)))
    alog_bc = np.ascontiguousarray(np.broadcast_to(a_log[hs][None, :], (128, 8)))
    d_bc = np.ascontiguousarray(np.broadcast_to(np.repeat(d_skip[hs], 64)[None, :], (128, 512)))
    ng_bc = np.ascontiguousarray(np.broadcast_to(norm_g[c * 512:(c + 1) * 512][None, :], (128, 512)))
    bre = np.zeros((128, 8, 128), np.float32); bim = np.zeros((128, 8, 128), np.float32)
    cre = np.zeros((128, 8, 128), np.float32); cim = np.zeros((128, 8, 128), np.float32)
    are_l = np.zeros((128, 8), np.float32); aim_l = np.zeros((128, 8), np.float32); ldt_l = np.zeros((128, 8), np.float32)
    for j in range(8):
        a, q = j // 4, (j % 4) * 32
        for m in range(2):
            g = 16 * c + 2 * j + m
            bre[q + m * 16:q + (m + 1) * 16, j, m * 64:(m + 1) * 64] = b_re[g].T
            bim[q + m * 16:q + (m + 1) * 16, j, m * 64:(m + 1) * 64] = b_im[g].T
            cre[m * 64:(m + 1) * 64, j, q + m * 16:q + (m + 1) * 16] = c_re[g].T
            cim[m * 64:(m + 1) * 64, j, q + m * 16:q + (m + 1) * 16] = c_im[g].T
            are_l[m * 64:(m + 1) * 64, j] = a_re[g]
            aim_l[m * 64:(m + 1) * 64, j] = a_im[g]
            ldt_l[m * 64:(m + 1) * 64, j] = log_dt[g]
    d_l = np.ascontiguousarray(s5_d[c * 256:(c + 1) * 256].reshape(2, 128).T)
    return dict(w_ssd=w_ssd, w_s5=w_s5, convw=convw, convb=convb, dtb_bc=dtb_bc, alog_bc=alog_bc, d_bc=d_bc, ng_bc=ng_bc,
                bre=bre, bim=bim, cre=cre, cim=cim, are_l=are_l, aim_l=aim_l, ldt_l=ldt_l, d_l=d_l)


def glu_w_tile(W):
    return np.ascontiguousarray(W.reshape(16, 128, 16, 128).transpose(2, 1, 0, 3).reshape(16, 128, 2048))

from concourse.bass_utils import run_bass_kernel_spmd

BFNP = ml_dtypes.bfloat16
T_ALL = 16400
TC = 2064
NCORE = 8
_PROGS = {}

HYB_SHAPES = dict(w_ssd=[128, 32, 1288], w_s5=[128, 32, 512], convw=[128, 6, 4], convb=[128, 6], dtb_bc=[128, 8],
                  alog_bc=[128, 8], d_bc=[128, 512], ng_bc=[128, 512], bre=[128, 8, 128], bim=[128, 8, 128],
                  cre=[128, 8, 128], cim=[128, 8, 128], are_l=[128, 8], aim_l=[128, 8], ldt_l=[128, 8], d_l=[128, 2],
                  Umat=[128, 128], ident=[128, 128])


def _new():
    return bass.Bass("TRN2", target_bir_lowering=False)


def prog_norm():
    nc = _new()
    with contextlib.ExitStack() as st:
        k = KB(nc, st)
        hT = k.dram("hT", [D, TC], F32, kind="ExternalInput")
        g_l = k.dram("g_l", [128, 32], F32, kind="ExternalInput")
        hnT = k.dram("hnT", [D, TC], BF16, kind="ExternalOutput")
        stage_out(k, hT, None, None, g_l, None, hnT, TC, 0, True, BF16)
        k.final_wait("sp", [hnT])
        k.emit()
    return nc


def prog_out(hyb, final):
    nc = _new()
    nkt = 48 if hyb else 32
    with contextlib.ExitStack() as st:
        k = KB(nc, st)
        hT = k.dram("hT", [D, TC], F32, kind="ExternalInput")
        yT = k.dram("yT", [D, TC], BF16, kind="ExternalInput")
        wl = k.dram("wl", [32, 128, nkt * 128], F32, kind="ExternalInput")
        g_l = k.dram("g_l", [128, 32], F32, kind="ExternalInput")
        glu = None
        if hyb:
            g_all = k.dram("g_all", [2048, TC], BF16, kind="ExternalInput")
            sg_all = k.dram("sg_all", [2048, TC], BF16, kind="ExternalInput")
            wglu_l = k.dram("wglu_l", [16, 128, 2048], F32, kind="ExternalInput")
            glu = (g_all, sg_all, wglu_l)
        outs = []
        if not final:
            hT_new = k.dram("hT_new", [D, TC], F32, kind="ExternalOutput")
            outs.append(hT_new)
        else:
            hT_new = k.dram("hT_new", [D, TC], F32)
        hnT = k.dram("hnT", [D, TC], F32 if final else BF16, kind="ExternalOutput")
        outs.append(hnT)
        ybT = None
        if hyb:
            ybT = k.dram("ybT", [2048, TC], BF16)
            with k.scope():
                stage_glu(k, g_all, sg_all, wglu_l, ybT, TC)
        with k.scope():
            stage_out(k, hT, yT, wl, g_l, hT_new, hnT, TC, nkt, False, F32 if final else BF16, glu=ybT)
        k.final_wait("sp", outs)
        k.emit()
    return nc


def prog_hyb():
    nc = _new()
    T = T_ALL
    with contextlib.ExitStack() as st:
        k = KB(nc, st)
        hn_meta = k.dram("hn_meta", [128, 32, 16], BF16, kind="ExternalInput")
        hn_own = k.dram("hn_own", [(T - 16) // 256, 128, 32, 256], BF16, kind="ExternalInput")
        d = {n: k.dram(n, s, F32, kind="ExternalInput") for n, s in HYB_SHAPES.items()}
        yT = k.dram("yT", [512, T], BF16, kind="ExternalOutput")
        gT = k.dram("gT", [256, T], BF16, kind="ExternalOutput")
        sgT = k.dram("sgT", [256, T], BF16, kind="ExternalOutput")
        with k.scope():
            hyb_ssd(k, T, hn_meta, hn_own, d["w_ssd"], d["convw"], d["convb"], d["dtb_bc"], d["alog_bc"], d["d_bc"],
                    d["ng_bc"], d["Umat"], d["ident"], yT)
        with k.scope():
            hyb_s5(k, T, hn_meta, hn_own, d["w_s5"], d["bre"], d["bim"], d["cre"], d["cim"], d["are_l"], d["aim_l"],
                   d["ldt_l"], d["d_l"], gT, sgT)
        k.final_wait("sp", [yT, gT, sgT])
        k.emit()
    return nc


def prog_mla():
    nc = _new()
    T = T_ALL
    with contextlib.ExitStack() as st:
        k = KB(nc, st)
        hn_meta = k.dram("hn_meta", [128, 32, 16], BF16, kind="ExternalInput")
        hn_own = k.dram("hn_own", [(T - 16) // 256, 128, 32, 256], BF16, kind="ExternalInput")
        wq_in = k.dram("wq_in", [128, 32, 1024], F32, kind="ExternalInput")
        wkv_in = k.dram("wkv_in", [128, 32, 1152], F32, kind="ExternalInput")
        wuq = k.dram("wuq", [128, 8, 1024], F32, kind="ExternalInput")
        wukv = k.dram("wukv", [128, 4, 1024], F32, kind="ExternalInput")
        gq = k.dram("gq", [128, 8], F32, kind="ExternalInput")
        gkv = k.dram("gkv", [128, 4], F32, kind="ExternalInput")
        cos2 = k.dram("cos2", [64, T], F32, kind="ExternalInput")
        sin2s = k.dram("sin2s", [64, T], F32, kind="ExternalInput")
        yT = k.dram("yT", [512, T], BF16, kind="ExternalOutput")
        stage_mla(k, T, hn_meta, hn_own, wq_in, wkv_in, wuq, wukv, gq, gkv, cos2, sin2s, yT)
        k.final_wait("sp", [yT])
        k.emit()
    return nc


def _get(name, fn, *a):
    if name not in _PROGS:
        _PROGS[name] = fn(*a)
    return _PROGS[name]


def _run(nc, in_maps):
    res = run_bass_kernel_spmd(nc, in_maps, core_ids=list(range(NCORE)))
    return res.results


def _tok_idx(c):
    return np.concatenate([np.arange(16), 16 + 2048 * c + np.arange(2048)])


def _gather_tokens(per_core):
    return np.concatenate([per_core[0][:, 0:16]] + [per_core[c][:, 16:] for c in range(NCORE)], axis=1)


def _split_tokens(full):
    return [np.ascontiguousarray(full[:, _tok_idx(c)]) for c in range(NCORE)]


def kernel(x, meta, hyb_norm, hyb_w_in, ssd_conv_w, ssd_conv_b, ssd_dt_bias, ssd_a_log, ssd_d, ssd_norm,
           s5_a_re, s5_a_im, s5_log_dt, s5_b_re, s5_b_im, s5_c_re, s5_c_im, s5_d, s5_w_glu, hyb_w_out,
           mla_norm, mla_w_in, mla_q_norm, mla_w_uq, mla_kv_norm, mla_w_ukv, mla_w_out, final_norm):
    f32 = lambda a: np.asarray(a, dtype=np.float32)
    x, meta = f32(x), f32(meta)
    h_full_T = np.ascontiguousarray(np.concatenate([meta, x[0]], axis=0).T)
    hT = _split_tokens(h_full_T)
    del h_full_T
    U, ident = const_mats()
    cos2, sin2s = rope_tables(T_ALL)

    g0 = vec_tile(f32(hyb_norm[0]))
    res = _run(_get("norm", prog_norm), [dict(hT=hT[c], g_l=g0) for c in range(NCORE)])
    hn = [np.asarray(r["hnT"]) for r in res]

    for layer in range(4):
        i = layer // 2
        hn_l = hn_layout(_gather_tokens(hn))
        last = (layer == 3)
        if layer % 2 == 0:
            ims = []
            for c in range(NCORE):
                im = hyb_weights(c, f32(hyb_w_in[i]), f32(ssd_conv_w[i]), f32(ssd_conv_b[i]), f32(ssd_dt_bias[i]),
                                 f32(ssd_a_log[i]), f32(ssd_d[i]), f32(ssd_norm[i]), f32(s5_a_re[i]), f32(s5_a_im[i]),
                                 f32(s5_log_dt[i]), f32(s5_b_re[i]), f32(s5_b_im[i]), f32(s5_c_re[i]), f32(s5_c_im[i]),
                                 f32(s5_d[i]))
                im.update(Umat=U, ident=ident, **hn_l)
                ims.append(im)
            res = _run(_get("hyb", prog_hyb), ims)
            del ims
            y_all = _split_tokens(np.concatenate([np.asarray(r["yT"]) for r in res], axis=0))
            g_all = _split_tokens(np.concatenate([np.asarray(r["gT"]) for r in res], axis=0))
            sg_all = _split_tokens(np.concatenate([np.asarray(r["sgT"]) for r in res], axis=0))
            wl = out_w_tile(f32(hyb_w_out[i]))
            wglu_l = glu_w_tile(f32(s5_w_glu[i]))
            gn = vec_tile(f32(mla_norm[i]))
            ims = [dict(hT=hT[c], yT=y_all[c], wl=wl, g_l=gn, g_all=g_all[c], sg_all=sg_all[c], wglu_l=wglu_l)
                   for c in range(NCORE)]
            res = _run(_get("out_hyb", prog_out, True, False), ims)
        else:
            mw = None
            ims = []
            for c in range(NCORE):
                im = mla_weights(c, f32(mla_w_in[i]), f32(mla_q_norm[i]), f32(mla_w_uq[i]), f32(mla_kv_norm[i]),
                                 f32(mla_w_ukv[i]))
                im.update(cos2=cos2, sin2s=sin2s, **hn_l)
                ims.append(im)
            res = _run(_get("mla", prog_mla), ims)
            del ims
            y_all = _split_tokens(np.concatenate([np.asarray(r["yT"]) for r in res], axis=0))
            wl = out_w_tile(f32(mla_w_out[i]))
            gn = vec_tile(f32(final_norm) if last else f32(hyb_norm[i + 1]))
            ims = [dict(hT=hT[c], yT=y_all[c], wl=wl, g_l=gn) for c in range(NCORE)]
            res = _run(_get("out_mla_final" if last else "out_mla", prog_out, False, last), ims)
        del ims
        if not last:
            hT = [np.asarray(r["hT_new"]) for r in res]
        hn = [np.asarray(r["hnT"]) for r in res]

    out = np.concatenate([hn[c][:, 16:].T for c in range(NCORE)], axis=0)
    return np.ascontiguousarray(out[None].astype(np.float32))
```

```python
import contextlib
import math
import os
import numpy as np
import ml_dtypes


import concourse.bass as bass
import concourse.mybir as mybir

F32 = mybir.dt.float32
BF16 = mybir.dt.bfloat16
I32 = mybir.dt.int32
AF = mybir.ActivationFunctionType
ALU = mybir.AluOpType
AX = mybir.AxisListType

COMPUTE = ("pe", "act", "dve", "pool")


class View:
    __slots__ = ("tl", "ap")

    def __init__(self, tl, ap):
        self.tl = tl
        self.ap = ap

    def __getitem__(self, idx):
        return View(self.tl, self.ap[idx])

    def rearrange(self, pat, **kw):
        return View(self.tl, self.ap.rearrange(pat, **kw))

    def broadcast_to(self, shape):
        return View(self.tl, self.ap.broadcast_to(list(shape)))

    def unsqueeze(self, ax):
        return View(self.tl, self.ap.unsqueeze(ax))

    def partition_broadcast(self, n):
        return View(self.tl, self.ap.partition_broadcast(n))

    def bitcast(self, dt):
        return View(self.tl, self.ap.bitcast(dt))

    @property
    def shape(self):
        return self.ap.shape


class Tl:
    __slots__ = ("t", "name", "lw", "rd", "dsem", "dcnt", "is_dram", "is_psum")

    def __init__(self, t, name, is_dram=False, is_psum=False):
        self.is_psum = is_psum
        self.t = t
        self.name = name
        self.lw = {}
        self.rd = {}
        self.dsem = None
        self.dcnt = 0
        self.is_dram = is_dram

    def __getitem__(self, idx):
        return View(self, self.t[idx])

    def rearrange(self, pat, **kw):
        return View(self, self.t.rearrange(pat, **kw))

    @property
    def v(self):
        return View(self, self.t[:])


def _is_view(x):
    return isinstance(x, View)


class KB:
    def __init__(self, nc, stack):
        self.nc = nc
        self.stack = stack
        self.root = stack
        self.lists = {e: [] for e in ("pe", "act", "dve", "pool", "sp")}
        self.psem = {}
        self.pcnt = {}
        for e in COMPUTE:
            self.psem[e] = stack.enter_context(nc.semaphore("prog_" + e))
            self.pcnt[e] = 0
        self.known = {e: {} for e in self.lists}
        self.cinst = {e: [] for e in COMPUTE}
        self.ntile = 0
        self.tiles = []
        self.sem_pool = []
        self.n_sem = 4
        self.n_inst = 0
        self.n_wait = 0

    def sb(self, shape, dt, name=None):
        self.ntile += 1
        name = name or f"t{self.ntile}"
        t = self.stack.enter_context(self.nc.sbuf_tensor(name, list(shape), dt))
        tl = Tl(t, name)
        self.tiles.append(tl)
        return tl

    def ps(self, shape, dt, name=None):
        self.ntile += 1
        name = name or f"p{self.ntile}"
        t = self.stack.enter_context(self.nc.psum_tensor(name, list(shape), dt))
        return Tl(t, name, is_psum=True)

    def dram(self, name, shape, dt, kind="Internal", **kw):
        t = self.nc.dram_tensor(name, list(shape), dt, kind=kind, **kw)
        tl = Tl(t.ap(), name, is_dram=True)
        self.tiles.append(tl)
        return tl

    def _need(self, eng, ev, waits):
        if ev is None:
            return
        if ev[0] == "c":
            _, src, idx = ev
            if src == "pe" and eng == "pe":
                return
            key = ("c", src)
        else:
            _, sem, idx = ev
            key = id(sem)
        kn = self.known[eng]
        if kn.get(key, 0) >= idx:
            return
        kn[key] = idx
        if ev[0] == "c":
            self.cinst[src][idx - 1][4] = True
        waits[key] = ev

    def _deps(self, eng, reads, writes):
        waits = {}
        for t in reads:
            for ev in t.lw.values():
                self._need(eng, ev, waits)
            if t.is_psum:
                for ev in t.rd.values():
                    if not (ev[0] == "c" and ev[1] == eng):
                        self._need(eng, ev, waits)
        for t in writes:
            for ev in t.lw.values():
                self._need(eng, ev, waits)
            for ev in t.rd.values():
                self._need(eng, ev, waits)
        return list(waits.values())

    @staticmethod
    def _evkey(ev):
        return ("c", ev[1]) if ev[0] == "c" else id(ev[1])

    def _record(self, ev, reads, writes):
        key = self._evkey(ev)
        for t in reads:
            t.rd[key] = ev
        for t in writes:
            if t.is_dram and ev[0] == "d":
                t.lw[key] = ev
            else:
                t.lw = {key: ev}
            t.rd = {}

    def op(self, eng, fn, reads=(), writes=(), inc=True):
        reads = [r.tl if _is_view(r) else r for r in reads]
        writes = [w.tl if _is_view(w) else w for w in writes]
        waits = self._deps(eng, reads, writes)
        ent = [waits, fn, "c", eng, False]
        self.cinst[eng].append(ent)
        ev = ("c", eng, len(self.cinst[eng]))
        self.lists[eng].append(ent)
        self._record(ev, reads, writes)
        self.n_inst += 1
        self.n_wait += len(waits)

    def dma(self, q, out, in_, sem_tile=None, **kw):
        reads = [in_.tl]
        writes = [out.tl]
        waits = self._deps(q, reads, writes)
        st = sem_tile or (in_.tl if out.tl.is_dram and not in_.tl.is_dram else out.tl)
        if st.dsem is None:
            st.dsem, st.dcnt = self.get_sem("d_" + st.name)
        st.dcnt += 16
        ev = ("d", st.dsem, st.dcnt)
        oap, iap = out.ap, in_.ap
        self.lists[q].append([waits, lambda e: e.dma_start(out=oap, in_=iap, **kw), "d", st.dsem, True])
        self._record(ev, reads, writes)
        self.n_inst += 1
        self.n_wait += len(waits)

    def collective(self, kind, out, in_, op=None):
        reads = [in_.tl]
        writes = [out.tl]
        waits = self._deps("pool", reads, writes)
        st = out.tl
        if st.dsem is None:
            st.dsem, st.dcnt = self.get_sem("c_" + st.name)
        st.dcnt += 16
        ev = ("d", st.dsem, st.dcnt)
        oap, iap = out.ap, in_.ap
        aop = op if op is not None else ALU.bypass
        groups = [list(range(8))]
        self.lists["pool"].append([waits, lambda e: e.collective_compute(kind, aop, replica_groups=groups, ins=[iap], outs=[oap]), "d", st.dsem, True])
        self._record(ev, reads, writes)
        self.n_inst += 1

    def get_sem(self, name):
        if self.sem_pool:
            return self.sem_pool.pop()
        self.n_sem += 1
        return self.root.enter_context(self.nc.semaphore(name)), 0

    @contextlib.contextmanager
    def scope(self):
        old_stack, old_tiles = self.stack, self.tiles
        with contextlib.ExitStack() as st:
            self.stack = st
            self.tiles = []
            yield
            self.tiles = old_tiles + self.tiles
            self.barrier()
            new = self.tiles[len(old_tiles):]
            self.release([t for t in new if not t.is_dram])
            self.tiles = old_tiles + [t for t in new if t.is_dram]
            self.stack = old_stack

    def release(self, tiles):
        for t in tiles:
            if t.dsem is not None:
                self.sem_pool.append((t.dsem, t.dcnt))
                t.dsem = None

    def barrier(self):
        evs = [("c", e, len(self.cinst[e])) for e in COMPUTE if self.cinst[e]]
        evs += [("d", s, c) for (s, c) in self.all_dsems() if c > 0]
        for eng in self.lists:
            waits = {}
            for ev in evs:
                if ev[0] == "c" and ev[1] == eng:
                    continue
                saved = None
                if ev[0] == "c" and ev[1] == "pe" and eng == "pe":
                    continue
                self._need(eng, ev, waits)
            if waits:
                self.lists[eng].append([list(waits.values()), None, None, None, False])

    def all_dsems(self):
        out = [(t.dsem, t.dcnt) for t in self.tiles if t.dsem is not None]
        out += list(self.sem_pool)
        return out

    def final_wait(self, eng, tiles):
        waits = {}
        for t in tiles:
            for ev in t.lw.values():
                self._need(eng, ev, waits)
        self.lists[eng].append([list(waits.values()), None, None, None, False])

    def matmul(self, out, lhsT, rhs, start=True, stop=True):
        o, l, r = out.ap, lhsT.ap, rhs.ap
        self.op("pe", lambda e: e.matmul(o, lhsT=l, rhs=r, start=start, stop=stop), [lhsT, rhs], [out], inc=bool(stop))

    def transpose(self, out, in_, ident):
        o, i, d = out.ap, in_.ap, ident.ap
        self.op("pe", lambda e: e.transpose(o, i, d), [in_, ident], [out])

    def act(self, out, in_, func, bias=None, scale=None, accum_out=None):
        o, i = out.ap, in_.ap
        kw = {}
        rd = [in_]
        wr = [out]
        if bias is not None:
            if _is_view(bias):
                rd.append(bias)
                kw["bias"] = bias.ap
            else:
                kw["bias"] = bias
        if scale is not None:
            if _is_view(scale):
                rd.append(scale)
                kw["scale"] = scale.ap
            else:
                kw["scale"] = scale
        if accum_out is not None:
            wr.append(accum_out)
            kw["accum_out"] = accum_out.ap
        self.op("act", lambda e: e.activation(out=o, in_=i, func=func, **kw), rd, wr)

    def tt(self, out, in0, in1, op, eng="dve"):
        o, a, b = out.ap, in0.ap, in1.ap
        self.op(eng, lambda e: e.tensor_tensor(out=o, in0=a, in1=b, op=op), [in0, in1], [out])

    def ts(self, out, in0, s1, op0, s2=None, op1=None, eng="dve", accum_out=None):
        o, a = out.ap, in0.ap
        rd = [in0]
        wr = [out]
        a1 = s1
        a2 = s2
        if _is_view(s1):
            rd.append(s1)
            a1 = s1.ap
        if _is_view(s2):
            rd.append(s2)
            a2 = s2.ap
        kw = {}
        if op1 is not None:
            kw["op1"] = op1
        if accum_out is not None:
            wr.append(accum_out)
            kw["accum_out"] = accum_out.ap
        self.op(eng, lambda e: e.tensor_scalar(out=o, in0=a, scalar1=a1, scalar2=a2, op0=op0, **kw), rd, wr)

    def stt(self, out, in0, scalar, in1, op0, op1):
        o, a, b = out.ap, in0.ap, in1.ap
        rd = [in0, in1]
        s = scalar
        if _is_view(scalar):
            rd.append(scalar)
            s = scalar.ap
        self.op("dve", lambda e: e.scalar_tensor_tensor(out=o, in0=a, scalar=s, in1=b, op0=op0, op1=op1), rd, [out])

    def copy(self, out, in_, eng="dve"):
        o, i = out.ap, in_.ap
        if eng == "act":
            self.op("act", lambda e: e.copy(out=o, in_=i), [in_], [out])
        else:
            self.op(eng, lambda e: e.tensor_copy(out=o, in_=i), [in_], [out])

    def memset(self, out, val, eng="pool"):
        o = out.ap
        self.op(eng, lambda e: e.memset(o, val), [], [out])

    def recip(self, out, in_):
        o, i = out.ap, in_.ap
        self.op("dve", lambda e: e.reciprocal(out=o, in_=i), [in_], [out])

    def scan(self, out, d0, d1, initial, op0=ALU.mult, op1=ALU.add):
        o, a, b = out.ap, d0.ap, d1.ap
        rd = [d0, d1]
        ini = initial
        if _is_view(initial):
            rd.append(initial)
            ini = initial.ap
        self.op("dve", lambda e: e.tensor_tensor_scan(out=o, data0=a, data1=b, initial=ini, op0=op0, op1=op1), rd, [out])

    def reduce(self, out, in_, op, axis=AX.X):
        o, i = out.ap, in_.ap
        self.op("dve", lambda e: e.tensor_reduce(out=o, in_=i, axis=axis, op=op), [in_], [out])

    def emit(self):
        nc = self.nc
        lists = self.lists
        cum = {}
        for eng in COMPUTE:
            c = 0
            arr = []
            for ent in self.cinst[eng]:
                if ent[4]:
                    c += 1
                arr.append(c)
            cum[eng] = arr
        self.n_marked = {e: (cum[e][-1] if cum[e] else 0) for e in COMPUTE}
        psem = self.psem

        def run(e, items):
            for ent in items:
                waits, fn, kind, who, mark = ent
                for ev in waits:
                    if ev[0] == "c":
                        e.wait_ge(psem[ev[1]], cum[ev[1]][ev[2] - 1])
                    else:
                        e.wait_ge(ev[1], ev[2])
                if fn is None:
                    continue
                if kind == "d":
                    fn(e).then_inc(who, 16)
                elif mark:
                    fn(e).then_inc(psem[who], 1)
                else:
                    fn(e)

        with nc.Block() as block:
            @block.tensor
            def _(e):
                run(e, lists["pe"])

            @block.scalar
            def _(e):
                run(e, lists["act"])

            @block.vector
            def _(e):
                run(e, lists["dve"])

            @block.gpsimd
            def _(e):
                run(e, lists["pool"])

            @block.sync
            def _(e):
                run(e, lists["sp"])


EPS = 1e-6
D = 4096
NDT = 32


def col_groups(Tc, gmax=1024):
    groups = []
    s = 0
    while s < Tc:
        e = min(s + gmax, Tc)
        if 0 < Tc - e < 64:
            e = Tc
        groups.append((s, e))
        s = e
    return groups


def stage_glu(k, g_all, sg_all, wglu_l, ybT, Tc, pfx="gl"):
    GW = 1040
    gb = k.sb([128, 16, GW], BF16, pfx + "gb")
    sgb = k.sb([128, 16, GW], BF16, pfx + "sgb")
    gst = [k.sb([128, 2048], F32, pfx + f"gst{i}") for i in range(3)]
    gwb = [k.sb([128, 2048], BF16, pfx + f"gwb{i}") for i in range(2)]
    sig = [k.sb([128, 512], F32, pfx + f"sig{i}") for i in range(3)]
    yo = [k.sb([128, 512], BF16, pfx + f"yo{i}") for i in range(3)]
    acc = [k.ps([128, 512], F32, pfx + f"acc{i}") for i in range(4)]
    gv = g_all.rearrange("(kt p) t -> p kt t", p=128)
    sgv = sg_all.rearrange("(kt p) t -> p kt t", p=128)
    gcnt = 0
    u = 0
    for (c0, c1) in col_groups(Tc, 1024):
        gw = c1 - c0
        chunks = [(s, min(s + 512, c1)) for s in range(c0, c1, 512)]
        for kt0 in range(0, 16, 8):
            k.dma("sp", gb[:, kt0:kt0 + 8, 0:gw], gv[:, kt0:kt0 + 8, c0:c1])
            k.dma("sp", sgb[:, kt0:kt0 + 8, 0:gw], sgv[:, kt0:kt0 + 8, c0:c1])
        for mt in range(16):
            gs, gw_ = gst[gcnt % 3], gwb[gcnt % 2]
            k.dma("act", gs.v, wglu_l[mt])
            k.copy(gw_.v, gs.v, eng="pool")
            gcnt += 1
            for ci, (s0, s1) in enumerate(chunks):
                n = s1 - s0
                ac = acc[u % 4]
                for kt in range(16):
                    k.matmul(ac[:, 0:n], gw_[:, kt * 128:(kt + 1) * 128], gb[:, kt, s0 - c0:s1 - c0], start=(kt == 0), stop=(kt == 15))
                sg_ = sig[u % 3]
                y_ = yo[u % 3]
                k.act(sg_[:, 0:n], ac[:, 0:n], AF.Sigmoid)
                k.tt(sg_[:, 0:n], sg_[:, 0:n], gb[:, mt, s0 - c0:s1 - c0], ALU.mult)
                k.tt(y_[:, 0:n], sg_[:, 0:n], sgb[:, mt, s0 - c0:s1 - c0], ALU.mult, eng="pool")
                k.dma("sp", ybT[mt * 128:(mt + 1) * 128, s0:s1], y_[:, 0:n])
                u += 1


def stage_out(k, hT, yT, wl, g_l, hT_new, hnT, Tc, nkt, first, out_dt, pfx="o", glu=None):
    ones = k.sb([128, 128], F32, pfx + "ones")
    k.memset(ones.v, 1.0)
    gt = k.sb([128, NDT], F32, pfx + "g")
    k.dma("sp", gt.v, g_l.v)
    GW = 1040
    GMAX = 1024
    if not first:
        yb = k.sb([128, nkt, GW], BF16, pfx + "yb")
        KH = nkt // 2
        NST = 3 if nkt > 32 else 4
        wst = [k.sb([128, KH * 128], F32, pfx + f"wst{i}") for i in range(NST)]
        wbf = [k.sb([128, nkt * 128], BF16, pfx + f"wbf{i}") for i in range(2)]
        acc = [k.ps([128, 512], F32, pfx + f"acc{i}") for i in range(4)]
        yTv = yT.rearrange("(kt p) t -> p kt t", p=128)
    ssq = [k.ps([128, 512], F32, pfx + f"ssq{i}") for i in range(3)]
    hin = [k.sb([128, 512], F32, pfx + f"hin{i}") for i in range(3)]
    hnw = [k.sb([128, 512], F32, pfx + f"hnw{i}") for i in range(3)]
    sq = [k.sb([128, 512], F32, pfx + f"sq{i}") for i in range(3)]
    rstd = k.sb([128, GW], F32, pfx + "rstd")
    hno = [k.sb([128, 512], out_dt, pfx + f"hno{i}") for i in range(3)]
    hsrc = hT if first else hT_new
    u = 0
    wcnt = 0
    if glu is not None:
        ybv = glu.rearrange("(kt p) t -> p kt t", p=128)
    for (c0, c1) in col_groups(Tc, GMAX):
        gw = c1 - c0
        chunks = [(s, min(s + 512, c1)) for s in range(c0, c1, 512)]
        assert len(chunks) <= 3 and gw <= GW
        if not first:
            nkt_y = nkt - 16 if glu is not None else nkt
            for kt0 in range(0, nkt_y, 8):
                k.dma("sp", yb[:, kt0:kt0 + 8, 0:gw], yTv[:, kt0:kt0 + 8, c0:c1])
        if glu is not None:
            for kt0 in range(0, 16, 8):
                k.dma("sp", yb[:, 32 + kt0:32 + kt0 + 8, 0:gw], ybv[:, kt0:kt0 + 8, c0:c1])
        pend = None
        for d in range(NDT):
            if not first:
                wb = wbf[wcnt % 2]
                for hh in range(2):
                    ws = wst[(2 * wcnt + hh) % NST]
                    k.dma("act" if hh == 0 else "sp", ws.v, wl[d, :, hh * KH * 128:(hh + 1) * KH * 128])
                    k.copy(wb[:, hh * KH * 128:(hh + 1) * KH * 128], ws.v, eng="pool")
                wcnt += 1
            for ci, (s0, s1) in enumerate(chunks):
                n = s1 - s0
                hi = hin[u % 3]
                hw = hnw[u % 3]
                sqt = sq[u % 3]
                k.dma("sp", hi[:, 0:n], hT[d * 128:(d + 1) * 128, s0:s1])
                if not first:
                    ac = acc[u % 4]
                    for kt in range(nkt):
                        k.matmul(ac[:, 0:n], wb[:, kt * 128:(kt + 1) * 128], yb[:, kt, s0 - c0:s1 - c0],
                                 start=(kt == 0), stop=(kt == nkt - 1))
                    k.tt(hw[:, 0:n], ac[:, 0:n], hi[:, 0:n], ALU.add)
                    k.dma("sp", hT_new[d * 128:(d + 1) * 128, s0:s1], hw[:, 0:n])
                    src = hw
                else:
                    src = hi
                k.act(sqt[:, 0:n], src[:, 0:n], AF.Square)
                if pend is not None:
                    k.matmul(*pend[0], **pend[1])
                pend = ((ssq[ci][:, 0:n], ones.v, sqt[:, 0:n]), dict(start=(d == 0), stop=(d == NDT - 1)))
                u += 1
        if pend is not None:
            k.matmul(*pend[0], **pend[1])
            pend = None
        for ci, (s0, s1) in enumerate(chunks):
            n = s1 - s0
            k.ts(rstd[:, s0 - c0:s1 - c0], ssq[ci][:, 0:n], 1.0 / D, ALU.mult, EPS, ALU.add)
            k.act(rstd[:, s0 - c0:s1 - c0], rstd[:, s0 - c0:s1 - c0], AF.Sqrt)
            k.recip(rstd[:, s0 - c0:s1 - c0], rstd[:, s0 - c0:s1 - c0])
        for d in range(NDT):
            for ci, (s0, s1) in enumerate(chunks):
                n = s1 - s0
                hi = hin[u % 3]
                ho = hno[u % 3]
                k.dma("sp", hi[:, 0:n], hsrc[d * 128:(d + 1) * 128, s0:s1])
                k.stt(ho[:, 0:n], hi[:, 0:n], gt[:, d:d + 1], rstd[:, s0 - c0:s1 - c0], ALU.mult, ALU.mult)
                k.dma("act", hnT[d * 128:(d + 1) * 128, s0:s1], ho[:, 0:n])
                u += 1

DBG_NB = int(os.environ.get('DBG_NB', '0'))
DBG_SKIP = os.environ.get('DBG_SKIP', '')
DBG_START = int(os.environ.get('DBG_START', '0'))

EPS = 1e-6
NH = 4
QSCALE = 192 ** -0.5


def tok_blocks(T, bs=512):
    assert (T - 16) % bs == 0
    return [(0, 16)] + [(s, s + bs) for s in range(16, T, bs)]


def load_cast(k, dst, src_dram, nkt, ncols, stg, q="act", ceng="pool"):
    if 'lc' in DBG_SKIP:
        k.memset(dst.v, 0.01)
        return
    for kt in range(nkt):
        s = stg[kt % len(stg)]
        k.dma(q, s[:, 0:ncols], src_dram[:, kt, :])
        k.copy(dst[:, kt, :], s[:, 0:ncols], eng=ceng)


def rstd_from_ssq(k, rstd, ssq, n, dim):
    k.ts(rstd[:, 0:n], ssq[:, 0:n], 1.0 / dim, ALU.mult, EPS, ALU.add)
    k.act(rstd[:, 0:n], rstd[:, 0:n], AF.Sqrt)
    k.recip(rstd[:, 0:n], rstd[:, 0:n])


BS_A = 256


def _a_common(k, pfx, hb=True):
    ones = k.sb([128, 128], BF16, pfx + "ones")
    k.memset(ones.v, 1.0)
    stg = [k.sb([128, 1152], F32, pfx + f"stg{i}") for i in range(2)]
    hb = [k.sb([128, 32, BS_A], BF16, pfx + f"hb{i}") for i in range(2)] if hb else None
    cst = [k.sb([128, BS_A], F32, pfx + f"cos{i}") for i in range(2)]
    snt = [k.sb([128, BS_A], F32, pfx + f"sin{i}") for i in range(2)]
    rstd = k.sb([128, BS_A], F32, pfx + "rstd")
    acc = [k.ps([128, 512], F32, pfx + f"acc{i}") for i in range(3)]
    ssq = k.ps([128, 512], F32, pfx + "ssq")
    up = [k.ps([128, 512], F32, pfx + f"up{i}") for i in range(3)]
    sqb = [k.sb([128, BS_A], BF16, pfx + f"sqb{i}") for i in range(2)]
    ra = [k.sb([128, BS_A], F32, pfx + f"ra{i}") for i in range(2)]
    rb = [k.sb([128, BS_A], F32, pfx + f"rb{i}") for i in range(2)]
    ob = [k.sb([128, 512], BF16, pfx + f"ob{i}") for i in range(4)]
    return ones, stg, hb, cst, snt, rstd, acc, ssq, up, sqb, ra, rb, ob


def mla_a1(k, T, hn_meta, hn_own, wq_in, wuq, gq, cos2, sin2s, qnT, qrT, pfx="m1"):
    blocks = tok_blocks(T, BS_A)
    hbm = k.sb([128, 32, 16], BF16, pfx + "hbm")
    ones, stg, hb, cst, snt, rstd, acc, ssq, up, sqb, ra, rb, ob = _a_common(k, pfx)
    oc = [0]

    def nob():
        oc[0] += 1
        return ob[oc[0] % 4]

    w1 = k.sb([128, 32, 1024], BF16, pfx + "w1")
    load_cast(k, w1, wq_in, 32, 1024, stg)
    wu = k.sb([128, 8, NH * 256], BF16, pfx + "wu")
    load_cast(k, wu, wuq, 8, NH * 256, stg)
    gqt = k.sb([128, 8], F32, pfx + "gq")
    if 'gq' not in DBG_SKIP:
        k.dma("sp", gqt.v, gq.v)
    cq = k.sb([128, 8, BS_A], F32, pfx + "cq")
    cqn = k.sb([128, 8, BS_A], BF16, pfx + "cqn")
    for bi, (t0, t1) in enumerate(blocks):
        if DBG_NB and bi >= DBG_NB:
            break
        if bi < DBG_START:
            continue
        n = t1 - t0
        if bi == 0:
            h = hbm
            k.dma("sp", h.v, hn_meta.v)
        else:
            h = hb[bi % 2]
            k.dma(os.environ.get("HQ", "sp"), h.v, hn_own[bi - 1])
        ct, sn = cst[bi % 2], snt[bi % 2]
        if 'cs' not in DBG_SKIP:
            _q = os.environ.get("CSQ", "sp")
            _o = 0 if os.environ.get("CS0") else t0
            k.dma(_q, ct[0:64, 0:n], cos2[:, _o:_o + n])
            k.dma(_q, sn[0:64, 0:n], sin2s[:, _o:_o + n])
        for m in range(8):
            a = acc[m % 3]
            for kt in range(32):
                k.matmul(a[:, 0:n], w1[:, kt, m * 128:(m + 1) * 128], h[:, kt, 0:n], start=(kt == 0), stop=(kt == 31))
            sq = sqb[m % 2]
            if 'sq' not in DBG_SKIP:
                k.act(sq[:, 0:n], a[:, 0:n], AF.Square)
            if 'cp' not in DBG_SKIP:
                k.copy(cq[:, m, 0:n], a[:, 0:n], eng=os.environ.get("CPENG","dve"))
            if 'ssq' not in DBG_SKIP:
                k.matmul(ssq[:, 0:n], ones.v, sq[:, 0:n], start=(m == 0), stop=(m == 7))
        if 'rstd' not in DBG_SKIP:
            rstd_from_ssq(k, rstd, ssq, n, 1024)
        for m in range(8):
            if 'stt' in DBG_SKIP:
                break
            k.stt(cqn[:, m, 0:n], cq[:, m, 0:n], gqt[:, m:m + 1], rstd[:, 0:n], ALU.mult, ALU.mult)
        for hd in range(NH):
            if 'up' in DBG_SKIP:
                break
            c0 = hd * 256
            u0 = up[0]
            for kt in range(8):
                k.matmul(u0[:, 0:n], wu[:, kt, c0:c0 + 128], cqn[:, kt, 0:n], start=(kt == 0), stop=(kt == 7))
            o = nob()
            k.act(o[:, 0:n], u0[:, 0:n], AF.Copy, scale=QSCALE)
            k.dma("act", qnT[hd, :, t0:t1], o[:, 0:n])
            u1, u2 = up[1], up[2]
            for kt in range(8):
                k.matmul(u1[0:64, 0:n], wu[:, kt, c0 + 128:c0 + 192], cqn[:, kt, 0:n], start=(kt == 0), stop=(kt == 7))
            for kt in range(8):
                k.matmul(u2[0:64, 0:n], wu[:, kt, c0 + 192:c0 + 256], cqn[:, kt, 0:n], start=(kt == 0), stop=(kt == 7))
            a_, b_ = ra[hd % 2], rb[hd % 2]
            k.tt(a_[0:64, 0:n], u1[0:64, 0:n], ct[0:64, 0:n], ALU.mult)
            k.tt(b_[0:64, 0:n], u2[0:64, 0:n], sn[0:64, 0:n], ALU.mult)
            o = nob()
            k.tt(a_[0:64, 0:n], a_[0:64, 0:n], b_[0:64, 0:n], ALU.add, eng="pool")
            k.act(o[0:64, 0:n], a_[0:64, 0:n], AF.Copy, scale=QSCALE)
            k.dma("act", qrT[hd, :, t0:t1], o[0:64, 0:n])


def mla_a2(k, T, hn_meta, hn_own, wkv_in, wukv, gkv, cos2, sin2s, knT, krT, vtok, gT, pfx="m2"):
    blocks = tok_blocks(T, BS_A)
    hbm = k.sb([128, 32, 16], BF16, pfx + "hbm")
    ones, stg, hb, cst, snt, rstd, acc, ssq, up, sqb, ra, rb, ob = _a_common(k, pfx)
    oc = [0]

    def nob():
        oc[0] += 1
        return ob[oc[0] % 4]

    w2 = k.sb([128, 32, 1152], BF16, pfx + "w2")
    load_cast(k, w2, wkv_in, 32, 1152, stg)
    wk = k.sb([128, 4, 1024], BF16, pfx + "wk")
    load_cast(k, wk, wukv, 4, 1024, stg)
    gkt = k.sb([128, 4], F32, pfx + "gk")
    k.dma("sp", gkt.v, gkv.v)
    ckv = k.sb([128, 4, BS_A], F32, pfx + "ckv")
    ckn = k.sb([128, 4, BS_A], BF16, pfx + "ckn")
    for bi, (t0, t1) in enumerate(blocks):
        n = t1 - t0
        if bi == 0:
            h = hbm
            k.dma("sp", h.v, hn_meta.v)
        else:
            h = hb[bi % 2]
            k.dma(os.environ.get("HQ", "sp"), h.v, hn_own[bi - 1])
        ct, sn = cst[bi % 2], snt[bi % 2]
        k.dma("sp", ct[0:64, 0:n], cos2[:, t0:t1])
        k.dma("sp", sn[0:64, 0:n], sin2s[:, t0:t1])
        for m in range(4):
            a = acc[m % 3]
            for kt in range(32):
                k.matmul(a[:, 0:n], w2[:, kt, m * 128:(m + 1) * 128], h[:, kt, 0:n], start=(kt == 0), stop=(kt == 31))
            sq = sqb[m % 2]
            k.act(sq[:, 0:n], a[:, 0:n], AF.Square)
            k.copy(ckv[:, m, 0:n], a[:, 0:n], eng="dve")
            k.matmul(ssq[:, 0:n], ones.v, sq[:, 0:n], start=(m == 0), stop=(m == 3))
        rstd_from_ssq(k, rstd, ssq, n, 512)
        for m in range(4):
            k.stt(ckn[:, m, 0:n], ckv[:, m, 0:n], gkt[:, m:m + 1], rstd[:, 0:n], ALU.mult, ALU.mult)
        u1, u2 = up[1], up[2]
        for kt in range(32):
            k.matmul(u1[0:64, 0:n], w2[:, kt, 512:576], h[:, kt, 0:n], start=(kt == 0), stop=(kt == 31))
        for kt in range(32):
            k.matmul(u2[0:64, 0:n], w2[:, kt, 576:640], h[:, kt, 0:n], start=(kt == 0), stop=(kt == 31))
        a_, b_ = ra[0], rb[0]
        k.tt(a_[0:64, 0:n], u1[0:64, 0:n], ct[0:64, 0:n], ALU.mult)
        k.tt(b_[0:64, 0:n], u2[0:64, 0:n], sn[0:64, 0:n], ALU.mult)
        o = nob()
        k.tt(o[0:64, 0:n], a_[0:64, 0:n], b_[0:64, 0:n], ALU.add)
        k.dma("act", krT[:, t0:t1], o[0:64, 0:n])
        for m in range(4):
            a = acc[m % 3]
            for kt in range(32):
                k.matmul(a[:, 0:n], w2[:, kt, 640 + m * 128:640 + (m + 1) * 128], h[:, kt, 0:n], start=(kt == 0), stop=(kt == 31))
            o = nob()
            k.act(o[:, 0:n], a[:, 0:n], AF.Silu)
            k.dma("act", gT[m * 128:(m + 1) * 128, t0:t1], o[:, 0:n])
        for hd in range(NH):
            u0 = up[0]
            for kt in range(4):
                k.matmul(u0[:, 0:n], wk[:, kt, hd * 128:(hd + 1) * 128], ckn[:, kt, 0:n], start=(kt == 0), stop=(kt == 3))
            o = nob()
            k.copy(o[:, 0:n], u0[:, 0:n], eng="act")
            k.dma("act", knT[hd, :, t0:t1], o[:, 0:n])
        for s0 in range(0, n, 128):
            ns = min(128, n - s0)
            a = acc[(s0 // 128) % 3]
            for kt in range(4):
                k.matmul(a[0:ns, :], ckn[:, kt, s0:s0 + ns], wk[:, kt, 512:1024], start=(kt == 0), stop=(kt == 3))
            o = nob()
            k.copy(o[0:ns, :], a[0:ns, :], eng="dve")
            kb = 0 if bi == 0 else 1 + (t0 + s0 - 16) // 128
            for hd in range(NH):
                k.dma("act", vtok[hd, 0:ns, kb, :], o[0:ns, hd * 128:(hd + 1) * 128])


def mla_pre(k, Tc, hnT, wq_in, wkv3, gq, gkv, cos2c, sin2sc, cqnT, ckvnT, krT, pfx="mp"):
    blocks = tok_blocks(Tc, BS_A)
    hv = hnT.rearrange("(kt p) t -> p kt t", p=128)
    ones, stg, hb, cst, snt, rstd, acc, ssq, up, sqb, ra, rb, ob = _a_common(k, pfx)
    oc = [0]

    def nob():
        oc[0] += 1
        return ob[oc[0] % 4]

    w1 = k.sb([128, 32, 1024], BF16, pfx + "w1")
    load_cast(k, w1, wq_in, 32, 1024, stg)
    w2 = k.sb([128, 32, 640], BF16, pfx + "w2")
    load_cast(k, w2, wkv3, 32, 640, stg)
    gqt = k.sb([128, 8], F32, pfx + "gq")
    gkt = k.sb([128, 4], F32, pfx + "gk")
    k.dma("sp", gqt.v, gq.v)
    k.dma("sp", gkt.v, gkv.v)
    cq = k.sb([128, 8, BS_A], F32, pfx + "cq")
    for bi, (t0, t1) in enumerate(blocks):
        n = t1 - t0
        h = hb[bi % 2]
        for kt0 in range(0, 32, 8):
            k.dma("sp", h[:, kt0:kt0 + 8, 0:n], hv[:, kt0:kt0 + 8, t0:t1])
        ct, sn = cst[bi % 2], snt[bi % 2]
        k.dma("sp", ct[0:64, 0:n], cos2c[:, t0:t1])
        k.dma("sp", sn[0:64, 0:n], sin2sc[:, t0:t1])
        for (wt, nm, gtile, dim, dst) in ((w1, 8, gqt, 1024, cqnT), (w2, 4, gkt, 512, ckvnT)):
            for m in range(nm):
                a = acc[m % 3]
                for kt in range(32):
                    k.matmul(a[:, 0:n], wt[:, kt, m * 128:(m + 1) * 128], h[:, kt, 0:n], start=(kt == 0), stop=(kt == 31))
                sq = sqb[m % 2]
                k.act(sq[:, 0:n], a[:, 0:n], AF.Square)
                k.copy(cq[:, m, 0:n], a[:, 0:n], eng="dve")
                k.matmul(ssq[:, 0:n], ones.v, sq[:, 0:n], start=(m == 0), stop=(m == nm - 1))
            rstd_from_ssq(k, rstd, ssq, n, dim)
            for m in range(nm):
                o = nob()
                k.stt(o[:, 0:n], cq[:, m, 0:n], gtile[:, m:m + 1], rstd[:, 0:n], ALU.mult, ALU.mult)
                k.dma("act", dst[m * 128:(m + 1) * 128, t0:t1], o[:, 0:n])
        u1, u2 = up[1], up[2]
        for kt in range(32):
            k.matmul(u1[0:64, 0:n], w2[:, kt, 512:576], h[:, kt, 0:n], start=(kt == 0), stop=(kt == 31))
        for kt in range(32):
            k.matmul(u2[0:64, 0:n], w2[:, kt, 576:640], h[:, kt, 0:n], start=(kt == 0), stop=(kt == 31))
        a_, b_ = ra[0], rb[0]
        k.tt(a_[0:64, 0:n], u1[0:64, 0:n], ct[0:64, 0:n], ALU.mult)
        k.tt(b_[0:64, 0:n], u2[0:64, 0:n], sn[0:64, 0:n], ALU.mult)
        o = nob()
        k.tt(o[0:64, 0:n], a_[0:64, 0:n], b_[0:64, 0:n], ALU.add)
        k.dma("act", krT[:, t0:t1], o[0:64, 0:n])


def mla_a1p(k, T, cq_meta, cq_own, wuq, cos2, sin2s, qnT, qrT, pfx="m1"):
    blocks = tok_blocks(T, BS_A)
    ones, stg, hb_unused, cst, snt, rstd, acc, ssq, up, sqb, ra, rb, ob = _a_common(k, pfx, hb=False)
    oc = [0]

    def nob():
        oc[0] += 1
        return ob[oc[0] % 4]

    wu = k.sb([128, 8, NH * 256], BF16, pfx + "wu")
    load_cast(k, wu, wuq, 8, NH * 256, stg)
    cqm = k.sb([128, 8, 16], BF16, pfx + "cqm")
    cqb = [k.sb([128, 8, BS_A], BF16, pfx + f"cqb{i}") for i in range(3)]
    for bi, (t0, t1) in enumerate(blocks):
        n = t1 - t0
        if bi == 0:
            cqn = cqm
            k.dma("sp", cqn.v, cq_meta.v)
        else:
            cqn = cqb[bi % 3]
            k.dma("sp", cqn.v, cq_own[bi - 1])
        ct, sn = cst[bi % 2], snt[bi % 2]
        k.dma("sp", ct[0:64, 0:n], cos2[:, t0:t1])
        k.dma("sp", sn[0:64, 0:n], sin2s[:, t0:t1])
        for hd in range(NH):
            c0 = hd * 256
            u0 = up[0] if hd % 2 == 0 else acc[0]
            for kt in range(8):
                k.matmul(u0[:, 0:n], wu[:, kt, c0:c0 + 128], cqn[:, kt, 0:n], start=(kt == 0), stop=(kt == 7))
            o = nob()
            k.act(o[:, 0:n], u0[:, 0:n], AF.Copy, scale=QSCALE)
            k.dma("act", qnT[hd, :, t0:t1], o[:, 0:n])
            u1, u2 = (up[1], up[2]) if hd % 2 == 0 else (acc[1], acc[2])
            for kt in range(8):
                k.matmul(u1[0:64, 0:n], wu[:, kt, c0 + 128:c0 + 192], cqn[:, kt, 0:n], start=(kt == 0), stop=(kt == 7))
            for kt in range(8):
                k.matmul(u2[0:64, 0:n], wu[:, kt, c0 + 192:c0 + 256], cqn[:, kt, 0:n], start=(kt == 0), stop=(kt == 7))
            a_, b_ = ra[hd % 2], rb[hd % 2]
            k.tt(a_[0:64, 0:n], u1[0:64, 0:n], ct[0:64, 0:n], ALU.mult)
            k.tt(b_[0:64, 0:n], u2[0:64, 0:n], sn[0:64, 0:n], ALU.mult)
            o = nob()
            k.tt(a_[0:64, 0:n], a_[0:64, 0:n], b_[0:64, 0:n], ALU.add, eng="pool")
            k.act(o[0:64, 0:n], a_[0:64, 0:n], AF.Copy, scale=QSCALE)
            k.dma("act", qrT[hd, :, t0:t1], o[0:64, 0:n])


def mla_a2p(k, T, hn_meta, hn_own, ck_meta, ck_own, wgate, wukv, knT, vtok, gT, pfx="m2"):
    blocks = tok_blocks(T, BS_A)
    ones, stg, hb, cst, snt, rstd, acc, ssq, up, sqb, ra, rb, ob = _a_common(k, pfx)
    hbm = k.sb([128, 32, 16], BF16, pfx + "hbm")
    oc = [0]

    def nob():
        oc[0] += 1
        return ob[oc[0] % 4]

    w2 = k.sb([128, 32, 512], BF16, pfx + "w2")
    load_cast(k, w2, wgate, 32, 512, stg)
    wk = k.sb([128, 4, 1024], BF16, pfx + "wk")
    load_cast(k, wk, wukv, 4, 1024, stg)
    ckm = k.sb([128, 4, 16], BF16, pfx + "ckm")
    ckb = [k.sb([128, 4, BS_A], BF16, pfx + f"ckb{i}") for i in range(3)]
    for bi, (t0, t1) in enumerate(blocks):
        n = t1 - t0
        if bi == 0:
            h, ckn = hbm, ckm
            k.dma("sp", h.v, hn_meta.v)
            k.dma("sp", ckn.v, ck_meta.v)
        else:
            h, ckn = hb[bi % 2], ckb[bi % 3]
            k.dma("sp", h.v, hn_own[bi - 1])
            k.dma("sp", ckn.v, ck_own[bi - 1])
        for m in range(4):
            a = acc[m % 3]
            for kt in range(32):
                k.matmul(a[:, 0:n], w2[:, kt, m * 128:(m + 1) * 128], h[:, kt, 0:n], start=(kt == 0), stop=(kt == 31))
            o = nob()
            k.act(o[:, 0:n], a[:, 0:n], AF.Silu)
            k.dma("act", gT[m * 128:(m + 1) * 128, t0:t1], o[:, 0:n])
        for hd in range(NH):
            u0 = up[hd % 3]
            for kt in range(4):
                k.matmul(u0[:, 0:n], wk[:, kt, hd * 128:(hd + 1) * 128], ckn[:, kt, 0:n], start=(kt == 0), stop=(kt == 3))
            o = nob()
            k.copy(o[:, 0:n], u0[:, 0:n], eng=("act" if hd % 2 else "dve"))
            k.dma("act", knT[hd, :, t0:t1], o[:, 0:n])
        for s0 in range(0, n, 128):
            ns = min(128, n - s0)
            a = acc[(s0 // 128) % 3]
            for kt in range(4):
                k.matmul(a[0:ns, :], ckn[:, kt, s0:s0 + ns], wk[:, kt, 512:1024], start=(kt == 0), stop=(kt == 3))
            o = nob()
            k.copy(o[0:ns, :], a[0:ns, :], eng="dve")
            kb = 0 if bi == 0 else 1 + (t0 + s0 - 16) // 128
            for hd in range(NH):
                k.dma("act", vtok[hd, 0:ns, kb, :], o[0:ns, hd * 128:(hd + 1) * 128])


def mla_phase_b(k, T, qnT, qrT, knT, krT, vtok, gT, yT, pfx="mb"):
    NB = (T - 16) // 512
    NKB = (T - 16) // 128
    onesf = k.sb([128, 128], F32, pfx + "onesf")
    k.memset(onesf.v, 1.0)
    kr = k.sb([128, T], BF16, pfx + "kr")
    k.dma("sp", kr[0:64, :], krT.v)
    kn = k.sb([128, T], BF16, pfx + "kn")
    vv = k.sb([128, NKB + 1, 128], BF16, pfx + "vv")
    qn = [k.sb([128, 512], BF16, pfx + f"qn{i}") for i in range(2)]
    qr = [k.sb([128, 512], BF16, pfx + f"qr{i}") for i in range(2)]
    gt = [k.sb([128, 512], BF16, pfx + f"gt{i}") for i in range(2)]
    pt = [k.sb([128, 512], BF16, pfx + f"pt{i}") for i in range(4)]
    sc = [k.ps([128, 512], F32, pfx + f"sc{i}") for i in range(4)]
    oT = [k.ps([128, 512], F32, pfx + f"oT{i}") for i in range(2)]
    dn = k.ps([128, 512], F32, pfx + "dn")
    dacc = [k.sb([128, 512], F32, pfx + f"dacc{i}") for i in range(2)]
    rden = [k.sb([128, 512], F32, pfx + f"rden{i}") for i in range(2)]
    yo = [k.sb([128, 512], F32, pfx + f"yo{i}") for i in range(2)]
    yb = [k.sb([128, 512], BF16, pfx + f"yb{i}") for i in range(2)]
    u = 0
    g = 0
    for hd in range(NH):
        k.dma("sp", kn.v, knT[hd, :, :])
        k.dma("act", vv.v, vtok[hd])
        groups = [(0, 16, -1)] + [(16 + 512 * i, 16 + 512 * (i + 1), i) for i in range(NB)]
        for (q0, q1, gi) in groups:
            n = q1 - q0
            qnt, qrt, gtt = qn[g % 2], qr[g % 2], gt[g % 2]
            o_ = oT[g % 2]
            k.dma("sp", qnt[:, 0:n], qnT[hd, :, q0:q1])
            k.dma("sp", qrt[0:64, 0:n], qrT[hd, :, q0:q1])
            k.dma("sp", gtt[:, 0:n], gT[hd * 128:(hd + 1) * 128, q0:q1])
            k.memset(dacc[0][:, 0:n], 0.0, eng="dve")
            k.memset(dacc[1][:, 0:n], 0.0, eng="pool")
            kbs = [(0, 16, 0, 0, False)]
            if gi >= 0:
                for j in range(4 * gi):
                    kbs.append((16 + 128 * j, 128, 1 + j, 0, False))
                for dgi in range(4):
                    j = 4 * gi + dgi
                    kbs.append((16 + 128 * j, 128, 1 + j, 128 * dgi, True))

            def scores(idx, uu):
                kc, nk, vb, qs, diag = kbs[idx]
                s_ = sc[uu % 4]
                k.matmul(s_[0:nk, qs:n], kn[:, kc:kc + nk], qnt[:, qs:n], start=True, stop=False)
                k.matmul(s_[0:nk, qs:n], kr[0:64, kc:kc + nk], qrt[0:64, qs:n], start=False, stop=True)

            scores(0, u)
            for idx, (kc, nk, vb, qs, diag) in enumerate(kbs):
                if idx + 1 < len(kbs):
                    scores(idx + 1, u + 1)
                s_ = sc[u % 4]
                p_ = pt[u % 4]
                k.act(p_[0:nk, qs:n], s_[0:nk, qs:n], AF.Exp)
                if diag:
                    k.memset(p_[64:128, qs:qs + 64], 0.0, eng="pool")
                last = (idx == len(kbs) - 1)
                k.matmul(o_[:, qs:n], vv[0:nk, vb, :], p_[0:nk, qs:n], start=(idx == 0), stop=last)
                da = dacc[u % 2]
                k.tt(da[0:nk, qs:n], da[0:nk, qs:n], p_[0:nk, qs:n], ALU.add, eng=("dve" if u % 2 == 0 else "pool"))
                u += 1
            k.matmul(dn[:, 0:n], onesf.v, dacc[0][:, 0:n], start=True, stop=False)
            k.matmul(dn[:, 0:n], onesf.v, dacc[1][:, 0:n], start=False, stop=True)
            rd, y1, y2 = rden[g % 2], yo[g % 2], yb[g % 2]
            k.recip(rd[:, 0:n], dn[:, 0:n])
            k.tt(y1[:, 0:n], o_[:, 0:n], rd[:, 0:n], ALU.mult)
            k.tt(y2[:, 0:n], y1[:, 0:n], gtt[:, 0:n], ALU.mult, eng="pool")
            k.dma("act", yT[hd * 128:(hd + 1) * 128, q0:q1], y2[:, 0:n])
            g += 1


def stage_mla(k, T, hn_meta, hn_own, wq_in, wkv_in, wuq, wukv, gq, gkv, cos2, sin2s, yT, pfx="ml", kind="Internal", phases="12b"):
    qnT = k.dram(pfx + "_qnT", [NH, 128, T], BF16, kind=kind)
    qrT = k.dram(pfx + "_qrT", [NH, 64, T], BF16, kind=kind)
    knT = k.dram(pfx + "_knT", [NH, 128, T], BF16, kind=kind)
    krT = k.dram(pfx + "_krT", [64, T], BF16, kind=kind)
    vtok = k.dram(pfx + "_vtok", [NH, 128, (T - 16) // 128 + 1, 128], BF16, kind=kind)
    gT = k.dram(pfx + "_gT", [512, T], BF16, kind=kind)
    if "1" in phases:
      with (contextlib.nullcontext() if os.environ.get("NOSCOPE") else k.scope()):
        mla_a1(k, T, hn_meta, hn_own, wq_in, wuq, gq, cos2, sin2s, qnT, qrT, pfx + "1")
    if "2" in phases:
      with k.scope():
        mla_a2(k, T, hn_meta, hn_own, wkv_in, wukv, gkv, cos2, sin2s, knT, krT, vtok, gT, pfx + "2")
    if "b" in phases:
      with k.scope():
        mla_phase_b(k, T, qnT, qrT, knT, krT, vtok, gT, yT, pfx + "b")
    return dict(qnT=qnT, qrT=qrT, knT=knT, krT=krT, vtok=vtok, gT=gT)


def stage_mla2(k, T, hn_meta, hn_own, cq_meta, cq_own, ck_meta, ck_own, krT, wgate, wuq, wukv, cos2, sin2s, yT, pfx="ml"):
    qnT = k.dram(pfx + "_qnT", [NH, 128, T], BF16)
    qrT = k.dram(pfx + "_qrT", [NH, 64, T], BF16)
    knT = k.dram(pfx + "_knT", [NH, 128, T], BF16)
    vtok = k.dram(pfx + "_vtok", [NH, 128, (T - 16) // 128 + 1, 128], BF16)
    gT = k.dram(pfx + "_gT", [512, T], BF16)
    with k.scope():
        mla_a1p(k, T, cq_meta, cq_own, wuq, cos2, sin2s, qnT, qrT, pfx + "1")
    with k.scope():
        mla_a2p(k, T, hn_meta, hn_own, ck_meta, ck_own, wgate, wukv, knT, vtok, gT, pfx + "2")
    with k.scope():
        mla_phase_b(k, T, qnT, qrT, knT, krT, vtok, gT, yT, pfx + "b")


EPS = 1e-6
GELU_C = 1.5957691216057308


def hyb_ssd(k, T, hn_meta, hn_own, w_ssd, convw, convb, dtb_bc, alog_bc, d_bc, ng_bc, Umat, ident, yT, pfx="hs"):
    blocks = tok_blocks(T, BS_A)
    NW = 1288
    stg = [k.sb([128, NW], F32, pfx + f"stg{i}") for i in range(2)]
    w = k.sb([128, 32, NW], BF16, pfx + "w")
    load_cast(k, w, w_ssd, 32, NW, stg)
    hbm = k.sb([128, 32, 16], BF16, pfx + "hbm")
    hb = [k.sb([128, 32, BS_A], BF16, pfx + f"hb{i}") for i in range(2)]
    cw = k.sb([128, 6, 4], F32, pfx + "cw")
    cb = k.sb([128, 6], F32, pfx + "cb")
    k.dma("sp", cw.v, convw.v)
    k.dma("sp", cb.v, convb.v)
    dtb = k.sb([128, 8], F32, pfx + "dtb")
    aneg = k.sb([128, 8], F32, pfx + "aneg")
    dbc = k.sb([128, 512], F32, pfx + "dbc")
    ngb = k.sb([128, 512], F32, pfx + "ngb")
    U = k.sb([128, 128], F32, pfx + "U")
    idb = k.sb([128, 128], BF16, pfx + "idb")
    idf = k.sb([128, 128], F32, pfx + "idf")
    k.dma("sp", dtb.v, dtb_bc.v)
    k.dma("sp", aneg.v, alog_bc.v)
    k.dma("sp", dbc.v, d_bc.v)
    k.dma("sp", ngb.v, ng_bc.v)
    k.dma("sp", U.v, Umat.v)
    k.dma("sp", idf.v, ident.v)
    k.copy(idb.v, idf.v, eng="pool")
    k.act(aneg.v, aneg.v, AF.Exp)
    k.ts(aneg.v, aneg.v, -1.0, ALU.mult)
    ones = k.sb([128, 128], F32, pfx + "ones")
    k.memset(ones.v, 1.0)

    cin = [k.sb([128, 3 + BS_A], F32, pfx + f"cin{m}") for m in range(6)]
    for m in range(6):
        k.memset(cin[m].v, 0.0)
    cacc = [k.sb([128, BS_A], F32, pfx + f"cacc{i}") for i in range(2)]
    fT = [k.sb([128, BS_A], BF16, pfx + f"fT{m}") for m in range(6)]
    L_zs = [k.sb([128, 512], F32, pfx + f"zs{i}") for i in range(2)]
    L_ctk = [k.sb([128, 128], BF16, pfx + f"ctk{i}") for i in range(2)]
    L_dt = [k.sb([128, 8], F32, pfx + f"dt{i}") for i in range(2)]
    L_da = [k.sb([128, 8], F32, pfx + f"da{i}") for i in range(2)]
    L_dab = [k.sb([128, 8, 128], F32, pfx + f"dab{i}") for i in range(2)]
    L_acum = [k.sb([128, 8], F32, pfx + f"acum{i}") for i in range(2)]
    L_nacum = [k.sb([128, 8], F32, pfx + f"nacum{i}") for i in range(2)]
    L_aend = [k.sb([128, 8], F32, pfx + f"aend{i}") for i in range(2)]
    L_eend = [k.sb([128, 8], F32, pfx + f"eend{i}") for i in range(2)]
    L_eac = [k.sb([128, 8], F32, pfx + f"eac{i}") for i in range(2)]
    L_dte = [k.sb([128, 8], F32, pfx + f"dte{i}") for i in range(2)]
    L_xtok = [k.sb([128, 512], BF16, pfx + f"xtok{i}") for i in range(2)]
    L_btok = [k.sb([128, 128], BF16, pfx + f"btok{i}") for i in range(2)]
    L_xdt = [k.sb([128, 512], BF16, pfx + f"xdt{i}") for i in range(2)]
    L_xw = [k.sb([128, 512], BF16, pfx + f"xw{i}") for i in range(2)]
    L_segc = [k.sb([128, 8, 128], F32, pfx + f"segc{i}") for i in range(2)]
    L_cbm = [k.sb([128, 128], F32, pfx + f"cbm{i}") for i in range(2)]
    L_MT = [k.sb([128, 8, 128], BF16, pfx + f"MT{i}") for i in range(2)]
    S = k.sb([128, 512], F32, pfx + "S")
    Sb = k.sb([128, 512], BF16, pfx + "Sb")
    k.memset(S.v, 0.0)
    k.memset(Sb.v, 0.0)
    L_t1 = [k.sb([128, 512], F32, pfx + f"t1{i}") for i in range(2)]
    L_t2 = [k.sb([128, 512], F32, pfx + f"t2{i}") for i in range(2)]
    L_ssq = [k.sb([128, 1], F32, pfx + f"ssq{i}") for i in range(2)]
    L_yn = [k.sb([128, 512], BF16, pfx + f"yn{i}") for i in range(2)]
    yTs = [k.sb([128, 128], BF16, pfx + f"yTs{i}") for i in range(4)]

    accA = k.ps([128, 512], F32, pfx + "accA")
    accB = k.ps([128, 512], F32, pfx + "accB")
    misc = k.ps([128, 512], F32, pfx + "misc")
    AB = k.ps([128, 8, 128], F32, pfx + "AB")
    ydg = k.ps([128, 512], F32, pfx + "ydg")
    yof = k.ps([128, 512], F32, pfx + "yof")
    tr = k.ps([128, 512], BF16, pfx + "tr")
    accs = [accA, accB]

    def stageA(c):
        cl, cs, h, par = c['cl'], c['cs'], c['h'], c['par']
        zs = L_zs[par]
        dt = L_dt[par]
        da = L_da[par]
        dab = L_dab[par]
        acum = L_acum[par]
        nacum = L_nacum[par]
        aend = L_aend[par]
        eend = L_eend[par]
        eac = L_eac[par]
        dte = L_dte[par]
        xtok = L_xtok[par]
        btok = L_btok[par]
        xdt = L_xdt[par]
        xw = L_xw[par]
        segc = L_segc[par]
        cbm = L_cbm[par]
        MT = L_MT[par]
        t1 = L_t1[par]
        t2 = L_t2[par]
        ssq = L_ssq[par]
        yn = L_yn[par]
        ctk = L_ctk[par]
        for kt in range(32):
            k.matmul(accA[0:cl, :], h[:, kt, cs], w[:, kt, 0:512], start=(kt == 0), stop=(kt == 31))
        k.act(zs[0:cl, :], accA[0:cl, :], AF.Silu)
        for kt in range(32):
            k.matmul(misc[0:cl, 0:8], h[:, kt, cs], w[:, kt, 1280:1288], start=(kt == 0), stop=(kt == 31))
        k.tt(dt[0:cl, :], misc[0:cl, 0:8], dtb[0:cl, :], ALU.add)
        k.act(dt[0:cl, :], dt[0:cl, :], AF.Exp)
        k.act(dt[0:cl, :], dt[0:cl, :], AF.Ln, bias=1.0)
        k.tt(da[0:cl, :], dt[0:cl, :], aneg[0:cl, :], ALU.mult)
        for m in range(4):
            k.transpose(tr[0:cl, m * 128:(m + 1) * 128], fT[m][:, cs], idb.v)
        k.copy(xtok[0:cl, :], tr[0:cl, :], eng="act")
        k.transpose(tr[0:cl, 0:128], fT[4][:, cs], idb.v)
        k.copy(btok[0:cl, :], tr[0:cl, 0:128], eng="act")
        k.tt(xdt[0:cl, :].rearrange("p (h d) -> p h d", h=8), xtok[0:cl, :].rearrange("p (h d) -> p h d", h=8),
             dt[0:cl, :].unsqueeze(2).broadcast_to([cl, 8, 64]), ALU.mult)
        k.matmul(misc[0:cl, 8:16], U[0:cl, 0:cl], da[0:cl, :])
        k.copy(acum[0:cl, :], misc[0:cl, 8:16], eng="dve")
        k.ts(nacum[0:cl, :], acum[0:cl, :], -1.0, ALU.mult)
        k.act(eac[0:cl, :], acum[0:cl, :], AF.Exp)
        k.tt(dab[0:cl, :, 0:cl], ones[0:cl, 0:cl].unsqueeze(1).broadcast_to([cl, 8, cl]),
             da[0:cl, :].unsqueeze(2).broadcast_to([cl, 8, cl]), ALU.mult, eng="pool")
        for hh in range(8):
            k.matmul(AB[0:cl, hh, 0:cl], dab[0:cl, hh, 0:cl], U[0:cl, 0:cl])
        k.tt(segc[0:cl, :, 0:cl], AB[0:cl, :, 0:cl], nacum[0:cl, :].unsqueeze(2).broadcast_to([cl, 8, cl]), ALU.add)
        k.copy(aend[0:cl, :], AB[0:cl, :, cl - 1], eng="dve")
        k.ts(segc[0:cl, :, 0:cl], segc[0:cl, :, 0:cl], 0.0, ALU.min, eng="pool")
        k.act(segc[0:cl, :, 0:cl], segc[0:cl, :, 0:cl], AF.Exp)
        k.matmul(misc[0:cl, 128:128 + cl], fT[4][:, cs], fT[5][:, cs])
        k.tt(cbm[0:cl, 0:cl], misc[0:cl, 128:128 + cl], U[0:cl, 0:cl], ALU.mult)
        k.tt(MT[0:cl, :, 0:cl], segc[0:cl, :, 0:cl], cbm[0:cl, 0:cl].unsqueeze(1).broadcast_to([cl, 8, cl]), ALU.mult, eng="pool")
        k.copy(ctk[:, 0:cl], fT[5][:, cs], eng="pool")

    def stageB(c):
        cl, cs, par, tok0 = c['cl'], c['cs'], c['par'], c['tok0']
        zs = L_zs[par]
        dt = L_dt[par]
        da = L_da[par]
        dab = L_dab[par]
        acum = L_acum[par]
        nacum = L_nacum[par]
        aend = L_aend[par]
        eend = L_eend[par]
        eac = L_eac[par]
        dte = L_dte[par]
        xtok = L_xtok[par]
        btok = L_btok[par]
        xdt = L_xdt[par]
        xw = L_xw[par]
        segc = L_segc[par]
        cbm = L_cbm[par]
        MT = L_MT[par]
        t1 = L_t1[par]
        t2 = L_t2[par]
        ssq = L_ssq[par]
        yn = L_yn[par]
        ctk = L_ctk[par]
        for hh in range(8):
            k.matmul(ydg[0:cl, hh * 64:(hh + 1) * 64], MT[0:cl, hh, 0:cl], xdt[0:cl, hh * 64:(hh + 1) * 64])
        k.matmul(yof[0:cl, :], ctk[:, 0:cl], Sb.v)
        k.tt(t1[0:cl, :].rearrange("p (h d) -> p h d", h=8), yof[0:cl, :].rearrange("p (h d) -> p h d", h=8),
             eac[0:cl, :].unsqueeze(2).broadcast_to([cl, 8, 64]), ALU.mult)
        k.tt(t1[0:cl, :], t1[0:cl, :], ydg[0:cl, :], ALU.add)
        k.tt(t2[0:cl, :], xtok[0:cl, :], dbc[0:cl, :], ALU.mult, eng="pool")
        k.tt(t1[0:cl, :], t1[0:cl, :], t2[0:cl, :], ALU.add)
        k.tt(t1[0:cl, :], t1[0:cl, :], zs[0:cl, :], ALU.mult)
        k.act(t2[0:cl, :], t1[0:cl, :], AF.Square, accum_out=ssq[0:cl, :])
        k.ts(ssq[0:cl, :], ssq[0:cl, :], 1.0 / 512, ALU.mult, EPS, ALU.add)
        k.act(ssq[0:cl, :], ssq[0:cl, :], AF.Sqrt)
        k.recip(ssq[0:cl, :], ssq[0:cl, :])
        k.stt(yn[0:cl, :], t1[0:cl, :], ssq[0:cl, 0:1], ngb[0:cl, :], ALU.mult, ALU.mult)
        for m in range(4):
            k.transpose(tr[:, m * 128:m * 128 + cl], yn[0:cl, m * 128:(m + 1) * 128], idb[0:cl, 0:cl])
        for m in range(4):
            k.copy(yTs[m][:, 0:cl], tr[:, m * 128:m * 128 + cl], eng=("act" if m % 2 else "dve"))
            k.dma("act", yT[m * 128:(m + 1) * 128, tok0:tok0 + cl], yTs[m][:, 0:cl])
        k.ts(dte[0:cl, :], aend[0:cl, :], 1.0 / cl, ALU.mult)
        k.matmul(misc[:, 16:24], ones[0:cl, :], dte[0:cl, :])
        k.act(eend.v, misc[:, 16:24], AF.Exp)
        k.tt(dte[0:cl, :], aend[0:cl, :], acum[0:cl, :], ALU.subtract)
        k.act(dte[0:cl, :], dte[0:cl, :], AF.Exp)
        k.tt(xw[0:cl, :].rearrange("p (h d) -> p h d", h=8), xdt[0:cl, :].rearrange("p (h d) -> p h d", h=8),
             dte[0:cl, :].unsqueeze(2).broadcast_to([cl, 8, 64]), ALU.mult)
        k.matmul(yof.v, btok[0:cl, :], xw[0:cl, :])
        k.tt(S.v.rearrange("p (h d) -> p h d", h=8), S.v.rearrange("p (h d) -> p h d", h=8),
             eend.v.unsqueeze(2).broadcast_to([128, 8, 64]), ALU.mult)
        k.tt(S.v, S.v, yof.v, ALU.add)
        k.copy(Sb.v, S.v, eng="pool")


    pendB = None
    nchunk = 0
    for bi, (t0, t1_) in enumerate(blocks):
        n = t1_ - t0
        if bi == 0:
            h = hbm
            k.dma("sp", h.v, hn_meta.v)
        else:
            h = hb[bi % 2]
            k.dma("sp", h.v, hn_own[bi - 1])
        for m in range(6):
            a = accs[m % 2]
            c0 = 512 + m * 128
            for kt in range(32):
                k.matmul(a[:, 0:n], w[:, kt, c0:c0 + 128], h[:, kt, 0:n], start=(kt == 0), stop=(kt == 31))
            ci = cin[m]
            k.copy(ci[:, 3:3 + n], a[:, 0:n], eng="act")
            ca = cacc[m % 2]
            k.ts(ca[:, 0:n], ci[:, 0:n], cw[:, m, 0:1], ALU.mult, cb[:, m:m + 1], ALU.add)
            for j in range(1, 4):
                k.stt(ca[:, 0:n], ci[:, j:j + n], cw[:, m, j:j + 1], ca[:, 0:n], ALU.mult, ALU.add)
            k.act(fT[m][:, 0:n], ca[:, 0:n], AF.Silu)
            k.copy(ci[:, 0:3], ci[:, n:n + 3], eng="pool")
        for s0 in range(0, n, 128):
            cl = min(128, n - s0)
            ctx = dict(cl=cl, cs=slice(s0, s0 + cl), tok0=t0 + s0, h=h, par=nchunk % 2)
            stageA(ctx)
            if pendB is not None:
                stageB(pendB)
            pendB = ctx
            nchunk += 1
    if pendB is not None:
        stageB(pendB)


def hyb_s5(k, T, hn_meta, hn_own, w_s5, bre, bim, cre, cim, are_l, aim_l, ldt_l, d_l, gT, sgT, pfx="h5"):
    blocks = tok_blocks(T, BS_A)
    L = BS_A
    stg = [k.sb([128, 512], F32, pfx + f"stg{i}") for i in range(2)]
    w = k.sb([128, 32, 512], BF16, pfx + "w")
    load_cast(k, w, w_s5, 32, 512, stg)
    hbm = k.sb([128, 32, 16], BF16, pfx + "hbm")
    hb = [k.sb([128, 32, BS_A], BF16, pfx + f"hb{i}") for i in range(2)]
    f_bre = k.sb([128, 8, 128], F32, pfx + "fbre"); f_bim = k.sb([128, 8, 128], F32, pfx + "fbim")
    f_cre = k.sb([128, 8, 128], F32, pfx + "fcre"); f_cim = k.sb([128, 8, 128], F32, pfx + "fcim")
    Bre = k.sb([128, 8, 128], BF16, pfx + "Bre"); Bim = k.sb([128, 8, 128], BF16, pfx + "Bim")
    Cre = k.sb([128, 8, 128], BF16, pfx + "Cre"); Cim = k.sb([128, 8, 128], BF16, pfx + "Cim")
    for (dst, f, src) in ((Bre, f_bre, bre), (Bim, f_bim, bim), (Cre, f_cre, cre), (Cim, f_cim, cim)):
        k.dma("sp", f.v, src.v)
        k.copy(dst.v, f.v, eng="pool")
    are = k.sb([128, 8], F32, pfx + "are"); aim = k.sb([128, 8], F32, pfx + "aim"); dtt = k.sb([128, 8], F32, pfx + "dtt")
    dsk = k.sb([128, 2], F32, pfx + "dsk")
    k.dma("sp", are.v, are_l.v); k.dma("sp", aim.v, aim_l.v); k.dma("sp", dtt.v, ldt_l.v); k.dma("sp", dsk.v, d_l.v)
    k.act(dtt.v, dtt.v, AF.Exp)
    th = k.sb([128, 8], F32, pfx + "th"); rho = k.sb([128, 8], F32, pfx + "rho")
    k.tt(th.v, dtt.v, aim.v, ALU.mult)
    k.tt(rho.v, dtt.v, are.v, ALU.mult)
    k.act(rho.v, rho.v, AF.Exp)
    ki = k.sb([128, 8], I32, pfx + "ki"); kf = k.sb([128, 8], F32, pfx + "kf")
    hh_ = k.sb([128, 8], F32, pfx + "hh"); sh = k.sb([128, 8], F32, pfx + "sh"); ch = k.sb([128, 8], F32, pfx + "ch")
    k.ts(kf.v, th.v, 1.0 / (2 * math.pi), ALU.mult)
    k.copy(ki.v, kf.v, eng="dve")
    k.copy(kf.v, ki.v, eng="dve")
    k.stt(hh_.v, kf.v, -2 * math.pi, th.v, ALU.mult, ALU.add)
    k.ts(hh_.v, hh_.v, 0.5, ALU.mult)
    k.act(sh.v, hh_.v, AF.Sin)
    q4 = k.sb([128, 8], F32, pfx + "q4")
    k.act(q4.v, hh_.v, AF.Sin, scale=0.5)
    k.tt(q4.v, q4.v, q4.v, ALU.mult)
    k.ts(ch.v, q4.v, -2.0, ALU.mult, 1.0, ALU.add)
    zr = k.sb([128, 8, 9], F32, pfx + "zr"); zi = k.sb([128, 8, 9], F32, pfx + "zi"); nzi = k.sb([128, 8, 9], F32, pfx + "nzi")
    tmp8 = k.sb([128, 8], F32, pfx + "tmp8"); tmp8b = k.sb([128, 8], F32, pfx + "tmp8b")
    k.tt(tmp8.v, sh.v, sh.v, ALU.mult)
    k.ts(zr[:, :, 0], tmp8.v, -2.0, ALU.mult, 1.0, ALU.add)
    k.tt(tmp8.v, sh.v, ch.v, ALU.mult)
    k.ts(zi[:, :, 0], tmp8.v, 2.0, ALU.mult)
    for m_ in range(8):
        k.tt(tmp8.v, zr[:, :, m_], zr[:, :, m_], ALU.mult)
        k.tt(tmp8b.v, zi[:, :, m_], zi[:, :, m_], ALU.mult)
        k.tt(zr[:, :, m_ + 1], tmp8.v, tmp8b.v, ALU.subtract)
        k.tt(tmp8.v, zr[:, :, m_], zi[:, :, m_], ALU.mult)
        k.ts(zi[:, :, m_ + 1], tmp8.v, 2.0, ALU.mult)
    k.ts(nzi.v, zi.v, -1.0, ALU.mult)
    abr = k.sb([128, 8], F32, pfx + "abr"); abi = k.sb([128, 8], F32, pfx + "abi"); den = k.sb([128, 8], F32, pfx + "den")
    kre = k.sb([128, 8], F32, pfx + "kre"); kim = k.sb([128, 8], F32, pfx + "kim"); nkre = k.sb([128, 8], F32, pfx + "nkre")
    k.tt(abr.v, rho.v, zr[:, :, 0], ALU.mult)
    k.ts(abr.v, abr.v, -1.0, ALU.add)
    k.tt(abi.v, rho.v, zi[:, :, 0], ALU.mult)
    k.tt(den.v, are.v, are.v, ALU.mult)
    k.tt(tmp8.v, aim.v, aim.v, ALU.mult)
    k.tt(den.v, den.v, tmp8.v, ALU.add)
    k.recip(den.v, den.v)
    k.tt(kre.v, abr.v, are.v, ALU.mult)
    k.tt(tmp8.v, abi.v, aim.v, ALU.mult)
    k.tt(kre.v, kre.v, tmp8.v, ALU.add)
    k.tt(kre.v, kre.v, den.v, ALU.mult)
    k.tt(kim.v, abi.v, are.v, ALU.mult)
    k.tt(tmp8.v, abr.v, aim.v, ALU.mult)
    k.tt(kim.v, kim.v, tmp8.v, ALU.subtract)
    k.tt(kim.v, kim.v, den.v, ALU.mult)
    k.ts(nkre.v, kre.v, -1.0, ALU.mult)
    Fc = k.sb([128, 8, L], F32, pfx + "Fc"); Fs = k.sb([128, 8, L], F32, pfx + "Fs")
    Ere = k.sb([128, 8, L], F32, pfx + "Ere"); Eim = k.sb([128, 8, L], F32, pfx + "Eim")
    rhoT = k.sb([128, 8, L], F32, pfx + "rhoT")
    tl = k.sb([128, L], F32, pfx + "tl")
    k.memset(Fc.v, 1.0)
    k.memset(Fs.v, 0.0)
    k.memset(rhoT.v, 1.0)
    for j in range(8):
        for m_ in range(8):
            lo = slice(0, 2 ** m_)
            hi = slice(2 ** m_, 2 ** (m_ + 1))
            w_ = 2 ** m_
            k.ts(tl[:, 0:w_], Fs[:, j, lo], zi[:, j, m_:m_ + 1], ALU.mult)
            k.stt(Fc[:, j, hi], Fc[:, j, lo], zr[:, j, m_:m_ + 1], tl[:, 0:w_], ALU.mult, ALU.subtract)
            k.ts(tl[:, 0:w_], Fc[:, j, lo], zi[:, j, m_:m_ + 1], ALU.mult)
            k.stt(Fs[:, j, hi], Fs[:, j, lo], zr[:, j, m_:m_ + 1], tl[:, 0:w_], ALU.mult, ALU.add)
        k.ts(tl.v, Fs[:, j, :], kim[:, j:j + 1], ALU.mult)
        k.stt(Ere[:, j, :], Fc[:, j, :], kre[:, j:j + 1], tl.v, ALU.mult, ALU.add)
        k.ts(tl.v, Fs[:, j, :], nkre[:, j:j + 1], ALU.mult)
        k.stt(Eim[:, j, :], Fc[:, j, :], kim[:, j:j + 1], tl.v, ALU.mult, ALU.add)
        k.ts(rhoT[:, j, :], rhoT[:, j, :], rho[:, j:j + 1], ALU.mult)
    uf = [k.sb([128, BS_A], F32, pfx + f"uf{a}") for a in range(2)]
    ub = [k.sb([128, BS_A], BF16, pfx + f"ub{a}") for a in range(2)]
    go = [k.sb([128, BS_A], BF16, pfx + f"go{i}") for i in range(2)]
    L5 = {nm: [k.sb([128, BS_A], F32, pfx + f"{nm}{i}") for i in range(2)]
          for nm in ("vre", "vim", "p1", "p2", "p3", "p4", "wre", "wim", "q1", "q2", "q3", "q4")}
    sre = [k.sb([128, BS_A], BF16, pfx + f"sre{i}") for i in range(2)]
    sim_ = [k.sb([128, BS_A], BF16, pfx + f"sim{i}") for i in range(2)]
    wir = k.sb([128, 8], F32, pfx + "wir"); wii = k.sb([128, 8], F32, pfx + "wii")
    k.memset(wir.v, 0.0)
    k.memset(wii.v, 0.0)
    c1 = k.sb([128, 1], F32, pfx + "c1")
    x2 = k.sb([128, BS_A], F32, pfx + "x2"); x3 = k.sb([128, BS_A], F32, pfx + "x3"); yv = k.sb([128, BS_A], F32, pfx + "yv")
    acc = [k.ps([128, 512], F32, pfx + f"acc{i}") for i in range(2)]
    Pp = k.ps([128, 512], F32, pfx + "Pp"); Qp = k.ps([128, 512], F32, pfx + "Qp")
    yp = [k.ps([128, 512], F32, pfx + f"yp{a}") for a in range(2)]
    for bi, (t0, t1_) in enumerate(blocks):
        n = t1_ - t0
        if bi == 0:
            h = hbm
            k.dma("sp", h.v, hn_meta.v)
        else:
            h = hb[bi % 2]
            k.dma("sp", h.v, hn_own[bi - 1])
        lev = 4 if bi == 0 else 8
        for a in range(2):
            for kt in range(32):
                k.matmul(acc[0][:, 0:n], w[:, kt, a * 128:(a + 1) * 128], h[:, kt, 0:n], start=(kt == 0), stop=(kt == 31))
            k.copy(uf[a][:, 0:n], acc[0][:, 0:n], eng="dve")
            k.copy(ub[a][:, 0:n], uf[a][:, 0:n], eng="pool")
            for kt in range(32):
                k.matmul(acc[1][:, 0:n], w[:, kt, 256 + a * 128:256 + (a + 1) * 128], h[:, kt, 0:n], start=(kt == 0), stop=(kt == 31))
            o = go[a]
            k.act(o[:, 0:n], acc[1][:, 0:n], AF.Silu)
            k.dma("act", sgT[a * 128:(a + 1) * 128, t0:t1_], o[:, 0:n])
        def pq(j):
            a = j // 4
            k.matmul(Pp[:, 0:n], Bre[:, j, :], ub[a][:, 0:n])
            k.matmul(Qp[:, 0:n], Bim[:, j, :], ub[a][:, 0:n])

        pq(0)
        for j in range(8):
            a = j // 4
            vre, vim, p1, p2, p3, p4, wre, wim, q1, q2, q3, q4 = (L5[nm][j % 2] for nm in
                ("vre", "vim", "p1", "p2", "p3", "p4", "wre", "wim", "q1", "q2", "q3", "q4"))
            k.tt(p1[:, 0:n], Pp[:, 0:n], Ere[:, j, 0:n], ALU.mult)
            k.tt(p2[:, 0:n], Qp[:, 0:n], Eim[:, j, 0:n], ALU.mult)
            k.tt(p3[:, 0:n], Qp[:, 0:n], Ere[:, j, 0:n], ALU.mult)
            k.tt(p4[:, 0:n], Pp[:, 0:n], Eim[:, j, 0:n], ALU.mult)
            if j + 1 < 8:
                pq(j + 1)
            k.tt(vre[:, 0:n], p1[:, 0:n], p2[:, 0:n], ALU.subtract, eng="pool")
            k.tt(vim[:, 0:n], p3[:, 0:n], p4[:, 0:n], ALU.add, eng="pool")
            k.scan(wre[:, 0:n], rhoT[:, j, 0:n], vre[:, 0:n], wir[:, j:j + 1])
            k.scan(wim[:, 0:n], rhoT[:, j, 0:n], vim[:, 0:n], wii[:, j:j + 1])
            k.ts(c1.v, wim[:, n - 1:n], nzi[:, j, lev:lev + 1], ALU.mult)
            k.stt(wir[:, j:j + 1], wre[:, n - 1:n], zr[:, j, lev:lev + 1], c1.v, ALU.mult, ALU.add)
            k.ts(c1.v, wre[:, n - 1:n], zi[:, j, lev:lev + 1], ALU.mult)
            k.stt(wii[:, j:j + 1], wim[:, n - 1:n], zr[:, j, lev:lev + 1], c1.v, ALU.mult, ALU.add)
            sr, si = sre[j % 2], sim_[j % 2]
            k.tt(q1[:, 0:n], wre[:, 0:n], Fc[:, j, 0:n], ALU.mult, eng="pool")
            k.tt(q2[:, 0:n], wim[:, 0:n], Fs[:, j, 0:n], ALU.mult, eng="pool")
            k.tt(sr[:, 0:n], q1[:, 0:n], q2[:, 0:n], ALU.subtract, eng="pool")
            k.tt(q3[:, 0:n], wre[:, 0:n], Fs[:, j, 0:n], ALU.mult, eng="pool")
            k.tt(q4[:, 0:n], wim[:, 0:n], Fc[:, j, 0:n], ALU.mult, eng="pool")
            k.stt(si[:, 0:n], q3[:, 0:n], -1.0, q4[:, 0:n], ALU.mult, ALU.subtract)
            k.matmul(yp[a][:, 0:n], Cre[:, j, :], sr[:, 0:n], start=(j % 4 == 0), stop=False)
            k.matmul(yp[a][:, 0:n], Cim[:, j, :], si[:, 0:n], start=False, stop=(j % 4 == 3))
        for a in range(2):
            k.stt(yv[:, 0:n], uf[a][:, 0:n], dsk[:, a:a + 1], yp[a][:, 0:n], ALU.mult, ALU.add)
            k.tt(x2[:, 0:n], yv[:, 0:n], yv[:, 0:n], ALU.mult, eng="pool")
            k.ts(x2[:, 0:n], x2[:, 0:n], 0.044715, ALU.mult, 1.0, ALU.add, eng="pool")
            k.tt(x3[:, 0:n], x2[:, 0:n], yv[:, 0:n], ALU.mult, eng="pool")
            k.act(x3[:, 0:n], x3[:, 0:n], AF.Sigmoid, scale=GELU_C)
            o = go[a]
            k.tt(o[:, 0:n], x3[:, 0:n], yv[:, 0:n], ALU.mult)
            k.dma("act", gT[a * 128:(a + 1) * 128, t0:t1_], o[:, 0:n])


PERM = np.concatenate([np.arange(32, 64), np.arange(0, 32)])


def ktile(w):
    K, M = w.shape
    return np.ascontiguousarray(w.reshape(K // 128, 128, M).transpose(1, 0, 2))


def vec_tile(g):
    return np.ascontiguousarray(g.reshape(-1, 128).T)


def rope_tables(T):
    pos = np.arange(T, dtype=np.float32)
    inv_freq = (np.float32(10000.0) ** (-np.arange(0, 64, 2, dtype=np.float32) / np.float32(64))).astype(np.float32)
    ang = (pos[:, None] * inv_freq[None, :]).astype(np.float32)
    cos = np.cos(ang).astype(np.float32).T
    sin = np.sin(ang).astype(np.float32).T
    cos2 = np.ascontiguousarray(np.concatenate([cos, cos], 0))
    sin2s = np.ascontiguousarray(np.concatenate([-sin, sin], 0))
    return cos2, sin2s


def mla_weights(c, w_in, q_norm, w_uq, kv_norm, w_ukv):
    kr = w_in[:, 1536:1600]
    wkv = np.concatenate([w_in[:, 1024:1536], kr, kr[:, PERM], w_in[:, 1600 + c * 512:1600 + (c + 1) * 512]], 1)
    uq = []
    kn = []
    vv = []
    for h in range(4):
        b = (4 * c + h) * 192
        rope = w_uq[:, b + 128:b + 192]
        uq += [w_uq[:, b:b + 128], rope, rope[:, PERM]]
        b2 = (4 * c + h) * 256
        kn.append(w_ukv[:, b2:b2 + 128])
        vv.append(w_ukv[:, b2 + 128:b2 + 256])
    return dict(
        wq_in=ktile(w_in[:, 0:1024]),
        wkv_in=ktile(wkv),
        wuq=ktile(np.concatenate(uq, 1)),
        wukv=ktile(np.concatenate(kn + vv, 1)),
        gq=vec_tile(q_norm),
        gkv=vec_tile(kv_norm),
    )


def out_w_tile(W):
    K = W.shape[0]
    nkt = K // 128
    return np.ascontiguousarray(W.reshape(nkt, 128, 32, 128).transpose(2, 1, 0, 3).reshape(32, 128, nkt * 128))


def hn_layout(hnT):
    T = hnT.shape[1]
    nb = (T - 16) // 256
    v = hnT.reshape(32, 128, T)
    meta = np.ascontiguousarray(v[:, :, 0:16].transpose(1, 0, 2))
    own = np.ascontiguousarray(v[:, :, 16:].reshape(32, 128, nb, 256).transpose(2, 1, 0, 3))
    return dict(hn_meta=meta, hn_own=own)


def const_mats():
    U = np.triu(np.ones((128, 128), np.float32))
    ident = np.eye(128, dtype=np.float32)
    return U, ident


def hyb_weights(c, w_in, conv_w, conv_b, dt_bias, a_log, d_skip, norm_g,
                a_re, a_im, log_dt, b_re, b_im, c_re, c_im, s5_d):
    z = w_in[:, c * 512:(c + 1) * 512]
    x = w_in[:, 4096 + c * 512:4096 + (c + 1) * 512]
    B = w_in[:, 8192 + c * 128:8192 + (c + 1) * 128]
    C = w_in[:, 9216 + c * 128:9216 + (c + 1) * 128]
    dt = w_in[:, 10240 + c * 8:10240 + (c + 1) * 8]
    w_ssd = ktile(np.concatenate([z, x, B, C, dt], 1))
    u = w_in[:, 10304 + c * 256:10304 + (c + 1) * 256]
    gate = w_in[:, 12352 + c * 256:12352 + (c + 1) * 256]
    w_s5 = ktile(np.concatenate([u, gate], 1))
    chans = [np.arange(c * 512 + m * 128, c * 512 + (m + 1) * 128) for m in range(4)]
    chans.append(np.arange(4096 + c * 128, 4096 + (c + 1) * 128))
    chans.append(np.arange(5120 + c * 128, 5120 + (c + 1) * 128))
    convw = np.ascontiguousarray(np.stack([conv_w[:, ch].T for ch in chans], 1))
    convb = np.ascontiguousarray(np.stack([conv_b[ch] for ch in chans], 1))
    hs = slice(c * 8, (c + 1) * 8)
    dtb_bc = np.ascontiguousarray(np.broadcast_to(dt_bias[hs][None, :], (128, 8)))
    alog_bc = np.ascontiguousarray(np.broadcast_to(a_log[hs][None, :], (128, 8)))
    d_bc = np.ascontiguousarray(np.broadcast_to(np.repeat(d_skip[hs], 64)[None, :], (128, 512)))
    ng_bc = np.ascontiguousarray(np.broadcast_to(norm_g[c * 512:(c + 1) * 512][None, :], (128, 512)))
    bre = np.zeros((128, 8, 128), np.float32); bim = np.zeros((128, 8, 128), np.float32)
    cre = np.zeros((128, 8, 128), np.float32); cim = np.zeros((128, 8, 128), np.float32)
    are_l = np.zeros((128, 8), np.float32); aim_l = np.zeros((128, 8), np.float32); ldt_l = np.zeros((128, 8), np.float32)
    for j in range(8):
        a, q = j // 4, (j % 4) * 32
        for m in range(2):
            g = 16 * c + 2 * j + m
            bre[q + m * 16:q + (m + 1) * 16, j, m * 64:(m + 1) * 64] = b_re[g].T
            bim[q + m * 16:q + (m + 1) * 16, j, m * 64:(m + 1) * 64] = b_im[g].T
            cre[m * 64:(m + 1) * 64, j, q + m * 16:q + (m + 1) * 16] = c_re[g].T
            cim[m * 64:(m + 1) * 64, j, q + m * 16:q + (m + 1) * 16] = c_im[g].T
            are_l[m * 64:(m + 1) * 64, j] = a_re[g]
            aim_l[m * 64:(m + 1) * 64, j] = a_im[g]
            ldt_l[m * 64:(m + 1) * 64, j] = log_dt[g]
    d_l = np.ascontiguousarray(s5_d[c * 256:(c + 1) * 256].reshape(2, 128).T)
    return dict(w_ssd=w_ssd, w_s5=w_s5, convw=convw, convb=convb, dtb_bc=dtb_bc, alog_bc=alog_bc, d_bc=d_bc, ng_bc=ng_bc,
                bre=bre, bim=bim, cre=cre, cim=cim, are_l=are_l, aim_l=aim_l, ldt_l=ldt_l, d_l=d_l)


def glu_w_tile(W):
    return np.ascontiguousarray(W.reshape(16, 128, 16, 128).transpose(2, 1, 0, 3).reshape(16, 128, 2048))


def blk_layout(xT, nkt):
    T = xT.shape[1]
    nb = (T - 16) // 256
    v = xT.reshape(nkt, 128, T)
    meta = np.ascontiguousarray(v[:, :, 0:16].transpose(1, 0, 2))
    own = np.ascontiguousarray(v[:, :, 16:].reshape(nkt, 128, nb, 256).transpose(2, 1, 0, 3))
    return meta, own


def mla_weights2(c, w_in, q_norm, w_uq, kv_norm, w_ukv):
    kr = w_in[:, 1536:1600]
    wkv3 = np.concatenate([w_in[:, 1024:1536], kr, kr[:, PERM]], 1)
    base = mla_weights(c, w_in, q_norm, w_uq, kv_norm, w_ukv)
    return dict(wq_in=base["wq_in"], wkv3=ktile(wkv3), gq=base["gq"], gkv=base["gkv"],
                wgate=ktile(w_in[:, 1600 + c * 512:1600 + (c + 1) * 512]), wuq=base["wuq"], wukv=base["wukv"])

from concourse.bass_utils import run_bass_kernel_spmd

BFNP = ml_dtypes.bfloat16
T_ALL = 16400
TC = 2064
NCORE = 8
_PROGS = {}

HYB_SHAPES = dict(w_ssd=[128, 32, 1288], w_s5=[128, 32, 512], convw=[128, 6, 4], convb=[128, 6], dtb_bc=[128, 8],
                  alog_bc=[128, 8], d_bc=[128, 512], ng_bc=[128, 512], bre=[128, 8, 128], bim=[128, 8, 128],
                  cre=[128, 8, 128], cim=[128, 8, 128], are_l=[128, 8], aim_l=[128, 8], ldt_l=[128, 8], d_l=[128, 2],
                  Umat=[128, 128], ident=[128, 128])


def _new():
    return bass.Bass("TRN2", target_bir_lowering=False)


def prog_norm():
    nc = _new()
    with contextlib.ExitStack() as st:
        k = KB(nc, st)
        hT = k.dram("hT", [D, TC], F32, kind="ExternalInput")
        g_l = k.dram("g_l", [128, 32], F32, kind="ExternalInput")
        hnT = k.dram("hnT", [D, TC], BF16, kind="ExternalOutput")
        stage_out(k, hT, None, None, g_l, None, hnT, TC, 0, True, BF16)
        k.final_wait("sp", [hnT])
        k.emit()
    return nc


def prog_out(hyb, final):
    nc = _new()
    nkt = 48 if hyb else 32
    with contextlib.ExitStack() as st:
        k = KB(nc, st)
        hT = k.dram("hT", [D, TC], F32, kind="ExternalInput")
        yT = k.dram("yT", [D, TC], BF16, kind="ExternalInput")
        wl = k.dram("wl", [32, 128, nkt * 128], F32, kind="ExternalInput")
        g_l = k.dram("g_l", [128, 32], F32, kind="ExternalInput")
        glu = None
        if hyb:
            g_all = k.dram("g_all", [2048, TC], BF16, kind="ExternalInput")
            sg_all = k.dram("sg_all", [2048, TC], BF16, kind="ExternalInput")
            wglu_l = k.dram("wglu_l", [16, 128, 2048], F32, kind="ExternalInput")
            glu = (g_all, sg_all, wglu_l)
        outs = []
        if not final:
            hT_new = k.dram("hT_new", [D, TC], F32, kind="ExternalOutput")
            outs.append(hT_new)
        else:
            hT_new = k.dram("hT_new", [D, TC], F32)
        hnT = k.dram("hnT", [D, TC], F32 if final else BF16, kind="ExternalOutput")
        outs.append(hnT)
        ybT = None
        if hyb:
            ybT = k.dram("ybT", [2048, TC], BF16)
            with k.scope():
                stage_glu(k, g_all, sg_all, wglu_l, ybT, TC)
        with k.scope():
            stage_out(k, hT, yT, wl, g_l, hT_new, hnT, TC, nkt, False, F32 if final else BF16, glu=ybT)
        if hyb:
            wq_in = k.dram("wq_in", [128, 32, 1024], F32, kind="ExternalInput")
            wkv3 = k.dram("wkv3", [128, 32, 640], F32, kind="ExternalInput")
            gq = k.dram("gq", [128, 8], F32, kind="ExternalInput")
            gkv = k.dram("gkv", [128, 4], F32, kind="ExternalInput")
            cos2c = k.dram("cos2c", [64, TC], F32, kind="ExternalInput")
            sin2sc = k.dram("sin2sc", [64, TC], F32, kind="ExternalInput")
            cqnT = k.dram("cqnT", [1024, TC], BF16, kind="ExternalOutput")
            ckvnT = k.dram("ckvnT", [512, TC], BF16, kind="ExternalOutput")
            krT = k.dram("krT", [64, TC], BF16, kind="ExternalOutput")
            outs += [cqnT, ckvnT, krT]
            with k.scope():
                mla_pre(k, TC, hnT, wq_in, wkv3, gq, gkv, cos2c, sin2sc, cqnT, ckvnT, krT)
        k.final_wait("sp", outs)
        k.emit()
    return nc


def prog_hyb():
    nc = _new()
    T = T_ALL
    with contextlib.ExitStack() as st:
        k = KB(nc, st)
        hn_meta = k.dram("hn_meta", [128, 32, 16], BF16, kind="ExternalInput")
        hn_own = k.dram("hn_own", [(T - 16) // 256, 128, 32, 256], BF16, kind="ExternalInput")
        d = {n: k.dram(n, s, F32, kind="ExternalInput") for n, s in HYB_SHAPES.items()}
        yT = k.dram("yT", [512, T], BF16, kind="ExternalOutput")
        gT = k.dram("gT", [256, T], BF16, kind="ExternalOutput")
        sgT = k.dram("sgT", [256, T], BF16, kind="ExternalOutput")
        with k.scope():
            hyb_ssd(k, T, hn_meta, hn_own, d["w_ssd"], d["convw"], d["convb"], d["dtb_bc"], d["alog_bc"], d["d_bc"],
                    d["ng_bc"], d["Umat"], d["ident"], yT)
        with k.scope():
            hyb_s5(k, T, hn_meta, hn_own, d["w_s5"], d["bre"], d["bim"], d["cre"], d["cim"], d["are_l"], d["aim_l"],
                   d["ldt_l"], d["d_l"], gT, sgT)
        k.final_wait("sp", [yT, gT, sgT])
        k.emit()
    return nc


def prog_mla():
    nc = _new()
    T = T_ALL
    NB = (T - 16) // 256
    with contextlib.ExitStack() as st:
        k = KB(nc, st)
        hn_meta = k.dram("hn_meta", [128, 32, 16], BF16, kind="ExternalInput")
        hn_own = k.dram("hn_own", [NB, 128, 32, 256], BF16, kind="ExternalInput")
        cq_meta = k.dram("cq_meta", [128, 8, 16], BF16, kind="ExternalInput")
        cq_own = k.dram("cq_own", [NB, 128, 8, 256], BF16, kind="ExternalInput")
        ck_meta = k.dram("ck_meta", [128, 4, 16], BF16, kind="ExternalInput")
        ck_own = k.dram("ck_own", [NB, 128, 4, 256], BF16, kind="ExternalInput")
        krT = k.dram("krT", [64, T], BF16, kind="ExternalInput")
        wgate = k.dram("wgate", [128, 32, 512], F32, kind="ExternalInput")
        wuq = k.dram("wuq", [128, 8, 1024], F32, kind="ExternalInput")
        wukv = k.dram("wukv", [128, 4, 1024], F32, kind="ExternalInput")
        cos2 = k.dram("cos2", [64, T], F32, kind="ExternalInput")
        sin2s = k.dram("sin2s", [64, T], F32, kind="ExternalInput")
        yT = k.dram("yT", [512, T], BF16, kind="ExternalOutput")
        stage_mla2(k, T, hn_meta, hn_own, cq_meta, cq_own, ck_meta, ck_own, krT, wgate, wuq, wukv, cos2, sin2s, yT)
        k.final_wait("sp", [yT])
        k.emit()
    return nc


def _get(name, fn, *a):
    if name not in _PROGS:
        _PROGS[name] = fn(*a)
    return _PROGS[name]


def _run(nc, in_maps):
    res = run_bass_kernel_spmd(nc, in_maps, core_ids=list(range(NCORE)))
    return res.results


def _tok_idx(c):
    return np.concatenate([np.arange(16), 16 + 2048 * c + np.arange(2048)])


def _gather_tokens(per_core):
    return np.concatenate([per_core[0][:, 0:16]] + [per_core[c][:, 16:] for c in range(NCORE)], axis=1)


def _split_tokens(full):
    return [np.ascontiguousarray(full[:, _tok_idx(c)]) for c in range(NCORE)]


def kernel(x, meta, hyb_norm, hyb_w_in, ssd_conv_w, ssd_conv_b, ssd_dt_bias, ssd_a_log, ssd_d, ssd_norm,
           s5_a_re, s5_a_im, s5_log_dt, s5_b_re, s5_b_im, s5_c_re, s5_c_im, s5_d, s5_w_glu, hyb_w_out,
           mla_norm, mla_w_in, mla_q_norm, mla_w_uq, mla_kv_norm, mla_w_ukv, mla_w_out, final_norm):
    f32 = lambda a: np.asarray(a, dtype=np.float32)
    x, meta = f32(x), f32(meta)
    h_full_T = np.ascontiguousarray(np.concatenate([meta, x[0]], axis=0).T)
    hT = _split_tokens(h_full_T)
    del h_full_T
    U, ident = const_mats()
    cos2, sin2s = rope_tables(T_ALL)

    g0 = vec_tile(f32(hyb_norm[0]))
    res = _run(_get("norm", prog_norm), [dict(hT=hT[c], g_l=g0) for c in range(NCORE)])
    hn = [np.asarray(r["hnT"]) for r in res]

    for layer in range(4):
        i = layer // 2
        hn_l = hn_layout(_gather_tokens(hn))
        last = (layer == 3)
        if layer % 2 == 0:
            ims = []
            for c in range(NCORE):
                im = hyb_weights(c, f32(hyb_w_in[i]), f32(ssd_conv_w[i]), f32(ssd_conv_b[i]), f32(ssd_dt_bias[i]),
                                 f32(ssd_a_log[i]), f32(ssd_d[i]), f32(ssd_norm[i]), f32(s5_a_re[i]), f32(s5_a_im[i]),
                                 f32(s5_log_dt[i]), f32(s5_b_re[i]), f32(s5_b_im[i]), f32(s5_c_re[i]), f32(s5_c_im[i]),
                                 f32(s5_d[i]))
                im.update(Umat=U, ident=ident, **hn_l)
                ims.append(im)
            res = _run(_get("hyb", prog_hyb), ims)
            del ims
            y_all = _split_tokens(np.concatenate([np.asarray(r["yT"]) for r in res], axis=0))
            g_all = _split_tokens(np.concatenate([np.asarray(r["gT"]) for r in res], axis=0))
            sg_all = _split_tokens(np.concatenate([np.asarray(r["sgT"]) for r in res], axis=0))
            wl = out_w_tile(f32(hyb_w_out[i]))
            wglu_l = glu_w_tile(f32(s5_w_glu[i]))
            gn = vec_tile(f32(mla_norm[i]))
            mw = [mla_weights2(c, f32(mla_w_in[i]), f32(mla_q_norm[i]), f32(mla_w_uq[i]), f32(mla_kv_norm[i]),
                               f32(mla_w_ukv[i])) for c in range(NCORE)]
            ims = [dict(hT=hT[c], yT=y_all[c], wl=wl, g_l=gn, g_all=g_all[c], sg_all=sg_all[c], wglu_l=wglu_l,
                        wq_in=mw[c]["wq_in"], wkv3=mw[c]["wkv3"], gq=mw[c]["gq"], gkv=mw[c]["gkv"],
                        cos2c=np.ascontiguousarray(cos2[:, _tok_idx(c)]), sin2sc=np.ascontiguousarray(sin2s[:, _tok_idx(c)]))
                   for c in range(NCORE)]
            res = _run(_get("out_hyb", prog_out, True, False), ims)
            cq_l = blk_layout(_gather_tokens([np.asarray(r["cqnT"]) for r in res]), 8)
            ck_l = blk_layout(_gather_tokens([np.asarray(r["ckvnT"]) for r in res]), 4)
            kr_full = np.ascontiguousarray(_gather_tokens([np.asarray(r["krT"]) for r in res]))
        else:
            ims = []
            for c in range(NCORE):
                im = dict(wgate=mw[c]["wgate"], wuq=mw[c]["wuq"], wukv=mw[c]["wukv"], cos2=cos2, sin2s=sin2s,
                          cq_meta=cq_l[0], cq_own=cq_l[1], ck_meta=ck_l[0], ck_own=ck_l[1], krT=kr_full, **hn_l)
                ims.append(im)
            res = _run(_get("mla", prog_mla), ims)
            del ims
            y_all = _split_tokens(np.concatenate([np.asarray(r["yT"]) for r in res], axis=0))
            wl = out_w_tile(f32(mla_w_out[i]))
            gn = vec_tile(f32(final_norm) if last else f32(hyb_norm[i + 1]))
            ims = [dict(hT=hT[c], yT=y_all[c], wl=wl, g_l=gn) for c in range(NCORE)]
            res = _run(_get("out_mla_final" if last else "out_mla", prog_out, False, last), ims)
        del ims
        if not last:
            hT = [np.asarray(r["hT_new"]) for r in res]
        hn = [np.asarray(r["hnT"]) for r in res]

    out = np.concatenate([hn[c][:, 16:].T for c in range(NCORE)], axis=0)
    return np.ascontiguousarray(out[None].astype(np.float32))
```

```python
import contextlib
import math
import os
import numpy as np
import ml_dtypes


import concourse.bass as bass
import concourse.mybir as mybir

F32 = mybir.dt.float32
BF16 = mybir.dt.bfloat16
I32 = mybir.dt.int32
AF = mybir.ActivationFunctionType
ALU = mybir.AluOpType
AX = mybir.AxisListType

COMPUTE = ("pe", "act", "dve", "pool")


class View:
    __slots__ = ("tl", "ap")

    def __init__(self, tl, ap):
        self.tl = tl
        self.ap = ap

    def __getitem__(self, idx):
        return View(self.tl, self.ap[idx])

    def rearrange(self, pat, **kw):
        return View(self.tl, self.ap.rearrange(pat, **kw))

    def broadcast_to(self, shape):
        return View(self.tl, self.ap.broadcast_to(list(shape)))

    def unsqueeze(self, ax):
        return View(self.tl, self.ap.unsqueeze(ax))

    def partition_broadcast(self, n):
        return View(self.tl, self.ap.partition_broadcast(n))

    def bitcast(self, dt):
        return View(self.tl, self.ap.bitcast(dt))

    @property
    def shape(self):
        return self.ap.shape


class Tl:
    __slots__ = ("t", "name", "lw", "rd", "dsem", "dcnt", "is_dram", "is_psum")

    def __init__(self, t, name, is_dram=False, is_psum=False):
        self.is_psum = is_psum
        self.t = t
        self.name = name
        self.lw = {}
        self.rd = {}
        self.dsem = None
        self.dcnt = 0
        self.is_dram = is_dram

    def __getitem__(self, idx):
        return View(self, self.t[idx])

    def rearrange(self, pat, **kw):
        return View(self, self.t.rearrange(pat, **kw))

    @property
    def v(self):
        return View(self, self.t[:])


def _is_view(x):
    return isinstance(x, View)


class KB:
    def __init__(self, nc, stack):
        self.nc = nc
        self.stack = stack
        self.root = stack
        self.lists = {e: [] for e in ("pe", "act", "dve", "pool", "sp")}
        self.psem = {}
        self.pcnt = {}
        for e in COMPUTE:
            self.psem[e] = stack.enter_context(nc.semaphore("prog_" + e))
            self.pcnt[e] = 0
        self.known = {e: {} for e in self.lists}
        self.cinst = {e: [] for e in COMPUTE}
        self.ntile = 0
        self.tiles = []
        self.sem_pool = []
        self.n_sem = 4
        self.n_inst = 0
        self.n_wait = 0

    def sb(self, shape, dt, name=None):
        self.ntile += 1
        name = name or f"t{self.ntile}"
        t = self.stack.enter_context(self.nc.sbuf_tensor(name, list(shape), dt))
        tl = Tl(t, name)
        self.tiles.append(tl)
        return tl

    def ps(self, shape, dt, name=None):
        self.ntile += 1
        name = name or f"p{self.ntile}"
        t = self.stack.enter_context(self.nc.psum_tensor(name, list(shape), dt))
        return Tl(t, name, is_psum=True)

    def dram(self, name, shape, dt, kind="Internal", **kw):
        t = self.nc.dram_tensor(name, list(shape), dt, kind=kind, **kw)
        tl = Tl(t.ap(), name, is_dram=True)
        self.tiles.append(tl)
        return tl

    def _need(self, eng, ev, waits):
        if ev is None:
            return
        if ev[0] == "c":
            _, src, idx = ev
            if src == "pe" and eng == "pe":
                return
            key = ("c", src)
        else:
            _, sem, idx = ev
            key = id(sem)
        kn = self.known[eng]
        if kn.get(key, 0) >= idx:
            return
        kn[key] = idx
        if ev[0] == "c":
            self.cinst[src][idx - 1][4] = True
        waits[key] = ev

    def _deps(self, eng, reads, writes):
        waits = {}
        for t in reads:
            for ev in t.lw.values():
                self._need(eng, ev, waits)
            if t.is_psum:
                for ev in t.rd.values():
                    if not (ev[0] == "c" and ev[1] == eng):
                        self._need(eng, ev, waits)
        for t in writes:
            for ev in t.lw.values():
                self._need(eng, ev, waits)
            for ev in t.rd.values():
                self._need(eng, ev, waits)
        return list(waits.values())

    @staticmethod
    def _evkey(ev):
        return ("c", ev[1]) if ev[0] == "c" else id(ev[1])

    def _record(self, ev, reads, writes):
        key = self._evkey(ev)
        for t in reads:
            t.rd[key] = ev
        for t in writes:
            if t.is_dram and ev[0] == "d":
                t.lw[key] = ev
            else:
                t.lw = {key: ev}
            t.rd = {}

    def op(self, eng, fn, reads=(), writes=(), inc=True):
        reads = [r.tl if _is_view(r) else r for r in reads]
        writes = [w.tl if _is_view(w) else w for w in writes]
        waits = self._deps(eng, reads, writes)
        ent = [waits, fn, "c", eng, False]
        self.cinst[eng].append(ent)
        ev = ("c", eng, len(self.cinst[eng]))
        self.lists[eng].append(ent)
        self._record(ev, reads, writes)
        self.n_inst += 1
        self.n_wait += len(waits)

    def dma(self, q, out, in_, sem_tile=None, **kw):
        reads = [in_.tl]
        writes = [out.tl]
        waits = self._deps(q, reads, writes)
        st = sem_tile or (in_.tl if out.tl.is_dram and not in_.tl.is_dram else out.tl)
        if st.dsem is None:
            st.dsem, st.dcnt = self.get_sem("d_" + st.name)
        st.dcnt += 16
        ev = ("d", st.dsem, st.dcnt)
        oap, iap = out.ap, in_.ap
        self.lists[q].append([waits, lambda e: e.dma_start(out=oap, in_=iap, **kw), "d", st.dsem, True])
        self._record(ev, reads, writes)
        self.n_inst += 1
        self.n_wait += len(waits)

    def collective(self, kind, out, in_, op=None):
        reads = [in_.tl]
        writes = [out.tl]
        waits = self._deps("pool", reads, writes)
        st = out.tl
        if st.dsem is None:
            st.dsem, st.dcnt = self.get_sem("c_" + st.name)
        st.dcnt += 16
        ev = ("d", st.dsem, st.dcnt)
        oap, iap = out.ap, in_.ap
        aop = op if op is not None else ALU.bypass
        groups = [list(range(8))]
        self.lists["pool"].append([waits, lambda e: e.collective_compute(kind, aop, replica_groups=groups, ins=[iap], outs=[oap]), "d", st.dsem, True])
        self._record(ev, reads, writes)
        self.n_inst += 1

    def get_sem(self, name):
        if self.sem_pool:
            return self.sem_pool.pop()
        self.n_sem += 1
        return self.root.enter_context(self.nc.semaphore(name)), 0

    @contextlib.contextmanager
    def scope(self):
        old_stack, old_tiles = self.stack, self.tiles
        with contextlib.ExitStack() as st:
            self.stack = st
            self.tiles = []
            yield
            self.tiles = old_tiles + self.tiles
            self.barrier()
            new = self.tiles[len(old_tiles):]
            self.release([t for t in new if not t.is_dram])
            self.tiles = old_tiles + [t for t in new if t.is_dram]
            self.stack = old_stack

    def release(self, tiles):
        for t in tiles:
            if t.dsem is not None:
                self.sem_pool.append((t.dsem, t.dcnt))
                t.dsem = None

    def barrier(self):
        evs = [("c", e, len(self.cinst[e])) for e in COMPUTE if self.cinst[e]]
        evs += [("d", s, c) for (s, c) in self.all_dsems() if c > 0]
        for eng in self.lists:
            waits = {}
            for ev in evs:
                if ev[0] == "c" and ev[1] == eng:
                    continue
                saved = None
                if ev[0] == "c" and ev[1] == "pe" and eng == "pe":
                    continue
                self._need(eng, ev, waits)
            if waits:
                self.lists[eng].append([list(waits.values()), None, None, None, False])

    def all_dsems(self):
        out = [(t.dsem, t.dcnt) for t in self.tiles if t.dsem is not None]
        out += list(self.sem_pool)
        return out

    def final_wait(self, eng, tiles):
        waits = {}
        for t in tiles:
            for ev in t.lw.values():
                self._need(eng, ev, waits)
        self.lists[eng].append([list(waits.values()), None, None, None, False])

    def matmul(self, out, lhsT, rhs, start=True, stop=True):
        o, l, r = out.ap, lhsT.ap, rhs.ap
        self.op("pe", lambda e: e.matmul(o, lhsT=l, rhs=r, start=start, stop=stop), [lhsT, rhs], [out], inc=bool(stop))

    def transpose(self, out, in_, ident):
        o, i, d = out.ap, in_.ap, ident.ap
        self.op("pe", lambda e: e.transpose(o, i, d), [in_, ident], [out])

    def act(self, out, in_, func, bias=None, scale=None, accum_out=None):
        o, i = out.ap, in_.ap
        kw = {}
        rd = [in_]
        wr = [out]
        if bias is not None:
            if _is_view(bias):
                rd.append(bias)
                kw["bias"] = bias.ap
            else:
                kw["bias"] = bias
        if scale is not None:
            if _is_view(scale):
                rd.append(scale)
                kw["scale"] = scale.ap
            else:
                kw["scale"] = scale
        if accum_out is not None:
            wr.append(accum_out)
            kw["accum_out"] = accum_out.ap
        self.op("act", lambda e: e.activation(out=o, in_=i, func=func, **kw), rd, wr)

    def tt(self, out, in0, in1, op, eng="dve"):
        o, a, b = out.ap, in0.ap, in1.ap
        self.op(eng, lambda e: e.tensor_tensor(out=o, in0=a, in1=b, op=op), [in0, in1], [out])

    def ts(self, out, in0, s1, op0, s2=None, op1=None, eng="dve", accum_out=None):
        o, a = out.ap, in0.ap
        rd = [in0]
        wr = [out]
        a1 = s1
        a2 = s2
        if _is_view(s1):
            rd.append(s1)
            a1 = s1.ap
        if _is_view(s2):
            rd.append(s2)
            a2 = s2.ap
        kw = {}
        if op1 is not None:
            kw["op1"] = op1
        if accum_out is not None:
            wr.append(accum_out)
            kw["accum_out"] = accum_out.ap
        self.op(eng, lambda e: e.tensor_scalar(out=o, in0=a, scalar1=a1, scalar2=a2, op0=op0, **kw), rd, wr)

    def stt(self, out, in0, scalar, in1, op0, op1):
        o, a, b = out.ap, in0.ap, in1.ap
        rd = [in0, in1]
        s = scalar
        if _is_view(scalar):
            rd.append(scalar)
            s = scalar.ap
        self.op("dve", lambda e: e.scalar_tensor_tensor(out=o, in0=a, scalar=s, in1=b, op0=op0, op1=op1), rd, [out])

    def copy(self, out, in_, eng="dve"):
        o, i = out.ap, in_.ap
        if eng == "act":
            self.op("act", lambda e: e.copy(out=o, in_=i), [in_], [out])
        else:
            self.op(eng, lambda e: e.tensor_copy(out=o, in_=i), [in_], [out])

    def memset(self, out, val, eng="pool"):
        o = out.ap
        self.op(eng, lambda e: e.memset(o, val), [], [out])

    def recip(self, out, in_):
        o, i = out.ap, in_.ap
        self.op("dve", lambda e: e.reciprocal(out=o, in_=i), [in_], [out])

    def scan(self, out, d0, d1, initial, op0=ALU.mult, op1=ALU.add):
        o, a, b = out.ap, d0.ap, d1.ap
        rd = [d0, d1]
        ini = initial
        if _is_view(initial):
            rd.append(initial)
            ini = initial.ap
        self.op("dve", lambda e: e.tensor_tensor_scan(out=o, data0=a, data1=b, initial=ini, op0=op0, op1=op1), rd, [out])

    def reduce(self, out, in_, op, axis=AX.X):
        o, i = out.ap, in_.ap
        self.op("dve", lambda e: e.tensor_reduce(out=o, in_=i, axis=axis, op=op), [in_], [out])

    def emit(self):
        nc = self.nc
        lists = self.lists
        cum = {}
        for eng in COMPUTE:
            c = 0
            arr = []
            for ent in self.cinst[eng]:
                if ent[4]:
                    c += 1
                arr.append(c)
            cum[eng] = arr
        self.n_marked = {e: (cum[e][-1] if cum[e] else 0) for e in COMPUTE}
        psem = self.psem

        def run(e, items):
            for ent in items:
                waits, fn, kind, who, mark = ent
                for ev in waits:
                    if ev[0] == "c":
                        e.wait_ge(psem[ev[1]], cum[ev[1]][ev[2] - 1])
                    else:
                        e.wait_ge(ev[1], ev[2])
                if fn is None:
                    continue
                if kind == "d":
                    fn(e).then_inc(who, 16)
                elif mark:
                    fn(e).then_inc(psem[who], 1)
                else:
                    fn(e)

        with nc.Block() as block:
            @block.tensor
            def _(e):
                run(e, lists["pe"])

            @block.scalar
            def _(e):
                run(e, lists["act"])

            @block.vector
            def _(e):
                run(e, lists["dve"])

            @block.gpsimd
            def _(e):
                run(e, lists["pool"])

            @block.sync
            def _(e):
                run(e, lists["sp"])


EPS = 1e-6
D = 4096
NDT = 32


def col_groups(Tc, gmax=1024):
    groups = []
    s = 0
    while s < Tc:
        e = min(s + gmax, Tc)
        if 0 < Tc - e < 64:
            e = Tc
        groups.append((s, e))
        s = e
    return groups


def stage_glu(k, g_all, sg_all, wglu_l, ybT, Tc, pfx="gl"):
    GW = 1040
    gb = k.sb([128, 16, GW], BF16, pfx + "gb")
    sgb = k.sb([128, 16, GW], BF16, pfx + "sgb")
    gst = [k.sb([128, 2048], F32, pfx + f"gst{i}") for i in range(3)]
    gwb = [k.sb([128, 2048], BF16, pfx + f"gwb{i}") for i in range(2)]
    sig = [k.sb([128, 512], F32, pfx + f"sig{i}") for i in range(3)]
    yo = [k.sb([128, 512], BF16, pfx + f"yo{i}") for i in range(3)]
    acc = [k.ps([128, 512], F32, pfx + f"acc{i}") for i in range(4)]
    gv = g_all.rearrange("(kt p) t -> p kt t", p=128)
    sgv = sg_all.rearrange("(kt p) t -> p kt t", p=128)
    gcnt = 0
    u = 0
    for (c0, c1) in col_groups(Tc, 1024):
        gw = c1 - c0
        chunks = [(s, min(s + 512, c1)) for s in range(c0, c1, 512)]
        for kt0 in range(0, 16, 8):
            k.dma("sp", gb[:, kt0:kt0 + 8, 0:gw], gv[:, kt0:kt0 + 8, c0:c1])
            k.dma("sp", sgb[:, kt0:kt0 + 8, 0:gw], sgv[:, kt0:kt0 + 8, c0:c1])
        for mt in range(16):
            gs, gw_ = gst[gcnt % 3], gwb[gcnt % 2]
            k.dma("act", gs.v, wglu_l[mt])
            k.copy(gw_.v, gs.v, eng="pool")
            gcnt += 1
            for ci, (s0, s1) in enumerate(chunks):
                n = s1 - s0
                ac = acc[u % 4]
                for kt in range(16):
                    k.matmul(ac[:, 0:n], gw_[:, kt * 128:(kt + 1) * 128], gb[:, kt, s0 - c0:s1 - c0], start=(kt == 0), stop=(kt == 15))
                sg_ = sig[u % 3]
                y_ = yo[u % 3]
                k.act(sg_[:, 0:n], ac[:, 0:n], AF.Sigmoid)
                k.tt(sg_[:, 0:n], sg_[:, 0:n], gb[:, mt, s0 - c0:s1 - c0], ALU.mult)
                k.tt(y_[:, 0:n], sg_[:, 0:n], sgb[:, mt, s0 - c0:s1 - c0], ALU.mult, eng="pool")
                k.dma("sp", ybT[mt * 128:(mt + 1) * 128, s0:s1], y_[:, 0:n])
                u += 1


def stage_out(k, hT, yT, wl, g_l, hT_new, hnT, Tc, nkt, first, out_dt, pfx="o", glu=None):
    ones = k.sb([128, 128], F32, pfx + "ones")
    k.memset(ones.v, 1.0)
    gt = k.sb([128, NDT], F32, pfx + "g")
    k.dma("sp", gt.v, g_l.v)
    GW = 1040
    GMAX = 1024
    if not first:
        yb = k.sb([128, nkt, GW], BF16, pfx + "yb")
        KH = nkt // 2
        NST = 3 if nkt > 32 else 4
        wst = [k.sb([128, KH * 128], F32, pfx + f"wst{i}") for i in range(NST)]
        wbf = [k.sb([128, nkt * 128], BF16, pfx + f"wbf{i}") for i in range(2)]
        acc = [k.ps([128, 512], F32, pfx + f"acc{i}") for i in range(4)]
        yTv = yT.rearrange("(kt p) t -> p kt t", p=128)
    ssq = [k.ps([128, 512], F32, pfx + f"ssq{i}") for i in range(3)]
    hin = [k.sb([128, 512], F32, pfx + f"hin{i}") for i in range(3)]
    hnw = [k.sb([128, 512], F32, pfx + f"hnw{i}") for i in range(3)]
    sq = [k.sb([128, 512], F32, pfx + f"sq{i}") for i in range(3)]
    rstd = k.sb([128, GW], F32, pfx + "rstd")
    hno = [k.sb([128, 512], out_dt, pfx + f"hno{i}") for i in range(3)]
    hsrc = hT if first else hT_new
    u = 0
    wcnt = 0
    if glu is not None:
        ybv = glu.rearrange("(kt p) t -> p kt t", p=128)
    for (c0, c1) in col_groups(Tc, GMAX):
        gw = c1 - c0
        chunks = [(s, min(s + 512, c1)) for s in range(c0, c1, 512)]
        assert len(chunks) <= 3 and gw <= GW
        if not first:
            nkt_y = nkt - 16 if glu is not None else nkt
            for kt0 in range(0, nkt_y, 8):
                k.dma("sp", yb[:, kt0:kt0 + 8, 0:gw], yTv[:, kt0:kt0 + 8, c0:c1])
        if glu is not None:
            for kt0 in range(0, 16, 8):
                k.dma("sp", yb[:, 32 + kt0:32 + kt0 + 8, 0:gw], ybv[:, kt0:kt0 + 8, c0:c1])
        pend = None
        for d in range(NDT):
            if not first:
                wb = wbf[wcnt % 2]
                for hh in range(2):
                    ws = wst[(2 * wcnt + hh) % NST]
                    k.dma("act" if hh == 0 else "sp", ws.v, wl[d, :, hh * KH * 128:(hh + 1) * KH * 128])
                    k.copy(wb[:, hh * KH * 128:(hh + 1) * KH * 128], ws.v, eng="pool")
                wcnt += 1
            for ci, (s0, s1) in enumerate(chunks):
                n = s1 - s0
                hi = hin[u % 3]
                hw = hnw[u % 3]
                sqt = sq[u % 3]
                k.dma("sp", hi[:, 0:n], hT[d * 128:(d + 1) * 128, s0:s1])
                if not first:
                    ac = acc[u % 4]
                    for kt in range(nkt):
                        k.matmul(ac[:, 0:n], wb[:, kt * 128:(kt + 1) * 128], yb[:, kt, s0 - c0:s1 - c0],
                                 start=(kt == 0), stop=(kt == nkt - 1))
                    k.tt(hw[:, 0:n], ac[:, 0:n], hi[:, 0:n], ALU.add)
                    k.dma("sp", hT_new[d * 128:(d + 1) * 128, s0:s1], hw[:, 0:n])
                    src = hw
                else:
                    src = hi
                k.act(sqt[:, 0:n], src[:, 0:n], AF.Square)
                if pend is not None:
                    k.matmul(*pend[0], **pend[1])
                pend = ((ssq[ci][:, 0:n], ones.v, sqt[:, 0:n]), dict(start=(d == 0), stop=(d == NDT - 1)))
                u += 1
        if pend is not None:
            k.matmul(*pend[0], **pend[1])
            pend = None
        for ci, (s0, s1) in enumerate(chunks):
            n = s1 - s0
            k.ts(rstd[:, s0 - c0:s1 - c0], ssq[ci][:, 0:n], 1.0 / D, ALU.mult, EPS, ALU.add)
            k.act(rstd[:, s0 - c0:s1 - c0], rstd[:, s0 - c0:s1 - c0], AF.Sqrt)
            k.recip(rstd[:, s0 - c0:s1 - c0], rstd[:, s0 - c0:s1 - c0])
        for d in range(NDT):
            for ci, (s0, s1) in enumerate(chunks):
                n = s1 - s0
                hi = hin[u % 3]
                ho = hno[u % 3]
                k.dma("sp", hi[:, 0:n], hsrc[d * 128:(d + 1) * 128, s0:s1])
                k.stt(ho[:, 0:n], hi[:, 0:n], gt[:, d:d + 1], rstd[:, s0 - c0:s1 - c0], ALU.mult, ALU.mult)
                k.dma("act", hnT[d * 128:(d + 1) * 128, s0:s1], ho[:, 0:n])
                u += 1

DBG_NB = int(os.environ.get('DBG_NB', '0'))
DBG_SKIP = os.environ.get('DBG_SKIP', '')
DBG_START = int(os.environ.get('DBG_START', '0'))

EPS = 1e-6
NH = 4
QSCALE = 192 ** -0.5


def tok_blocks(T, bs=512):
    assert (T - 16) % bs == 0
    return [(0, 16)] + [(s, s + bs) for s in range(16, T, bs)]


def load_cast(k, dst, src_dram, nkt, ncols, stg, q="act", ceng="pool"):
    if 'lc' in DBG_SKIP:
        k.memset(dst.v, 0.01)
        return
    for kt in range(nkt):
        s = stg[kt % len(stg)]
        k.dma(q, s[:, 0:ncols], src_dram[:, kt, :])
        k.copy(dst[:, kt, :], s[:, 0:ncols], eng=ceng)


def rstd_from_ssq(k, rstd, ssq, n, dim):
    k.ts(rstd[:, 0:n], ssq[:, 0:n], 1.0 / dim, ALU.mult, EPS, ALU.add)
    k.act(rstd[:, 0:n], rstd[:, 0:n], AF.Sqrt)
    k.recip(rstd[:, 0:n], rstd[:, 0:n])


BS_A = 256


def _a_common(k, pfx, hb=True):
    ones = k.sb([128, 128], BF16, pfx + "ones")
    k.memset(ones.v, 1.0)
    stg = [k.sb([128, 1152], F32, pfx + f"stg{i}") for i in range(2)]
    hb = [k.sb([128, 32, BS_A], BF16, pfx + f"hb{i}") for i in range(2)] if hb else None
    cst = [k.sb([128, BS_A], F32, pfx + f"cos{i}") for i in range(2)]
    snt = [k.sb([128, BS_A], F32, pfx + f"sin{i}") for i in range(2)]
    rstd = k.sb([128, BS_A], F32, pfx + "rstd")
    acc = [k.ps([128, 512], F32, pfx + f"acc{i}") for i in range(3)]
    ssq = k.ps([128, 512], F32, pfx + "ssq")
    up = [k.ps([128, 512], F32, pfx + f"up{i}") for i in range(3)]
    sqb = [k.sb([128, BS_A], BF16, pfx + f"sqb{i}") for i in range(2)]
    ra = [k.sb([128, BS_A], F32, pfx + f"ra{i}") for i in range(2)]
    rb = [k.sb([128, BS_A], F32, pfx + f"rb{i}") for i in range(2)]
    ob = [k.sb([128, 512], BF16, pfx + f"ob{i}") for i in range(4)]
    return ones, stg, hb, cst, snt, rstd, acc, ssq, up, sqb, ra, rb, ob


def mla_a1(k, T, hn_meta, hn_own, wq_in, wuq, gq, cos2, sin2s, qnT, qrT, pfx="m1"):
    blocks = tok_blocks(T, BS_A)
    hbm = k.sb([128, 32, 16], BF16, pfx + "hbm")
    ones, stg, hb, cst, snt, rstd, acc, ssq, up, sqb, ra, rb, ob = _a_common(k, pfx)
    oc = [0]

    def nob():
        oc[0] += 1
        return ob[oc[0] % 4]

    w1 = k.sb([128, 32, 1024], BF16, pfx + "w1")
    load_cast(k, w1, wq_in, 32, 1024, stg)
    wu = k.sb([128, 8, NH * 256], BF16, pfx + "wu")
    load_cast(k, wu, wuq, 8, NH * 256, stg)
    gqt = k.sb([128, 8], F32, pfx + "gq")
    if 'gq' not in DBG_SKIP:
        k.dma("sp", gqt.v, gq.v)
    cq = k.sb([128, 8, BS_A], F32, pfx + "cq")
    cqn = k.sb([128, 8, BS_A], BF16, pfx + "cqn")
    for bi, (t0, t1) in enumerate(blocks):
        if DBG_NB and bi >= DBG_NB:
            break
        if bi < DBG_START:
            continue
        n = t1 - t0
        if bi == 0:
            h = hbm
            k.dma("sp", h.v, hn_meta.v)
        else:
            h = hb[bi % 2]
            k.dma(os.environ.get("HQ", "sp"), h.v, hn_own[bi - 1])
        ct, sn = cst[bi % 2], snt[bi % 2]
        if 'cs' not in DBG_SKIP:
            _q = os.environ.get("CSQ", "sp")
            _o = 0 if os.environ.get("CS0") else t0
            k.dma(_q, ct[0:64, 0:n], cos2[:, _o:_o + n])
            k.dma(_q, sn[0:64, 0:n], sin2s[:, _o:_o + n])
        for m in range(8):
            a = acc[m % 3]
            for kt in range(32):
                k.matmul(a[:, 0:n], w1[:, kt, m * 128:(m + 1) * 128], h[:, kt, 0:n], start=(kt == 0), stop=(kt == 31))
            sq = sqb[m % 2]
            if 'sq' not in DBG_SKIP:
                k.act(sq[:, 0:n], a[:, 0:n], AF.Square)
            if 'cp' not in DBG_SKIP:
                k.copy(cq[:, m, 0:n], a[:, 0:n], eng=os.environ.get("CPENG","dve"))
            if 'ssq' not in DBG_SKIP:
                k.matmul(ssq[:, 0:n], ones.v, sq[:, 0:n], start=(m == 0), stop=(m == 7))
        if 'rstd' not in DBG_SKIP:
            rstd_from_ssq(k, rstd, ssq, n, 1024)
        for m in range(8):
            if 'stt' in DBG_SKIP:
                break
            k.stt(cqn[:, m, 0:n], cq[:, m, 0:n], gqt[:, m:m + 1], rstd[:, 0:n], ALU.mult, ALU.mult)
        for hd in range(NH):
            if 'up' in DBG_SKIP:
                break
            c0 = hd * 256
            u0 = up[0]
            for kt in range(8):
                k.matmul(u0[:, 0:n], wu[:, kt, c0:c0 + 128], cqn[:, kt, 0:n], start=(kt == 0), stop=(kt == 7))
            o = nob()
            k.act(o[:, 0:n], u0[:, 0:n], AF.Copy, scale=QSCALE)
            k.dma("act", qnT[hd, :, t0:t1], o[:, 0:n])
            u1, u2 = up[1], up[2]
            for kt in range(8):
                k.matmul(u1[0:64, 0:n], wu[:, kt, c0 + 128:c0 + 192], cqn[:, kt, 0:n], start=(kt == 0), stop=(kt == 7))
            for kt in range(8):
                k.matmul(u2[0:64, 0:n], wu[:, kt, c0 + 192:c0 + 256], cqn[:, kt, 0:n], start=(kt == 0), stop=(kt == 7))
            a_, b_ = ra[hd % 2], rb[hd % 2]
            k.tt(a_[0:64, 0:n], u1[0:64, 0:n], ct[0:64, 0:n], ALU.mult)
            k.tt(b_[0:64, 0:n], u2[0:64, 0:n], sn[0:64, 0:n], ALU.mult)
            o = nob()
            k.tt(a_[0:64, 0:n], a_[0:64, 0:n], b_[0:64, 0:n], ALU.add, eng="pool")
            k.act(o[0:64, 0:n], a_[0:64, 0:n], AF.Copy, scale=QSCALE)
            k.dma("act", qrT[hd, :, t0:t1], o[0:64, 0:n])


def mla_a2(k, T, hn_meta, hn_own, wkv_in, wukv, gkv, cos2, sin2s, knT, krT, vtok, gT, pfx="m2"):
    blocks = tok_blocks(T, BS_A)
    hbm = k.sb([128, 32, 16], BF16, pfx + "hbm")
    ones, stg, hb, cst, snt, rstd, acc, ssq, up, sqb, ra, rb, ob = _a_common(k, pfx)
    oc = [0]

    def nob():
        oc[0] += 1
        return ob[oc[0] % 4]

    w2 = k.sb([128, 32, 1152], BF16, pfx + "w2")
    load_cast(k, w2, wkv_in, 32, 1152, stg)
    wk = k.sb([128, 4, 1024], BF16, pfx + "wk")
    load_cast(k, wk, wukv, 4, 1024, stg)
    gkt = k.sb([128, 4], F32, pfx + "gk")
    k.dma("sp", gkt.v, gkv.v)
    ckv = k.sb([128, 4, BS_A], F32, pfx + "ckv")
    ckn = k.sb([128, 4, BS_A], BF16, pfx + "ckn")
    for bi, (t0, t1) in enumerate(blocks):
        n = t1 - t0
        if bi == 0:
            h = hbm
            k.dma("sp", h.v, hn_meta.v)
        else:
            h = hb[bi % 2]
            k.dma(os.environ.get("HQ", "sp"), h.v, hn_own[bi - 1])
        ct, sn = cst[bi % 2], snt[bi % 2]
        k.dma("sp", ct[0:64, 0:n], cos2[:, t0:t1])
        k.dma("sp", sn[0:64, 0:n], sin2s[:, t0:t1])
        for m in range(4):
            a = acc[m % 3]
            for kt in range(32):
                k.matmul(a[:, 0:n], w2[:, kt, m * 128:(m + 1) * 128], h[:, kt, 0:n], start=(kt == 0), stop=(kt == 31))
            sq = sqb[m % 2]
            k.act(sq[:, 0:n], a[:, 0:n], AF.Square)
            k.copy(ckv[:, m, 0:n], a[:, 0:n], eng="dve")
            k.matmul(ssq[:, 0:n], ones.v, sq[:, 0:n], start=(m == 0), stop=(m == 3))
        rstd_from_ssq(k, rstd, ssq, n, 512)
        for m in range(4):
            k.stt(ckn[:, m, 0:n], ckv[:, m, 0:n], gkt[:, m:m + 1], rstd[:, 0:n], ALU.mult, ALU.mult)
        u1, u2 = up[1], up[2]
        for kt in range(32):
            k.matmul(u1[0:64, 0:n], w2[:, kt, 512:576], h[:, kt, 0:n], start=(kt == 0), stop=(kt == 31))
        for kt in range(32):
            k.matmul(u2[0:64, 0:n], w2[:, kt, 576:640], h[:, kt, 0:n], start=(kt == 0), stop=(kt == 31))
        a_, b_ = ra[0], rb[0]
        k.tt(a_[0:64, 0:n], u1[0:64, 0:n], ct[0:64, 0:n], ALU.mult)
        k.tt(b_[0:64, 0:n], u2[0:64, 0:n], sn[0:64, 0:n], ALU.mult)
        o = nob()
        k.tt(o[0:64, 0:n], a_[0:64, 0:n], b_[0:64, 0:n], ALU.add)
        k.dma("act", krT[:, t0:t1], o[0:64, 0:n])
        for m in range(4):
            a = acc[m % 3]
            for kt in range(32):
                k.matmul(a[:, 0:n], w2[:, kt, 640 + m * 128:640 + (m + 1) * 128], h[:, kt, 0:n], start=(kt == 0), stop=(kt == 31))
            o = nob()
            k.act(o[:, 0:n], a[:, 0:n], AF.Silu)
            k.dma("act", gT[m * 128:(m + 1) * 128, t0:t1], o[:, 0:n])
        for hd in range(NH):
            u0 = up[0]
            for kt in range(4):
                k.matmul(u0[:, 0:n], wk[:, kt, hd * 128:(hd + 1) * 128], ckn[:, kt, 0:n], start=(kt == 0), stop=(kt == 3))
            o = nob()
            k.copy(o[:, 0:n], u0[:, 0:n], eng="act")
            k.dma("act", knT[hd, :, t0:t1], o[:, 0:n])
        for s0 in range(0, n, 128):
            ns = min(128, n - s0)
            a = acc[(s0 // 128) % 3]
            for kt in range(4):
                k.matmul(a[0:ns, :], ckn[:, kt, s0:s0 + ns], wk[:, kt, 512:1024], start=(kt == 0), stop=(kt == 3))
            o = nob()
            k.copy(o[0:ns, :], a[0:ns, :], eng="dve")
            kb = 0 if bi == 0 else 1 + (t0 + s0 - 16) // 128
            for hd in range(NH):
                k.dma("act", vtok[hd, 0:ns, kb, :], o[0:ns, hd * 128:(hd + 1) * 128])


def mla_pre(k, Tc, hnT, wq_in, wkv3, gq, gkv, cos2c, sin2sc, cqnT, ckvnT, krT, pfx="mp"):
    blocks = tok_blocks(Tc, BS_A)
    hv = hnT.rearrange("(kt p) t -> p kt t", p=128)
    ones, stg, hb, cst, snt, rstd, acc, ssq, up, sqb, ra, rb, ob = _a_common(k, pfx)
    oc = [0]

    def nob():
        oc[0] += 1
        return ob[oc[0] % 4]

    w1 = k.sb([128, 32, 1024], BF16, pfx + "w1")
    load_cast(k, w1, wq_in, 32, 1024, stg)
    w2 = k.sb([128, 32, 640], BF16, pfx + "w2")
    load_cast(k, w2, wkv3, 32, 640, stg)
    gqt = k.sb([128, 8], F32, pfx + "gq")
    gkt = k.sb([128, 4], F32, pfx + "gk")
    k.dma("sp", gqt.v, gq.v)
    k.dma("sp", gkt.v, gkv.v)
    cq = k.sb([128, 8, BS_A], F32, pfx + "cq")
    for bi, (t0, t1) in enumerate(blocks):
        n = t1 - t0
        h = hb[bi % 2]
        for kt0 in range(0, 32, 8):
            k.dma("sp", h[:, kt0:kt0 + 8, 0:n], hv[:, kt0:kt0 + 8, t0:t1])
        ct, sn = cst[bi % 2], snt[bi % 2]
        k.dma("sp", ct[0:64, 0:n], cos2c[:, t0:t1])
        k.dma("sp", sn[0:64, 0:n], sin2sc[:, t0:t1])
        for (wt, nm, gtile, dim, dst) in ((w1, 8, gqt, 1024, cqnT), (w2, 4, gkt, 512, ckvnT)):
            for m in range(nm):
                a = acc[m % 3]
                for kt in range(32):
                    k.matmul(a[:, 0:n], wt[:, kt, m * 128:(m + 1) * 128], h[:, kt, 0:n], start=(kt == 0), stop=(kt == 31))
                sq = sqb[m % 2]
                k.act(sq[:, 0:n], a[:, 0:n], AF.Square)
                k.copy(cq[:, m, 0:n], a[:, 0:n], eng="dve")
                k.matmul(ssq[:, 0:n], ones.v, sq[:, 0:n], start=(m == 0), stop=(m == nm - 1))
            rstd_from_ssq(k, rstd, ssq, n, dim)
            for m in range(nm):
                o = nob()
                k.stt(o[:, 0:n], cq[:, m, 0:n], gtile[:, m:m + 1], rstd[:, 0:n], ALU.mult, ALU.mult)
                k.dma("act", dst[m * 128:(m + 1) * 128, t0:t1], o[:, 0:n])
        u1, u2 = up[1], up[2]
        for kt in range(32):
            k.matmul(u1[0:64, 0:n], w2[:, kt, 512:576], h[:, kt, 0:n], start=(kt == 0), stop=(kt == 31))
        for kt in range(32):
            k.matmul(u2[0:64, 0:n], w2[:, kt, 576:640], h[:, kt, 0:n], start=(kt == 0), stop=(kt == 31))
        a_, b_ = ra[0], rb[0]
        k.tt(a_[0:64, 0:n], u1[0:64, 0:n], ct[0:64, 0:n], ALU.mult)
        k.tt(b_[0:64, 0:n], u2[0:64, 0:n], sn[0:64, 0:n], ALU.mult)
        o = nob()
        k.tt(o[0:64, 0:n], a_[0:64, 0:n], b_[0:64, 0:n], ALU.add)
        k.dma("act", krT[:, t0:t1], o[0:64, 0:n])


def mla_a1p(k, T, cq_meta, cq_own, wuq, cos2, sin2s, qnT, qrT, pfx="m1"):
    blocks = tok_blocks(T, BS_A)
    ones, stg, hb_unused, cst, snt, rstd, acc, ssq, up, sqb, ra, rb, ob = _a_common(k, pfx, hb=False)
    oc = [0]

    def nob():
        oc[0] += 1
        return ob[oc[0] % 4]

    wu = k.sb([128, 8, NH * 256], BF16, pfx + "wu")
    load_cast(k, wu, wuq, 8, NH * 256, stg)
    cqm = k.sb([128, 8, 16], BF16, pfx + "cqm")
    cqb = [k.sb([128, 8, BS_A], BF16, pfx + f"cqb{i}") for i in range(3)]
    for bi, (t0, t1) in enumerate(blocks):
        n = t1 - t0
        if bi == 0:
            cqn = cqm
            k.dma("sp", cqn.v, cq_meta.v)
        else:
            cqn = cqb[bi % 3]
            k.dma("sp", cqn.v, cq_own[bi - 1])
        ct, sn = cst[bi % 2], snt[bi % 2]
        k.dma("sp", ct[0:64, 0:n], cos2[:, t0:t1])
        k.dma("sp", sn[0:64, 0:n], sin2s[:, t0:t1])
        for hd in range(NH):
            c0 = hd * 256
            u0 = up[0] if hd % 2 == 0 else acc[0]
            for kt in range(8):
                k.matmul(u0[:, 0:n], wu[:, kt, c0:c0 + 128], cqn[:, kt, 0:n], start=(kt == 0), stop=(kt == 7))
            o = nob()
            k.act(o[:, 0:n], u0[:, 0:n], AF.Copy, scale=QSCALE)
            k.dma("act", qnT[hd, :, t0:t1], o[:, 0:n])
            u1, u2 = (up[1], up[2]) if hd % 2 == 0 else (acc[1], acc[2])
            for kt in range(8):
                k.matmul(u1[0:64, 0:n], wu[:, kt, c0 + 128:c0 + 192], cqn[:, kt, 0:n], start=(kt == 0), stop=(kt == 7))
            for kt in range(8):
                k.matmul(u2[0:64, 0:n], wu[:, kt, c0 + 192:c0 + 256], cqn[:, kt, 0:n], start=(kt == 0), stop=(kt == 7))
            a_, b_ = ra[hd % 2], rb[hd % 2]
            k.tt(a_[0:64, 0:n], u1[0:64, 0:n], ct[0:64, 0:n], ALU.mult)
            k.tt(b_[0:64, 0:n], u2[0:64, 0:n], sn[0:64, 0:n], ALU.mult)
            o = nob()
            k.tt(a_[0:64, 0:n], a_[0:64, 0:n], b_[0:64, 0:n], ALU.add, eng="pool")
            k.act(o[0:64, 0:n], a_[0:64, 0:n], AF.Copy, scale=QSCALE)
            k.dma("act", qrT[hd, :, t0:t1], o[0:64, 0:n])


def mla_a2p(k, T, hn_meta, hn_own, ck_meta, ck_own, wgate, wukv, knT, vtok, gT, pfx="m2"):
    blocks = tok_blocks(T, BS_A)
    ones, stg, hb, cst, snt, rstd, acc, ssq, up, sqb, ra, rb, ob = _a_common(k, pfx)
    hbm = k.sb([128, 32, 16], BF16, pfx + "hbm")
    oc = [0]

    def nob():
        oc[0] += 1
        return ob[oc[0] % 4]

    w2 = k.sb([128, 32, 512], BF16, pfx + "w2")
    load_cast(k, w2, wgate, 32, 512, stg)
    wk = k.sb([128, 4, 1024], BF16, pfx + "wk")
    load_cast(k, wk, wukv, 4, 1024, stg)
    ckm = k.sb([128, 4, 16], BF16, pfx + "ckm")
    ckb = [k.sb([128, 4, BS_A], BF16, pfx + f"ckb{i}") for i in range(3)]
    for bi, (t0, t1) in enumerate(blocks):
        n = t1 - t0
        if bi == 0:
            h, ckn = hbm, ckm
            k.dma("sp", h.v, hn_meta.v)
            k.dma("sp", ckn.v, ck_meta.v)
        else:
            h, ckn = hb[bi % 2], ckb[bi % 3]
            k.dma("sp", h.v, hn_own[bi - 1])
            k.dma("sp", ckn.v, ck_own[bi - 1])
        for m in range(4):
            a = acc[m % 3]
            for kt in range(32):
                k.matmul(a[:, 0:n], w2[:, kt, m * 128:(m + 1) * 128], h[:, kt, 0:n], start=(kt == 0), stop=(kt == 31))
            o = nob()
            k.act(o[:, 0:n], a[:, 0:n], AF.Silu)
            k.dma("act", gT[m * 128:(m + 1) * 128, t0:t1], o[:, 0:n])
        for hd in range(NH):
            u0 = up[hd % 3]
            for kt in range(4):
                k.matmul(u0[:, 0:n], wk[:, kt, hd * 128:(hd + 1) * 128], ckn[:, kt, 0:n], start=(kt == 0), stop=(kt == 3))
            o = nob()
            k.copy(o[:, 0:n], u0[:, 0:n], eng=("act" if hd % 2 else "dve"))
            k.dma("act", knT[hd, :, t0:t1], o[:, 0:n])
        for s0 in range(0, n, 128):
            ns = min(128, n - s0)
            a = acc[(s0 // 128) % 3]
            for kt in range(4):
                k.matmul(a[0:ns, :], ckn[:, kt, s0:s0 + ns], wk[:, kt, 512:1024], start=(kt == 0), stop=(kt == 3))
            o = nob()
            k.copy(o[0:ns, :], a[0:ns, :], eng="dve")
            kb = 0 if bi == 0 else 1 + (t0 + s0 - 16) // 128
            for hd in range(NH):
                k.dma("act", vtok[hd, 0:ns, kb, :], o[0:ns, hd * 128:(hd + 1) * 128])


def mla_phase_b(k, T, qnT, qrT, knT, krT, vtok, gT, yT, pfx="mb"):
    NB = (T - 16) // 512
    NKB = (T - 16) // 128
    onesf = k.sb([128, 128], F32, pfx + "onesf")
    k.memset(onesf.v, 1.0)
    kr = k.sb([128, T], BF16, pfx + "kr")
    k.dma("sp", kr[0:64, :], krT.v)
    kn = k.sb([128, T], BF16, pfx + "kn")
    vv = k.sb([128, NKB + 1, 128], BF16, pfx + "vv")
    qn = [k.sb([128, 512], BF16, pfx + f"qn{i}") for i in range(2)]
    qr = [k.sb([128, 512], BF16, pfx + f"qr{i}") for i in range(2)]
    gt = [k.sb([128, 512], BF16, pfx + f"gt{i}") for i in range(2)]
    pt = [k.sb([128, 512], BF16, pfx + f"pt{i}") for i in range(6)]
    sc = [k.ps([128, 512], F32, pfx + f"sc{i}") for i in range(4)]
    oT = [k.ps([128, 512], F32, pfx + f"oT{i}") for i in range(2)]
    dn = k.ps([128, 512], F32, pfx + "dn")
    dacc = [k.sb([128, 512], F32, pfx + f"dacc{i}") for i in range(2)]
    rden = [k.sb([128, 512], F32, pfx + f"rden{i}") for i in range(2)]
    yo = [k.sb([128, 512], F32, pfx + f"yo{i}") for i in range(2)]
    yb = [k.sb([128, 512], BF16, pfx + f"yb{i}") for i in range(2)]
    u = 0
    g = 0
    for hd in range(NH):
        k.dma("sp", kn.v, knT[hd, :, :])
        k.dma("act", vv.v, vtok[hd])
        groups = [(0, 16, -1)] + [(16 + 512 * i, 16 + 512 * (i + 1), i) for i in range(NB)]
        for (q0, q1, gi) in groups:
            n = q1 - q0
            qnt, qrt, gtt = qn[g % 2], qr[g % 2], gt[g % 2]
            o_ = oT[g % 2]
            k.dma("sp", qnt[:, 0:n], qnT[hd, :, q0:q1])
            k.dma("sp", qrt[0:64, 0:n], qrT[hd, :, q0:q1])
            k.dma("sp", gtt[:, 0:n], gT[hd * 128:(hd + 1) * 128, q0:q1])
            k.memset(dacc[0][:, 0:n], 0.0, eng="dve")
            k.memset(dacc[1][:, 0:n], 0.0, eng="pool")
            kbs = [(0, 16, 0, 0, False)]
            if gi >= 0:
                for j in range(4 * gi):
                    kbs.append((16 + 128 * j, 128, 1 + j, 0, False))
                for dgi in range(4):
                    j = 4 * gi + dgi
                    kbs.append((16 + 128 * j, 128, 1 + j, 128 * dgi, True))

            def scores(idx, uu):
                kc, nk, vb, qs, diag = kbs[idx]
                s_ = sc[uu % 4]
                k.matmul(s_[0:nk, qs:n], kn[:, kc:kc + nk], qnt[:, qs:n], start=True, stop=False)
                k.matmul(s_[0:nk, qs:n], kr[0:64, kc:kc + nk], qrt[0:64, qs:n], start=False, stop=True)

            scores(0, u)
            if len(kbs) > 1:
                scores(1, u + 1)
            for idx, (kc, nk, vb, qs, diag) in enumerate(kbs):
                if idx + 2 < len(kbs):
                    scores(idx + 2, u + 2)
                s_ = sc[u % 4]
                p_ = pt[u % 6]
                k.act(p_[0:nk, qs:n], s_[0:nk, qs:n], AF.Exp)
                if diag:
                    k.memset(p_[64:128, qs:qs + 64], 0.0, eng="pool")
                last = (idx == len(kbs) - 1)
                k.matmul(o_[:, qs:n], vv[0:nk, vb, :], p_[0:nk, qs:n], start=(idx == 0), stop=last)
                da = dacc[u % 2]
                k.tt(da[0:nk, qs:n], da[0:nk, qs:n], p_[0:nk, qs:n], ALU.add, eng=("dve" if u % 2 == 0 else "pool"))
                u += 1
            k.matmul(dn[:, 0:n], onesf.v, dacc[0][:, 0:n], start=True, stop=False)
            k.matmul(dn[:, 0:n], onesf.v, dacc[1][:, 0:n], start=False, stop=True)
            rd, y1, y2 = rden[g % 2], yo[g % 2], yb[g % 2]
            k.recip(rd[:, 0:n], dn[:, 0:n])
            k.tt(y1[:, 0:n], o_[:, 0:n], rd[:, 0:n], ALU.mult)
            k.tt(y2[:, 0:n], y1[:, 0:n], gtt[:, 0:n], ALU.mult, eng="pool")
            k.dma("act", yT[hd * 128:(hd + 1) * 128, q0:q1], y2[:, 0:n])
            g += 1


def stage_mla(k, T, hn_meta, hn_own, wq_in, wkv_in, wuq, wukv, gq, gkv, cos2, sin2s, yT, pfx="ml", kind="Internal", phases="12b"):
    qnT = k.dram(pfx + "_qnT", [NH, 128, T], BF16, kind=kind)
    qrT = k.dram(pfx + "_qrT", [NH, 64, T], BF16, kind=kind)
    knT = k.dram(pfx + "_knT", [NH, 128, T], BF16, kind=kind)
    krT = k.dram(pfx + "_krT", [64, T], BF16, kind=kind)
    vtok = k.dram(pfx + "_vtok", [NH, 128, (T - 16) // 128 + 1, 128], BF16, kind=kind)
    gT = k.dram(pfx + "_gT", [512, T], BF16, kind=kind)
    if "1" in phases:
      with (contextlib.nullcontext() if os.environ.get("NOSCOPE") else k.scope()):
        mla_a1(k, T, hn_meta, hn_own, wq_in, wuq, gq, cos2, sin2s, qnT, qrT, pfx + "1")
    if "2" in phases:
      with k.scope():
        mla_a2(k, T, hn_meta, hn_own, wkv_in, wukv, gkv, cos2, sin2s, knT, krT, vtok, gT, pfx + "2")
    if "b" in phases:
      with k.scope():
        mla_phase_b(k, T, qnT, qrT, knT, krT, vtok, gT, yT, pfx + "b")
    return dict(qnT=qnT, qrT=qrT, knT=knT, krT=krT, vtok=vtok, gT=gT)


def stage_mla2(k, T, hn_meta, hn_own, cq_meta, cq_own, ck_meta, ck_own, krT, wgate, wuq, wukv, cos2, sin2s, yT, pfx="ml"):
    qnT = k.dram(pfx + "_qnT", [NH, 128, T], BF16)
    qrT = k.dram(pfx + "_qrT", [NH, 64, T], BF16)
    knT = k.dram(pfx + "_knT", [NH, 128, T], BF16)
    vtok = k.dram(pfx + "_vtok", [NH, 128, (T - 16) // 128 + 1, 128], BF16)
    gT = k.dram(pfx + "_gT", [512, T], BF16)
    with k.scope():
        mla_a1p(k, T, cq_meta, cq_own, wuq, cos2, sin2s, qnT, qrT, pfx + "1")
    with k.scope():
        mla_a2p(k, T, hn_meta, hn_own, ck_meta, ck_own, wgate, wukv, knT, vtok, gT, pfx + "2")
    with k.scope():
        mla_phase_b(k, T, qnT, qrT, knT, krT, vtok, gT, yT, pfx + "b")


EPS = 1e-6
GELU_C = 1.5957691216057308


def hyb_ssd(k, T, hn_meta, hn_own, w_ssd, convw, convb, dtb_bc, alog_bc, d_bc, ng_bc, Umat, ident, yT, pfx="hs"):
    blocks = tok_blocks(T, BS_A)
    NW = 1288
    stg = [k.sb([128, NW], F32, pfx + f"stg{i}") for i in range(2)]
    w = k.sb([128, 32, NW], BF16, pfx + "w")
    load_cast(k, w, w_ssd, 32, NW, stg)
    hbm = k.sb([128, 32, 16], BF16, pfx + "hbm")
    hb = [k.sb([128, 32, BS_A], BF16, pfx + f"hb{i}") for i in range(2)]
    cw = k.sb([128, 6, 4], F32, pfx + "cw")
    cb = k.sb([128, 6], F32, pfx + "cb")
    k.dma("sp", cw.v, convw.v)
    k.dma("sp", cb.v, convb.v)
    dtb = k.sb([128, 8], F32, pfx + "dtb")
    aneg = k.sb([128, 8], F32, pfx + "aneg")
    dbc = k.sb([128, 512], F32, pfx + "dbc")
    ngb = k.sb([128, 512], F32, pfx + "ngb")
    U = k.sb([128, 128], F32, pfx + "U")
    idb = k.sb([128, 128], BF16, pfx + "idb")
    idf = k.sb([128, 128], F32, pfx + "idf")
    k.dma("sp", dtb.v, dtb_bc.v)
    k.dma("sp", aneg.v, alog_bc.v)
    k.dma("sp", dbc.v, d_bc.v)
    k.dma("sp", ngb.v, ng_bc.v)
    k.dma("sp", U.v, Umat.v)
    k.dma("sp", idf.v, ident.v)
    k.copy(idb.v, idf.v, eng="pool")
    k.act(aneg.v, aneg.v, AF.Exp)
    k.ts(aneg.v, aneg.v, -1.0, ALU.mult)
    ones = k.sb([128, 128], F32, pfx + "ones")
    k.memset(ones.v, 1.0)

    cin = [k.sb([128, 3 + BS_A], F32, pfx + f"cin{m}") for m in range(6)]
    for m in range(6):
        k.memset(cin[m].v, 0.0)
    cacc = [k.sb([128, BS_A], F32, pfx + f"cacc{i}") for i in range(2)]
    fT = [k.sb([128, BS_A], BF16, pfx + f"fT{m}") for m in range(6)]
    L_zs = [k.sb([128, 512], F32, pfx + f"zs{i}") for i in range(2)]
    L_ctk = [k.sb([128, 128], BF16, pfx + f"ctk{i}") for i in range(2)]
    L_dt = [k.sb([128, 8], F32, pfx + f"dt{i}") for i in range(2)]
    L_da = [k.sb([128, 8], F32, pfx + f"da{i}") for i in range(2)]
    L_dab = [k.sb([128, 8, 128], F32, pfx + f"dab{i}") for i in range(2)]
    L_acum = [k.sb([128, 8], F32, pfx + f"acum{i}") for i in range(2)]
    L_nacum = [k.sb([128, 8], F32, pfx + f"nacum{i}") for i in range(2)]
    L_aend = [k.sb([128, 8], F32, pfx + f"aend{i}") for i in range(2)]
    L_eend = [k.sb([128, 8], F32, pfx + f"eend{i}") for i in range(2)]
    L_eac = [k.sb([128, 8], F32, pfx + f"eac{i}") for i in range(2)]
    L_dte = [k.sb([128, 8], F32, pfx + f"dte{i}") for i in range(2)]
    L_xtok = [k.sb([128, 512], BF16, pfx + f"xtok{i}") for i in range(2)]
    L_btok = [k.sb([128, 128], BF16, pfx + f"btok{i}") for i in range(2)]
    L_xdt = [k.sb([128, 512], BF16, pfx + f"xdt{i}") for i in range(2)]
    L_xw = [k.sb([128, 512], BF16, pfx + f"xw{i}") for i in range(2)]
    L_segc = [k.sb([128, 8, 128], F32, pfx + f"segc{i}") for i in range(2)]
    L_cbm = [k.sb([128, 128], F32, pfx + f"cbm{i}") for i in range(2)]
    L_MT = [k.sb([128, 8, 128], BF16, pfx + f"MT{i}") for i in range(2)]
    S = k.sb([128, 512], F32, pfx + "S")
    Sb = k.sb([128, 512], BF16, pfx + "Sb")
    k.memset(S.v, 0.0)
    k.memset(Sb.v, 0.0)
    L_t1 = [k.sb([128, 512], F32, pfx + f"t1{i}") for i in range(2)]
    L_t2 = [k.sb([128, 512], F32, pfx + f"t2{i}") for i in range(2)]
    L_ssq = [k.sb([128, 1], F32, pfx + f"ssq{i}") for i in range(2)]
    L_yn = [k.sb([128, 512], BF16, pfx + f"yn{i}") for i in range(2)]
    yTs = [k.sb([128, 128], BF16, pfx + f"yTs{i}") for i in range(4)]

    accA = k.ps([128, 512], F32, pfx + "accA")
    accB = k.ps([128, 512], F32, pfx + "accB")
    misc = k.ps([128, 512], F32, pfx + "misc")
    AB = k.ps([128, 8, 128], F32, pfx + "AB")
    ydg = k.ps([128, 512], F32, pfx + "ydg")
    yof = k.ps([128, 512], F32, pfx + "yof")
    tr = k.ps([128, 512], BF16, pfx + "tr")
    accs = [accA, accB]

    def stageA(c):
        cl, cs, h, par = c['cl'], c['cs'], c['h'], c['par']
        zs = L_zs[par]
        dt = L_dt[par]
        da = L_da[par]
        dab = L_dab[par]
        acum = L_acum[par]
        nacum = L_nacum[par]
        aend = L_aend[par]
        eend = L_eend[par]
        eac = L_eac[par]
        dte = L_dte[par]
        xtok = L_xtok[par]
        btok = L_btok[par]
        xdt = L_xdt[par]
        xw = L_xw[par]
        segc = L_segc[par]
        cbm = L_cbm[par]
        MT = L_MT[par]
        t1 = L_t1[par]
        t2 = L_t2[par]
        ssq = L_ssq[par]
        yn = L_yn[par]
        ctk = L_ctk[par]
        for kt in range(32):
            k.matmul(accA[0:cl, :], h[:, kt, cs], w[:, kt, 0:512], start=(kt == 0), stop=(kt == 31))
        k.act(zs[0:cl, :], accA[0:cl, :], AF.Silu)
        for kt in range(32):
            k.matmul(misc[0:cl, 0:8], h[:, kt, cs], w[:, kt, 1280:1288], start=(kt == 0), stop=(kt == 31))
        k.tt(dt[0:cl, :], misc[0:cl, 0:8], dtb[0:cl, :], ALU.add)
        k.act(dt[0:cl, :], dt[0:cl, :], AF.Exp)
        k.act(dt[0:cl, :], dt[0:cl, :], AF.Ln, bias=1.0)
        k.tt(da[0:cl, :], dt[0:cl, :], aneg[0:cl, :], ALU.mult)
        for m in range(4):
            k.transpose(tr[0:cl, m * 128:(m + 1) * 128], fT[m][:, cs], idb.v)
        k.copy(xtok[0:cl, :], tr[0:cl, :], eng="act")
        k.transpose(tr[0:cl, 0:128], fT[4][:, cs], idb.v)
        k.copy(btok[0:cl, :], tr[0:cl, 0:128], eng="act")
        k.tt(xdt[0:cl, :].rearrange("p (h d) -> p h d", h=8), xtok[0:cl, :].rearrange("p (h d) -> p h d", h=8),
             dt[0:cl, :].unsqueeze(2).broadcast_to([cl, 8, 64]), ALU.mult)
        k.matmul(misc[0:cl, 8:16], U[0:cl, 0:cl], da[0:cl, :])
        k.copy(acum[0:cl, :], misc[0:cl, 8:16], eng="dve")
        k.ts(nacum[0:cl, :], acum[0:cl, :], -1.0, ALU.mult)
        k.act(eac[0:cl, :], acum[0:cl, :], AF.Exp)
        k.tt(dab[0:cl, :, 0:cl], ones[0:cl, 0:cl].unsqueeze(1).broadcast_to([cl, 8, cl]),
             da[0:cl, :].unsqueeze(2).broadcast_to([cl, 8, cl]), ALU.mult, eng="pool")
        for hh in range(8):
            k.matmul(AB[0:cl, hh, 0:cl], dab[0:cl, hh, 0:cl], U[0:cl, 0:cl])
        k.tt(segc[0:cl, :, 0:cl], AB[0:cl, :, 0:cl], nacum[0:cl, :].unsqueeze(2).broadcast_to([cl, 8, cl]), ALU.add)
        k.copy(aend[0:cl, :], AB[0:cl, :, cl - 1], eng="dve")
        k.ts(segc[0:cl, :, 0:cl], segc[0:cl, :, 0:cl], 0.0, ALU.min, eng="pool")
        k.act(segc[0:cl, :, 0:cl], segc[0:cl, :, 0:cl], AF.Exp)
        k.matmul(misc[0:cl, 128:128 + cl], fT[4][:, cs], fT[5][:, cs])
        k.tt(cbm[0:cl, 0:cl], misc[0:cl, 128:128 + cl], U[0:cl, 0:cl], ALU.mult)
        k.tt(MT[0:cl, :, 0:cl], segc[0:cl, :, 0:cl], cbm[0:cl, 0:cl].unsqueeze(1).broadcast_to([cl, 8, cl]), ALU.mult, eng="pool")
        k.copy(ctk[:, 0:cl], fT[5][:, cs], eng="pool")

    def stageB(c):
        cl, cs, par, tok0 = c['cl'], c['cs'], c['par'], c['tok0']
        zs = L_zs[par]
        dt = L_dt[par]
        da = L_da[par]
        dab = L_dab[par]
        acum = L_acum[par]
        nacum = L_nacum[par]
        aend = L_aend[par]
        eend = L_eend[par]
        eac = L_eac[par]
        dte = L_dte[par]
        xtok = L_xtok[par]
        btok = L_btok[par]
        xdt = L_xdt[par]
        xw = L_xw[par]
        segc = L_segc[par]
        cbm = L_cbm[par]
        MT = L_MT[par]
        t1 = L_t1[par]
        t2 = L_t2[par]
        ssq = L_ssq[par]
        yn = L_yn[par]
        ctk = L_ctk[par]
        for hh in range(8):
            k.matmul(ydg[0:cl, hh * 64:(hh + 1) * 64], MT[0:cl, hh, 0:cl], xdt[0:cl, hh * 64:(hh + 1) * 64])
        k.matmul(yof[0:cl, :], ctk[:, 0:cl], Sb.v)
        k.tt(t1[0:cl, :].rearrange("p (h d) -> p h d", h=8), yof[0:cl, :].rearrange("p (h d) -> p h d", h=8),
             eac[0:cl, :].unsqueeze(2).broadcast_to([cl, 8, 64]), ALU.mult)
        k.tt(t1[0:cl, :], t1[0:cl, :], ydg[0:cl, :], ALU.add)
        k.tt(t2[0:cl, :], xtok[0:cl, :], dbc[0:cl, :], ALU.mult, eng="pool")
        k.tt(t1[0:cl, :], t1[0:cl, :], t2[0:cl, :], ALU.add)
        k.tt(t1[0:cl, :], t1[0:cl, :], zs[0:cl, :], ALU.mult)
        k.act(t2[0:cl, :], t1[0:cl, :], AF.Square, accum_out=ssq[0:cl, :])
        k.ts(ssq[0:cl, :], ssq[0:cl, :], 1.0 / 512, ALU.mult, EPS, ALU.add)
        k.act(ssq[0:cl, :], ssq[0:cl, :], AF.Sqrt)
        k.recip(ssq[0:cl, :], ssq[0:cl, :])
        k.stt(yn[0:cl, :], t1[0:cl, :], ssq[0:cl, 0:1], ngb[0:cl, :], ALU.mult, ALU.mult)
        for m in range(4):
            k.transpose(tr[:, m * 128:m * 128 + cl], yn[0:cl, m * 128:(m + 1) * 128], idb[0:cl, 0:cl])
        for m in range(4):
            k.copy(yTs[m][:, 0:cl], tr[:, m * 128:m * 128 + cl], eng=("act" if m % 2 else "dve"))
            k.dma("act", yT[m * 128:(m + 1) * 128, tok0:tok0 + cl], yTs[m][:, 0:cl])
        k.ts(dte[0:cl, :], aend[0:cl, :], 1.0 / cl, ALU.mult)
        k.matmul(misc[:, 16:24], ones[0:cl, :], dte[0:cl, :])
        k.act(eend.v, misc[:, 16:24], AF.Exp)
        k.tt(dte[0:cl, :], aend[0:cl, :], acum[0:cl, :], ALU.subtract)
        k.act(dte[0:cl, :], dte[0:cl, :], AF.Exp)
        k.tt(xw[0:cl, :].rearrange("p (h d) -> p h d", h=8), xdt[0:cl, :].rearrange("p (h d) -> p h d", h=8),
             dte[0:cl, :].unsqueeze(2).broadcast_to([cl, 8, 64]), ALU.mult)
        k.matmul(yof.v, btok[0:cl, :], xw[0:cl, :])
        k.tt(S.v.rearrange("p (h d) -> p h d", h=8), S.v.rearrange("p (h d) -> p h d", h=8),
             eend.v.unsqueeze(2).broadcast_to([128, 8, 64]), ALU.mult)
        k.tt(S.v, S.v, yof.v, ALU.add)
        k.copy(Sb.v, S.v, eng="pool")


    pendB = None
    nchunk = 0
    for bi, (t0, t1_) in enumerate(blocks):
        n = t1_ - t0
        if bi == 0:
            h = hbm
            k.dma("sp", h.v, hn_meta.v)
        else:
            h = hb[bi % 2]
            k.dma("sp", h.v, hn_own[bi - 1])
        for m in range(6):
            a = accs[m % 2]
            c0 = 512 + m * 128
            for kt in range(32):
                k.matmul(a[:, 0:n], w[:, kt, c0:c0 + 128], h[:, kt, 0:n], start=(kt == 0), stop=(kt == 31))
            ci = cin[m]
            k.copy(ci[:, 3:3 + n], a[:, 0:n], eng="act")
            ca = cacc[m % 2]
            k.ts(ca[:, 0:n], ci[:, 0:n], cw[:, m, 0:1], ALU.mult, cb[:, m:m + 1], ALU.add)
            for j in range(1, 4):
                k.stt(ca[:, 0:n], ci[:, j:j + n], cw[:, m, j:j + 1], ca[:, 0:n], ALU.mult, ALU.add)
            k.act(fT[m][:, 0:n], ca[:, 0:n], AF.Silu)
            k.copy(ci[:, 0:3], ci[:, n:n + 3], eng="pool")
        for s0 in range(0, n, 128):
            cl = min(128, n - s0)
            ctx = dict(cl=cl, cs=slice(s0, s0 + cl), tok0=t0 + s0, h=h, par=nchunk % 2)
            stageA(ctx)
            if pendB is not None:
                stageB(pendB)
            pendB = ctx
            nchunk += 1
    if pendB is not None:
        stageB(pendB)


def hyb_s5(k, T, hn_meta, hn_own, w_s5, bre, bim, cre, cim, are_l, aim_l, ldt_l, d_l, gT, sgT, pfx="h5"):
    blocks = tok_blocks(T, BS_A)
    L = BS_A
    stg = [k.sb([128, 512], F32, pfx + f"stg{i}") for i in range(2)]
    w = k.sb([128, 32, 512], BF16, pfx + "w")
    load_cast(k, w, w_s5, 32, 512, stg)
    hbm = k.sb([128, 32, 16], BF16, pfx + "hbm")
    hb = [k.sb([128, 32, BS_A], BF16, pfx + f"hb{i}") for i in range(2)]
    f_bre = k.sb([128, 8, 128], F32, pfx + "fbre"); f_bim = k.sb([128, 8, 128], F32, pfx + "fbim")
    f_cre = k.sb([128, 8, 128], F32, pfx + "fcre"); f_cim = k.sb([128, 8, 128], F32, pfx + "fcim")
    Bre = k.sb([128, 8, 128], BF16, pfx + "Bre"); Bim = k.sb([128, 8, 128], BF16, pfx + "Bim")
    Cre = k.sb([128, 8, 128], BF16, pfx + "Cre"); Cim = k.sb([128, 8, 128], BF16, pfx + "Cim")
    for (dst, f, src) in ((Bre, f_bre, bre), (Bim, f_bim, bim), (Cre, f_cre, cre), (Cim, f_cim, cim)):
        k.dma("sp", f.v, src.v)
        k.copy(dst.v, f.v, eng="pool")
    are = k.sb([128, 8], F32, pfx + "are"); aim = k.sb([128, 8], F32, pfx + "aim"); dtt = k.sb([128, 8], F32, pfx + "dtt")
    dsk = k.sb([128, 2], F32, pfx + "dsk")
    k.dma("sp", are.v, are_l.v); k.dma("sp", aim.v, aim_l.v); k.dma("sp", dtt.v, ldt_l.v); k.dma("sp", dsk.v, d_l.v)
    k.act(dtt.v, dtt.v, AF.Exp)
    th = k.sb([128, 8], F32, pfx + "th"); rho = k.sb([128, 8], F32, pfx + "rho")
    k.tt(th.v, dtt.v, aim.v, ALU.mult)
    k.tt(rho.v, dtt.v, are.v, ALU.mult)
    k.act(rho.v, rho.v, AF.Exp)
    ki = k.sb([128, 8], I32, pfx + "ki"); kf = k.sb([128, 8], F32, pfx + "kf")
    hh_ = k.sb([128, 8], F32, pfx + "hh"); sh = k.sb([128, 8], F32, pfx + "sh"); ch = k.sb([128, 8], F32, pfx + "ch")
    k.ts(kf.v, th.v, 1.0 / (2 * math.pi), ALU.mult)
    k.copy(ki.v, kf.v, eng="dve")
    k.copy(kf.v, ki.v, eng="dve")
    k.stt(hh_.v, kf.v, -2 * math.pi, th.v, ALU.mult, ALU.add)
    k.ts(hh_.v, hh_.v, 0.5, ALU.mult)
    k.act(sh.v, hh_.v, AF.Sin)
    q4 = k.sb([128, 8], F32, pfx + "q4")
    k.act(q4.v, hh_.v, AF.Sin, scale=0.5)
    k.tt(q4.v, q4.v, q4.v, ALU.mult)
    k.ts(ch.v, q4.v, -2.0, ALU.mult, 1.0, ALU.add)
    zr = k.sb([128, 8, 9], F32, pfx + "zr"); zi = k.sb([128, 8, 9], F32, pfx + "zi"); nzi = k.sb([128, 8, 9], F32, pfx + "nzi")
    tmp8 = k.sb([128, 8], F32, pfx + "tmp8"); tmp8b = k.sb([128, 8], F32, pfx + "tmp8b")
    k.tt(tmp8.v, sh.v, sh.v, ALU.mult)
    k.ts(zr[:, :, 0], tmp8.v, -2.0, ALU.mult, 1.0, ALU.add)
    k.tt(tmp8.v, sh.v, ch.v, ALU.mult)
    k.ts(zi[:, :, 0], tmp8.v, 2.0, ALU.mult)
    for m_ in range(8):
        k.tt(tmp8.v, zr[:, :, m_], zr[:, :, m_], ALU.mult)
        k.tt(tmp8b.v, zi[:, :, m_], zi[:, :, m_], ALU.mult)
        k.tt(zr[:, :, m_ + 1], tmp8.v, tmp8b.v, ALU.subtract)
        k.tt(tmp8.v, zr[:, :, m_], zi[:, :, m_], ALU.mult)
        k.ts(zi[:, :, m_ + 1], tmp8.v, 2.0, ALU.mult)
    k.ts(nzi.v, zi.v, -1.0, ALU.mult)
    abr = k.sb([128, 8], F32, pfx + "abr"); abi = k.sb([128, 8], F32, pfx + "abi"); den = k.sb([128, 8], F32, pfx + "den")
    kre = k.sb([128, 8], F32, pfx + "kre"); kim = k.sb([128, 8], F32, pfx + "kim"); nkre = k.sb([128, 8], F32, pfx + "nkre")
    k.tt(abr.v, rho.v, zr[:, :, 0], ALU.mult)
    k.ts(abr.v, abr.v, -1.0, ALU.add)
    k.tt(abi.v, rho.v, zi[:, :, 0], ALU.mult)
    k.tt(den.v, are.v, are.v, ALU.mult)
    k.tt(tmp8.v, aim.v, aim.v, ALU.mult)
    k.tt(den.v, den.v, tmp8.v, ALU.add)
    k.recip(den.v, den.v)
    k.tt(kre.v, abr.v, are.v, ALU.mult)
    k.tt(tmp8.v, abi.v, aim.v, ALU.mult)
    k.tt(kre.v, kre.v, tmp8.v, ALU.add)
    k.tt(kre.v, kre.v, den.v, ALU.mult)
    k.tt(kim.v, abi.v, are.v, ALU.mult)
    k.tt(tmp8.v, abr.v, aim.v, ALU.mult)
    k.tt(kim.v, kim.v, tmp8.v, ALU.subtract)
    k.tt(kim.v, kim.v, den.v, ALU.mult)
    k.ts(nkre.v, kre.v, -1.0, ALU.mult)
    Fc = k.sb([128, 8, L], F32, pfx + "Fc"); Fs = k.sb([128, 8, L], F32, pfx + "Fs")
    Ere = k.sb([128, 8, L], F32, pfx + "Ere"); Eim = k.sb([128, 8, L], F32, pfx + "Eim")
    rhoT = k.sb([128, 8, L], F32, pfx + "rhoT")
    tl = k.sb([128, L], F32, pfx + "tl")
    k.memset(Fc.v, 1.0)
    k.memset(Fs.v, 0.0)
    k.memset(rhoT.v, 1.0)
    for j in range(8):
        for m_ in range(8):
            lo = slice(0, 2 ** m_)
            hi = slice(2 ** m_, 2 ** (m_ + 1))
            w_ = 2 ** m_
            k.ts(tl[:, 0:w_], Fs[:, j, lo], zi[:, j, m_:m_ + 1], ALU.mult)
            k.stt(Fc[:, j, hi], Fc[:, j, lo], zr[:, j, m_:m_ + 1], tl[:, 0:w_], ALU.mult, ALU.subtract)
            k.ts(tl[:, 0:w_], Fc[:, j, lo], zi[:, j, m_:m_ + 1], ALU.mult)
            k.stt(Fs[:, j, hi], Fs[:, j, lo], zr[:, j, m_:m_ + 1], tl[:, 0:w_], ALU.mult, ALU.add)
        k.ts(tl.v, Fs[:, j, :], kim[:, j:j + 1], ALU.mult)
        k.stt(Ere[:, j, :], Fc[:, j, :], kre[:, j:j + 1], tl.v, ALU.mult, ALU.add)
        k.ts(tl.v, Fs[:, j, :], nkre[:, j:j + 1], ALU.mult)
        k.stt(Eim[:, j, :], Fc[:, j, :], kim[:, j:j + 1], tl.v, ALU.mult, ALU.add)
        k.ts(rhoT[:, j, :], rhoT[:, j, :], rho[:, j:j + 1], ALU.mult)
    uf = [k.sb([128, BS_A], F32, pfx + f"uf{a}") for a in range(2)]
    ub = [k.sb([128, BS_A], BF16, pfx + f"ub{a}") for a in range(2)]
    go = [k.sb([128, BS_A], BF16, pfx + f"go{i}") for i in range(2)]
    L5 = {nm: [k.sb([128, BS_A], F32, pfx + f"{nm}{i}") for i in range(2)]
          for nm in ("vre", "vim", "p1", "p2", "p3", "p4", "wre", "wim", "q1", "q2", "q3", "q4")}
    sre = [k.sb([128, BS_A], BF16, pfx + f"sre{i}") for i in range(2)]
    sim_ = [k.sb([128, BS_A], BF16, pfx + f"sim{i}") for i in range(2)]
    wir = k.sb([128, 8], F32, pfx + "wir"); wii = k.sb([128, 8], F32, pfx + "wii")
    k.memset(wir.v, 0.0)
    k.memset(wii.v, 0.0)
    c1 = k.sb([128, 1], F32, pfx + "c1")
    x2 = k.sb([128, BS_A], F32, pfx + "x2"); x3 = k.sb([128, BS_A], F32, pfx + "x3"); yv = k.sb([128, BS_A], F32, pfx + "yv")
    acc = [k.ps([128, 512], F32, pfx + f"acc{i}") for i in range(2)]
    Pp = k.ps([128, 512], F32, pfx + "Pp"); Qp = k.ps([128, 512], F32, pfx + "Qp")
    yp = [k.ps([128, 512], F32, pfx + f"yp{a}") for a in range(2)]
    for bi, (t0, t1_) in enumerate(blocks):
        n = t1_ - t0
        if bi == 0:
            h = hbm
            k.dma("sp", h.v, hn_meta.v)
        else:
            h = hb[bi % 2]
            k.dma("sp", h.v, hn_own[bi - 1])
        lev = 4 if bi == 0 else 8
        for a in range(2):
            for kt in range(32):
                k.matmul(acc[0][:, 0:n], w[:, kt, a * 128:(a + 1) * 128], h[:, kt, 0:n], start=(kt == 0), stop=(kt == 31))
            k.copy(uf[a][:, 0:n], acc[0][:, 0:n], eng="dve")
            k.copy(ub[a][:, 0:n], uf[a][:, 0:n], eng="pool")
            for kt in range(32):
                k.matmul(acc[1][:, 0:n], w[:, kt, 256 + a * 128:256 + (a + 1) * 128], h[:, kt, 0:n], start=(kt == 0), stop=(kt == 31))
            o = go[a]
            k.act(o[:, 0:n], acc[1][:, 0:n], AF.Silu)
            k.dma("act", sgT[a * 128:(a + 1) * 128, t0:t1_], o[:, 0:n])
        def pq(j):
            a = j // 4
            k.matmul(Pp[:, 0:n], Bre[:, j, :], ub[a][:, 0:n])
            k.matmul(Qp[:, 0:n], Bim[:, j, :], ub[a][:, 0:n])

        pq(0)
        for j in range(8):
            a = j // 4
            vre, vim, p1, p2, p3, p4, wre, wim, q1, q2, q3, q4 = (L5[nm][j % 2] for nm in
                ("vre", "vim", "p1", "p2", "p3", "p4", "wre", "wim", "q1", "q2", "q3", "q4"))
            k.tt(p1[:, 0:n], Pp[:, 0:n], Ere[:, j, 0:n], ALU.mult)
            k.tt(p2[:, 0:n], Qp[:, 0:n], Eim[:, j, 0:n], ALU.mult)
            k.tt(p3[:, 0:n], Qp[:, 0:n], Ere[:, j, 0:n], ALU.mult)
            k.tt(p4[:, 0:n], Pp[:, 0:n], Eim[:, j, 0:n], ALU.mult)
            if j + 1 < 8:
                pq(j + 1)
            k.tt(vre[:, 0:n], p1[:, 0:n], p2[:, 0:n], ALU.subtract, eng="pool")
            k.tt(vim[:, 0:n], p3[:, 0:n], p4[:, 0:n], ALU.add, eng="pool")
            k.scan(wre[:, 0:n], rhoT[:, j, 0:n], vre[:, 0:n], wir[:, j:j + 1])
            k.scan(wim[:, 0:n], rhoT[:, j, 0:n], vim[:, 0:n], wii[:, j:j + 1])
            k.ts(c1.v, wim[:, n - 1:n], nzi[:, j, lev:lev + 1], ALU.mult)
            k.stt(wir[:, j:j + 1], wre[:, n - 1:n], zr[:, j, lev:lev + 1], c1.v, ALU.mult, ALU.add)
            k.ts(c1.v, wre[:, n - 1:n], zi[:, j, lev:lev + 1], ALU.mult)
            k.stt(wii[:, j:j + 1], wim[:, n - 1:n], zr[:, j, lev:lev + 1], c1.v, ALU.mult, ALU.add)
            sr, si = sre[j % 2], sim_[j % 2]
            k.tt(q1[:, 0:n], wre[:, 0:n], Fc[:, j, 0:n], ALU.mult, eng="pool")
            k.tt(q2[:, 0:n], wim[:, 0:n], Fs[:, j, 0:n], ALU.mult, eng="pool")
            k.tt(sr[:, 0:n], q1[:, 0:n], q2[:, 0:n], ALU.subtract, eng="pool")
            k.tt(q3[:, 0:n], wre[:, 0:n], Fs[:, j, 0:n], ALU.mult, eng="pool")
            k.tt(q4[:, 0:n], wim[:, 0:n], Fc[:, j, 0:n], ALU.mult, eng="pool")
            k.stt(si[:, 0:n], q3[:, 0:n], -1.0, q4[:, 0:n], ALU.mult, ALU.subtract)
            k.matmul(yp[a][:, 0:n], Cre[:, j, :], sr[:, 0:n], start=(j % 4 == 0), stop=False)
            k.matmul(yp[a][:, 0:n], Cim[:, j, :], si[:, 0:n], start=False, stop=(j % 4 == 3))
        for a in range(2):
            k.stt(yv[:, 0:n], uf[a][:, 0:n], dsk[:, a:a + 1], yp[a][:, 0:n], ALU.mult, ALU.add)
            k.tt(x2[:, 0:n], yv[:, 0:n], yv[:, 0:n], ALU.mult, eng="pool")
            k.ts(x2[:, 0:n], x2[:, 0:n], 0.044715, ALU.mult, 1.0, ALU.add, eng="pool")
            k.tt(x3[:, 0:n], x2[:, 0:n], yv[:, 0:n], ALU.mult, eng="pool")
            k.act(x3[:, 0:n], x3[:, 0:n], AF.Sigmoid, scale=GELU_C)
            o = go[a]
            k.tt(o[:, 0:n], x3[:, 0:n], yv[:, 0:n], ALU.mult)
            k.dma("act", gT[a * 128:(a + 1) * 128, t0:t1_], o[:, 0:n])


PERM = np.concatenate([np.arange(32, 64), np.arange(0, 32)])


def ktile(w):
    K, M = w.shape
    return np.ascontiguousarray(w.reshape(K // 128, 128, M).transpose(1, 0, 2))


def vec_tile(g):
    return np.ascontiguousarray(g.reshape(-1, 128).T)


def rope_tables(T):
    pos = np.arange(T, dtype=np.float32)
    inv_freq = (np.float32(10000.0) ** (-np.arange(0, 64, 2, dtype=np.float32) / np.float32(64))).astype(np.float32)
    ang = (pos[:, None] * inv_freq[None, :]).astype(np.float32)
    cos = np.cos(ang).astype(np.float32).T
    sin = np.sin(ang).astype(np.float32).T
    cos2 = np.ascontiguousarray(np.concatenate([cos, cos], 0))
    sin2s = np.ascontiguousarray(np.concatenate([-sin, sin], 0))
    return cos2, sin2s


def mla_weights(c, w_in, q_norm, w_uq, kv_norm, w_ukv):
    kr = w_in[:, 1536:1600]
    wkv = np.concatenate([w_in[:, 1024:1536], kr, kr[:, PERM], w_in[:, 1600 + c * 512:1600 + (c + 1) * 512]], 1)
    uq = []
    kn = []
    vv = []
    for h in range(4):
        b = (4 * c + h) * 192
        rope = w_uq[:, b + 128:b + 192]
        uq += [w_uq[:, b:b + 128], rope, rope[:, PERM]]
        b2 = (4 * c + h) * 256
        kn.append(w_ukv[:, b2:b2 + 128])
        vv.append(w_ukv[:, b2 + 128:b2 + 256])
    return dict(
        wq_in=ktile(w_in[:, 0:1024]),
        wkv_in=ktile(wkv),
        wuq=ktile(np.concatenate(uq, 1)),
        wukv=ktile(np.concatenate(kn + vv, 1)),
        gq=vec_tile(q_norm),
        gkv=vec_tile(kv_norm),
    )


def out_w_tile(W):
    K = W.shape[0]
    nkt = K // 128
    return np.ascontiguousarray(W.reshape(nkt, 128, 32, 128).transpose(2, 1, 0, 3).reshape(32, 128, nkt * 128))


def hn_layout(hnT):
    T = hnT.shape[1]
    nb = (T - 16) // 256
    v = hnT.reshape(32, 128, T)
    meta = np.ascontiguousarray(v[:, :, 0:16].transpose(1, 0, 2))
    own = np.ascontiguousarray(v[:, :, 16:].reshape(32, 128, nb, 256).transpose(2, 1, 0, 3))
    return dict(hn_meta=meta, hn_own=own)


def const_mats():
    U = np.triu(np.ones((128, 128), np.float32))
    ident = np.eye(128, dtype=np.float32)
    return U, ident


def hyb_weights(c, w_in, conv_w, conv_b, dt_bias, a_log, d_skip, norm_g,
                a_re, a_im, log_dt, b_re, b_im, c_re, c_im, s5_d):
    z = w_in[:, c * 512:(c + 1) * 512]
    x = w_in[:, 4096 + c * 512:4096 + (c + 1) * 512]
    B = w_in[:, 8192 + c * 128:8192 + (c + 1) * 128]
    C = w_in[:, 9216 + c * 128:9216 + (c + 1) * 128]
    dt = w_in[:, 10240 + c * 8:10240 + (c + 1) * 8]
    w_ssd = ktile(np.concatenate([z, x, B, C, dt], 1))
    u = w_in[:, 10304 + c * 256:10304 + (c + 1) * 256]
    gate = w_in[:, 12352 + c * 256:12352 + (c + 1) * 256]
    w_s5 = ktile(np.concatenate([u, gate], 1))
    chans = [np.arange(c * 512 + m * 128, c * 512 + (m + 1) * 128) for m in range(4)]
    chans.append(np.arange(4096 + c * 128, 4096 + (c + 1) * 128))
    chans.append(np.arange(5120 + c * 128, 5120 + (c + 1) * 128))
    convw = np.ascontiguousarray(np.stack([conv_w[:, ch].T for ch in chans], 1))
    convb = np.ascontiguousarray(np.stack([conv_b[ch] for ch in chans], 1))
    hs = slice(c * 8, (c + 1) * 8)
    dtb_bc = np.ascontiguousarray(np.broadcast_to(dt_bias[hs][None, :], (128, 8)))
    alog_bc = np.ascontiguousarray(np.broadcast_to(a_log[hs][None, :], (128, 8)))
    d_bc = np.ascontiguousarray(np.broadcast_to(np.repeat(d_skip[hs], 64)[None, :], (128, 512)))
    ng_bc = np.ascontiguousarray(np.broadcast_to(norm_g[c * 512:(c + 1) * 512][None, :], (128, 512)))
    bre = np.zeros((128, 8, 128), np.float32); bim = np.zeros((128, 8, 128), np.float32)
    cre = np.zeros((128, 8, 128), np.float32); cim = np.zeros((128, 8, 128), np.float32)
    are_l = np.zeros((128, 8), np.float32); aim_l = np.zeros((128, 8), np.float32); ldt_l = np.zeros((128, 8), np.float32)
    for j in range(8):
        a, q = j // 4, (j % 4) * 32
        for m in range(2):
            g = 16 * c + 2 * j + m
            bre[q + m * 16:q + (m + 1) * 16, j, m * 64:(m + 1) * 64] = b_re[g].T
            bim[q + m * 16:q + (m + 1) * 16, j, m * 64:(m + 1) * 64] = b_im[g].T
            cre[m * 64:(m + 1) * 64, j, q + m * 16:q + (m + 1) * 16] = c_re[g].T
            cim[m * 64:(m + 1) * 64, j, q + m * 16:q + (m + 1) * 16] = c_im[g].T
            are_l[m * 64:(m + 1) * 64, j] = a_re[g]
            aim_l[m * 64:(m + 1) * 64, j] = a_im[g]
            ldt_l[m * 64:(m + 1) * 64, j] = log_dt[g]
    d_l = np.ascontiguousarray(s5_d[c * 256:(c + 1) * 256].reshape(2, 128).T)
    return dict(w_ssd=w_ssd, w_s5=w_s5, convw=convw, convb=convb, dtb_bc=dtb_bc, alog_bc=alog_bc, d_bc=d_bc, ng_bc=ng_bc,
                bre=bre, bim=bim, cre=cre, cim=cim, are_l=are_l, aim_l=aim_l, ldt_l=ldt_l, d_l=d_l)


def glu_w_tile(W):
    return np.ascontiguousarray(W.reshape(16, 128, 16, 128).transpose(2, 1, 0, 3).reshape(16, 128, 2048))


def blk_layout(xT, nkt):
    T = xT.shape[1]
    nb = (T - 16) // 256
    v = xT.reshape(nkt, 128, T)
    meta = np.ascontiguousarray(v[:, :, 0:16].transpose(1, 0, 2))
    own = np.ascontiguousarray(v[:, :, 16:].reshape(nkt, 128, nb, 256).transpose(2, 1, 0, 3))
    return meta, own


def mla_weights2(c, w_in, q_norm, w_uq, kv_norm, w_ukv):
    kr = w_in[:, 1536:1600]
    wkv3 = np.concatenate([w_in[:, 1024:1536], kr, kr[:, PERM]], 1)
    base = mla_weights(c, w_in, q_norm, w_uq, kv_norm, w_ukv)
    return dict(wq_in=base["wq_in"], wkv3=ktile(wkv3), gq=base["gq"], gkv=base["gkv"],
                wgate=ktile(w_in[:, 1600 + c * 512:1600 + (c + 1) * 512]), wuq=base["wuq"], wukv=base["wukv"])

from concourse.bass_utils import run_bass_kernel_spmd

BFNP = ml_dtypes.bfloat16
T_ALL = 16400
TC = 2064
NCORE = 8
_PROGS = {}

HYB_SHAPES = dict(w_ssd=[128, 32, 1288], w_s5=[128, 32, 512], convw=[128, 6, 4], convb=[128, 6], dtb_bc=[128, 8],
                  alog_bc=[128, 8], d_bc=[128, 512], ng_bc=[128, 512], bre=[128, 8, 128], bim=[128, 8, 128],
                  cre=[128, 8, 128], cim=[128, 8, 128], are_l=[128, 8], aim_l=[128, 8], ldt_l=[128, 8], d_l=[128, 2],
                  Umat=[128, 128], ident=[128, 128])


def _new():
    return bass.Bass("TRN2", target_bir_lowering=False)


def prog_norm():
    nc = _new()
    with contextlib.ExitStack() as st:
        k = KB(nc, st)
        hT = k.dram("hT", [D, TC], F32, kind="ExternalInput")
        g_l = k.dram("g_l", [128, 32], F32, kind="ExternalInput")
        hnT = k.dram("hnT", [D, TC], BF16, kind="ExternalOutput")
        stage_out(k, hT, None, None, g_l, None, hnT, TC, 0, True, BF16)
        k.final_wait("sp", [hnT])
        k.emit()
    return nc


def prog_out(hyb, final):
    nc = _new()
    nkt = 48 if hyb else 32
    with contextlib.ExitStack() as st:
        k = KB(nc, st)
        hT = k.dram("hT", [D, TC], F32, kind="ExternalInput")
        yT = k.dram("yT", [D, TC], BF16, kind="ExternalInput")
        wl = k.dram("wl", [32, 128, nkt * 128], F32, kind="ExternalInput")
        g_l = k.dram("g_l", [128, 32], F32, kind="ExternalInput")
        glu = None
        if hyb:
            g_all = k.dram("g_all", [2048, TC], BF16, kind="ExternalInput")
            sg_all = k.dram("sg_all", [2048, TC], BF16, kind="ExternalInput")
            wglu_l = k.dram("wglu_l", [16, 128, 2048], F32, kind="ExternalInput")
            glu = (g_all, sg_all, wglu_l)
        outs = []
        if not final:
            hT_new = k.dram("hT_new", [D, TC], F32, kind="ExternalOutput")
            outs.append(hT_new)
        else:
            hT_new = k.dram("hT_new", [D, TC], F32)
        hnT = k.dram("hnT", [D, TC], F32 if final else BF16, kind="ExternalOutput")
        outs.append(hnT)
        ybT = None
        if hyb:
            ybT = k.dram("ybT", [2048, TC], BF16)
            with k.scope():
                stage_glu(k, g_all, sg_all, wglu_l, ybT, TC)
        with k.scope():
            stage_out(k, hT, yT, wl, g_l, hT_new, hnT, TC, nkt, False, F32 if final else BF16, glu=ybT)
        if hyb:
            wq_in = k.dram("wq_in", [128, 32, 1024], F32, kind="ExternalInput")
            wkv3 = k.dram("wkv3", [128, 32, 640], F32, kind="ExternalInput")
            gq = k.dram("gq", [128, 8], F32, kind="ExternalInput")
            gkv = k.dram("gkv", [128, 4], F32, kind="ExternalInput")
            cos2c = k.dram("cos2c", [64, TC], F32, kind="ExternalInput")
            sin2sc = k.dram("sin2sc", [64, TC], F32, kind="ExternalInput")
            cqnT = k.dram("cqnT", [1024, TC], BF16, kind="ExternalOutput")
            ckvnT = k.dram("ckvnT", [512, TC], BF16, kind="ExternalOutput")
            krT = k.dram("krT", [64, TC], BF16, kind="ExternalOutput")
            outs += [cqnT, ckvnT, krT]
            with k.scope():
                mla_pre(k, TC, hnT, wq_in, wkv3, gq, gkv, cos2c, sin2sc, cqnT, ckvnT, krT)
        k.final_wait("sp", outs)
        k.emit()
    return nc


def prog_hyb():
    nc = _new()
    T = T_ALL
    with contextlib.ExitStack() as st:
        k = KB(nc, st)
        hn_meta = k.dram("hn_meta", [128, 32, 16], BF16, kind="ExternalInput")
        hn_own = k.dram("hn_own", [(T - 16) // 256, 128, 32, 256], BF16, kind="ExternalInput")
        d = {n: k.dram(n, s, F32, kind="ExternalInput") for n, s in HYB_SHAPES.items()}
        yT = k.dram("yT", [512, T], BF16, kind="ExternalOutput")
        gT = k.dram("gT", [256, T], BF16, kind="ExternalOutput")
        sgT = k.dram("sgT", [256, T], BF16, kind="ExternalOutput")
        with k.scope():
            hyb_ssd(k, T, hn_meta, hn_own, d["w_ssd"], d["convw"], d["convb"], d["dtb_bc"], d["alog_bc"], d["d_bc"],
                    d["ng_bc"], d["Umat"], d["ident"], yT)
        with k.scope():
            hyb_s5(k, T, hn_meta, hn_own, d["w_s5"], d["bre"], d["bim"], d["cre"], d["cim"], d["are_l"], d["aim_l"],
                   d["ldt_l"], d["d_l"], gT, sgT)
        k.final_wait("sp", [yT, gT, sgT])
        k.emit()
    return nc


def prog_mla():
    nc = _new()
    T = T_ALL
    NB = (T - 16) // 256
    with contextlib.ExitStack() as st:
        k = KB(nc, st)
        hn_meta = k.dram("hn_meta", [128, 32, 16], BF16, kind="ExternalInput")
        hn_own = k.dram("hn_own", [NB, 128, 32, 256], BF16, kind="ExternalInput")
        cq_meta = k.dram("cq_meta", [128, 8, 16], BF16, kind="ExternalInput")
        cq_own = k.dram("cq_own", [NB, 128, 8, 256], BF16, kind="ExternalInput")
        ck_meta = k.dram("ck_meta", [128, 4, 16], BF16, kind="ExternalInput")
        ck_own = k.dram("ck_own", [NB, 128, 4, 256], BF16, kind="ExternalInput")
        krT = k.dram("krT", [64, T], BF16, kind="ExternalInput")
        wgate = k.dram("wgate", [128, 32, 512], F32, kind="ExternalInput")
        wuq = k.dram("wuq", [128, 8, 1024], F32, kind="ExternalInput")
        wukv = k.dram("wukv", [128, 4, 1024], F32, kind="ExternalInput")
        cos2 = k.dram("cos2", [64, T], F32, kind="ExternalInput")
        sin2s = k.dram("sin2s", [64, T], F32, kind="ExternalInput")
        yT = k.dram("yT", [512, T], BF16, kind="ExternalOutput")
        stage_mla2(k, T, hn_meta, hn_own, cq_meta, cq_own, ck_meta, ck_own, krT, wgate, wuq, wukv, cos2, sin2s, yT)
        k.final_wait("sp", [yT])
        k.emit()
    return nc


def _get(name, fn, *a):
    if name not in _PROGS:
        _PROGS[name] = fn(*a)
    return _PROGS[name]


def _run(nc, in_maps):
    res = run_bass_kernel_spmd(nc, in_maps, core_ids=list(range(NCORE)))
    return res.results


def _tok_idx(c):
    return np.concatenate([np.arange(16), 16 + 2048 * c + np.arange(2048)])


def _gather_tokens(per_core):
    return np.concatenate([per_core[0][:, 0:16]] + [per_core[c][:, 16:] for c in range(NCORE)], axis=1)


def _split_tokens(full):
    return [np.ascontiguousarray(full[:, _tok_idx(c)]) for c in range(NCORE)]


def kernel(x, meta, hyb_norm, hyb_w_in, ssd_conv_w, ssd_conv_b, ssd_dt_bias, ssd_a_log, ssd_d, ssd_norm,
           s5_a_re, s5_a_im, s5_log_dt, s5_b_re, s5_b_im, s5_c_re, s5_c_im, s5_d, s5_w_glu, hyb_w_out,
           mla_norm, mla_w_in, mla_q_norm, mla_w_uq, mla_kv_norm, mla_w_ukv, mla_w_out, final_norm):
    f32 = lambda a: np.asarray(a, dtype=np.float32)
    x, meta = f32(x), f32(meta)
    h_full_T = np.ascontiguousarray(np.concatenate([meta, x[0]], axis=0).T)
    hT = _split_tokens(h_full_T)
    del h_full_T
    U, ident = const_mats()
    cos2, sin2s = rope_tables(T_ALL)

    g0 = vec_tile(f32(hyb_norm[0]))
    res = _run(_get("norm", prog_norm), [dict(hT=hT[c], g_l=g0) for c in range(NCORE)])
    hn = [np.asarray(r["hnT"]) for r in res]

    for layer in range(4):
        i = layer // 2
        hn_l = hn_layout(_gather_tokens(hn))
        last = (layer == 3)
        if layer % 2 == 0:
            ims = []
            for c in range(NCORE):
                im = hyb_weights(c, f32(hyb_w_in[i]), f32(ssd_conv_w[i]), f32(ssd_conv_b[i]), f32(ssd_dt_bias[i]),
                                 f32(ssd_a_log[i]), f32(ssd_d[i]), f32(ssd_norm[i]), f32(s5_a_re[i]), f32(s5_a_im[i]),
                                 f32(s5_log_dt[i]), f32(s5_b_re[i]), f32(s5_b_im[i]), f32(s5_c_re[i]), f32(s5_c_im[i]),
                                 f32(s5_d[i]))
                im.update(Umat=U, ident=ident, **hn_l)
                ims.append(im)
            res = _run(_get("hyb", prog_hyb), ims)
            del ims
            y_all = _split_tokens(np.concatenate([np.asarray(r["yT"]) for r in res], axis=0))
            g_all = _split_tokens(np.concatenate([np.asarray(r["gT"]) for r in res], axis=0))
            sg_all = _split_tokens(np.concatenate([np.asarray(r["sgT"]) for r in res], axis=0))
            wl = out_w_tile(f32(hyb_w_out[i]))
            wglu_l = glu_w_tile(f32(s5_w_glu[i]))
            gn = vec_tile(f32(mla_norm[i]))
            mw = [mla_weights2(c, f32(mla_w_in[i]), f32(mla_q_norm[i]), f32(mla_w_uq[i]), f32(mla_kv_norm[i]),
                               f32(mla_w_ukv[i])) for c in range(NCORE)]
            ims = [dict(hT=hT[c], yT=y_all[c], wl=wl, g_l=gn, g_all=g_all[c], sg_all=sg_all[c], wglu_l=wglu_l,
                        wq_in=mw[c]["wq_in"], wkv3=mw[c]["wkv3"], gq=mw[c]["gq"], gkv=mw[c]["gkv"],
                        cos2c=np.ascontiguousarray(cos2[:, _tok_idx(c)]), sin2sc=np.ascontiguousarray(sin2s[:, _tok_idx(c)]))
                   for c in range(NCORE)]
            res = _run(_get("out_hyb", prog_out, True, False), ims)
            cq_l = blk_layout(_gather_tokens([np.asarray(r["cqnT"]) for r in res]), 8)
            ck_l = blk_layout(_gather_tokens([np.asarray(r["ckvnT"]) for r in res]), 4)
            kr_full = np.ascontiguousarray(_gather_tokens([np.asarray(r["krT"]) for r in res]))
        else:
            ims = []
            for c in range(NCORE):
                im = dict(wgate=mw[c]["wgate"], wuq=mw[c]["wuq"], wukv=mw[c]["wukv"], cos2=cos2, sin2s=sin2s,
                          cq_meta=cq_l[0], cq_own=cq_l[1], ck_meta=ck_l[0], ck_own=ck_l[1], krT=kr_full, **hn_l)
                ims.append(im)
            res = _run(_get("mla", prog_mla), ims)
            del ims
            y_all = _split_tokens(np.concatenate([np.asarray(r["yT"]) for r in res], axis=0))
            wl = out_w_tile(f32(mla_w_out[i]))
            gn = vec_tile(f32(final_norm) if last else f32(hyb_norm[i + 1]))
            ims = [dict(hT=hT[c], yT=y_all[c], wl=wl, g_l=gn) for c in range(NCORE)]
            res = _run(_get("out_mla_final" if last else "out_mla", prog_out, False, last), ims)
        del ims
        if not last:
            hT = [np.asarray(r["hT_new"]) for r in res]
        hn = [np.asarray(r["hnT"]) for r in res]

    out = np.concatenate([hn[c][:, 16:].T for c in range(NCORE)], axis=0)
    return np.ascontiguousarray(out[None].astype(np.float32))
```

```python
import contextlib
import math
import os
import numpy as np
import ml_dtypes


import concourse.bass as bass
import concourse.mybir as mybir

F32 = mybir.dt.float32
BF16 = mybir.dt.bfloat16
I32 = mybir.dt.int32
AF = mybir.ActivationFunctionType
ALU = mybir.AluOpType
AX = mybir.AxisListType

COMPUTE = ("pe", "act", "dve", "pool")


class View:
    __slots__ = ("tl", "ap")

    def __init__(self, tl, ap):
        self.tl = tl
        self.ap = ap

    def __getitem__(self, idx):
        return View(self.tl, self.ap[idx])

    def rearrange(self, pat, **kw):
        return View(self.tl, self.ap.rearrange(pat, **kw))

    def broadcast_to(self, shape):
        return View(self.tl, self.ap.broadcast_to(list(shape)))

    def unsqueeze(self, ax):
        return View(self.tl, self.ap.unsqueeze(ax))

    def partition_broadcast(self, n):
        return View(self.tl, self.ap.partition_broadcast(n))

    def bitcast(self, dt):
        return View(self.tl, self.ap.bitcast(dt))

    @property
    def shape(self):
        return self.ap.shape


class Tl:
    __slots__ = ("t", "name", "lw", "rd", "dsem", "dcnt", "is_dram", "is_psum")

    def __init__(self, t, name, is_dram=False, is_psum=False):
        self.is_psum = is_psum
        self.t = t
        self.name = name
        self.lw = {}
        self.rd = {}
        self.dsem = None
        self.dcnt = 0
        self.is_dram = is_dram

    def __getitem__(self, idx):
        return View(self, self.t[idx])

    def rearrange(self, pat, **kw):
        return View(self, self.t.rearrange(pat, **kw))

    @property
    def v(self):
        return View(self, self.t[:])


def _is_view(x):
    return isinstance(x, View)


class KB:
    def __init__(self, nc, stack):
        self.nc = nc
        self.stack = stack
        self.root = stack
        self.lists = {e: [] for e in ("pe", "act", "dve", "pool", "sp")}
        self.psem = {}
        self.pcnt = {}
        for e in COMPUTE:
            self.psem[e] = stack.enter_context(nc.semaphore("prog_" + e))
            self.pcnt[e] = 0
        self.known = {e: {} for e in self.lists}
        self.cinst = {e: [] for e in COMPUTE}
        self.defer = None
        self.ntile = 0
        self.tiles = []
        self.sem_pool = []
        self.n_sem = 4
        self.n_inst = 0
        self.n_wait = 0

    def sb(self, shape, dt, name=None):
        self.ntile += 1
        name = name or f"t{self.ntile}"
        t = self.stack.enter_context(self.nc.sbuf_tensor(name, list(shape), dt))
        tl = Tl(t, name)
        self.tiles.append(tl)
        return tl

    def ps(self, shape, dt, name=None):
        self.ntile += 1
        name = name or f"p{self.ntile}"
        t = self.stack.enter_context(self.nc.psum_tensor(name, list(shape), dt))
        return Tl(t, name, is_psum=True)

    def dram(self, name, shape, dt, kind="Internal", **kw):
        t = self.nc.dram_tensor(name, list(shape), dt, kind=kind, **kw)
        tl = Tl(t.ap(), name, is_dram=True)
        self.tiles.append(tl)
        return tl

    def _need(self, eng, ev, waits):
        if ev is None:
            return
        if ev[0] == "c":
            _, src, idx = ev
            if src == "pe" and eng == "pe":
                return
            key = ("c", src)
        else:
            _, sem, idx = ev
            key = id(sem)
        kn = self.known[eng]
        if kn.get(key, 0) >= idx:
            return
        kn[key] = idx
        if ev[0] == "c":
            self.cinst[src][idx - 1][4] = True
        waits[key] = ev

    def _deps(self, eng, reads, writes):
        waits = {}
        for t in reads:
            for ev in t.lw.values():
                self._need(eng, ev, waits)
            if t.is_psum:
                for ev in t.rd.values():
                    if not (ev[0] == "c" and ev[1] == eng):
                        self._need(eng, ev, waits)
        for t in writes:
            for ev in t.lw.values():
                self._need(eng, ev, waits)
            for ev in t.rd.values():
                self._need(eng, ev, waits)
        return list(waits.values())

    @staticmethod
    def _evkey(ev):
        return ("c", ev[1]) if ev[0] == "c" else id(ev[1])

    def _record(self, ev, reads, writes):
        key = self._evkey(ev)
        for t in reads:
            t.rd[key] = ev
        for t in writes:
            if t.is_dram and ev[0] == "d":
                t.lw[key] = ev
            else:
                t.lw = {key: ev}
            t.rd = {}

    def op(self, eng, fn, reads=(), writes=(), inc=True):
        if self.defer is not None:
            self.defer.append(("op", (eng, fn, list(reads), list(writes), inc)))
            return
        reads = [r.tl if _is_view(r) else r for r in reads]
        writes = [w.tl if _is_view(w) else w for w in writes]
        waits = self._deps(eng, reads, writes)
        ent = [waits, fn, "c", eng, False]
        self.cinst[eng].append(ent)
        ev = ("c", eng, len(self.cinst[eng]))
        self.lists[eng].append(ent)
        self._record(ev, reads, writes)
        self.n_inst += 1
        self.n_wait += len(waits)

    def dma(self, q, out, in_, sem_tile=None, **kw):
        if self.defer is not None:
            self.defer.append(("dma", (q, out, in_, sem_tile, kw)))
            return
        reads = [in_.tl]
        writes = [out.tl]
        waits = self._deps(q, reads, writes)
        st = sem_tile or (in_.tl if out.tl.is_dram and not in_.tl.is_dram else out.tl)
        if st.dsem is None:
            st.dsem, st.dcnt = self.get_sem("d_" + st.name)
        st.dcnt += 16
        ev = ("d", st.dsem, st.dcnt)
        oap, iap = out.ap, in_.ap
        self.lists[q].append([waits, lambda e: e.dma_start(out=oap, in_=iap, **kw), "d", st.dsem, True])
        self._record(ev, reads, writes)
        self.n_inst += 1
        self.n_wait += len(waits)

    def collective(self, kind, out, in_, op=None):
        reads = [in_.tl]
        writes = [out.tl]
        waits = self._deps("pool", reads, writes)
        st = out.tl
        if st.dsem is None:
            st.dsem, st.dcnt = self.get_sem("c_" + st.name)
        st.dcnt += 16
        ev = ("d", st.dsem, st.dcnt)
        oap, iap = out.ap, in_.ap
        aop = op if op is not None else ALU.bypass
        groups = [list(range(8))]
        self.lists["pool"].append([waits, lambda e: e.collective_compute(kind, aop, replica_groups=groups, ins=[iap], outs=[oap]), "d", st.dsem, True])
        self._record(ev, reads, writes)
        self.n_inst += 1

    def record(self, fn, *a):
        assert self.defer is None
        self.defer = []
        try:
            fn(*a)
            return self.defer
        finally:
            self.defer = None

    def replay(self, *lists):
        lists = [l for l in lists if l]
        pos = [0] * len(lists)
        total = sum(len(l) for l in lists)
        for _ in range(total):
            bi = min((i for i in range(len(lists)) if pos[i] < len(lists[i])), key=lambda i: pos[i] / len(lists[i]))
            kind, args = lists[bi][pos[bi]]
            pos[bi] += 1
            if kind == "op":
                self.op(*args[:4], inc=args[4])
            else:
                q, out, in_, sem_tile, kw = args
                self.dma(q, out, in_, sem_tile, **kw)

    def get_sem(self, name):
        if self.sem_pool:
            return self.sem_pool.pop()
        self.n_sem += 1
        return self.root.enter_context(self.nc.semaphore(name)), 0

    @contextlib.contextmanager
    def scope(self):
        old_stack, old_tiles = self.stack, self.tiles
        with contextlib.ExitStack() as st:
            self.stack = st
            self.tiles = []
            yield
            self.tiles = old_tiles + self.tiles
            self.barrier()
            new = self.tiles[len(old_tiles):]
            self.release([t for t in new if not t.is_dram])
            self.tiles = old_tiles + [t for t in new if t.is_dram]
            self.stack = old_stack

    def release(self, tiles):
        for t in tiles:
            if t.dsem is not None:
                self.sem_pool.append((t.dsem, t.dcnt))
                t.dsem = None

    def barrier(self):
        evs = [("c", e, len(self.cinst[e])) for e in COMPUTE if self.cinst[e]]
        evs += [("d", s, c) for (s, c) in self.all_dsems() if c > 0]
        for eng in self.lists:
            waits = {}
            for ev in evs:
                if ev[0] == "c" and ev[1] == eng:
                    continue
                saved = None
                if ev[0] == "c" and ev[1] == "pe" and eng == "pe":
                    continue
                self._need(eng, ev, waits)
            if waits:
                self.lists[eng].append([list(waits.values()), None, None, None, False])

    def all_dsems(self):
        out = [(t.dsem, t.dcnt) for t in self.tiles if t.dsem is not None]
        out += list(self.sem_pool)
        return out

    def final_wait(self, eng, tiles):
        waits = {}
        for t in tiles:
            for ev in t.lw.values():
                self._need(eng, ev, waits)
        self.lists[eng].append([list(waits.values()), None, None, None, False])

    def matmul(self, out, lhsT, rhs, start=True, stop=True):
        o, l, r = out.ap, lhsT.ap, rhs.ap
        self.op("pe", lambda e: e.matmul(o, lhsT=l, rhs=r, start=start, stop=stop), [lhsT, rhs], [out], inc=bool(stop))

    def transpose(self, out, in_, ident):
        o, i, d = out.ap, in_.ap, ident.ap
        self.op("pe", lambda e: e.transpose(o, i, d), [in_, ident], [out])

    def act(self, out, in_, func, bias=None, scale=None, accum_out=None):
        o, i = out.ap, in_.ap
        kw = {}
        rd = [in_]
        wr = [out]
        if bias is not None:
            if _is_view(bias):
                rd.append(bias)
                kw["bias"] = bias.ap
            else:
                kw["bias"] = bias
        if scale is not None:
            if _is_view(scale):
                rd.append(scale)
                kw["scale"] = scale.ap
            else:
                kw["scale"] = scale
        if accum_out is not None:
            wr.append(accum_out)
            kw["accum_out"] = accum_out.ap
        self.op("act", lambda e: e.activation(out=o, in_=i, func=func, **kw), rd, wr)

    def tt(self, out, in0, in1, op, eng="dve"):
        o, a, b = out.ap, in0.ap, in1.ap
        self.op(eng, lambda e: e.tensor_tensor(out=o, in0=a, in1=b, op=op), [in0, in1], [out])

    def ts(self, out, in0, s1, op0, s2=None, op1=None, eng="dve", accum_out=None):
        o, a = out.ap, in0.ap
        rd = [in0]
        wr = [out]
        a1 = s1
        a2 = s2
        if _is_view(s1):
            rd.append(s1)
            a1 = s1.ap
        if _is_view(s2):
            rd.append(s2)
            a2 = s2.ap
        kw = {}
        if op1 is not None:
            kw["op1"] = op1
        if accum_out is not None:
            wr.append(accum_out)
            kw["accum_out"] = accum_out.ap
        self.op(eng, lambda e: e.tensor_scalar(out=o, in0=a, scalar1=a1, scalar2=a2, op0=op0, **kw), rd, wr)

    def stt(self, out, in0, scalar, in1, op0, op1):
        o, a, b = out.ap, in0.ap, in1.ap
        rd = [in0, in1]
        s = scalar
        if _is_view(scalar):
            rd.append(scalar)
            s = scalar.ap
        self.op("dve", lambda e: e.scalar_tensor_tensor(out=o, in0=a, scalar=s, in1=b, op0=op0, op1=op1), rd, [out])

    def copy(self, out, in_, eng="dve"):
        o, i = out.ap, in_.ap
        if eng == "act":
            self.op("act", lambda e: e.copy(out=o, in_=i), [in_], [out])
        else:
            self.op(eng, lambda e: e.tensor_copy(out=o, in_=i), [in_], [out])

    def memset(self, out, val, eng="pool"):
        o = out.ap
        self.op(eng, lambda e: e.memset(o, val), [], [out])

    def recip(self, out, in_):
        o, i = out.ap, in_.ap
        self.op("dve", lambda e: e.reciprocal(out=o, in_=i), [in_], [out])

    def scan(self, out, d0, d1, initial, op0=ALU.mult, op1=ALU.add):
        o, a, b = out.ap, d0.ap, d1.ap
        rd = [d0, d1]
        ini = initial
        if _is_view(initial):
            rd.append(initial)
            ini = initial.ap
        self.op("dve", lambda e: e.tensor_tensor_scan(out=o, data0=a, data1=b, initial=ini, op0=op0, op1=op1), rd, [out])

    def reduce(self, out, in_, op, axis=AX.X):
        o, i = out.ap, in_.ap
        self.op("dve", lambda e: e.tensor_reduce(out=o, in_=i, axis=axis, op=op), [in_], [out])

    def emit(self):
        nc = self.nc
        lists = self.lists
        cum = {}
        for eng in COMPUTE:
            c = 0
            arr = []
            for ent in self.cinst[eng]:
                if ent[4]:
                    c += 1
                arr.append(c)
            cum[eng] = arr
        self.n_marked = {e: (cum[e][-1] if cum[e] else 0) for e in COMPUTE}
        psem = self.psem

        def run(e, items):
            for ent in items:
                waits, fn, kind, who, mark = ent
                for ev in waits:
                    if ev[0] == "c":
                        e.wait_ge(psem[ev[1]], cum[ev[1]][ev[2] - 1])
                    else:
                        e.wait_ge(ev[1], ev[2])
                if fn is None:
                    continue
                if kind == "d":
                    fn(e).then_inc(who, 16)
                elif mark:
                    fn(e).then_inc(psem[who], 1)
                else:
                    fn(e)

        with nc.Block() as block:
            @block.tensor
            def _(e):
                run(e, lists["pe"])

            @block.scalar
            def _(e):
                run(e, lists["act"])

            @block.vector
            def _(e):
                run(e, lists["dve"])

            @block.gpsimd
            def _(e):
                run(e, lists["pool"])

            @block.sync
            def _(e):
                run(e, lists["sp"])


EPS = 1e-6
D = 4096
NDT = 32


def col_groups(Tc, gmax=1024):
    groups = []
    s = 0
    while s < Tc:
        e = min(s + gmax, Tc)
        if 0 < Tc - e < 64:
            e = Tc
        groups.append((s, e))
        s = e
    return groups


def stage_glu(k, g_all, sg_all, wglu_l, ybT, Tc, pfx="gl"):
    GW = 1040
    gb = k.sb([128, 16, GW], BF16, pfx + "gb")
    sgb = k.sb([128, 16, GW], BF16, pfx + "sgb")
    gst = [k.sb([128, 2048], F32, pfx + f"gst{i}") for i in range(3)]
    gwb = [k.sb([128, 2048], BF16, pfx + f"gwb{i}") for i in range(2)]
    sig = [k.sb([128, 512], F32, pfx + f"sig{i}") for i in range(3)]
    yo = [k.sb([128, 512], BF16, pfx + f"yo{i}") for i in range(3)]
    acc = [k.ps([128, 512], F32, pfx + f"acc{i}") for i in range(4)]
    gv = g_all.rearrange("(kt p) t -> p kt t", p=128)
    sgv = sg_all.rearrange("(kt p) t -> p kt t", p=128)
    gcnt = 0
    u = 0
    for (c0, c1) in col_groups(Tc, 1024):
        gw = c1 - c0
        chunks = [(s, min(s + 512, c1)) for s in range(c0, c1, 512)]
        for kt0 in range(0, 16, 8):
            k.dma("sp", gb[:, kt0:kt0 + 8, 0:gw], gv[:, kt0:kt0 + 8, c0:c1])
            k.dma("sp", sgb[:, kt0:kt0 + 8, 0:gw], sgv[:, kt0:kt0 + 8, c0:c1])
        for mt in range(16):
            gs, gw_ = gst[gcnt % 3], gwb[gcnt % 2]
            k.dma("act", gs.v, wglu_l[mt])
            k.copy(gw_.v, gs.v, eng="pool")
            gcnt += 1
            for ci, (s0, s1) in enumerate(chunks):
                n = s1 - s0
                ac = acc[u % 4]
                for kt in range(16):
                    k.matmul(ac[:, 0:n], gw_[:, kt * 128:(kt + 1) * 128], gb[:, kt, s0 - c0:s1 - c0], start=(kt == 0), stop=(kt == 15))
                sg_ = sig[u % 3]
                y_ = yo[u % 3]
                k.act(sg_[:, 0:n], ac[:, 0:n], AF.Sigmoid)
                k.tt(sg_[:, 0:n], sg_[:, 0:n], gb[:, mt, s0 - c0:s1 - c0], ALU.mult)
                k.tt(y_[:, 0:n], sg_[:, 0:n], sgb[:, mt, s0 - c0:s1 - c0], ALU.mult, eng="pool")
                k.dma("sp", ybT[mt * 128:(mt + 1) * 128, s0:s1], y_[:, 0:n])
                u += 1


def stage_out(k, hT, yT, wl, g_l, hT_new, hnT, Tc, nkt, first, out_dt, pfx="o", glu=None):
    ones = k.sb([128, 128], F32, pfx + "ones")
    k.memset(ones.v, 1.0)
    gt = k.sb([128, NDT], F32, pfx + "g")
    k.dma("sp", gt.v, g_l.v)
    GW = 1040
    GMAX = 1024
    if not first:
        yb = k.sb([128, nkt, GW], BF16, pfx + "yb")
        KH = nkt // 2
        NST = 3 if nkt > 32 else 4
        wst = [k.sb([128, KH * 128], F32, pfx + f"wst{i}") for i in range(NST)]
        wbf = [k.sb([128, nkt * 128], BF16, pfx + f"wbf{i}") for i in range(2)]
        acc = [k.ps([128, 512], F32, pfx + f"acc{i}") for i in range(4)]
        yTv = yT.rearrange("(kt p) t -> p kt t", p=128)
    ssq = [k.ps([128, 512], F32, pfx + f"ssq{i}") for i in range(3)]
    hin = [k.sb([128, 512], F32, pfx + f"hin{i}") for i in range(3)]
    hnw = [k.sb([128, 512], F32, pfx + f"hnw{i}") for i in range(3)]
    sq = [k.sb([128, 512], F32, pfx + f"sq{i}") for i in range(3)]
    rstd = k.sb([128, GW], F32, pfx + "rstd")
    hno = [k.sb([128, 512], out_dt, pfx + f"hno{i}") for i in range(3)]
    hsrc = hT if first else hT_new
    u = 0
    wcnt = 0
    if glu is not None:
        ybv = glu.rearrange("(kt p) t -> p kt t", p=128)
    for (c0, c1) in col_groups(Tc, GMAX):
        gw = c1 - c0
        chunks = [(s, min(s + 512, c1)) for s in range(c0, c1, 512)]
        assert len(chunks) <= 3 and gw <= GW
        if not first:
            nkt_y = nkt - 16 if glu is not None else nkt
            for kt0 in range(0, nkt_y, 8):
                k.dma("sp", yb[:, kt0:kt0 + 8, 0:gw], yTv[:, kt0:kt0 + 8, c0:c1])
        if glu is not None:
            for kt0 in range(0, 16, 8):
                k.dma("sp", yb[:, 32 + kt0:32 + kt0 + 8, 0:gw], ybv[:, kt0:kt0 + 8, c0:c1])
        pend = None
        for d in range(NDT):
            if not first:
                wb = wbf[wcnt % 2]
                for hh in range(2):
                    ws = wst[(2 * wcnt + hh) % NST]
                    k.dma("act" if hh == 0 else "sp", ws.v, wl[d, :, hh * KH * 128:(hh + 1) * KH * 128])
                    k.copy(wb[:, hh * KH * 128:(hh + 1) * KH * 128], ws.v, eng="pool")
                wcnt += 1
            for ci, (s0, s1) in enumerate(chunks):
                n = s1 - s0
                hi = hin[u % 3]
                hw = hnw[u % 3]
                sqt = sq[u % 3]
                k.dma("sp", hi[:, 0:n], hT[d * 128:(d + 1) * 128, s0:s1])
                if not first:
                    ac = acc[u % 4]
                    for kt in range(nkt):
                        k.matmul(ac[:, 0:n], wb[:, kt * 128:(kt + 1) * 128], yb[:, kt, s0 - c0:s1 - c0],
                                 start=(kt == 0), stop=(kt == nkt - 1))
                    k.tt(hw[:, 0:n], ac[:, 0:n], hi[:, 0:n], ALU.add)
                    k.dma("sp", hT_new[d * 128:(d + 1) * 128, s0:s1], hw[:, 0:n])
                    src = hw
                else:
                    src = hi
                k.act(sqt[:, 0:n], src[:, 0:n], AF.Square)
                if pend is not None:
                    k.matmul(*pend[0], **pend[1])
                pend = ((ssq[ci][:, 0:n], ones.v, sqt[:, 0:n]), dict(start=(d == 0), stop=(d == NDT - 1)))
                u += 1
        if pend is not None:
            k.matmul(*pend[0], **pend[1])
            pend = None
        for ci, (s0, s1) in enumerate(chunks):
            n = s1 - s0
            k.ts(rstd[:, s0 - c0:s1 - c0], ssq[ci][:, 0:n], 1.0 / D, ALU.mult, EPS, ALU.add)
            k.act(rstd[:, s0 - c0:s1 - c0], rstd[:, s0 - c0:s1 - c0], AF.Sqrt)
            k.recip(rstd[:, s0 - c0:s1 - c0], rstd[:, s0 - c0:s1 - c0])
        for d in range(NDT):
            for ci, (s0, s1) in enumerate(chunks):
                n = s1 - s0
                hi = hin[u % 3]
                ho = hno[u % 3]
                k.dma("sp", hi[:, 0:n], hsrc[d * 128:(d + 1) * 128, s0:s1])
                k.stt(ho[:, 0:n], hi[:, 0:n], gt[:, d:d + 1], rstd[:, s0 - c0:s1 - c0], ALU.mult, ALU.mult)
                k.dma("act", hnT[d * 128:(d + 1) * 128, s0:s1], ho[:, 0:n])
                u += 1

DBG_NB = int(os.environ.get('DBG_NB', '0'))
DBG_SKIP = os.environ.get('DBG_SKIP', '')
DBG_START = int(os.environ.get('DBG_START', '0'))

EPS = 1e-6
NH = 4
QSCALE = 192 ** -0.5


def tok_blocks(T, bs=512):
    assert (T - 16) % bs == 0
    return [(0, 16)] + [(s, s + bs) for s in range(16, T, bs)]


def load_cast(k, dst, src_dram, nkt, ncols, stg, q="act", ceng="pool"):
    if 'lc' in DBG_SKIP:
        k.memset(dst.v, 0.01)
        return
    for kt in range(nkt):
        s = stg[kt % len(stg)]
        k.dma(q, s[:, 0:ncols], src_dram[:, kt, :])
        k.copy(dst[:, kt, :], s[:, 0:ncols], eng=ceng)


def rstd_from_ssq(k, rstd, ssq, n, dim):
    k.ts(rstd[:, 0:n], ssq[:, 0:n], 1.0 / dim, ALU.mult, EPS, ALU.add)
    k.act(rstd[:, 0:n], rstd[:, 0:n], AF.Sqrt)
    k.recip(rstd[:, 0:n], rstd[:, 0:n])


BS_A = 256


def _a_common(k, pfx, hb=True):
    ones = k.sb([128, 128], BF16, pfx + "ones")
    k.memset(ones.v, 1.0)
    stg = [k.sb([128, 1152], F32, pfx + f"stg{i}") for i in range(2)]
    hb = [k.sb([128, 32, BS_A], BF16, pfx + f"hb{i}") for i in range(2)] if hb else None
    cst = [k.sb([128, BS_A], F32, pfx + f"cos{i}") for i in range(2)]
    snt = [k.sb([128, BS_A], F32, pfx + f"sin{i}") for i in range(2)]
    rstd = k.sb([128, BS_A], F32, pfx + "rstd")
    acc = [k.ps([128, 512], F32, pfx + f"acc{i}") for i in range(3)]
    ssq = k.ps([128, 512], F32, pfx + "ssq")
    up = [k.ps([128, 512], F32, pfx + f"up{i}") for i in range(3)]
    sqb = [k.sb([128, BS_A], BF16, pfx + f"sqb{i}") for i in range(2)]
    ra = [k.sb([128, BS_A], F32, pfx + f"ra{i}") for i in range(2)]
    rb = [k.sb([128, BS_A], F32, pfx + f"rb{i}") for i in range(2)]
    ob = [k.sb([128, 512], BF16, pfx + f"ob{i}") for i in range(4)]
    return ones, stg, hb, cst, snt, rstd, acc, ssq, up, sqb, ra, rb, ob


def mla_a1(k, T, hn_meta, hn_own, wq_in, wuq, gq, cos2, sin2s, qnT, qrT, pfx="m1"):
    blocks = tok_blocks(T, BS_A)
    hbm = k.sb([128, 32, 16], BF16, pfx + "hbm")
    ones, stg, hb, cst, snt, rstd, acc, ssq, up, sqb, ra, rb, ob = _a_common(k, pfx)
    oc = [0]

    def nob():
        oc[0] += 1
        return ob[oc[0] % 4]

    w1 = k.sb([128, 32, 1024], BF16, pfx + "w1")
    load_cast(k, w1, wq_in, 32, 1024, stg)
    wu = k.sb([128, 8, NH * 256], BF16, pfx + "wu")
    load_cast(k, wu, wuq, 8, NH * 256, stg)
    gqt = k.sb([128, 8], F32, pfx + "gq")
    if 'gq' not in DBG_SKIP:
        k.dma("sp", gqt.v, gq.v)
    cq = k.sb([128, 8, BS_A], F32, pfx + "cq")
    cqn = k.sb([128, 8, BS_A], BF16, pfx + "cqn")
    for bi, (t0, t1) in enumerate(blocks):
        if DBG_NB and bi >= DBG_NB:
            break
        if bi < DBG_START:
            continue
        n = t1 - t0
        if bi == 0:
            h = hbm
            k.dma("sp", h.v, hn_meta.v)
        else:
            h = hb[bi % 2]
            k.dma(os.environ.get("HQ", "sp"), h.v, hn_own[bi - 1])
        ct, sn = cst[bi % 2], snt[bi % 2]
        if 'cs' not in DBG_SKIP:
            _q = os.environ.get("CSQ", "sp")
            _o = 0 if os.environ.get("CS0") else t0
            k.dma(_q, ct[0:64, 0:n], cos2[:, _o:_o + n])
            k.dma(_q, sn[0:64, 0:n], sin2s[:, _o:_o + n])
        for m in range(8):
            a = acc[m % 3]
            for kt in range(32):
                k.matmul(a[:, 0:n], w1[:, kt, m * 128:(m + 1) * 128], h[:, kt, 0:n], start=(kt == 0), stop=(kt == 31))
            sq = sqb[m % 2]
            if 'sq' not in DBG_SKIP:
                k.act(sq[:, 0:n], a[:, 0:n], AF.Square)
            if 'cp' not in DBG_SKIP:
                k.copy(cq[:, m, 0:n], a[:, 0:n], eng=os.environ.get("CPENG","dve"))
            if 'ssq' not in DBG_SKIP:
                k.matmul(ssq[:, 0:n], ones.v, sq[:, 0:n], start=(m == 0), stop=(m == 7))
        if 'rstd' not in DBG_SKIP:
            rstd_from_ssq(k, rstd, ssq, n, 1024)
        for m in range(8):
            if 'stt' in DBG_SKIP:
                break
            k.stt(cqn[:, m, 0:n], cq[:, m, 0:n], gqt[:, m:m + 1], rstd[:, 0:n], ALU.mult, ALU.mult)
        for hd in range(NH):
            if 'up' in DBG_SKIP:
                break
            c0 = hd * 256
            u0 = up[0]
            for kt in range(8):
                k.matmul(u0[:, 0:n], wu[:, kt, c0:c0 + 128], cqn[:, kt, 0:n], start=(kt == 0), stop=(kt == 7))
            o = nob()
            k.act(o[:, 0:n], u0[:, 0:n], AF.Copy, scale=QSCALE)
            k.dma("act", qnT[hd, :, t0:t1], o[:, 0:n])
            u1, u2 = up[1], up[2]
            for kt in range(8):
                k.matmul(u1[0:64, 0:n], wu[:, kt, c0 + 128:c0 + 192], cqn[:, kt, 0:n], start=(kt == 0), stop=(kt == 7))
            for kt in range(8):
                k.matmul(u2[0:64, 0:n], wu[:, kt, c0 + 192:c0 + 256], cqn[:, kt, 0:n], start=(kt == 0), stop=(kt == 7))
            a_, b_ = ra[hd % 2], rb[hd % 2]
            k.tt(a_[0:64, 0:n], u1[0:64, 0:n], ct[0:64, 0:n], ALU.mult)
            k.tt(b_[0:64, 0:n], u2[0:64, 0:n], sn[0:64, 0:n], ALU.mult)
            o = nob()
            k.tt(a_[0:64, 0:n], a_[0:64, 0:n], b_[0:64, 0:n], ALU.add, eng="pool")
            k.act(o[0:64, 0:n], a_[0:64, 0:n], AF.Copy, scale=QSCALE)
            k.dma("act", qrT[hd, :, t0:t1], o[0:64, 0:n])


def mla_a2(k, T, hn_meta, hn_own, wkv_in, wukv, gkv, cos2, sin2s, knT, krT, vtok, gT, pfx="m2"):
    blocks = tok_blocks(T, BS_A)
    hbm = k.sb([128, 32, 16], BF16, pfx + "hbm")
    ones, stg, hb, cst, snt, rstd, acc, ssq, up, sqb, ra, rb, ob = _a_common(k, pfx)
    oc = [0]

    def nob():
        oc[0] += 1
        return ob[oc[0] % 4]

    w2 = k.sb([128, 32, 1152], BF16, pfx + "w2")
    load_cast(k, w2, wkv_in, 32, 1152, stg)
    wk = k.sb([128, 4, 1024], BF16, pfx + "wk")
    load_cast(k, wk, wukv, 4, 1024, stg)
    gkt = k.sb([128, 4], F32, pfx + "gk")
    k.dma("sp", gkt.v, gkv.v)
    ckv = k.sb([128, 4, BS_A], F32, pfx + "ckv")
    ckn = k.sb([128, 4, BS_A], BF16, pfx + "ckn")
    for bi, (t0, t1) in enumerate(blocks):
        n = t1 - t0
        if bi == 0:
            h = hbm
            k.dma("sp", h.v, hn_meta.v)
        else:
            h = hb[bi % 2]
            k.dma(os.environ.get("HQ", "sp"), h.v, hn_own[bi - 1])
        ct, sn = cst[bi % 2], snt[bi % 2]
        k.dma("sp", ct[0:64, 0:n], cos2[:, t0:t1])
        k.dma("sp", sn[0:64, 0:n], sin2s[:, t0:t1])
        for m in range(4):
            a = acc[m % 3]
            for kt in range(32):
                k.matmul(a[:, 0:n], w2[:, kt, m * 128:(m + 1) * 128], h[:, kt, 0:n], start=(kt == 0), stop=(kt == 31))
            sq = sqb[m % 2]
            k.act(sq[:, 0:n], a[:, 0:n], AF.Square)
            k.copy(ckv[:, m, 0:n], a[:, 0:n], eng="dve")
            k.matmul(ssq[:, 0:n], ones.v, sq[:, 0:n], start=(m == 0), stop=(m == 3))
        rstd_from_ssq(k, rstd, ssq, n, 512)
        for m in range(4):
            k.stt(ckn[:, m, 0:n], ckv[:, m, 0:n], gkt[:, m:m + 1], rstd[:, 0:n], ALU.mult, ALU.mult)
        u1, u2 = up[1], up[2]
        for kt in range(32):
            k.matmul(u1[0:64, 0:n], w2[:, kt, 512:576], h[:, kt, 0:n], start=(kt == 0), stop=(kt == 31))
        for kt in range(32):
            k.matmul(u2[0:64, 0:n], w2[:, kt, 576:640], h[:, kt, 0:n], start=(kt == 0), stop=(kt == 31))
        a_, b_ = ra[0], rb[0]
        k.tt(a_[0:64, 0:n], u1[0:64, 0:n], ct[0:64, 0:n], ALU.mult)
        k.tt(b_[0:64, 0:n], u2[0:64, 0:n], sn[0:64, 0:n], ALU.mult)
        o = nob()
        k.tt(o[0:64, 0:n], a_[0:64, 0:n], b_[0:64, 0:n], ALU.add)
        k.dma("act", krT[:, t0:t1], o[0:64, 0:n])
        for m in range(4):
            a = acc[m % 3]
            for kt in range(32):
                k.matmul(a[:, 0:n], w2[:, kt, 640 + m * 128:640 + (m + 1) * 128], h[:, kt, 0:n], start=(kt == 0), stop=(kt == 31))
            o = nob()
            k.act(o[:, 0:n], a[:, 0:n], AF.Silu)
            k.dma("act", gT[m * 128:(m + 1) * 128, t0:t1], o[:, 0:n])
        for hd in range(NH):
            u0 = up[0]
            for kt in range(4):
                k.matmul(u0[:, 0:n], wk[:, kt, hd * 128:(hd + 1) * 128], ckn[:, kt, 0:n], start=(kt == 0), stop=(kt == 3))
            o = nob()
            k.copy(o[:, 0:n], u0[:, 0:n], eng="act")
            k.dma("act", knT[hd, :, t0:t1], o[:, 0:n])
        for s0 in range(0, n, 128):
            ns = min(128, n - s0)
            a = acc[(s0 // 128) % 3]
            for kt in range(4):
                k.matmul(a[0:ns, :], ckn[:, kt, s0:s0 + ns], wk[:, kt, 512:1024], start=(kt == 0), stop=(kt == 3))
            o = nob()
            k.copy(o[0:ns, :], a[0:ns, :], eng="dve")
            kb = 0 if bi == 0 else 1 + (t0 + s0 - 16) // 128
            for hd in range(NH):
                k.dma("act", vtok[hd, 0:ns, kb, :], o[0:ns, hd * 128:(hd + 1) * 128])


def mla_pre(k, Tc, hnT, wq_in, wkv3, gq, gkv, cos2c, sin2sc, cqnT, ckvnT, krT, pfx="mp"):
    blocks = tok_blocks(Tc, BS_A)
    hv = hnT.rearrange("(kt p) t -> p kt t", p=128)
    ones, stg, hb, cst, snt, rstd, acc, ssq, up, sqb, ra, rb, ob = _a_common(k, pfx)
    oc = [0]

    def nob():
        oc[0] += 1
        return ob[oc[0] % 4]

    w1 = k.sb([128, 32, 1024], BF16, pfx + "w1")
    load_cast(k, w1, wq_in, 32, 1024, stg)
    w2 = k.sb([128, 32, 640], BF16, pfx + "w2")
    load_cast(k, w2, wkv3, 32, 640, stg)
    gqt = k.sb([128, 8], F32, pfx + "gq")
    gkt = k.sb([128, 4], F32, pfx + "gk")
    k.dma("sp", gqt.v, gq.v)
    k.dma("sp", gkt.v, gkv.v)
    cq = k.sb([128, 8, BS_A], F32, pfx + "cq")
    for bi, (t0, t1) in enumerate(blocks):
        n = t1 - t0
        h = hb[bi % 2]
        for kt0 in range(0, 32, 8):
            k.dma("sp", h[:, kt0:kt0 + 8, 0:n], hv[:, kt0:kt0 + 8, t0:t1])
        ct, sn = cst[bi % 2], snt[bi % 2]
        k.dma("sp", ct[0:64, 0:n], cos2c[:, t0:t1])
        k.dma("sp", sn[0:64, 0:n], sin2sc[:, t0:t1])
        for (wt, nm, gtile, dim, dst) in ((w1, 8, gqt, 1024, cqnT), (w2, 4, gkt, 512, ckvnT)):
            for m in range(nm):
                a = acc[m % 3]
                for kt in range(32):
                    k.matmul(a[:, 0:n], wt[:, kt, m * 128:(m + 1) * 128], h[:, kt, 0:n], start=(kt == 0), stop=(kt == 31))
                sq = sqb[m % 2]
                k.act(sq[:, 0:n], a[:, 0:n], AF.Square)
                k.copy(cq[:, m, 0:n], a[:, 0:n], eng="dve")
                k.matmul(ssq[:, 0:n], ones.v, sq[:, 0:n], start=(m == 0), stop=(m == nm - 1))
            rstd_from_ssq(k, rstd, ssq, n, dim)
            for m in range(nm):
                o = nob()
                k.stt(o[:, 0:n], cq[:, m, 0:n], gtile[:, m:m + 1], rstd[:, 0:n], ALU.mult, ALU.mult)
                k.dma("act", dst[m * 128:(m + 1) * 128, t0:t1], o[:, 0:n])
        u1, u2 = up[1], up[2]
        for kt in range(32):
            k.matmul(u1[0:64, 0:n], w2[:, kt, 512:576], h[:, kt, 0:n], start=(kt == 0), stop=(kt == 31))
        for kt in range(32):
            k.matmul(u2[0:64, 0:n], w2[:, kt, 576:640], h[:, kt, 0:n], start=(kt == 0), stop=(kt == 31))
        a_, b_ = ra[0], rb[0]
        k.tt(a_[0:64, 0:n], u1[0:64, 0:n], ct[0:64, 0:n], ALU.mult)
        k.tt(b_[0:64, 0:n], u2[0:64, 0:n], sn[0:64, 0:n], ALU.mult)
        o = nob()
        k.tt(o[0:64, 0:n], a_[0:64, 0:n], b_[0:64, 0:n], ALU.add)
        k.dma("act", krT[:, t0:t1], o[0:64, 0:n])


def mla_a1p(k, T, cq_meta, cq_own, wuq, cos2, sin2s, qnT, qrT, pfx="m1"):
    blocks = tok_blocks(T, BS_A)
    ones, stg, hb_unused, cst, snt, rstd, acc, ssq, up, sqb, ra, rb, ob = _a_common(k, pfx, hb=False)
    oc = [0]

    def nob():
        oc[0] += 1
        return ob[oc[0] % 4]

    wu = k.sb([128, 8, NH * 256], BF16, pfx + "wu")
    load_cast(k, wu, wuq, 8, NH * 256, stg)
    cqm = k.sb([128, 8, 16], BF16, pfx + "cqm")
    cqb = [k.sb([128, 8, BS_A], BF16, pfx + f"cqb{i}") for i in range(3)]
    for bi, (t0, t1) in enumerate(blocks):
        n = t1 - t0
        if bi == 0:
            cqn = cqm
            k.dma("sp", cqn.v, cq_meta.v)
        else:
            cqn = cqb[bi % 3]
            k.dma("sp", cqn.v, cq_own[bi - 1])
        ct, sn = cst[bi % 2], snt[bi % 2]
        k.dma("sp", ct[0:64, 0:n], cos2[:, t0:t1])
        k.dma("sp", sn[0:64, 0:n], sin2s[:, t0:t1])
        for hd in range(NH):
            c0 = hd * 256
            u0 = up[0] if hd % 2 == 0 else acc[0]
            for kt in range(8):
                k.matmul(u0[:, 0:n], wu[:, kt, c0:c0 + 128], cqn[:, kt, 0:n], start=(kt == 0), stop=(kt == 7))
            o = nob()
            k.act(o[:, 0:n], u0[:, 0:n], AF.Copy, scale=QSCALE)
            k.dma("act", qnT[hd, :, t0:t1], o[:, 0:n])
            u1, u2 = (up[1], up[2]) if hd % 2 == 0 else (acc[1], acc[2])
            for kt in range(8):
                k.matmul(u1[0:64, 0:n], wu[:, kt, c0 + 128:c0 + 192], cqn[:, kt, 0:n], start=(kt == 0), stop=(kt == 7))
            for kt in range(8):
                k.matmul(u2[0:64, 0:n], wu[:, kt, c0 + 192:c0 + 256], cqn[:, kt, 0:n], start=(kt == 0), stop=(kt == 7))
            a_, b_ = ra[hd % 2], rb[hd % 2]
            k.tt(a_[0:64, 0:n], u1[0:64, 0:n], ct[0:64, 0:n], ALU.mult)
            k.tt(b_[0:64, 0:n], u2[0:64, 0:n], sn[0:64, 0:n], ALU.mult)
            o = nob()
            k.tt(a_[0:64, 0:n], a_[0:64, 0:n], b_[0:64, 0:n], ALU.add, eng="pool")
            k.act(o[0:64, 0:n], a_[0:64, 0:n], AF.Copy, scale=QSCALE)
            k.dma("act", qrT[hd, :, t0:t1], o[0:64, 0:n])


def mla_a2p(k, T, hn_meta, hn_own, ck_meta, ck_own, wgate, wukv, knT, vtok, gT, pfx="m2"):
    blocks = tok_blocks(T, BS_A)
    ones, stg, hb, cst, snt, rstd, acc, ssq, up, sqb, ra, rb, ob = _a_common(k, pfx)
    hbm = k.sb([128, 32, 16], BF16, pfx + "hbm")
    oc = [0]

    def nob():
        oc[0] += 1
        return ob[oc[0] % 4]

    w2 = k.sb([128, 32, 512], BF16, pfx + "w2")
    load_cast(k, w2, wgate, 32, 512, stg)
    wk = k.sb([128, 4, 1024], BF16, pfx + "wk")
    load_cast(k, wk, wukv, 4, 1024, stg)
    ckm = k.sb([128, 4, 16], BF16, pfx + "ckm")
    ckb = [k.sb([128, 4, BS_A], BF16, pfx + f"ckb{i}") for i in range(3)]
    for bi, (t0, t1) in enumerate(blocks):
        n = t1 - t0
        if bi == 0:
            h, ckn = hbm, ckm
            k.dma("sp", h.v, hn_meta.v)
            k.dma("sp", ckn.v, ck_meta.v)
        else:
            h, ckn = hb[bi % 2], ckb[bi % 3]
            k.dma("sp", h.v, hn_own[bi - 1])
            k.dma("sp", ckn.v, ck_own[bi - 1])
        for m in range(4):
            a = acc[m % 3]
            for kt in range(32):
                k.matmul(a[:, 0:n], w2[:, kt, m * 128:(m + 1) * 128], h[:, kt, 0:n], start=(kt == 0), stop=(kt == 31))
            o = nob()
            k.act(o[:, 0:n], a[:, 0:n], AF.Silu)
            k.dma("act", gT[m * 128:(m + 1) * 128, t0:t1], o[:, 0:n])
        for hd in range(NH):
            u0 = up[hd % 3]
            for kt in range(4):
                k.matmul(u0[:, 0:n], wk[:, kt, hd * 128:(hd + 1) * 128], ckn[:, kt, 0:n], start=(kt == 0), stop=(kt == 3))
            o = nob()
            k.copy(o[:, 0:n], u0[:, 0:n], eng=("act" if hd % 2 else "dve"))
            k.dma("act", knT[hd, :, t0:t1], o[:, 0:n])
        for s0 in range(0, n, 128):
            ns = min(128, n - s0)
            a = acc[(s0 // 128) % 3]
            for kt in range(4):
                k.matmul(a[0:ns, :], ckn[:, kt, s0:s0 + ns], wk[:, kt, 512:1024], start=(kt == 0), stop=(kt == 3))
            o = nob()
            k.copy(o[0:ns, :], a[0:ns, :], eng="dve")
            kb = 0 if bi == 0 else 1 + (t0 + s0 - 16) // 128
            for hd in range(NH):
                k.dma("act", vtok[hd, 0:ns, kb, :], o[0:ns, hd * 128:(hd + 1) * 128])


def mla_phase_b(k, T, qnT, qrT, knT, krT, vtok, gT, yT, pfx="mb"):
    NB = (T - 16) // 512
    NKB = (T - 16) // 128
    onesf = k.sb([128, 128], F32, pfx + "onesf")
    k.memset(onesf.v, 1.0)
    kr = k.sb([128, T], BF16, pfx + "kr")
    k.dma("sp", kr[0:64, :], krT.v)
    kn = k.sb([128, T], BF16, pfx + "kn")
    vv = k.sb([128, NKB + 1, 128], BF16, pfx + "vv")
    qn = [k.sb([128, 512], BF16, pfx + f"qn{i}") for i in range(2)]
    qr = [k.sb([128, 512], BF16, pfx + f"qr{i}") for i in range(2)]
    gt = [k.sb([128, 512], BF16, pfx + f"gt{i}") for i in range(2)]
    pt = [k.sb([128, 512], BF16, pfx + f"pt{i}") for i in range(6)]
    sc = [k.ps([128, 512], F32, pfx + f"sc{i}") for i in range(4)]
    oT = [k.ps([128, 512], F32, pfx + f"oT{i}") for i in range(2)]
    dn = k.ps([128, 512], F32, pfx + "dn")
    dacc = [k.sb([128, 512], F32, pfx + f"dacc{i}") for i in range(2)]
    rden = [k.sb([128, 512], F32, pfx + f"rden{i}") for i in range(2)]
    yo = [k.sb([128, 512], F32, pfx + f"yo{i}") for i in range(2)]
    yb = [k.sb([128, 512], BF16, pfx + f"yb{i}") for i in range(2)]
    u = 0
    g = 0
    for hd in range(NH):
        k.dma("sp", kn.v, knT[hd, :, :])
        k.dma("act", vv.v, vtok[hd])
        groups = [(0, 16, -1)] + [(16 + 512 * i, 16 + 512 * (i + 1), i) for i in range(NB)]
        for (q0, q1, gi) in groups:
            n = q1 - q0
            qnt, qrt, gtt = qn[g % 2], qr[g % 2], gt[g % 2]
            o_ = oT[g % 2]
            k.dma("sp", qnt[:, 0:n], qnT[hd, :, q0:q1])
            k.dma("sp", qrt[0:64, 0:n], qrT[hd, :, q0:q1])
            k.dma("sp", gtt[:, 0:n], gT[hd * 128:(hd + 1) * 128, q0:q1])
            k.memset(dacc[0][:, 0:n], 0.0, eng="dve")
            k.memset(dacc[1][:, 0:n], 0.0, eng="pool")
            kbs = [(0, 16, 0, 0, False)]
            if gi >= 0:
                for j in range(4 * gi):
                    kbs.append((16 + 128 * j, 128, 1 + j, 0, False))
                for dgi in range(4):
                    j = 4 * gi + dgi
                    kbs.append((16 + 128 * j, 128, 1 + j, 128 * dgi, True))

            def scores(idx, uu):
                kc, nk, vb, qs, diag = kbs[idx]
                s_ = sc[uu % 4]
                k.matmul(s_[0:nk, qs:n], kn[:, kc:kc + nk], qnt[:, qs:n], start=True, stop=False)
                k.matmul(s_[0:nk, qs:n], kr[0:64, kc:kc + nk], qrt[0:64, qs:n], start=False, stop=True)

            scores(0, u)
            if len(kbs) > 1:
                scores(1, u + 1)
            for idx, (kc, nk, vb, qs, diag) in enumerate(kbs):
                if idx + 2 < len(kbs):
                    scores(idx + 2, u + 2)
                s_ = sc[u % 4]
                p_ = pt[u % 6]
                k.act(p_[0:nk, qs:n], s_[0:nk, qs:n], AF.Exp)
                if diag:
                    k.memset(p_[64:128, qs:qs + 64], 0.0, eng="pool")
                last = (idx == len(kbs) - 1)
                k.matmul(o_[:, qs:n], vv[0:nk, vb, :], p_[0:nk, qs:n], start=(idx == 0), stop=last)
                da = dacc[u % 2]
                k.tt(da[0:nk, qs:n], da[0:nk, qs:n], p_[0:nk, qs:n], ALU.add, eng=("dve" if u % 2 == 0 else "pool"))
                u += 1
            k.matmul(dn[:, 0:n], onesf.v, dacc[0][:, 0:n], start=True, stop=False)
            k.matmul(dn[:, 0:n], onesf.v, dacc[1][:, 0:n], start=False, stop=True)
            rd, y1, y2 = rden[g % 2], yo[g % 2], yb[g % 2]
            k.recip(rd[:, 0:n], dn[:, 0:n])
            k.tt(y1[:, 0:n], o_[:, 0:n], rd[:, 0:n], ALU.mult)
            k.tt(y2[:, 0:n], y1[:, 0:n], gtt[:, 0:n], ALU.mult, eng="pool")
            k.dma("act", yT[hd * 128:(hd + 1) * 128, q0:q1], y2[:, 0:n])
            g += 1


def stage_mla(k, T, hn_meta, hn_own, wq_in, wkv_in, wuq, wukv, gq, gkv, cos2, sin2s, yT, pfx="ml", kind="Internal", phases="12b"):
    qnT = k.dram(pfx + "_qnT", [NH, 128, T], BF16, kind=kind)
    qrT = k.dram(pfx + "_qrT", [NH, 64, T], BF16, kind=kind)
    knT = k.dram(pfx + "_knT", [NH, 128, T], BF16, kind=kind)
    krT = k.dram(pfx + "_krT", [64, T], BF16, kind=kind)
    vtok = k.dram(pfx + "_vtok", [NH, 128, (T - 16) // 128 + 1, 128], BF16, kind=kind)
    gT = k.dram(pfx + "_gT", [512, T], BF16, kind=kind)
    if "1" in phases:
      with (contextlib.nullcontext() if os.environ.get("NOSCOPE") else k.scope()):
        mla_a1(k, T, hn_meta, hn_own, wq_in, wuq, gq, cos2, sin2s, qnT, qrT, pfx + "1")
    if "2" in phases:
      with k.scope():
        mla_a2(k, T, hn_meta, hn_own, wkv_in, wukv, gkv, cos2, sin2s, knT, krT, vtok, gT, pfx + "2")
    if "b" in phases:
      with k.scope():
        mla_phase_b(k, T, qnT, qrT, knT, krT, vtok, gT, yT, pfx + "b")
    return dict(qnT=qnT, qrT=qrT, knT=knT, krT=krT, vtok=vtok, gT=gT)


def stage_mla2(k, T, hn_meta, hn_own, cq_meta, cq_own, ck_meta, ck_own, krT, wgate, wuq, wukv, cos2, sin2s, yT, pfx="ml"):
    qnT = k.dram(pfx + "_qnT", [NH, 128, T], BF16)
    qrT = k.dram(pfx + "_qrT", [NH, 64, T], BF16)
    knT = k.dram(pfx + "_knT", [NH, 128, T], BF16)
    vtok = k.dram(pfx + "_vtok", [NH, 128, (T - 16) // 128 + 1, 128], BF16)
    gT = k.dram(pfx + "_gT", [512, T], BF16)
    with k.scope():
        mla_a1p(k, T, cq_meta, cq_own, wuq, cos2, sin2s, qnT, qrT, pfx + "1")
    with k.scope():
        mla_a2p(k, T, hn_meta, hn_own, ck_meta, ck_own, wgate, wukv, knT, vtok, gT, pfx + "2")
    with k.scope():
        mla_phase_b(k, T, qnT, qrT, knT, krT, vtok, gT, yT, pfx + "b")


EPS = 1e-6
GELU_C = 1.5957691216057308


def hyb_ssd(k, T, hn_meta, hn_own, w_ssd, convw, convb, dtb_bc, alog_bc, d_bc, ng_bc, Umat, ident, yT, pfx="hs"):
    blocks = tok_blocks(T, BS_A)
    NW = 1288
    stg = [k.sb([128, NW], F32, pfx + f"stg{i}") for i in range(2)]
    w = k.sb([128, 32, NW], BF16, pfx + "w")
    load_cast(k, w, w_ssd, 32, NW, stg)
    hbm = k.sb([128, 32, 16], BF16, pfx + "hbm")
    hb = [k.sb([128, 32, BS_A], BF16, pfx + f"hb{i}") for i in range(2)]
    cw = k.sb([128, 6, 4], F32, pfx + "cw")
    cb = k.sb([128, 6], F32, pfx + "cb")
    k.dma("sp", cw.v, convw.v)
    k.dma("sp", cb.v, convb.v)
    dtb = k.sb([128, 8], F32, pfx + "dtb")
    aneg = k.sb([128, 8], F32, pfx + "aneg")
    dbc = k.sb([128, 512], F32, pfx + "dbc")
    ngb = k.sb([128, 512], F32, pfx + "ngb")
    U = k.sb([128, 128], F32, pfx + "U")
    idb = k.sb([128, 128], BF16, pfx + "idb")
    idf = k.sb([128, 128], F32, pfx + "idf")
    k.dma("sp", dtb.v, dtb_bc.v)
    k.dma("sp", aneg.v, alog_bc.v)
    k.dma("sp", dbc.v, d_bc.v)
    k.dma("sp", ngb.v, ng_bc.v)
    k.dma("sp", U.v, Umat.v)
    k.dma("sp", idf.v, ident.v)
    k.copy(idb.v, idf.v, eng="pool")
    k.act(aneg.v, aneg.v, AF.Exp)
    k.ts(aneg.v, aneg.v, -1.0, ALU.mult)
    ones = k.sb([128, 128], F32, pfx + "ones")
    k.memset(ones.v, 1.0)

    cin = [k.sb([128, 3 + BS_A], F32, pfx + f"cin{m}") for m in range(6)]
    for m in range(6):
        k.memset(cin[m].v, 0.0)
    cacc = [k.sb([128, BS_A], F32, pfx + f"cacc{i}") for i in range(2)]
    fT = [k.sb([128, BS_A], BF16, pfx + f"fT{m}") for m in range(6)]
    L_zs = [k.sb([128, 512], F32, pfx + f"zs{i}") for i in range(2)]
    L_ctk = [k.sb([128, 128], BF16, pfx + f"ctk{i}") for i in range(2)]
    L_dt = [k.sb([128, 8], F32, pfx + f"dt{i}") for i in range(2)]
    L_da = [k.sb([128, 8], F32, pfx + f"da{i}") for i in range(2)]
    L_dab = [k.sb([128, 8, 128], F32, pfx + f"dab{i}") for i in range(2)]
    L_acum = [k.sb([128, 8], F32, pfx + f"acum{i}") for i in range(2)]
    L_nacum = [k.sb([128, 8], F32, pfx + f"nacum{i}") for i in range(2)]
    L_aend = [k.sb([128, 8], F32, pfx + f"aend{i}") for i in range(2)]
    L_eend = [k.sb([128, 8], F32, pfx + f"eend{i}") for i in range(2)]
    L_eac = [k.sb([128, 8], F32, pfx + f"eac{i}") for i in range(2)]
    L_dte = [k.sb([128, 8], F32, pfx + f"dte{i}") for i in range(2)]
    L_xtok = [k.sb([128, 512], BF16, pfx + f"xtok{i}") for i in range(2)]
    L_btok = [k.sb([128, 128], BF16, pfx + f"btok{i}") for i in range(2)]
    L_xdt = [k.sb([128, 512], BF16, pfx + f"xdt{i}") for i in range(2)]
    L_xw = [k.sb([128, 512], BF16, pfx + f"xw{i}") for i in range(2)]
    L_segc = [k.sb([128, 8, 128], F32, pfx + f"segc{i}") for i in range(2)]
    L_cbm = [k.sb([128, 128], F32, pfx + f"cbm{i}") for i in range(2)]
    L_MT = [k.sb([128, 8, 128], BF16, pfx + f"MT{i}") for i in range(2)]
    S = k.sb([128, 512], F32, pfx + "S")
    Sb = k.sb([128, 512], BF16, pfx + "Sb")
    k.memset(S.v, 0.0)
    k.memset(Sb.v, 0.0)
    L_t1 = [k.sb([128, 512], F32, pfx + f"t1{i}") for i in range(2)]
    L_t2 = [k.sb([128, 512], F32, pfx + f"t2{i}") for i in range(2)]
    L_ssq = [k.sb([128, 1], F32, pfx + f"ssq{i}") for i in range(2)]
    L_yn = [k.sb([128, 512], BF16, pfx + f"yn{i}") for i in range(2)]
    yTs = [k.sb([128, 128], BF16, pfx + f"yTs{i}") for i in range(4)]

    accA = k.ps([128, 512], F32, pfx + "accA")
    accB = k.ps([128, 512], F32, pfx + "accB")
    misc = k.ps([128, 512], F32, pfx + "misc")
    AB = k.ps([128, 8, 128], F32, pfx + "AB")
    ydg = k.ps([128, 512], F32, pfx + "ydg")
    yof = k.ps([128, 512], F32, pfx + "yof")
    tr = k.ps([128, 512], BF16, pfx + "tr")
    accs = [accA, accB]

    def stageA(c):
        cl, cs, h, par = c['cl'], c['cs'], c['h'], c['par']
        zs = L_zs[par]
        dt = L_dt[par]
        da = L_da[par]
        dab = L_dab[par]
        acum = L_acum[par]
        nacum = L_nacum[par]
        aend = L_aend[par]
        eend = L_eend[par]
        eac = L_eac[par]
        dte = L_dte[par]
        xtok = L_xtok[par]
        btok = L_btok[par]
        xdt = L_xdt[par]
        xw = L_xw[par]
        segc = L_segc[par]
        cbm = L_cbm[par]
        MT = L_MT[par]
        t1 = L_t1[par]
        t2 = L_t2[par]
        ssq = L_ssq[par]
        yn = L_yn[par]
        ctk = L_ctk[par]
        for kt in range(32):
            k.matmul(accA[0:cl, :], h[:, kt, cs], w[:, kt, 0:512], start=(kt == 0), stop=(kt == 31))
        k.act(zs[0:cl, :], accA[0:cl, :], AF.Silu)
        for kt in range(32):
            k.matmul(misc[0:cl, 0:8], h[:, kt, cs], w[:, kt, 1280:1288], start=(kt == 0), stop=(kt == 31))
        k.tt(dt[0:cl, :], misc[0:cl, 0:8], dtb[0:cl, :], ALU.add)
        k.act(dt[0:cl, :], dt[0:cl, :], AF.Exp)
        k.act(dt[0:cl, :], dt[0:cl, :], AF.Ln, bias=1.0)
        k.tt(da[0:cl, :], dt[0:cl, :], aneg[0:cl, :], ALU.mult)
        for m in range(4):
            k.transpose(tr[0:cl, m * 128:(m + 1) * 128], fT[m][:, cs], idb.v)
        k.copy(xtok[0:cl, :], tr[0:cl, :], eng="act")
        k.transpose(tr[0:cl, 0:128], fT[4][:, cs], idb.v)
        k.copy(btok[0:cl, :], tr[0:cl, 0:128], eng="act")
        k.tt(xdt[0:cl, :].rearrange("p (h d) -> p h d", h=8), xtok[0:cl, :].rearrange("p (h d) -> p h d", h=8),
             dt[0:cl, :].unsqueeze(2).broadcast_to([cl, 8, 64]), ALU.mult)
        k.matmul(misc[0:cl, 8:16], U[0:cl, 0:cl], da[0:cl, :])
        k.copy(acum[0:cl, :], misc[0:cl, 8:16], eng="dve")
        k.ts(nacum[0:cl, :], acum[0:cl, :], -1.0, ALU.mult)
        k.act(eac[0:cl, :], acum[0:cl, :], AF.Exp)
        k.tt(dab[0:cl, :, 0:cl], ones[0:cl, 0:cl].unsqueeze(1).broadcast_to([cl, 8, cl]),
             da[0:cl, :].unsqueeze(2).broadcast_to([cl, 8, cl]), ALU.mult, eng="pool")
        for hh in range(8):
            k.matmul(AB[0:cl, hh, 0:cl], dab[0:cl, hh, 0:cl], U[0:cl, 0:cl])
        k.tt(segc[0:cl, :, 0:cl], AB[0:cl, :, 0:cl], nacum[0:cl, :].unsqueeze(2).broadcast_to([cl, 8, cl]), ALU.add)
        k.copy(aend[0:cl, :], AB[0:cl, :, cl - 1], eng="dve")
        k.ts(segc[0:cl, :, 0:cl], segc[0:cl, :, 0:cl], 0.0, ALU.min, eng="pool")
        k.act(segc[0:cl, :, 0:cl], segc[0:cl, :, 0:cl], AF.Exp)
        k.matmul(misc[0:cl, 128:128 + cl], fT[4][:, cs], fT[5][:, cs])
        k.tt(cbm[0:cl, 0:cl], misc[0:cl, 128:128 + cl], U[0:cl, 0:cl], ALU.mult)
        k.tt(MT[0:cl, :, 0:cl], segc[0:cl, :, 0:cl], cbm[0:cl, 0:cl].unsqueeze(1).broadcast_to([cl, 8, cl]), ALU.mult, eng="pool")
        k.copy(ctk[:, 0:cl], fT[5][:, cs], eng="pool")

    def stageB(c):
        cl, cs, par, tok0 = c['cl'], c['cs'], c['par'], c['tok0']
        zs = L_zs[par]
        dt = L_dt[par]
        da = L_da[par]
        dab = L_dab[par]
        acum = L_acum[par]
        nacum = L_nacum[par]
        aend = L_aend[par]
        eend = L_eend[par]
        eac = L_eac[par]
        dte = L_dte[par]
        xtok = L_xtok[par]
        btok = L_btok[par]
        xdt = L_xdt[par]
        xw = L_xw[par]
        segc = L_segc[par]
        cbm = L_cbm[par]
        MT = L_MT[par]
        t1 = L_t1[par]
        t2 = L_t2[par]
        ssq = L_ssq[par]
        yn = L_yn[par]
        ctk = L_ctk[par]
        for hh in range(8):
            k.matmul(ydg[0:cl, hh * 64:(hh + 1) * 64], MT[0:cl, hh, 0:cl], xdt[0:cl, hh * 64:(hh + 1) * 64])
        k.matmul(yof[0:cl, :], ctk[:, 0:cl], Sb.v)
        k.tt(t1[0:cl, :].rearrange("p (h d) -> p h d", h=8), yof[0:cl, :].rearrange("p (h d) -> p h d", h=8),
             eac[0:cl, :].unsqueeze(2).broadcast_to([cl, 8, 64]), ALU.mult)
        k.tt(t1[0:cl, :], t1[0:cl, :], ydg[0:cl, :], ALU.add)
        k.tt(t2[0:cl, :], xtok[0:cl, :], dbc[0:cl, :], ALU.mult, eng="pool")
        k.tt(t1[0:cl, :], t1[0:cl, :], t2[0:cl, :], ALU.add)
        k.tt(t1[0:cl, :], t1[0:cl, :], zs[0:cl, :], ALU.mult)
        k.act(t2[0:cl, :], t1[0:cl, :], AF.Square, accum_out=ssq[0:cl, :])
        k.ts(ssq[0:cl, :], ssq[0:cl, :], 1.0 / 512, ALU.mult, EPS, ALU.add)
        k.act(ssq[0:cl, :], ssq[0:cl, :], AF.Sqrt)
        k.recip(ssq[0:cl, :], ssq[0:cl, :])
        k.stt(yn[0:cl, :], t1[0:cl, :], ssq[0:cl, 0:1], ngb[0:cl, :], ALU.mult, ALU.mult)
        for m in range(4):
            k.transpose(tr[:, m * 128:m * 128 + cl], yn[0:cl, m * 128:(m + 1) * 128], idb[0:cl, 0:cl])
        for m in range(4):
            k.copy(yTs[m][:, 0:cl], tr[:, m * 128:m * 128 + cl], eng=("act" if m % 2 else "dve"))
            k.dma("act", yT[m * 128:(m + 1) * 128, tok0:tok0 + cl], yTs[m][:, 0:cl])
        k.ts(dte[0:cl, :], aend[0:cl, :], 1.0 / cl, ALU.mult)
        k.matmul(misc[:, 16:24], ones[0:cl, :], dte[0:cl, :])
        k.act(eend.v, misc[:, 16:24], AF.Exp)
        k.tt(dte[0:cl, :], aend[0:cl, :], acum[0:cl, :], ALU.subtract)
        k.act(dte[0:cl, :], dte[0:cl, :], AF.Exp)
        k.tt(xw[0:cl, :].rearrange("p (h d) -> p h d", h=8), xdt[0:cl, :].rearrange("p (h d) -> p h d", h=8),
             dte[0:cl, :].unsqueeze(2).broadcast_to([cl, 8, 64]), ALU.mult)
        k.matmul(yof.v, btok[0:cl, :], xw[0:cl, :])
        k.tt(S.v.rearrange("p (h d) -> p h d", h=8), S.v.rearrange("p (h d) -> p h d", h=8),
             eend.v.unsqueeze(2).broadcast_to([128, 8, 64]), ALU.mult)
        k.tt(S.v, S.v, yof.v, ALU.add)
        k.copy(Sb.v, S.v, eng="pool")


    pendB = None
    nchunk = 0
    for bi, (t0, t1_) in enumerate(blocks):
        n = t1_ - t0
        if bi == 0:
            h = hbm
            k.dma("sp", h.v, hn_meta.v)
        else:
            h = hb[bi % 2]
            k.dma("sp", h.v, hn_own[bi - 1])
        for m in range(6):
            a = accs[m % 2]
            c0 = 512 + m * 128
            for kt in range(32):
                k.matmul(a[:, 0:n], w[:, kt, c0:c0 + 128], h[:, kt, 0:n], start=(kt == 0), stop=(kt == 31))
            ci = cin[m]
            k.copy(ci[:, 3:3 + n], a[:, 0:n], eng="act")
            ca = cacc[m % 2]
            k.ts(ca[:, 0:n], ci[:, 0:n], cw[:, m, 0:1], ALU.mult, cb[:, m:m + 1], ALU.add)
            for j in range(1, 4):
                k.stt(ca[:, 0:n], ci[:, j:j + n], cw[:, m, j:j + 1], ca[:, 0:n], ALU.mult, ALU.add)
            k.act(fT[m][:, 0:n], ca[:, 0:n], AF.Silu)
            k.copy(ci[:, 0:3], ci[:, n:n + 3], eng="pool")
        for s0 in range(0, n, 128):
            cl = min(128, n - s0)
            ctx = dict(cl=cl, cs=slice(s0, s0 + cl), tok0=t0 + s0, h=h, par=nchunk % 2)
            la = k.record(stageA, ctx)
            lb = k.record(stageB, pendB) if pendB is not None else []
            k.replay(la, lb)
            pendB = ctx
            nchunk += 1
    if pendB is not None:
        stageB(pendB)


def hyb_s5(k, T, hn_meta, hn_own, w_s5, bre, bim, cre, cim, are_l, aim_l, ldt_l, d_l, gT, sgT, pfx="h5"):
    blocks = tok_blocks(T, BS_A)
    L = BS_A
    stg = [k.sb([128, 512], F32, pfx + f"stg{i}") for i in range(2)]
    w = k.sb([128, 32, 512], BF16, pfx + "w")
    load_cast(k, w, w_s5, 32, 512, stg)
    hbm = k.sb([128, 32, 16], BF16, pfx + "hbm")
    hb = [k.sb([128, 32, BS_A], BF16, pfx + f"hb{i}") for i in range(2)]
    f_bre = k.sb([128, 8, 128], F32, pfx + "fbre"); f_bim = k.sb([128, 8, 128], F32, pfx + "fbim")
    f_cre = k.sb([128, 8, 128], F32, pfx + "fcre"); f_cim = k.sb([128, 8, 128], F32, pfx + "fcim")
    Bre = k.sb([128, 8, 128], BF16, pfx + "Bre"); Bim = k.sb([128, 8, 128], BF16, pfx + "Bim")
    Cre = k.sb([128, 8, 128], BF16, pfx + "Cre"); Cim = k.sb([128, 8, 128], BF16, pfx + "Cim")
    for (dst, f, src) in ((Bre, f_bre, bre), (Bim, f_bim, bim), (Cre, f_cre, cre), (Cim, f_cim, cim)):
        k.dma("sp", f.v, src.v)
        k.copy(dst.v, f.v, eng="pool")
    are = k.sb([128, 8], F32, pfx + "are"); aim = k.sb([128, 8], F32, pfx + "aim"); dtt = k.sb([128, 8], F32, pfx + "dtt")
    dsk = k.sb([128, 2], F32, pfx + "dsk")
    k.dma("sp", are.v, are_l.v); k.dma("sp", aim.v, aim_l.v); k.dma("sp", dtt.v, ldt_l.v); k.dma("sp", dsk.v, d_l.v)
    k.act(dtt.v, dtt.v, AF.Exp)
    th = k.sb([128, 8], F32, pfx + "th"); rho = k.sb([128, 8], F32, pfx + "rho")
    k.tt(th.v, dtt.v, aim.v, ALU.mult)
    k.tt(rho.v, dtt.v, are.v, ALU.mult)
    k.act(rho.v, rho.v, AF.Exp)
    ki = k.sb([128, 8], I32, pfx + "ki"); kf = k.sb([128, 8], F32, pfx + "kf")
    hh_ = k.sb([128, 8], F32, pfx + "hh"); sh = k.sb([128, 8], F32, pfx + "sh"); ch = k.sb([128, 8], F32, pfx + "ch")
    k.ts(kf.v, th.v, 1.0 / (2 * math.pi), ALU.mult)
    k.copy(ki.v, kf.v, eng="dve")
    k.copy(kf.v, ki.v, eng="dve")
    k.stt(hh_.v, kf.v, -2 * math.pi, th.v, ALU.mult, ALU.add)
    k.ts(hh_.v, hh_.v, 0.5, ALU.mult)
    k.act(sh.v, hh_.v, AF.Sin)
    q4 = k.sb([128, 8], F32, pfx + "q4")
    k.act(q4.v, hh_.v, AF.Sin, scale=0.5)
    k.tt(q4.v, q4.v, q4.v, ALU.mult)
    k.ts(ch.v, q4.v, -2.0, ALU.mult, 1.0, ALU.add)
    zr = k.sb([128, 8, 9], F32, pfx + "zr"); zi = k.sb([128, 8, 9], F32, pfx + "zi"); nzi = k.sb([128, 8, 9], F32, pfx + "nzi")
    tmp8 = k.sb([128, 8], F32, pfx + "tmp8"); tmp8b = k.sb([128, 8], F32, pfx + "tmp8b")
    k.tt(tmp8.v, sh.v, sh.v, ALU.mult)
    k.ts(zr[:, :, 0], tmp8.v, -2.0, ALU.mult, 1.0, ALU.add)
    k.tt(tmp8.v, sh.v, ch.v, ALU.mult)
    k.ts(zi[:, :, 0], tmp8.v, 2.0, ALU.mult)
    for m_ in range(8):
        k.tt(tmp8.v, zr[:, :, m_], zr[:, :, m_], ALU.mult)
        k.tt(tmp8b.v, zi[:, :, m_], zi[:, :, m_], ALU.mult)
        k.tt(zr[:, :, m_ + 1], tmp8.v, tmp8b.v, ALU.subtract)
        k.tt(tmp8.v, zr[:, :, m_], zi[:, :, m_], ALU.mult)
        k.ts(zi[:, :, m_ + 1], tmp8.v, 2.0, ALU.mult)
    k.ts(nzi.v, zi.v, -1.0, ALU.mult)
    abr = k.sb([128, 8], F32, pfx + "abr"); abi = k.sb([128, 8], F32, pfx + "abi"); den = k.sb([128, 8], F32, pfx + "den")
    kre = k.sb([128, 8], F32, pfx + "kre"); kim = k.sb([128, 8], F32, pfx + "kim"); nkre = k.sb([128, 8], F32, pfx + "nkre")
    k.tt(abr.v, rho.v, zr[:, :, 0], ALU.mult)
    k.ts(abr.v, abr.v, -1.0, ALU.add)
    k.tt(abi.v, rho.v, zi[:, :, 0], ALU.mult)
    k.tt(den.v, are.v, are.v, ALU.mult)
    k.tt(tmp8.v, aim.v, aim.v, ALU.mult)
    k.tt(den.v, den.v, tmp8.v, ALU.add)
    k.recip(den.v, den.v)
    k.tt(kre.v, abr.v, are.v, ALU.mult)
    k.tt(tmp8.v, abi.v, aim.v, ALU.mult)
    k.tt(kre.v, kre.v, tmp8.v, ALU.add)
    k.tt(kre.v, kre.v, den.v, ALU.mult)
    k.tt(kim.v, abi.v, are.v, ALU.mult)
    k.tt(tmp8.v, abr.v, aim.v, ALU.mult)
    k.tt(kim.v, kim.v, tmp8.v, ALU.subtract)
    k.tt(kim.v, kim.v, den.v, ALU.mult)
    k.ts(nkre.v, kre.v, -1.0, ALU.mult)
    Fc = k.sb([128, 8, L], F32, pfx + "Fc"); Fs = k.sb([128, 8, L], F32, pfx + "Fs")
    Ere = k.sb([128, 8, L], F32, pfx + "Ere"); Eim = k.sb([128, 8, L], F32, pfx + "Eim")
    rhoT = k.sb([128, 8, L], F32, pfx + "rhoT")
    Fcb = k.sb([128, 8, L], BF16, pfx + "Fcb"); Fsb = k.sb([128, 8, L], BF16, pfx + "Fsb"); NFsb = k.sb([128, 8, L], BF16, pfx + "NFsb")
    tl = k.sb([128, L], F32, pfx + "tl")
    k.memset(Fc.v, 1.0)
    k.memset(Fs.v, 0.0)
    k.memset(rhoT.v, 1.0)
    for j in range(8):
        for m_ in range(8):
            lo = slice(0, 2 ** m_)
            hi = slice(2 ** m_, 2 ** (m_ + 1))
            w_ = 2 ** m_
            k.ts(tl[:, 0:w_], Fs[:, j, lo], zi[:, j, m_:m_ + 1], ALU.mult)
            k.stt(Fc[:, j, hi], Fc[:, j, lo], zr[:, j, m_:m_ + 1], tl[:, 0:w_], ALU.mult, ALU.subtract)
            k.ts(tl[:, 0:w_], Fc[:, j, lo], zi[:, j, m_:m_ + 1], ALU.mult)
            k.stt(Fs[:, j, hi], Fs[:, j, lo], zr[:, j, m_:m_ + 1], tl[:, 0:w_], ALU.mult, ALU.add)
        k.ts(tl.v, Fs[:, j, :], kim[:, j:j + 1], ALU.mult)
        k.stt(Ere[:, j, :], Fc[:, j, :], kre[:, j:j + 1], tl.v, ALU.mult, ALU.add)
        k.ts(tl.v, Fs[:, j, :], nkre[:, j:j + 1], ALU.mult)
        k.stt(Eim[:, j, :], Fc[:, j, :], kim[:, j:j + 1], tl.v, ALU.mult, ALU.add)
        k.ts(rhoT[:, j, :], rhoT[:, j, :], rho[:, j:j + 1], ALU.mult)
        k.ts(NFsb[:, j, :], Fs[:, j, :], -1.0, ALU.mult, eng="pool")
        k.copy(Fcb[:, j, :], Fc[:, j, :], eng="act")
        k.copy(Fsb[:, j, :], Fs[:, j, :], eng="act")
    uf = [k.sb([128, BS_A], F32, pfx + f"uf{a}") for a in range(2)]
    ub = [k.sb([128, BS_A], BF16, pfx + f"ub{a}") for a in range(2)]
    go = [k.sb([128, BS_A], BF16, pfx + f"go{i}") for i in range(2)]
    L5 = {nm: [k.sb([128, BS_A], F32, pfx + f"{nm}{i}") for i in range(2)]
          for nm in ("vre", "vim", "p1", "p2", "p3", "p4", "wre", "wim")}
    L5.update({nm: [k.sb([128, BS_A], BF16, pfx + f"{nm}{i}") for i in range(2)]
               for nm in ("q1", "q2", "q3", "q4", "wrb", "wib")})
    sre = [k.sb([128, BS_A], BF16, pfx + f"sre{i}") for i in range(2)]
    sim_ = [k.sb([128, BS_A], BF16, pfx + f"sim{i}") for i in range(2)]
    wir = k.sb([128, 8], F32, pfx + "wir"); wii = k.sb([128, 8], F32, pfx + "wii")
    k.memset(wir.v, 0.0)
    k.memset(wii.v, 0.0)
    c1 = k.sb([128, 1], F32, pfx + "c1")
    x2 = k.sb([128, BS_A], F32, pfx + "x2"); x3 = k.sb([128, BS_A], F32, pfx + "x3"); yv = k.sb([128, BS_A], F32, pfx + "yv")
    acc = [k.ps([128, 512], F32, pfx + f"acc{i}") for i in range(2)]
    Pp = k.ps([128, 512], F32, pfx + "Pp"); Qp = k.ps([128, 512], F32, pfx + "Qp")
    yp = [k.ps([128, 512], F32, pfx + f"yp{a}") for a in range(2)]
    for bi, (t0, t1_) in enumerate(blocks):
        n = t1_ - t0
        if bi == 0:
            h = hbm
            k.dma("sp", h.v, hn_meta.v)
        else:
            h = hb[bi % 2]
            k.dma("sp", h.v, hn_own[bi - 1])
        lev = 4 if bi == 0 else 8
        for a in range(2):
            for kt in range(32):
                k.matmul(acc[0][:, 0:n], w[:, kt, a * 128:(a + 1) * 128], h[:, kt, 0:n], start=(kt == 0), stop=(kt == 31))
            k.copy(uf[a][:, 0:n], acc[0][:, 0:n], eng="dve")
            k.copy(ub[a][:, 0:n], uf[a][:, 0:n], eng="pool")
            for kt in range(32):
                k.matmul(acc[1][:, 0:n], w[:, kt, 256 + a * 128:256 + (a + 1) * 128], h[:, kt, 0:n], start=(kt == 0), stop=(kt == 31))
            o = go[a]
            k.act(o[:, 0:n], acc[1][:, 0:n], AF.Silu)
            k.dma("act", sgT[a * 128:(a + 1) * 128, t0:t1_], o[:, 0:n])
        def pq(j):
            a = j // 4
            k.matmul(Pp[:, 0:n], Bre[:, j, :], ub[a][:, 0:n])
            k.matmul(Qp[:, 0:n], Bim[:, j, :], ub[a][:, 0:n])

        pq(0)
        for j in range(8):
            a = j // 4
            vre, vim, p1, p2, p3, p4, wre, wim, q1, q2, q3, q4, wrb, wib = (L5[nm][j % 2] for nm in
                ("vre", "vim", "p1", "p2", "p3", "p4", "wre", "wim", "q1", "q2", "q3", "q4", "wrb", "wib"))
            k.tt(p1[:, 0:n], Pp[:, 0:n], Ere[:, j, 0:n], ALU.mult)
            k.tt(p2[:, 0:n], Qp[:, 0:n], Eim[:, j, 0:n], ALU.mult)
            k.tt(p3[:, 0:n], Qp[:, 0:n], Ere[:, j, 0:n], ALU.mult)
            k.tt(p4[:, 0:n], Pp[:, 0:n], Eim[:, j, 0:n], ALU.mult)
            if j + 1 < 8:
                pq(j + 1)
            k.tt(vre[:, 0:n], p1[:, 0:n], p2[:, 0:n], ALU.subtract, eng="pool")
            k.tt(vim[:, 0:n], p3[:, 0:n], p4[:, 0:n], ALU.add, eng="pool")
            k.scan(wre[:, 0:n], rhoT[:, j, 0:n], vre[:, 0:n], wir[:, j:j + 1])
            k.scan(wim[:, 0:n], rhoT[:, j, 0:n], vim[:, 0:n], wii[:, j:j + 1])
            k.ts(c1.v, wim[:, n - 1:n], nzi[:, j, lev:lev + 1], ALU.mult)
            k.stt(wir[:, j:j + 1], wre[:, n - 1:n], zr[:, j, lev:lev + 1], c1.v, ALU.mult, ALU.add)
            k.ts(c1.v, wre[:, n - 1:n], zi[:, j, lev:lev + 1], ALU.mult)
            k.stt(wii[:, j:j + 1], wim[:, n - 1:n], zr[:, j, lev:lev + 1], c1.v, ALU.mult, ALU.add)
            sr, si = sre[j % 2], sim_[j % 2]
            k.copy(wrb[:, 0:n], wre[:, 0:n], eng="act")
            k.copy(wib[:, 0:n], wim[:, 0:n], eng="act")
            k.tt(q1[:, 0:n], wrb[:, 0:n], Fcb[:, j, 0:n], ALU.mult)
            k.tt(q2[:, 0:n], wib[:, 0:n], Fsb[:, j, 0:n], ALU.mult)
            k.tt(sr[:, 0:n], q1[:, 0:n], q2[:, 0:n], ALU.subtract)
            k.tt(q3[:, 0:n], wrb[:, 0:n], NFsb[:, j, 0:n], ALU.mult)
            k.tt(q4[:, 0:n], wib[:, 0:n], Fcb[:, j, 0:n], ALU.mult)
            k.tt(si[:, 0:n], q3[:, 0:n], q4[:, 0:n], ALU.subtract)
            k.matmul(yp[a][:, 0:n], Cre[:, j, :], sr[:, 0:n], start=(j % 4 == 0), stop=False)
            k.matmul(yp[a][:, 0:n], Cim[:, j, :], si[:, 0:n], start=False, stop=(j % 4 == 3))
        for a in range(2):
            k.stt(yv[:, 0:n], uf[a][:, 0:n], dsk[:, a:a + 1], yp[a][:, 0:n], ALU.mult, ALU.add)
            k.tt(x2[:, 0:n], yv[:, 0:n], yv[:, 0:n], ALU.mult, eng="pool")
            k.ts(x2[:, 0:n], x2[:, 0:n], 0.044715, ALU.mult, 1.0, ALU.add, eng="pool")
            k.tt(x3[:, 0:n], x2[:, 0:n], yv[:, 0:n], ALU.mult, eng="pool")
            k.act(x3[:, 0:n], x3[:, 0:n], AF.Sigmoid, scale=GELU_C)
            o = go[a]
            k.tt(o[:, 0:n], x3[:, 0:n], yv[:, 0:n], ALU.mult)
            k.dma("act", gT[a * 128:(a + 1) * 128, t0:t1_], o[:, 0:n])


PERM = np.concatenate([np.arange(32, 64), np.arange(0, 32)])


def ktile(w):
    K, M = w.shape
    return np.ascontiguousarray(w.reshape(K // 128, 128, M).transpose(1, 0, 2))


def vec_tile(g):
    return np.ascontiguousarray(g.reshape(-1, 128).T)


def rope_tables(T):
    pos = np.arange(T, dtype=np.float32)
    inv_freq = (np.float32(10000.0) ** (-np.arange(0, 64, 2, dtype=np.float32) / np.float32(64))).astype(np.float32)
    ang = (pos[:, None] * inv_freq[None, :]).astype(np.float32)
    cos = np.cos(ang).astype(np.float32).T
    sin = np.sin(ang).astype(np.float32).T
    cos2 = np.ascontiguousarray(np.concatenate([cos, cos], 0))
    sin2s = np.ascontiguousarray(np.concatenate([-sin, sin], 0))
    return cos2, sin2s


def mla_weights(c, w_in, q_norm, w_uq, kv_norm, w_ukv):
    kr = w_in[:, 1536:1600]
    wkv = np.concatenate([w_in[:, 1024:1536], kr, kr[:, PERM], w_in[:, 1600 + c * 512:1600 + (c + 1) * 512]], 1)
    uq = []
    kn = []
    vv = []
    for h in range(4):
        b = (4 * c + h) * 192
        rope = w_uq[:, b + 128:b + 192]
        uq += [w_uq[:, b:b + 128], rope, rope[:, PERM]]
        b2 = (4 * c + h) * 256
        kn.append(w_ukv[:, b2:b2 + 128])
        vv.append(w_ukv[:, b2 + 128:b2 + 256])
    return dict(
        wq_in=ktile(w_in[:, 0:1024]),
        wkv_in=ktile(wkv),
        wuq=ktile(np.concatenate(uq, 1)),
        wukv=ktile(np.concatenate(kn + vv, 1)),
        gq=vec_tile(q_norm),
        gkv=vec_tile(kv_norm),
    )


def out_w_tile(W):
    K = W.shape[0]
    nkt = K // 128
    return np.ascontiguousarray(W.reshape(nkt, 128, 32, 128).transpose(2, 1, 0, 3).reshape(32, 128, nkt * 128))


def hn_layout(hnT):
    T = hnT.shape[1]
    nb = (T - 16) // 256
    v = hnT.reshape(32, 128, T)
    meta = np.ascontiguousarray(v[:, :, 0:16].transpose(1, 0, 2))
    own = np.ascontiguousarray(v[:, :, 16:].reshape(32, 128, nb, 256).transpose(2, 1, 0, 3))
    return dict(hn_meta=meta, hn_own=own)


def const_mats():
    U = np.triu(np.ones((128, 128), np.float32))
    ident = np.eye(128, dtype=np.float32)
    return U, ident


def hyb_weights(c, w_in, conv_w, conv_b, dt_bias, a_log, d_skip, norm_g,
                a_re, a_im, log_dt, b_re, b_im, c_re, c_im, s5_d):
    z = w_in[:, c * 512:(c + 1) * 512]
    x = w_in[:, 4096 + c * 512:4096 + (c + 1) * 512]
    B = w_in[:, 8192 + c * 128:8192 + (c + 1) * 128]
    C = w_in[:, 9216 + c * 128:9216 + (c + 1) * 128]
    dt = w_in[:, 10240 + c * 8:10240 + (c + 1) * 8]
    w_ssd = ktile(np.concatenate([z, x, B, C, dt], 1))
    u = w_in[:, 10304 + c * 256:10304 + (c + 1) * 256]
    gate = w_in[:, 12352 + c * 256:12352 + (c + 1) * 256]
    w_s5 = ktile(np.concatenate([u, gate], 1))
    chans = [np.arange(c * 512 + m * 128, c * 512 + (m + 1) * 128) for m in range(4)]
    chans.append(np.arange(4096 + c * 128, 4096 + (c + 1) * 128))
    chans.append(np.arange(5120 + c * 128, 5120 + (c + 1) * 128))
    convw = np.ascontiguousarray(np.stack([conv_w[:, ch].T for ch in chans], 1))
    convb = np.ascontiguousarray(np.stack([conv_b[ch] for ch in chans], 1))
    hs = slice(c * 8, (c + 1) * 8)
    dtb_bc = np.ascontiguousarray(np.broadcast_to(dt_bias[hs][None, :], (128, 8)))
    alog_bc = np.ascontiguousarray(np.broadcast_to(a_log[hs][None, :], (128, 8)))
    d_bc = np.ascontiguousarray(np.broadcast_to(np.repeat(d_skip[hs], 64)[None, :], (128, 512)))
    ng_bc = np.ascontiguousarray(np.broadcast_to(norm_g[c * 512:(c + 1) * 512][None, :], (128, 512)))
    bre = np.zeros((128, 8, 128), np.float32); bim = np.zeros((128, 8, 128), np.float32)
    cre = np.zeros((128, 8, 128), np.float32); cim = np.zeros((128, 8, 128), np.float32)
    are_l = np.zeros((128, 8), np.float32); aim_l = np.zeros((128, 8), np.float32); ldt_l = np.zeros((128, 8), np.float32)
    for j in range(8):
        a, q = j // 4, (j % 4) * 32
        for m in range(2):
            g = 16 * c + 2 * j + m
            bre[q + m * 16:q + (m + 1) * 16, j, m * 64:(m + 1) * 64] = b_re[g].T
            bim[q + m * 16:q + (m + 1) * 16, j, m * 64:(m + 1) * 64] = b_im[g].T
            cre[m * 64:(m + 1) * 64, j, q + m * 16:q + (m + 1) * 16] = c_re[g].T
            cim[m * 64:(m + 1) * 64, j, q + m * 16:q + (m + 1) * 16] = c_im[g].T
            are_l[m * 64:(m + 1) * 64, j] = a_re[g]
            aim_l[m * 64:(m + 1) * 64, j] = a_im[g]
            ldt_l[m * 64:(m + 1) * 64, j] = log_dt[g]
    d_l = np.ascontiguousarray(s5_d[c * 256:(c + 1) * 256].reshape(2, 128).T)
    return dict(w_ssd=w_ssd, w_s5=w_s5, convw=convw, convb=convb, dtb_bc=dtb_bc, alog_bc=alog_bc, d_bc=d_bc, ng_bc=ng_bc,
                bre=bre, bim=bim, cre=cre, cim=cim, are_l=are_l, aim_l=aim_l, ldt_l=ldt_l, d_l=d_l)


def glu_w_tile(W):
    return np.ascontiguousarray(W.reshape(16, 128, 16, 128).transpose(2, 1, 0, 3).reshape(16, 128, 2048))


def blk_layout(xT, nkt):
    T = xT.shape[1]
    nb = (T - 16) // 256
    v = xT.reshape(nkt, 128, T)
    meta = np.ascontiguousarray(v[:, :, 0:16].transpose(1, 0, 2))
    own = np.ascontiguousarray(v[:, :, 16:].reshape(nkt, 128, nb, 256).transpose(2, 1, 0, 3))
    return meta, own


def mla_weights2(c, w_in, q_norm, w_uq, kv_norm, w_ukv):
    kr = w_in[:, 1536:1600]
    wkv3 = np.concatenate([w_in[:, 1024:1536], kr, kr[:, PERM]], 1)
    base = mla_weights(c, w_in, q_norm, w_uq, kv_norm, w_ukv)
    return dict(wq_in=base["wq_in"], wkv3=ktile(wkv3), gq=base["gq"], gkv=base["gkv"],
                wgate=ktile(w_in[:, 1600 + c * 512:1600 + (c + 1) * 512]), wuq=base["wuq"], wukv=base["wukv"])

from concourse.bass_utils import run_bass_kernel_spmd

BFNP = ml_dtypes.bfloat16
T_ALL = 16400
TC = 2064
NCORE = 8
_PROGS = {}

HYB_SHAPES = dict(w_ssd=[128, 32, 1288], w_s5=[128, 32, 512], convw=[128, 6, 4], convb=[128, 6], dtb_bc=[128, 8],
                  alog_bc=[128, 8], d_bc=[128, 512], ng_bc=[128, 512], bre=[128, 8, 128], bim=[128, 8, 128],
                  cre=[128, 8, 128], cim=[128, 8, 128], are_l=[128, 8], aim_l=[128, 8], ldt_l=[128, 8], d_l=[128, 2],
                  Umat=[128, 128], ident=[128, 128])


def _new():
    return bass.Bass("TRN2", target_bir_lowering=False)


def prog_norm():
    nc = _new()
    with contextlib.ExitStack() as st:
        k = KB(nc, st)
        hT = k.dram("hT", [D, TC], F32, kind="ExternalInput")
        g_l = k.dram("g_l", [128, 32], F32, kind="ExternalInput")
        hnT = k.dram("hnT", [D, TC], BF16, kind="ExternalOutput")
        stage_out(k, hT, None, None, g_l, None, hnT, TC, 0, True, BF16)
        k.final_wait("sp", [hnT])
        k.emit()
    return nc


def prog_out(hyb, final):
    nc = _new()
    nkt = 48 if hyb else 32
    with contextlib.ExitStack() as st:
        k = KB(nc, st)
        hT = k.dram("hT", [D, TC], F32, kind="ExternalInput")
        yT = k.dram("yT", [D, TC], BF16, kind="ExternalInput")
        wl = k.dram("wl", [32, 128, nkt * 128], F32, kind="ExternalInput")
        g_l = k.dram("g_l", [128, 32], F32, kind="ExternalInput")
        glu = None
        if hyb:
            g_all = k.dram("g_all", [2048, TC], BF16, kind="ExternalInput")
            sg_all = k.dram("sg_all", [2048, TC], BF16, kind="ExternalInput")
            wglu_l = k.dram("wglu_l", [16, 128, 2048], F32, kind="ExternalInput")
            glu = (g_all, sg_all, wglu_l)
        outs = []
        if not final:
            hT_new = k.dram("hT_new", [D, TC], F32, kind="ExternalOutput")
            outs.append(hT_new)
        else:
            hT_new = k.dram("hT_new", [D, TC], F32)
        hnT = k.dram("hnT", [D, TC], F32 if final else BF16, kind="ExternalOutput")
        outs.append(hnT)
        ybT = None
        if hyb:
            ybT = k.dram("ybT", [2048, TC], BF16)
            with k.scope():
                stage_glu(k, g_all, sg_all, wglu_l, ybT, TC)
        with k.scope():
            stage_out(k, hT, yT, wl, g_l, hT_new, hnT, TC, nkt, False, F32 if final else BF16, glu=ybT)
        if hyb:
            wq_in = k.dram("wq_in", [128, 32, 1024], F32, kind="ExternalInput")
            wkv3 = k.dram("wkv3", [128, 32, 640], F32, kind="ExternalInput")
            gq = k.dram("gq", [128, 8], F32, kind="ExternalInput")
            gkv = k.dram("gkv", [128, 4], F32, kind="ExternalInput")
            cos2c = k.dram("cos2c", [64, TC], F32, kind="ExternalInput")
            sin2sc = k.dram("sin2sc", [64, TC], F32, kind="ExternalInput")
            cqnT = k.dram("cqnT", [1024, TC], BF16, kind="ExternalOutput")
            ckvnT = k.dram("ckvnT", [512, TC], BF16, kind="ExternalOutput")
            krT = k.dram("krT", [64, TC], BF16, kind="ExternalOutput")
            outs += [cqnT, ckvnT, krT]
            with k.scope():
                mla_pre(k, TC, hnT, wq_in, wkv3, gq, gkv, cos2c, sin2sc, cqnT, ckvnT, krT)
        k.final_wait("sp", outs)
        k.emit()
    return nc


def prog_hyb():
    nc = _new()
    T = T_ALL
    with contextlib.ExitStack() as st:
        k = KB(nc, st)
        hn_meta = k.dram("hn_meta", [128, 32, 16], BF16, kind="ExternalInput")
        hn_own = k.dram("hn_own", [(T - 16) // 256, 128, 32, 256], BF16, kind="ExternalInput")
        d = {n: k.dram(n, s, F32, kind="ExternalInput") for n, s in HYB_SHAPES.items()}
        yT = k.dram("yT", [512, T], BF16, kind="ExternalOutput")
        gT = k.dram("gT", [256, T], BF16, kind="ExternalOutput")
        sgT = k.dram("sgT", [256, T], BF16, kind="ExternalOutput")
        with k.scope():
            hyb_ssd(k, T, hn_meta, hn_own, d["w_ssd"], d["convw"], d["convb"], d["dtb_bc"], d["alog_bc"], d["d_bc"],
                    d["ng_bc"], d["Umat"], d["ident"], yT)
        with k.scope():
            hyb_s5(k, T, hn_meta, hn_own, d["w_s5"], d["bre"], d["bim"], d["cre"], d["cim"], d["are_l"], d["aim_l"],
                   d["ldt_l"], d["d_l"], gT, sgT)
        k.final_wait("sp", [yT, gT, sgT])
        k.emit()
    return nc


def prog_mla():
    nc = _new()
    T = T_ALL
    NB = (T - 16) // 256
    with contextlib.ExitStack() as st:
        k = KB(nc, st)
        hn_meta = k.dram("hn_meta", [128, 32, 16], BF16, kind="ExternalInput")
        hn_own = k.dram("hn_own", [NB, 128, 32, 256], BF16, kind="ExternalInput")
        cq_meta = k.dram("cq_meta", [128, 8, 16], BF16, kind="ExternalInput")
        cq_own = k.dram("cq_own", [NB, 128, 8, 256], BF16, kind="ExternalInput")
        ck_meta = k.dram("ck_meta", [128, 4, 16], BF16, kind="ExternalInput")
        ck_own = k.dram("ck_own", [NB, 128, 4, 256], BF16, kind="ExternalInput")
        krT = k.dram("krT", [64, T], BF16, kind="ExternalInput")
        wgate = k.dram("wgate", [128, 32, 512], F32, kind="ExternalInput")
        wuq = k.dram("wuq", [128, 8, 1024], F32, kind="ExternalInput")
        wukv = k.dram("wukv", [128, 4, 1024], F32, kind="ExternalInput")
        cos2 = k.dram("cos2", [64, T], F32, kind="ExternalInput")
        sin2s = k.dram("sin2s", [64, T], F32, kind="ExternalInput")
        yT = k.dram("yT", [512, T], BF16, kind="ExternalOutput")
        stage_mla2(k, T, hn_meta, hn_own, cq_meta, cq_own, ck_meta, ck_own, krT, wgate, wuq, wukv, cos2, sin2s, yT)
        k.final_wait("sp", [yT])
        k.emit()
    return nc


def _get(name, fn, *a):
    if name not in _PROGS:
        _PROGS[name] = fn(*a)
    return _PROGS[name]


def _run(nc, in_maps):
    res = run_bass_kernel_spmd(nc, in_maps, core_ids=list(range(NCORE)))
    return res.results


def _tok_idx(c):
    return np.concatenate([np.arange(16), 16 + 2048 * c + np.arange(2048)])


def _gather_tokens(per_core):
    return np.concatenate([per_core[0][:, 0:16]] + [per_core[c][:, 16:] for c in range(NCORE)], axis=1)


def _split_tokens(full):
    return [np.ascontiguousarray(full[:, _tok_idx(c)]) for c in range(NCORE)]


def kernel(x, meta, hyb_norm, hyb_w_in, ssd_conv_w, ssd_conv_b, ssd_dt_bias, ssd_a_log, ssd_d, ssd_norm,
           s5_a_re, s5_a_im, s5_log_dt, s5_b_re, s5_b_im, s5_c_re, s5_c_im, s5_d, s5_w_glu, hyb_w_out,
           mla_norm, mla_w_in, mla_q_norm, mla_w_uq, mla_kv_norm, mla_w_ukv, mla_w_out, final_norm):
    f32 = lambda a: np.asarray(a, dtype=np.float32)
    x, meta = f32(x), f32(meta)
    h_full_T = np.ascontiguousarray(np.concatenate([meta, x[0]], axis=0).T)
    hT = _split_tokens(h_full_T)
    del h_full_T
    U, ident = const_mats()
    cos2, sin2s = rope_tables(T_ALL)

    g0 = vec_tile(f32(hyb_norm[0]))
    res = _run(_get("norm", prog_norm), [dict(hT=hT[c], g_l=g0) for c in range(NCORE)])
    hn = [np.asarray(r["hnT"]) for r in res]

    for layer in range(4):
        i = layer // 2
        hn_l = hn_layout(_gather_tokens(hn))
        last = (layer == 3)
        if layer % 2 == 0:
            ims = []
            for c in range(NCORE):
                im = hyb_weights(c, f32(hyb_w_in[i]), f32(ssd_conv_w[i]), f32(ssd_conv_b[i]), f32(ssd_dt_bias[i]),
                                 f32(ssd_a_log[i]), f32(ssd_d[i]), f32(ssd_norm[i]), f32(s5_a_re[i]), f32(s5_a_im[i]),
                                 f32(s5_log_dt[i]), f32(s5_b_re[i]), f32(s5_b_im[i]), f32(s5_c_re[i]), f32(s5_c_im[i]),
                                 f32(s5_d[i]))
                im.update(Umat=U, ident=ident, **hn_l)
                ims.append(im)
            res = _run(_get("hyb", prog_hyb), ims)
            del ims
            y_all = _split_tokens(np.concatenate([np.asarray(r["yT"]) for r in res], axis=0))
            g_all = _split_tokens(np.concatenate([np.asarray(r["gT"]) for r in res], axis=0))
            sg_all = _split_tokens(np.concatenate([np.asarray(r["sgT"]) for r in res], axis=0))
            wl = out_w_tile(f32(hyb_w_out[i]))
            wglu_l = glu_w_tile(f32(s5_w_glu[i]))
            gn = vec_tile(f32(mla_norm[i]))
            mw = [mla_weights2(c, f32(mla_w_in[i]), f32(mla_q_norm[i]), f32(mla_w_uq[i]), f32(mla_kv_norm[i]),
                               f32(mla_w_ukv[i])) for c in range(NCORE)]
            ims = [dict(hT=hT[c], yT=y_all[c], wl=wl, g_l=gn, g_all=g_all[c], sg_all=sg_all[c], wglu_l=wglu_l,
                        wq_in=mw[c]["wq_in"], wkv3=mw[c]["wkv3"], gq=mw[c]["gq"], gkv=mw[c]["gkv"],
                        cos2c=np.ascontiguousarray(cos2[:, _tok_idx(c)]), sin2sc=np.ascontiguousarray(sin2s[:, _tok_idx(c)]))
                   for c in range(NCORE)]
            res = _run(_get("out_hyb", prog_out, True, False), ims)
            cq_l = blk_layout(_gather_tokens([np.asarray(r["cqnT"]) for r in res]), 8)
            ck_l = blk_layout(_gather_tokens([np.asarray(r["ckvnT"]) for r in res]), 4)
            kr_full = np.ascontiguousarray(_gather_tokens([np.asarray(r["krT"]) for r in res]))
        else:
            ims = []
            for c in range(NCORE):
                im = dict(wgate=mw[c]["wgate"], wuq=mw[c]["wuq"], wukv=mw[c]["wukv"], cos2=cos2, sin2s=sin2s,
                          cq_meta=cq_l[0], cq_own=cq_l[1], ck_meta=ck_l[0], ck_own=ck_l[1], krT=kr_full, **hn_l)
                ims.append(im)
            res = _run(_get("mla", prog_mla), ims)
            del ims
            y_all = _split_tokens(np.concatenate([np.asarray(r["yT"]) for r in res], axis=0))
            wl = out_w_tile(f32(mla_w_out[i]))
            gn = vec_tile(f32(final_norm) if last else f32(hyb_norm[i + 1]))
            ims = [dict(hT=hT[c], yT=y_all[c], wl=wl, g_l=gn) for c in range(NCORE)]
            res = _run(_get("out_mla_final" if last else "out_mla", prog_out, False, last), ims)
        del ims
        if not last:
            hT = [np.asarray(r["hT_new"]) for r in res]
        hn = [np.asarray(r["hnT"]) for r in res]

    out = np.concatenate([hn[c][:, 16:].T for c in range(NCORE)], axis=0)
    return np.ascontiguousarray(out[None].astype(np.float32))
```

```python
import contextlib
import math
import os
import numpy as np
import ml_dtypes


import concourse.bass as bass
import concourse.mybir as mybir

F32 = mybir.dt.float32
BF16 = mybir.dt.bfloat16
I32 = mybir.dt.int32
AF = mybir.ActivationFunctionType
ALU = mybir.AluOpType
AX = mybir.AxisListType

COMPUTE = ("pe", "act", "dve", "pool")


class View:
    __slots__ = ("tl", "ap")

    def __init__(self, tl, ap):
        self.tl = tl
        self.ap = ap

    def __getitem__(self, idx):
        return View(self.tl, self.ap[idx])

    def rearrange(self, pat, **kw):
        return View(self.tl, self.ap.rearrange(pat, **kw))

    def broadcast_to(self, shape):
        return View(self.tl, self.ap.broadcast_to(list(shape)))

    def unsqueeze(self, ax):
        return View(self.tl, self.ap.unsqueeze(ax))

    def partition_broadcast(self, n):
        return View(self.tl, self.ap.partition_broadcast(n))

    def bitcast(self, dt):
        return View(self.tl, self.ap.bitcast(dt))

    @property
    def shape(self):
        return self.ap.shape


class Tl:
    __slots__ = ("t", "name", "lw", "rd", "dsem", "dcnt", "is_dram", "is_psum")

    def __init__(self, t, name, is_dram=False, is_psum=False):
        self.is_psum = is_psum
        self.t = t
        self.name = name
        self.lw = {}
        self.rd = {}
        self.dsem = None
        self.dcnt = 0
        self.is_dram = is_dram

    def __getitem__(self, idx):
        return View(self, self.t[idx])

    def rearrange(self, pat, **kw):
        return View(self, self.t.rearrange(pat, **kw))

    @property
    def v(self):
        return View(self, self.t[:])


def _is_view(x):
    return isinstance(x, View)


class KB:
    def __init__(self, nc, stack):
        self.nc = nc
        self.stack = stack
        self.root = stack
        self.lists = {e: [] for e in ("pe", "act", "dve", "pool", "sp")}
        self.psem = {}
        self.pcnt = {}
        for e in COMPUTE:
            self.psem[e] = stack.enter_context(nc.semaphore("prog_" + e))
            self.pcnt[e] = 0
        self.known = {e: {} for e in self.lists}
        self.cinst = {e: [] for e in COMPUTE}
        self.defer = None
        self.ntile = 0
        self.tiles = []
        self.sem_pool = []
        self.n_sem = 4
        self.n_inst = 0
        self.n_wait = 0

    def sb(self, shape, dt, name=None):
        self.ntile += 1
        name = name or f"t{self.ntile}"
        t = self.stack.enter_context(self.nc.sbuf_tensor(name, list(shape), dt))
        tl = Tl(t, name)
        self.tiles.append(tl)
        return tl

    def ps(self, shape, dt, name=None):
        self.ntile += 1
        name = name or f"p{self.ntile}"
        t = self.stack.enter_context(self.nc.psum_tensor(name, list(shape), dt))
        return Tl(t, name, is_psum=True)

    def dram(self, name, shape, dt, kind="Internal", **kw):
        t = self.nc.dram_tensor(name, list(shape), dt, kind=kind, **kw)
        tl = Tl(t.ap(), name, is_dram=True)
        self.tiles.append(tl)
        return tl

    def _need(self, eng, ev, waits):
        if ev is None:
            return
        if ev[0] == "c":
            _, src, idx = ev
            if src == "pe" and eng == "pe":
                return
            key = ("c", src)
        else:
            _, sem, idx = ev
            key = id(sem)
        kn = self.known[eng]
        if kn.get(key, 0) >= idx:
            return
        kn[key] = idx
        if ev[0] == "c":
            self.cinst[src][idx - 1][4] = True
        waits[key] = ev

    def _deps(self, eng, reads, writes):
        waits = {}
        for t in reads:
            for ev in t.lw.values():
                self._need(eng, ev, waits)
            if t.is_psum:
                for ev in t.rd.values():
                    if not (ev[0] == "c" and ev[1] == eng):
                        self._need(eng, ev, waits)
        for t in writes:
            for ev in t.lw.values():
                self._need(eng, ev, waits)
            for ev in t.rd.values():
                self._need(eng, ev, waits)
        return list(waits.values())

    @staticmethod
    def _evkey(ev):
        return ("c", ev[1]) if ev[0] == "c" else id(ev[1])

    def _record(self, ev, reads, writes):
        key = self._evkey(ev)
        for t in reads:
            t.rd[key] = ev
        for t in writes:
            if t.is_dram and ev[0] == "d":
                t.lw[key] = ev
            else:
                t.lw = {key: ev}
            t.rd = {}

    def op(self, eng, fn, reads=(), writes=(), inc=True):
        if self.defer is not None:
            self.defer.append(("op", (eng, fn, list(reads), list(writes), inc)))
            return
        reads = [r.tl if _is_view(r) else r for r in reads]
        writes = [w.tl if _is_view(w) else w for w in writes]
        waits = self._deps(eng, reads, writes)
        ent = [waits, fn, "c", eng, False]
        self.cinst[eng].append(ent)
        ev = ("c", eng, len(self.cinst[eng]))
        self.lists[eng].append(ent)
        self._record(ev, reads, writes)
        self.n_inst += 1
        self.n_wait += len(waits)

    def dma(self, q, out, in_, sem_tile=None, **kw):
        if self.defer is not None:
            self.defer.append(("dma", (q, out, in_, sem_tile, kw)))
            return
        reads = [in_.tl]
        writes = [out.tl]
        waits = self._deps(q, reads, writes)
        st = sem_tile or (in_.tl if out.tl.is_dram and not in_.tl.is_dram else out.tl)
        if st.dsem is None:
            st.dsem, st.dcnt = self.get_sem("d_" + st.name)
        st.dcnt += 16
        ev = ("d", st.dsem, st.dcnt)
        oap, iap = out.ap, in_.ap
        self.lists[q].append([waits, lambda e: e.dma_start(out=oap, in_=iap, **kw), "d", st.dsem, True])
        self._record(ev, reads, writes)
        self.n_inst += 1
        self.n_wait += len(waits)

    def collective(self, kind, out, in_, op=None):
        reads = [in_.tl]
        writes = [out.tl]
        waits = self._deps("pool", reads, writes)
        st = out.tl
        if st.dsem is None:
            st.dsem, st.dcnt = self.get_sem("c_" + st.name)
        st.dcnt += 16
        ev = ("d", st.dsem, st.dcnt)
        oap, iap = out.ap, in_.ap
        aop = op if op is not None else ALU.bypass
        groups = [list(range(8))]
        self.lists["pool"].append([waits, lambda e: e.collective_compute(kind, aop, replica_groups=groups, ins=[iap], outs=[oap]), "d", st.dsem, True])
        self._record(ev, reads, writes)
        self.n_inst += 1

    def record(self, fn, *a):
        assert self.defer is None
        self.defer = []
        try:
            fn(*a)
            return self.defer
        finally:
            self.defer = None

    def replay(self, *lists):
        lists = [l for l in lists if l]
        pos = [0] * len(lists)
        total = sum(len(l) for l in lists)
        for _ in range(total):
            bi = min((i for i in range(len(lists)) if pos[i] < len(lists[i])), key=lambda i: pos[i] / len(lists[i]))
            kind, args = lists[bi][pos[bi]]
            pos[bi] += 1
            if kind == "op":
                self.op(*args[:4], inc=args[4])
            else:
                q, out, in_, sem_tile, kw = args
                self.dma(q, out, in_, sem_tile, **kw)

    def get_sem(self, name):
        if self.sem_pool:
            return self.sem_pool.pop()
        self.n_sem += 1
        return self.root.enter_context(self.nc.semaphore(name)), 0

    @contextlib.contextmanager
    def scope(self):
        old_stack, old_tiles = self.stack, self.tiles
        with contextlib.ExitStack() as st:
            self.stack = st
            self.tiles = []
            yield
            self.tiles = old_tiles + self.tiles
            self.barrier()
            new = self.tiles[len(old_tiles):]
            self.release([t for t in new if not t.is_dram])
            self.tiles = old_tiles + [t for t in new if t.is_dram]
            self.stack = old_stack

    def release(self, tiles):
        for t in tiles:
            if t.dsem is not None:
                self.sem_pool.append((t.dsem, t.dcnt))
                t.dsem = None

    def barrier(self):
        evs = [("c", e, len(self.cinst[e])) for e in COMPUTE if self.cinst[e]]
        evs += [("d", s, c) for (s, c) in self.all_dsems() if c > 0]
        for eng in self.lists:
            waits = {}
            for ev in evs:
                if ev[0] == "c" and ev[1] == eng:
                    continue
                saved = None
                if ev[0] == "c" and ev[1] == "pe" and eng == "pe":
                    continue
                self._need(eng, ev, waits)
            if waits:
                self.lists[eng].append([list(waits.values()), None, None, None, False])

    def all_dsems(self):
        out = [(t.dsem, t.dcnt) for t in self.tiles if t.dsem is not None]
        out += list(self.sem_pool)
        return out

    def final_wait(self, eng, tiles):
        waits = {}
        for t in tiles:
            for ev in t.lw.values():
                self._need(eng, ev, waits)
        self.lists[eng].append([list(waits.values()), None, None, None, False])

    def matmul(self, out, lhsT, rhs, start=True, stop=True):
        o, l, r = out.ap, lhsT.ap, rhs.ap
        self.op("pe", lambda e: e.matmul(o, lhsT=l, rhs=r, start=start, stop=stop), [lhsT, rhs], [out], inc=bool(stop))

    def transpose(self, out, in_, ident):
        o, i, d = out.ap, in_.ap, ident.ap
        self.op("pe", lambda e: e.transpose(o, i, d), [in_, ident], [out])

    def act(self, out, in_, func, bias=None, scale=None, accum_out=None):
        o, i = out.ap, in_.ap
        kw = {}
        rd = [in_]
        wr = [out]
        if bias is not None:
            if _is_view(bias):
                rd.append(bias)
                kw["bias"] = bias.ap
            else:
                kw["bias"] = bias
        if scale is not None:
            if _is_view(scale):
                rd.append(scale)
                kw["scale"] = scale.ap
            else:
                kw["scale"] = scale
        if accum_out is not None:
            wr.append(accum_out)
            kw["accum_out"] = accum_out.ap
        self.op("act", lambda e: e.activation(out=o, in_=i, func=func, **kw), rd, wr)

    def tt(self, out, in0, in1, op, eng="dve"):
        o, a, b = out.ap, in0.ap, in1.ap
        self.op(eng, lambda e: e.tensor_tensor(out=o, in0=a, in1=b, op=op), [in0, in1], [out])

    def ts(self, out, in0, s1, op0, s2=None, op1=None, eng="dve", accum_out=None):
        o, a = out.ap, in0.ap
        rd = [in0]
        wr = [out]
        a1 = s1
        a2 = s2
        if _is_view(s1):
            rd.append(s1)
            a1 = s1.ap
        if _is_view(s2):
            rd.append(s2)
            a2 = s2.ap
        kw = {}
        if op1 is not None:
            kw["op1"] = op1
        if accum_out is not None:
            wr.append(accum_out)
            kw["accum_out"] = accum_out.ap
        self.op(eng, lambda e: e.tensor_scalar(out=o, in0=a, scalar1=a1, scalar2=a2, op0=op0, **kw), rd, wr)

    def stt(self, out, in0, scalar, in1, op0, op1):
        o, a, b = out.ap, in0.ap, in1.ap
        rd = [in0, in1]
        s = scalar
        if _is_view(scalar):
            rd.append(scalar)
            s = scalar.ap
        self.op("dve", lambda e: e.scalar_tensor_tensor(out=o, in0=a, scalar=s, in1=b, op0=op0, op1=op1), rd, [out])

    def copy(self, out, in_, eng="dve"):
        o, i = out.ap, in_.ap
        if eng == "act":
            self.op("act", lambda e: e.copy(out=o, in_=i), [in_], [out])
        else:
            self.op(eng, lambda e: e.tensor_copy(out=o, in_=i), [in_], [out])

    def memset(self, out, val, eng="pool"):
        o = out.ap
        self.op(eng, lambda e: e.memset(o, val), [], [out])

    def recip(self, out, in_):
        o, i = out.ap, in_.ap
        self.op("dve", lambda e: e.reciprocal(out=o, in_=i), [in_], [out])

    def scan(self, out, d0, d1, initial, op0=ALU.mult, op1=ALU.add):
        o, a, b = out.ap, d0.ap, d1.ap
        rd = [d0, d1]
        ini = initial
        if _is_view(initial):
            rd.append(initial)
            ini = initial.ap
        self.op("dve", lambda e: e.tensor_tensor_scan(out=o, data0=a, data1=b, initial=ini, op0=op0, op1=op1), rd, [out])

    def reduce(self, out, in_, op, axis=AX.X):
        o, i = out.ap, in_.ap
        self.op("dve", lambda e: e.tensor_reduce(out=o, in_=i, axis=axis, op=op), [in_], [out])

    def emit(self):
        nc = self.nc
        lists = self.lists
        cum = {}
        for eng in COMPUTE:
            c = 0
            arr = []
            for ent in self.cinst[eng]:
                if ent[4]:
                    c += 1
                arr.append(c)
            cum[eng] = arr
        self.n_marked = {e: (cum[e][-1] if cum[e] else 0) for e in COMPUTE}
        psem = self.psem

        def run(e, items):
            for ent in items:
                waits, fn, kind, who, mark = ent
                for ev in waits:
                    if ev[0] == "c":
                        e.wait_ge(psem[ev[1]], cum[ev[1]][ev[2] - 1])
                    else:
                        e.wait_ge(ev[1], ev[2])
                if fn is None:
                    continue
                if kind == "d":
                    fn(e).then_inc(who, 16)
                elif mark:
                    fn(e).then_inc(psem[who], 1)
                else:
                    fn(e)

        with nc.Block() as block:
            @block.tensor
            def _(e):
                run(e, lists["pe"])

            @block.scalar
            def _(e):
                run(e, lists["act"])

            @block.vector
            def _(e):
                run(e, lists["dve"])

            @block.gpsimd
            def _(e):
                run(e, lists["pool"])

            @block.sync
            def _(e):
                run(e, lists["sp"])


EPS = 1e-6
D = 4096
NDT = 32


def col_groups(Tc, gmax=1024):
    groups = []
    s = 0
    while s < Tc:
        e = min(s + gmax, Tc)
        if 0 < Tc - e < 64:
            e = Tc
        groups.append((s, e))
        s = e
    return groups


def stage_glu(k, g_all, sg_all, wglu_l, ybT, Tc, pfx="gl"):
    GW = 1040
    gb = k.sb([128, 16, GW], BF16, pfx + "gb")
    sgb = k.sb([128, 16, GW], BF16, pfx + "sgb")
    gst = [k.sb([128, 2048], F32, pfx + f"gst{i}") for i in range(3)]
    gwb = [k.sb([128, 2048], BF16, pfx + f"gwb{i}") for i in range(2)]
    sig = [k.sb([128, 512], F32, pfx + f"sig{i}") for i in range(3)]
    yo = [k.sb([128, 512], BF16, pfx + f"yo{i}") for i in range(3)]
    acc = [k.ps([128, 512], F32, pfx + f"acc{i}") for i in range(4)]
    gv = g_all.rearrange("(kt p) t -> p kt t", p=128)
    sgv = sg_all.rearrange("(kt p) t -> p kt t", p=128)
    gcnt = 0
    u = 0
    for (c0, c1) in col_groups(Tc, 1024):
        gw = c1 - c0
        chunks = [(s, min(s + 512, c1)) for s in range(c0, c1, 512)]
        for kt0 in range(0, 16, 8):
            k.dma("sp", gb[:, kt0:kt0 + 8, 0:gw], gv[:, kt0:kt0 + 8, c0:c1])
            k.dma("sp", sgb[:, kt0:kt0 + 8, 0:gw], sgv[:, kt0:kt0 + 8, c0:c1])
        for mt in range(16):
            gs, gw_ = gst[gcnt % 3], gwb[gcnt % 2]
            k.dma("act", gs.v, wglu_l[mt])
            k.copy(gw_.v, gs.v, eng="pool")
            gcnt += 1
            for ci, (s0, s1) in enumerate(chunks):
                n = s1 - s0
                ac = acc[u % 4]
                for kt in range(16):
                    k.matmul(ac[:, 0:n], gw_[:, kt * 128:(kt + 1) * 128], gb[:, kt, s0 - c0:s1 - c0], start=(kt == 0), stop=(kt == 15))
                sg_ = sig[u % 3]
                y_ = yo[u % 3]
                k.act(sg_[:, 0:n], ac[:, 0:n], AF.Sigmoid)
                k.tt(sg_[:, 0:n], sg_[:, 0:n], gb[:, mt, s0 - c0:s1 - c0], ALU.mult)
                k.tt(y_[:, 0:n], sg_[:, 0:n], sgb[:, mt, s0 - c0:s1 - c0], ALU.mult, eng="pool")
                k.dma("sp", ybT[mt * 128:(mt + 1) * 128, s0:s1], y_[:, 0:n])
                u += 1


def stage_out(k, hT, yT, wl, g_l, hT_new, hnT, Tc, nkt, first, out_dt, pfx="o", glu=None):
    ones = k.sb([128, 128], F32, pfx + "ones")
    k.memset(ones.v, 1.0)
    gt = k.sb([128, NDT], F32, pfx + "g")
    k.dma("sp", gt.v, g_l.v)
    GW = 1040
    GMAX = 1024
    if not first:
        yb = k.sb([128, nkt, GW], BF16, pfx + "yb")
        KH = nkt // 2
        NST = 3 if nkt > 32 else 4
        wst = [k.sb([128, KH * 128], F32, pfx + f"wst{i}") for i in range(NST)]
        wbf = [k.sb([128, nkt * 128], BF16, pfx + f"wbf{i}") for i in range(2)]
        acc = [k.ps([128, 512], F32, pfx + f"acc{i}") for i in range(4)]
        yTv = yT.rearrange("(kt p) t -> p kt t", p=128)
    ssq = [k.ps([128, 512], F32, pfx + f"ssq{i}") for i in range(3)]
    hin = [k.sb([128, 512], F32, pfx + f"hin{i}") for i in range(3)]
    hnw = [k.sb([128, 512], F32, pfx + f"hnw{i}") for i in range(3)]
    sq = [k.sb([128, 512], F32, pfx + f"sq{i}") for i in range(3)]
    rstd = k.sb([128, GW], F32, pfx + "rstd")
    hno = [k.sb([128, 512], out_dt, pfx + f"hno{i}") for i in range(3)]
    hsrc = hT if first else hT_new
    u = 0
    wcnt = 0
    if glu is not None:
        ybv = glu.rearrange("(kt p) t -> p kt t", p=128)
    for (c0, c1) in col_groups(Tc, GMAX):
        gw = c1 - c0
        chunks = [(s, min(s + 512, c1)) for s in range(c0, c1, 512)]
        assert len(chunks) <= 3 and gw <= GW
        if not first:
            nkt_y = nkt - 16 if glu is not None else nkt
            for kt0 in range(0, nkt_y, 8):
                k.dma("sp", yb[:, kt0:kt0 + 8, 0:gw], yTv[:, kt0:kt0 + 8, c0:c1])
        if glu is not None:
            for kt0 in range(0, 16, 8):
                k.dma("sp", yb[:, 32 + kt0:32 + kt0 + 8, 0:gw], ybv[:, kt0:kt0 + 8, c0:c1])
        pend = None
        for d in range(NDT):
            if not first:
                wb = wbf[wcnt % 2]
                for hh in range(2):
                    ws = wst[(2 * wcnt + hh) % NST]
                    k.dma("act" if hh == 0 else "sp", ws.v, wl[d, :, hh * KH * 128:(hh + 1) * KH * 128])
                    k.copy(wb[:, hh * KH * 128:(hh + 1) * KH * 128], ws.v, eng="pool")
                wcnt += 1
            for ci, (s0, s1) in enumerate(chunks):
                n = s1 - s0
                hi = hin[u % 3]
                hw = hnw[u % 3]
                sqt = sq[u % 3]
                k.dma("sp", hi[:, 0:n], hT[d * 128:(d + 1) * 128, s0:s1])
                if not first:
                    ac = acc[u % 4]
                    for kt in range(nkt):
                        k.matmul(ac[:, 0:n], wb[:, kt * 128:(kt + 1) * 128], yb[:, kt, s0 - c0:s1 - c0],
                                 start=(kt == 0), stop=(kt == nkt - 1))
                    k.tt(hw[:, 0:n], ac[:, 0:n], hi[:, 0:n], ALU.add)
                    k.dma("sp", hT_new[d * 128:(d + 1) * 128, s0:s1], hw[:, 0:n])
                    src = hw
                else:
                    src = hi
                k.act(sqt[:, 0:n], src[:, 0:n], AF.Square)
                if pend is not None:
                    k.matmul(*pend[0], **pend[1])
                pend = ((ssq[ci][:, 0:n], ones.v, sqt[:, 0:n]), dict(start=(d == 0), stop=(d == NDT - 1)))
                u += 1
        if pend is not None:
            k.matmul(*pend[0], **pend[1])
            pend = None
        for ci, (s0, s1) in enumerate(chunks):
            n = s1 - s0
            k.ts(rstd[:, s0 - c0:s1 - c0], ssq[ci][:, 0:n], 1.0 / D, ALU.mult, EPS, ALU.add)
            k.act(rstd[:, s0 - c0:s1 - c0], rstd[:, s0 - c0:s1 - c0], AF.Sqrt)
            k.recip(rstd[:, s0 - c0:s1 - c0], rstd[:, s0 - c0:s1 - c0])
        for d in range(NDT):
            for ci, (s0, s1) in enumerate(chunks):
                n = s1 - s0
                hi = hin[u % 3]
                ho = hno[u % 3]
                k.dma("sp", hi[:, 0:n], hsrc[d * 128:(d + 1) * 128, s0:s1])
                k.stt(ho[:, 0:n], hi[:, 0:n], gt[:, d:d + 1], rstd[:, s0 - c0:s1 - c0], ALU.mult, ALU.mult)
                k.dma("act", hnT[d * 128:(d + 1) * 128, s0:s1], ho[:, 0:n])
                u += 1

DBG_NB = int(os.environ.get('DBG_NB', '0'))
DBG_SKIP = os.environ.get('DBG_SKIP', '')
DBG_START = int(os.environ.get('DBG_START', '0'))

EPS = 1e-6
NH = 4
QSCALE = 192 ** -0.5


def tok_blocks(T, bs=512):
    assert (T - 16) % bs == 0
    return [(0, 16)] + [(s, s + bs) for s in range(16, T, bs)]


def load_cast(k, dst, src_dram, nkt, ncols, stg, q="act", ceng="pool"):
    if 'lc' in DBG_SKIP:
        k.memset(dst.v, 0.01)
        return
    for kt in range(nkt):
        s = stg[kt % len(stg)]
        k.dma(q, s[:, 0:ncols], src_dram[:, kt, :])
        k.copy(dst[:, kt, :], s[:, 0:ncols], eng=ceng)


def rstd_from_ssq(k, rstd, ssq, n, dim):
    k.ts(rstd[:, 0:n], ssq[:, 0:n], 1.0 / dim, ALU.mult, EPS, ALU.add)
    k.act(rstd[:, 0:n], rstd[:, 0:n], AF.Sqrt)
    k.recip(rstd[:, 0:n], rstd[:, 0:n])


BS_A = 256


def _a_common(k, pfx, hb=True):
    ones = k.sb([128, 128], BF16, pfx + "ones")
    k.memset(ones.v, 1.0)
    stg = [k.sb([128, 1152], F32, pfx + f"stg{i}") for i in range(2)]
    hb = [k.sb([128, 32, BS_A], BF16, pfx + f"hb{i}") for i in range(2)] if hb else None
    cst = [k.sb([128, BS_A], F32, pfx + f"cos{i}") for i in range(2)]
    snt = [k.sb([128, BS_A], F32, pfx + f"sin{i}") for i in range(2)]
    rstd = k.sb([128, BS_A], F32, pfx + "rstd")
    acc = [k.ps([128, 512], F32, pfx + f"acc{i}") for i in range(3)]
    ssq = k.ps([128, 512], F32, pfx + "ssq")
    up = [k.ps([128, 512], F32, pfx + f"up{i}") for i in range(3)]
    sqb = [k.sb([128, BS_A], BF16, pfx + f"sqb{i}") for i in range(2)]
    ra = [k.sb([128, BS_A], F32, pfx + f"ra{i}") for i in range(2)]
    rb = [k.sb([128, BS_A], F32, pfx + f"rb{i}") for i in range(2)]
    ob = [k.sb([128, 512], BF16, pfx + f"ob{i}") for i in range(4)]
    return ones, stg, hb, cst, snt, rstd, acc, ssq, up, sqb, ra, rb, ob


def mla_a1(k, T, hn_meta, hn_own, wq_in, wuq, gq, cos2, sin2s, qnT, qrT, pfx="m1"):
    blocks = tok_blocks(T, BS_A)
    hbm = k.sb([128, 32, 16], BF16, pfx + "hbm")
    ones, stg, hb, cst, snt, rstd, acc, ssq, up, sqb, ra, rb, ob = _a_common(k, pfx)
    oc = [0]

    def nob():
        oc[0] += 1
        return ob[oc[0] % 4]

    w1 = k.sb([128, 32, 1024], BF16, pfx + "w1")
    load_cast(k, w1, wq_in, 32, 1024, stg)
    wu = k.sb([128, 8, NH * 256], BF16, pfx + "wu")
    load_cast(k, wu, wuq, 8, NH * 256, stg)
    gqt = k.sb([128, 8], F32, pfx + "gq")
    if 'gq' not in DBG_SKIP:
        k.dma("sp", gqt.v, gq.v)
    cq = k.sb([128, 8, BS_A], F32, pfx + "cq")
    cqn = k.sb([128, 8, BS_A], BF16, pfx + "cqn")
    for bi, (t0, t1) in enumerate(blocks):
        if DBG_NB and bi >= DBG_NB:
            break
        if bi < DBG_START:
            continue
        n = t1 - t0
        if bi == 0:
            h = hbm
            k.dma("sp", h.v, hn_meta.v)
        else:
            h = hb[bi % 2]
            k.dma(os.environ.get("HQ", "sp"), h.v, hn_own[bi - 1])
        ct, sn = cst[bi % 2], snt[bi % 2]
        if 'cs' not in DBG_SKIP:
            _q = os.environ.get("CSQ", "sp")
            _o = 0 if os.environ.get("CS0") else t0
            k.dma(_q, ct[0:64, 0:n], cos2[:, _o:_o + n])
            k.dma(_q, sn[0:64, 0:n], sin2s[:, _o:_o + n])
        for m in range(8):
            a = acc[m % 3]
            for kt in range(32):
                k.matmul(a[:, 0:n], w1[:, kt, m * 128:(m + 1) * 128], h[:, kt, 0:n], start=(kt == 0), stop=(kt == 31))
            sq = sqb[m % 2]
            if 'sq' not in DBG_SKIP:
                k.act(sq[:, 0:n], a[:, 0:n], AF.Square)
            if 'cp' not in DBG_SKIP:
                k.copy(cq[:, m, 0:n], a[:, 0:n], eng=os.environ.get("CPENG","dve"))
            if 'ssq' not in DBG_SKIP:
                k.matmul(ssq[:, 0:n], ones.v, sq[:, 0:n], start=(m == 0), stop=(m == 7))
        if 'rstd' not in DBG_SKIP:
            rstd_from_ssq(k, rstd, ssq, n, 1024)
        for m in range(8):
            if 'stt' in DBG_SKIP:
                break
            k.stt(cqn[:, m, 0:n], cq[:, m, 0:n], gqt[:, m:m + 1], rstd[:, 0:n], ALU.mult, ALU.mult)
        for hd in range(NH):
            if 'up' in DBG_SKIP:
                break
            c0 = hd * 256
            u0 = up[0]
            for kt in range(8):
                k.matmul(u0[:, 0:n], wu[:, kt, c0:c0 + 128], cqn[:, kt, 0:n], start=(kt == 0), stop=(kt == 7))
            o = nob()
            k.act(o[:, 0:n], u0[:, 0:n], AF.Copy, scale=QSCALE)
            k.dma("act", qnT[hd, :, t0:t1], o[:, 0:n])
            u1, u2 = up[1], up[2]
            for kt in range(8):
                k.matmul(u1[0:64, 0:n], wu[:, kt, c0 + 128:c0 + 192], cqn[:, kt, 0:n], start=(kt == 0), stop=(kt == 7))
            for kt in range(8):
                k.matmul(u2[0:64, 0:n], wu[:, kt, c0 + 192:c0 + 256], cqn[:, kt, 0:n], start=(kt == 0), stop=(kt == 7))
            a_, b_ = ra[hd % 2], rb[hd % 2]
            k.tt(a_[0:64, 0:n], u1[0:64, 0:n], ct[0:64, 0:n], ALU.mult)
            k.tt(b_[0:64, 0:n], u2[0:64, 0:n], sn[0:64, 0:n], ALU.mult)
            o = nob()
            k.tt(a_[0:64, 0:n], a_[0:64, 0:n], b_[0:64, 0:n], ALU.add, eng="pool")
            k.act(o[0:64, 0:n], a_[0:64, 0:n], AF.Copy, scale=QSCALE)
            k.dma("act", qrT[hd, :, t0:t1], o[0:64, 0:n])


def mla_a2(k, T, hn_meta, hn_own, wkv_in, wukv, gkv, cos2, sin2s, knT, krT, vtok, gT, pfx="m2"):
    blocks = tok_blocks(T, BS_A)
    hbm = k.sb([128, 32, 16], BF16, pfx + "hbm")
    ones, stg, hb, cst, snt, rstd, acc, ssq, up, sqb, ra, rb, ob = _a_common(k, pfx)
    oc = [0]

    def nob():
        oc[0] += 1
        return ob[oc[0] % 4]

    w2 = k.sb([128, 32, 1152], BF16, pfx + "w2")
    load_cast(k, w2, wkv_in, 32, 1152, stg)
    wk = k.sb([128, 4, 1024], BF16, pfx + "wk")
    load_cast(k, wk, wukv, 4, 1024, stg)
    gkt = k.sb([128, 4], F32, pfx + "gk")
    k.dma("sp", gkt.v, gkv.v)
    ckv = k.sb([128, 4, BS_A], F32, pfx + "ckv")
    ckn = k.sb([128, 4, BS_A], BF16, pfx + "ckn")
    for bi, (t0, t1) in enumerate(blocks):
        n = t1 - t0
        if bi == 0:
            h = hbm
            k.dma("sp", h.v, hn_meta.v)
        else:
            h = hb[bi % 2]
            k.dma(os.environ.get("HQ", "sp"), h.v, hn_own[bi - 1])
        ct, sn = cst[bi % 2], snt[bi % 2]
        k.dma("sp", ct[0:64, 0:n], cos2[:, t0:t1])
        k.dma("sp", sn[0:64, 0:n], sin2s[:, t0:t1])
        for m in range(4):
            a = acc[m % 3]
            for kt in range(32):
                k.matmul(a[:, 0:n], w2[:, kt, m * 128:(m + 1) * 128], h[:, kt, 0:n], start=(kt == 0), stop=(kt == 31))
            sq = sqb[m % 2]
            k.act(sq[:, 0:n], a[:, 0:n], AF.Square)
            k.copy(ckv[:, m, 0:n], a[:, 0:n], eng="dve")
            k.matmul(ssq[:, 0:n], ones.v, sq[:, 0:n], start=(m == 0), stop=(m == 3))
        rstd_from_ssq(k, rstd, ssq, n, 512)
        for m in range(4):
            k.stt(ckn[:, m, 0:n], ckv[:, m, 0:n], gkt[:, m:m + 1], rstd[:, 0:n], ALU.mult, ALU.mult)
        u1, u2 = up[1], up[2]
        for kt in range(32):
            k.matmul(u1[0:64, 0:n], w2[:, kt, 512:576], h[:, kt, 0:n], start=(kt == 0), stop=(kt == 31))
        for kt in range(32):
            k.matmul(u2[0:64, 0:n], w2[:, kt, 576:640], h[:, kt, 0:n], start=(kt == 0), stop=(kt == 31))
        a_, b_ = ra[0], rb[0]
        k.tt(a_[0:64, 0:n], u1[0:64, 0:n], ct[0:64, 0:n], ALU.mult)
        k.tt(b_[0:64, 0:n], u2[0:64, 0:n], sn[0:64, 0:n], ALU.mult)
        o = nob()
        k.tt(o[0:64, 0:n], a_[0:64, 0:n], b_[0:64, 0:n], ALU.add)
        k.dma("act", krT[:, t0:t1], o[0:64, 0:n])
        for m in range(4):
            a = acc[m % 3]
            for kt in range(32):
                k.matmul(a[:, 0:n], w2[:, kt, 640 + m * 128:640 + (m + 1) * 128], h[:, kt, 0:n], start=(kt == 0), stop=(kt == 31))
            o = nob()
            k.act(o[:, 0:n], a[:, 0:n], AF.Silu)
            k.dma("act", gT[m * 128:(m + 1) * 128, t0:t1], o[:, 0:n])
        for hd in range(NH):
            u0 = up[0]
            for kt in range(4):
                k.matmul(u0[:, 0:n], wk[:, kt, hd * 128:(hd + 1) * 128], ckn[:, kt, 0:n], start=(kt == 0), stop=(kt == 3))
            o = nob()
            k.copy(o[:, 0:n], u0[:, 0:n], eng="act")
            k.dma("act", knT[hd, :, t0:t1], o[:, 0:n])
        for s0 in range(0, n, 128):
            ns = min(128, n - s0)
            a = acc[(s0 // 128) % 3]
            for kt in range(4):
                k.matmul(a[0:ns, :], ckn[:, kt, s0:s0 + ns], wk[:, kt, 512:1024], start=(kt == 0), stop=(kt == 3))
            o = nob()
            k.copy(o[0:ns, :], a[0:ns, :], eng="dve")
            kb = 0 if bi == 0 else 1 + (t0 + s0 - 16) // 128
            for hd in range(NH):
                k.dma("act", vtok[hd, 0:ns, kb, :], o[0:ns, hd * 128:(hd + 1) * 128])


def mla_pre(k, Tc, hnT, wq_in, wkv3, gq, gkv, cos2c, sin2sc, cqnT, ckvnT, krT, pfx="mp"):
    blocks = tok_blocks(Tc, BS_A)
    hv = hnT.rearrange("(kt p) t -> p kt t", p=128)
    ones, stg, hb, cst, snt, rstd, acc, ssq, up, sqb, ra, rb, ob = _a_common(k, pfx)
    oc = [0]

    def nob():
        oc[0] += 1
        return ob[oc[0] % 4]

    w1 = k.sb([128, 32, 1024], BF16, pfx + "w1")
    load_cast(k, w1, wq_in, 32, 1024, stg)
    w2 = k.sb([128, 32, 640], BF16, pfx + "w2")
    load_cast(k, w2, wkv3, 32, 640, stg)
    gqt = k.sb([128, 8], F32, pfx + "gq")
    gkt = k.sb([128, 4], F32, pfx + "gk")
    k.dma("sp", gqt.v, gq.v)
    k.dma("sp", gkt.v, gkv.v)
    cq = k.sb([128, 8, BS_A], F32, pfx + "cq")
    for bi, (t0, t1) in enumerate(blocks):
        n = t1 - t0
        h = hb[bi % 2]
        for kt0 in range(0, 32, 8):
            k.dma("sp", h[:, kt0:kt0 + 8, 0:n], hv[:, kt0:kt0 + 8, t0:t1])
        ct, sn = cst[bi % 2], snt[bi % 2]
        k.dma("sp", ct[0:64, 0:n], cos2c[:, t0:t1])
        k.dma("sp", sn[0:64, 0:n], sin2sc[:, t0:t1])
        for (wt, nm, gtile, dim, dst) in ((w1, 8, gqt, 1024, cqnT), (w2, 4, gkt, 512, ckvnT)):
            for m in range(nm):
                a = acc[m % 3]
                for kt in range(32):
                    k.matmul(a[:, 0:n], wt[:, kt, m * 128:(m + 1) * 128], h[:, kt, 0:n], start=(kt == 0), stop=(kt == 31))
                sq = sqb[m % 2]
                k.act(sq[:, 0:n], a[:, 0:n], AF.Square)
                k.copy(cq[:, m, 0:n], a[:, 0:n], eng="dve")
                k.matmul(ssq[:, 0:n], ones.v, sq[:, 0:n], start=(m == 0), stop=(m == nm - 1))
            rstd_from_ssq(k, rstd, ssq, n, dim)
            for m in range(nm):
                o = nob()
                k.stt(o[:, 0:n], cq[:, m, 0:n], gtile[:, m:m + 1], rstd[:, 0:n], ALU.mult, ALU.mult)
                k.dma("act", dst[m * 128:(m + 1) * 128, t0:t1], o[:, 0:n])
        u1, u2 = up[1], up[2]
        for kt in range(32):
            k.matmul(u1[0:64, 0:n], w2[:, kt, 512:576], h[:, kt, 0:n], start=(kt == 0), stop=(kt == 31))
        for kt in range(32):
            k.matmul(u2[0:64, 0:n], w2[:, kt, 576:640], h[:, kt, 0:n], start=(kt == 0), stop=(kt == 31))
        a_, b_ = ra[0], rb[0]
        k.tt(a_[0:64, 0:n], u1[0:64, 0:n], ct[0:64, 0:n], ALU.mult)
        k.tt(b_[0:64, 0:n], u2[0:64, 0:n], sn[0:64, 0:n], ALU.mult)
        o = nob()
        k.tt(o[0:64, 0:n], a_[0:64, 0:n], b_[0:64, 0:n], ALU.add)
        k.dma("act", krT[:, t0:t1], o[0:64, 0:n])


def mla_a1p(k, T, cq_meta, cq_own, wuq, cos2, sin2s, qnT, qrT, pfx="m1"):
    blocks = tok_blocks(T, BS_A)
    ones, stg, hb_unused, cst, snt, rstd, acc, ssq, up, sqb, ra, rb, ob = _a_common(k, pfx, hb=False)
    oc = [0]

    def nob():
        oc[0] += 1
        return ob[oc[0] % 4]

    wu = k.sb([128, 8, NH * 256], BF16, pfx + "wu")
    load_cast(k, wu, wuq, 8, NH * 256, stg)
    cqm = k.sb([128, 8, 16], BF16, pfx + "cqm")
    cqb = [k.sb([128, 8, BS_A], BF16, pfx + f"cqb{i}") for i in range(3)]
    for bi, (t0, t1) in enumerate(blocks):
        n = t1 - t0
        if bi == 0:
            cqn = cqm
            k.dma("sp", cqn.v, cq_meta.v)
        else:
            cqn = cqb[bi % 3]
            k.dma("sp", cqn.v, cq_own[bi - 1])
        ct, sn = cst[bi % 2], snt[bi % 2]
        k.dma("sp", ct[0:64, 0:n], cos2[:, t0:t1])
        k.dma("sp", sn[0:64, 0:n], sin2s[:, t0:t1])
        for hd in range(NH):
            c0 = hd * 256
            u0 = up[0] if hd % 2 == 0 else acc[0]
            for kt in range(8):
                k.matmul(u0[:, 0:n], wu[:, kt, c0:c0 + 128], cqn[:, kt, 0:n], start=(kt == 0), stop=(kt == 7))
            o = nob()
            k.act(o[:, 0:n], u0[:, 0:n], AF.Copy, scale=QSCALE)
            k.dma("act", qnT[hd, :, t0:t1], o[:, 0:n])
            u1, u2 = (up[1], up[2]) if hd % 2 == 0 else (acc[1], acc[2])
            for kt in range(8):
                k.matmul(u1[0:64, 0:n], wu[:, kt, c0 + 128:c0 + 192], cqn[:, kt, 0:n], start=(kt == 0), stop=(kt == 7))
            for kt in range(8):
                k.matmul(u2[0:64, 0:n], wu[:, kt, c0 + 192:c0 + 256], cqn[:, kt, 0:n], start=(kt == 0), stop=(kt == 7))
            a_, b_ = ra[hd % 2], rb[hd % 2]
            k.tt(a_[0:64, 0:n], u1[0:64, 0:n], ct[0:64, 0:n], ALU.mult)
            k.tt(b_[0:64, 0:n], u2[0:64, 0:n], sn[0:64, 0:n], ALU.mult)
            o = nob()
            k.tt(a_[0:64, 0:n], a_[0:64, 0:n], b_[0:64, 0:n], ALU.add, eng="pool")
            k.act(o[0:64, 0:n], a_[0:64, 0:n], AF.Copy, scale=QSCALE)
            k.dma("act", qrT[hd, :, t0:t1], o[0:64, 0:n])


def mla_a2p(k, T, hn_meta, hn_own, ck_meta, ck_own, wgate, wukv, knT, vtok, gT, pfx="m2"):
    blocks = tok_blocks(T, BS_A)
    ones, stg, hb, cst, snt, rstd, acc, ssq, up, sqb, ra, rb, ob = _a_common(k, pfx)
    hbm = k.sb([128, 32, 16], BF16, pfx + "hbm")
    oc = [0]

    def nob():
        oc[0] += 1
        return ob[oc[0] % 4]

    w2 = k.sb([128, 32, 512], BF16, pfx + "w2")
    load_cast(k, w2, wgate, 32, 512, stg)
    wk = k.sb([128, 4, 1024], BF16, pfx + "wk")
    load_cast(k, wk, wukv, 4, 1024, stg)
    ckm = k.sb([128, 4, 16], BF16, pfx + "ckm")
    ckb = [k.sb([128, 4, BS_A], BF16, pfx + f"ckb{i}") for i in range(3)]
    for bi, (t0, t1) in enumerate(blocks):
        n = t1 - t0
        if bi == 0:
            h, ckn = hbm, ckm
            k.dma("sp", h.v, hn_meta.v)
            k.dma("sp", ckn.v, ck_meta.v)
        else:
            h, ckn = hb[bi % 2], ckb[bi % 3]
            k.dma("sp", h.v, hn_own[bi - 1])
            k.dma("sp", ckn.v, ck_own[bi - 1])
        for m in range(4):
            a = acc[m % 3]
            for kt in range(32):
                k.matmul(a[:, 0:n], w2[:, kt, m * 128:(m + 1) * 128], h[:, kt, 0:n], start=(kt == 0), stop=(kt == 31))
            o = nob()
            k.act(o[:, 0:n], a[:, 0:n], AF.Silu)
            k.dma("act", gT[m * 128:(m + 1) * 128, t0:t1], o[:, 0:n])
        for hd in range(NH):
            u0 = up[hd % 3]
            for kt in range(4):
                k.matmul(u0[:, 0:n], wk[:, kt, hd * 128:(hd + 1) * 128], ckn[:, kt, 0:n], start=(kt == 0), stop=(kt == 3))
            o = nob()
            k.copy(o[:, 0:n], u0[:, 0:n], eng=("act" if hd % 2 else "dve"))
            k.dma("act", knT[hd, :, t0:t1], o[:, 0:n])
        for s0 in range(0, n, 128):
            ns = min(128, n - s0)
            a = acc[(s0 // 128) % 3]
            for kt in range(4):
                k.matmul(a[0:ns, :], ckn[:, kt, s0:s0 + ns], wk[:, kt, 512:1024], start=(kt == 0), stop=(kt == 3))
            o = nob()
            k.copy(o[0:ns, :], a[0:ns, :], eng="dve")
            kb = 0 if bi == 0 else 1 + (t0 + s0 - 16) // 128
            for hd in range(NH):
                k.dma("act", vtok[hd, 0:ns, kb, :], o[0:ns, hd * 128:(hd + 1) * 128])


def mla_phase_b(k, T, qnT, qrT, knT, krT, vtok, gT, yT, pfx="mb"):
    NB = (T - 16) // 512
    NKB = (T - 16) // 128
    onesf = k.sb([128, 128], F32, pfx + "onesf")
    k.memset(onesf.v, 1.0)
    kr = k.sb([128, T], BF16, pfx + "kr")
    k.dma("sp", kr[0:64, :], krT.v)
    kn = k.sb([128, T], BF16, pfx + "kn")
    vv = k.sb([128, NKB + 1, 128], BF16, pfx + "vv")
    qn = [k.sb([128, 512], BF16, pfx + f"qn{i}") for i in range(2)]
    qr = [k.sb([128, 512], BF16, pfx + f"qr{i}") for i in range(2)]
    gt = [k.sb([128, 512], BF16, pfx + f"gt{i}") for i in range(2)]
    pt = [k.sb([128, 512], BF16, pfx + f"pt{i}") for i in range(6)]
    sc = [k.ps([128, 512], F32, pfx + f"sc{i}") for i in range(4)]
    oT = [k.ps([128, 512], F32, pfx + f"oT{i}") for i in range(2)]
    dn = k.ps([128, 512], F32, pfx + "dn")
    dacc = [k.sb([128, 512], F32, pfx + f"dacc{i}") for i in range(2)]
    rden = [k.sb([128, 512], F32, pfx + f"rden{i}") for i in range(2)]
    yo = [k.sb([128, 512], F32, pfx + f"yo{i}") for i in range(2)]
    yb = [k.sb([128, 512], BF16, pfx + f"yb{i}") for i in range(2)]
    u = 0
    g = 0
    for hd in range(NH):
        k.dma("sp", kn.v, knT[hd, :, :])
        k.dma("act", vv.v, vtok[hd])
        groups = [(0, 16, -1)] + [(16 + 512 * i, 16 + 512 * (i + 1), i) for i in range(NB)]
        for (q0, q1, gi) in groups:
            n = q1 - q0
            qnt, qrt, gtt = qn[g % 2], qr[g % 2], gt[g % 2]
            o_ = oT[g % 2]
            k.dma("sp", qnt[:, 0:n], qnT[hd, :, q0:q1])
            k.dma("sp", qrt[0:64, 0:n], qrT[hd, :, q0:q1])
            k.dma("sp", gtt[:, 0:n], gT[hd * 128:(hd + 1) * 128, q0:q1])
            k.memset(dacc[0][:, 0:n], 0.0, eng="dve")
            k.memset(dacc[1][:, 0:n], 0.0, eng="pool")
            kbs = [(0, 16, 0, 0, False)]
            if gi >= 0:
                for j in range(4 * gi):
                    kbs.append((16 + 128 * j, 128, 1 + j, 0, False))
                for dgi in range(4):
                    j = 4 * gi + dgi
                    kbs.append((16 + 128 * j, 128, 1 + j, 128 * dgi, True))

            def scores(idx, uu):
                kc, nk, vb, qs, diag = kbs[idx]
                s_ = sc[uu % 4]
                k.matmul(s_[0:nk, qs:n], kn[:, kc:kc + nk], qnt[:, qs:n], start=True, stop=False)
                k.matmul(s_[0:nk, qs:n], kr[0:64, kc:kc + nk], qrt[0:64, qs:n], start=False, stop=True)

            scores(0, u)
            if len(kbs) > 1:
                scores(1, u + 1)
            for idx, (kc, nk, vb, qs, diag) in enumerate(kbs):
                if idx + 2 < len(kbs):
                    scores(idx + 2, u + 2)
                s_ = sc[u % 4]
                p_ = pt[u % 6]
                k.act(p_[0:nk, qs:n], s_[0:nk, qs:n], AF.Exp)
                if diag:
                    k.memset(p_[64:128, qs:qs + 64], 0.0, eng="pool")
                last = (idx == len(kbs) - 1)
                k.matmul(o_[:, qs:n], vv[0:nk, vb, :], p_[0:nk, qs:n], start=(idx == 0), stop=last)
                da = dacc[u % 2]
                k.tt(da[0:nk, qs:n], da[0:nk, qs:n], p_[0:nk, qs:n], ALU.add, eng=("dve" if u % 2 == 0 else "pool"))
                u += 1
            k.matmul(dn[:, 0:n], onesf.v, dacc[0][:, 0:n], start=True, stop=False)
            k.matmul(dn[:, 0:n], onesf.v, dacc[1][:, 0:n], start=False, stop=True)
            rd, y1, y2 = rden[g % 2], yo[g % 2], yb[g % 2]
            k.recip(rd[:, 0:n], dn[:, 0:n])
            k.tt(y1[:, 0:n], o_[:, 0:n], rd[:, 0:n], ALU.mult)
            k.tt(y2[:, 0:n], y1[:, 0:n], gtt[:, 0:n], ALU.mult, eng="pool")
            k.dma("act", yT[hd * 128:(hd + 1) * 128, q0:q1], y2[:, 0:n])
            g += 1


def stage_mla(k, T, hn_meta, hn_own, wq_in, wkv_in, wuq, wukv, gq, gkv, cos2, sin2s, yT, pfx="ml", kind="Internal", phases="12b"):
    qnT = k.dram(pfx + "_qnT", [NH, 128, T], BF16, kind=kind)
    qrT = k.dram(pfx + "_qrT", [NH, 64, T], BF16, kind=kind)
    knT = k.dram(pfx + "_knT", [NH, 128, T], BF16, kind=kind)
    krT = k.dram(pfx + "_krT", [64, T], BF16, kind=kind)
    vtok = k.dram(pfx + "_vtok", [NH, 128, (T - 16) // 128 + 1, 128], BF16, kind=kind)
    gT = k.dram(pfx + "_gT", [512, T], BF16, kind=kind)
    if "1" in phases:
      with (contextlib.nullcontext() if os.environ.get("NOSCOPE") else k.scope()):
        mla_a1(k, T, hn_meta, hn_own, wq_in, wuq, gq, cos2, sin2s, qnT, qrT, pfx + "1")
    if "2" in phases:
      with k.scope():
        mla_a2(k, T, hn_meta, hn_own, wkv_in, wukv, gkv, cos2, sin2s, knT, krT, vtok, gT, pfx + "2")
    if "b" in phases:
      with k.scope():
        mla_phase_b(k, T, qnT, qrT, knT, krT, vtok, gT, yT, pfx + "b")
    return dict(qnT=qnT, qrT=qrT, knT=knT, krT=krT, vtok=vtok, gT=gT)


def stage_mla2(k, T, hn_meta, hn_own, cq_meta, cq_own, ck_meta, ck_own, krT, wgate, wuq, wukv, cos2, sin2s, yT, pfx="ml"):
    qnT = k.dram(pfx + "_qnT", [NH, 128, T], BF16)
    qrT = k.dram(pfx + "_qrT", [NH, 64, T], BF16)
    knT = k.dram(pfx + "_knT", [NH, 128, T], BF16)
    vtok = k.dram(pfx + "_vtok", [NH, 128, (T - 16) // 128 + 1, 128], BF16)
    gT = k.dram(pfx + "_gT", [512, T], BF16)
    with k.scope():
        mla_a1p(k, T, cq_meta, cq_own, wuq, cos2, sin2s, qnT, qrT, pfx + "1")
    with k.scope():
        mla_a2p(k, T, hn_meta, hn_own, ck_meta, ck_own, wgate, wukv, knT, vtok, gT, pfx + "2")
    with k.scope():
        mla_phase_b(k, T, qnT, qrT, knT, krT, vtok, gT, yT, pfx + "b")


EPS = 1e-6
GELU_C = 1.5957691216057308


def hyb_ssd(k, T, hn_meta, hn_own, w_ssd, convw, convb, dtb_bc, alog_bc, d_bc, ng_bc, Umat, ident, yT, pfx="hs"):
    blocks = tok_blocks(T, BS_A)
    NW = 1288
    stg = [k.sb([128, NW], F32, pfx + f"stg{i}") for i in range(2)]
    w = k.sb([128, 32, NW], BF16, pfx + "w")
    load_cast(k, w, w_ssd, 32, NW, stg)
    hbm = k.sb([128, 32, 16], BF16, pfx + "hbm")
    hb = [k.sb([128, 32, BS_A], BF16, pfx + f"hb{i}") for i in range(2)]
    cw = k.sb([128, 6, 4], F32, pfx + "cw")
    cb = k.sb([128, 6], F32, pfx + "cb")
    k.dma("sp", cw.v, convw.v)
    k.dma("sp", cb.v, convb.v)
    dtb = k.sb([128, 8], F32, pfx + "dtb")
    aneg = k.sb([128, 8], F32, pfx + "aneg")
    dbc = k.sb([128, 512], F32, pfx + "dbc")
    ngb = k.sb([128, 512], F32, pfx + "ngb")
    U = k.sb([128, 128], F32, pfx + "U")
    idb = k.sb([128, 128], BF16, pfx + "idb")
    idf = k.sb([128, 128], F32, pfx + "idf")
    k.dma("sp", dtb.v, dtb_bc.v)
    k.dma("sp", aneg.v, alog_bc.v)
    k.dma("sp", dbc.v, d_bc.v)
    k.dma("sp", ngb.v, ng_bc.v)
    k.dma("sp", U.v, Umat.v)
    k.dma("sp", idf.v, ident.v)
    k.copy(idb.v, idf.v, eng="pool")
    k.act(aneg.v, aneg.v, AF.Exp)
    k.ts(aneg.v, aneg.v, -1.0, ALU.mult)
    ones = k.sb([128, 128], F32, pfx + "ones")
    k.memset(ones.v, 1.0)

    cin = [k.sb([128, 3 + BS_A], F32, pfx + f"cin{m}") for m in range(6)]
    for m in range(6):
        k.memset(cin[m].v, 0.0)
    cacc = [k.sb([128, BS_A], F32, pfx + f"cacc{i}") for i in range(2)]
    fT = [k.sb([128, BS_A], BF16, pfx + f"fT{m}") for m in range(6)]
    L_zs = [k.sb([128, 512], F32, pfx + f"zs{i}") for i in range(2)]
    L_ctk = [k.sb([128, 128], BF16, pfx + f"ctk{i}") for i in range(2)]
    L_dt = [k.sb([128, 8], F32, pfx + f"dt{i}") for i in range(2)]
    L_da = [k.sb([128, 8], F32, pfx + f"da{i}") for i in range(2)]
    L_dab = [k.sb([128, 8, 128], F32, pfx + f"dab{i}") for i in range(2)]
    L_acum = [k.sb([128, 8], F32, pfx + f"acum{i}") for i in range(2)]
    L_nacum = [k.sb([128, 8], F32, pfx + f"nacum{i}") for i in range(2)]
    L_aend = [k.sb([128, 8], F32, pfx + f"aend{i}") for i in range(2)]
    L_eend = [k.sb([128, 8], F32, pfx + f"eend{i}") for i in range(2)]
    L_eac = [k.sb([128, 8], F32, pfx + f"eac{i}") for i in range(2)]
    L_dte = [k.sb([128, 8], F32, pfx + f"dte{i}") for i in range(2)]
    L_xtok = [k.sb([128, 512], BF16, pfx + f"xtok{i}") for i in range(2)]
    L_btok = [k.sb([128, 128], BF16, pfx + f"btok{i}") for i in range(2)]
    L_xdt = [k.sb([128, 512], BF16, pfx + f"xdt{i}") for i in range(2)]
    L_xw = [k.sb([128, 512], BF16, pfx + f"xw{i}") for i in range(2)]
    L_segc = [k.sb([128, 8, 128], F32, pfx + f"segc{i}") for i in range(2)]
    L_cbm = [k.sb([128, 128], F32, pfx + f"cbm{i}") for i in range(2)]
    L_MT = [k.sb([128, 8, 128], BF16, pfx + f"MT{i}") for i in range(2)]
    S = k.sb([128, 512], F32, pfx + "S")
    Sb = k.sb([128, 512], BF16, pfx + "Sb")
    k.memset(S.v, 0.0)
    k.memset(Sb.v, 0.0)
    L_t1 = [k.sb([128, 512], F32, pfx + f"t1{i}") for i in range(2)]
    L_t2 = [k.sb([128, 512], F32, pfx + f"t2{i}") for i in range(2)]
    L_ssq = [k.sb([128, 1], F32, pfx + f"ssq{i}") for i in range(2)]
    L_yn = [k.sb([128, 512], BF16, pfx + f"yn{i}") for i in range(2)]
    yTs = [k.sb([128, 128], BF16, pfx + f"yTs{i}") for i in range(4)]

    accA = k.ps([128, 512], F32, pfx + "accA")
    accB = k.ps([128, 512], F32, pfx + "accB")
    misc = k.ps([128, 512], F32, pfx + "misc")
    AB = k.ps([128, 8, 128], F32, pfx + "AB")
    ydg = k.ps([128, 512], F32, pfx + "ydg")
    yof = k.ps([128, 512], F32, pfx + "yof")
    tr = k.ps([128, 512], BF16, pfx + "tr")
    accs = [accA, accB]

    def stageA(c):
        cl, cs, h, par = c['cl'], c['cs'], c['h'], c['par']
        zs = L_zs[par]
        dt = L_dt[par]
        da = L_da[par]
        dab = L_dab[par]
        acum = L_acum[par]
        nacum = L_nacum[par]
        aend = L_aend[par]
        eend = L_eend[par]
        eac = L_eac[par]
        dte = L_dte[par]
        xtok = L_xtok[par]
        btok = L_btok[par]
        xdt = L_xdt[par]
        xw = L_xw[par]
        segc = L_segc[par]
        cbm = L_cbm[par]
        MT = L_MT[par]
        t1 = L_t1[par]
        t2 = L_t2[par]
        ssq = L_ssq[par]
        yn = L_yn[par]
        ctk = L_ctk[par]
        for kt in range(32):
            k.matmul(accA[0:cl, :], h[:, kt, cs], w[:, kt, 0:512], start=(kt == 0), stop=(kt == 31))
        k.act(zs[0:cl, :], accA[0:cl, :], AF.Silu)
        for kt in range(32):
            k.matmul(misc[0:cl, 0:8], h[:, kt, cs], w[:, kt, 1280:1288], start=(kt == 0), stop=(kt == 31))
        k.tt(dt[0:cl, :], misc[0:cl, 0:8], dtb[0:cl, :], ALU.add)
        k.act(dt[0:cl, :], dt[0:cl, :], AF.Exp)
        k.act(dt[0:cl, :], dt[0:cl, :], AF.Ln, bias=1.0)
        k.tt(da[0:cl, :], dt[0:cl, :], aneg[0:cl, :], ALU.mult)
        for m in range(4):
            k.transpose(tr[0:cl, m * 128:(m + 1) * 128], fT[m][:, cs], idb.v)
        k.copy(xtok[0:cl, :], tr[0:cl, :], eng="act")
        k.transpose(tr[0:cl, 0:128], fT[4][:, cs], idb.v)
        k.copy(btok[0:cl, :], tr[0:cl, 0:128], eng="act")
        k.tt(xdt[0:cl, :].rearrange("p (h d) -> p h d", h=8), xtok[0:cl, :].rearrange("p (h d) -> p h d", h=8),
             dt[0:cl, :].unsqueeze(2).broadcast_to([cl, 8, 64]), ALU.mult)
        k.matmul(misc[0:cl, 8:16], U[0:cl, 0:cl], da[0:cl, :])
        k.copy(acum[0:cl, :], misc[0:cl, 8:16], eng="dve")
        k.ts(nacum[0:cl, :], acum[0:cl, :], -1.0, ALU.mult)
        k.act(eac[0:cl, :], acum[0:cl, :], AF.Exp)
        k.tt(dab[0:cl, :, 0:cl], ones[0:cl, 0:cl].unsqueeze(1).broadcast_to([cl, 8, cl]),
             da[0:cl, :].unsqueeze(2).broadcast_to([cl, 8, cl]), ALU.mult, eng="pool")
        for hh in range(8):
            k.matmul(AB[0:cl, hh, 0:cl], dab[0:cl, hh, 0:cl], U[0:cl, 0:cl])
        k.tt(segc[0:cl, :, 0:cl], AB[0:cl, :, 0:cl], nacum[0:cl, :].unsqueeze(2).broadcast_to([cl, 8, cl]), ALU.add)
        k.copy(aend[0:cl, :], AB[0:cl, :, cl - 1], eng="dve")
        k.ts(segc[0:cl, :, 0:cl], segc[0:cl, :, 0:cl], 0.0, ALU.min, eng="pool")
        k.act(segc[0:cl, :, 0:cl], segc[0:cl, :, 0:cl], AF.Exp)
        k.matmul(misc[0:cl, 128:128 + cl], fT[4][:, cs], fT[5][:, cs])
        k.tt(cbm[0:cl, 0:cl], misc[0:cl, 128:128 + cl], U[0:cl, 0:cl], ALU.mult)
        k.tt(MT[0:cl, :, 0:cl], segc[0:cl, :, 0:cl], cbm[0:cl, 0:cl].unsqueeze(1).broadcast_to([cl, 8, cl]), ALU.mult, eng="pool")
        k.copy(ctk[:, 0:cl], fT[5][:, cs], eng="pool")

    def stageB(c):
        cl, cs, par, tok0 = c['cl'], c['cs'], c['par'], c['tok0']
        zs = L_zs[par]
        dt = L_dt[par]
        da = L_da[par]
        dab = L_dab[par]
        acum = L_acum[par]
        nacum = L_nacum[par]
        aend = L_aend[par]
        eend = L_eend[par]
        eac = L_eac[par]
        dte = L_dte[par]
        xtok = L_xtok[par]
        btok = L_btok[par]
        xdt = L_xdt[par]
        xw = L_xw[par]
        segc = L_segc[par]
        cbm = L_cbm[par]
        MT = L_MT[par]
        t1 = L_t1[par]
        t2 = L_t2[par]
        ssq = L_ssq[par]
        yn = L_yn[par]
        ctk = L_ctk[par]
        for hh in range(8):
            k.matmul(ydg[0:cl, hh * 64:(hh + 1) * 64], MT[0:cl, hh, 0:cl], xdt[0:cl, hh * 64:(hh + 1) * 64])
        k.matmul(yof[0:cl, :], ctk[:, 0:cl], Sb.v)
        k.tt(t1[0:cl, :].rearrange("p (h d) -> p h d", h=8), yof[0:cl, :].rearrange("p (h d) -> p h d", h=8),
             eac[0:cl, :].unsqueeze(2).broadcast_to([cl, 8, 64]), ALU.mult)
        k.tt(t1[0:cl, :], t1[0:cl, :], ydg[0:cl, :], ALU.add)
        k.tt(t2[0:cl, :], xtok[0:cl, :], dbc[0:cl, :], ALU.mult, eng="pool")
        k.tt(t1[0:cl, :], t1[0:cl, :], t2[0:cl, :], ALU.add)
        k.tt(t1[0:cl, :], t1[0:cl, :], zs[0:cl, :], ALU.mult)
        k.act(t2[0:cl, :], t1[0:cl, :], AF.Square, accum_out=ssq[0:cl, :])
        k.ts(ssq[0:cl, :], ssq[0:cl, :], 1.0 / 512, ALU.mult, EPS, ALU.add)
        k.act(ssq[0:cl, :], ssq[0:cl, :], AF.Sqrt)
        k.recip(ssq[0:cl, :], ssq[0:cl, :])
        k.stt(yn[0:cl, :], t1[0:cl, :], ssq[0:cl, 0:1], ngb[0:cl, :], ALU.mult, ALU.mult)
        for m in range(4):
            k.transpose(tr[:, m * 128:m * 128 + cl], yn[0:cl, m * 128:(m + 1) * 128], idb[0:cl, 0:cl])
        for m in range(4):
            k.copy(yTs[m][:, 0:cl], tr[:, m * 128:m * 128 + cl], eng=("act" if m % 2 else "dve"))
            k.dma("act", yT[m * 128:(m + 1) * 128, tok0:tok0 + cl], yTs[m][:, 0:cl])
        k.ts(dte[0:cl, :], aend[0:cl, :], 1.0 / cl, ALU.mult)
        k.matmul(misc[:, 16:24], ones[0:cl, :], dte[0:cl, :])
        k.act(eend.v, misc[:, 16:24], AF.Exp)
        k.tt(dte[0:cl, :], aend[0:cl, :], acum[0:cl, :], ALU.subtract)
        k.act(dte[0:cl, :], dte[0:cl, :], AF.Exp)
        k.tt(xw[0:cl, :].rearrange("p (h d) -> p h d", h=8), xdt[0:cl, :].rearrange("p (h d) -> p h d", h=8),
             dte[0:cl, :].unsqueeze(2).broadcast_to([cl, 8, 64]), ALU.mult)
        k.matmul(yof.v, btok[0:cl, :], xw[0:cl, :])
        k.tt(S.v.rearrange("p (h d) -> p h d", h=8), S.v.rearrange("p (h d) -> p h d", h=8),
             eend.v.unsqueeze(2).broadcast_to([128, 8, 64]), ALU.mult)
        k.tt(S.v, S.v, yof.v, ALU.add)
        k.copy(Sb.v, S.v, eng="pool")


    pendB = None
    nchunk = 0
    for bi, (t0, t1_) in enumerate(blocks):
        n = t1_ - t0
        if bi == 0:
            h = hbm
            k.dma("sp", h.v, hn_meta.v)
        else:
            h = hb[bi % 2]
            k.dma("sp", h.v, hn_own[bi - 1])
        for m in range(6):
            a = accs[m % 2]
            c0 = 512 + m * 128
            for kt in range(32):
                k.matmul(a[:, 0:n], w[:, kt, c0:c0 + 128], h[:, kt, 0:n], start=(kt == 0), stop=(kt == 31))
            ci = cin[m]
            k.copy(ci[:, 3:3 + n], a[:, 0:n], eng="act")
            ca = cacc[m % 2]
            k.ts(ca[:, 0:n], ci[:, 0:n], cw[:, m, 0:1], ALU.mult, cb[:, m:m + 1], ALU.add)
            for j in range(1, 4):
                k.stt(ca[:, 0:n], ci[:, j:j + n], cw[:, m, j:j + 1], ca[:, 0:n], ALU.mult, ALU.add)
            k.act(fT[m][:, 0:n], ca[:, 0:n], AF.Silu)
            k.copy(ci[:, 0:3], ci[:, n:n + 3], eng="pool")
        for s0 in range(0, n, 128):
            cl = min(128, n - s0)
            ctx = dict(cl=cl, cs=slice(s0, s0 + cl), tok0=t0 + s0, h=h, par=nchunk % 2)
            la = k.record(stageA, ctx)
            lb = k.record(stageB, pendB) if pendB is not None else []
            k.replay(la, lb)
            pendB = ctx
            nchunk += 1
    if pendB is not None:
        stageB(pendB)


def hyb_s5(k, T, hn_meta, hn_own, w_s5, bre, bim, cre, cim, are_l, aim_l, ldt_l, d_l, gT, sgT, pfx="h5"):
    blocks = tok_blocks(T, BS_A)
    L = BS_A
    stg = [k.sb([128, 512], F32, pfx + f"stg{i}") for i in range(2)]
    w = k.sb([128, 32, 512], BF16, pfx + "w")
    load_cast(k, w, w_s5, 32, 512, stg)
    hbm = k.sb([128, 32, 16], BF16, pfx + "hbm")
    hb = [k.sb([128, 32, BS_A], BF16, pfx + f"hb{i}") for i in range(2)]
    f_bre = k.sb([128, 8, 128], F32, pfx + "fbre"); f_bim = k.sb([128, 8, 128], F32, pfx + "fbim")
    f_cre = k.sb([128, 8, 128], F32, pfx + "fcre"); f_cim = k.sb([128, 8, 128], F32, pfx + "fcim")
    Bre = k.sb([128, 8, 128], BF16, pfx + "Bre"); Bim = k.sb([128, 8, 128], BF16, pfx + "Bim")
    Cre = k.sb([128, 8, 128], BF16, pfx + "Cre"); Cim = k.sb([128, 8, 128], BF16, pfx + "Cim")
    for (dst, f, src) in ((Bre, f_bre, bre), (Bim, f_bim, bim), (Cre, f_cre, cre), (Cim, f_cim, cim)):
        k.dma("sp", f.v, src.v)
        k.copy(dst.v, f.v, eng="pool")
    are = k.sb([128, 8], F32, pfx + "are"); aim = k.sb([128, 8], F32, pfx + "aim"); dtt = k.sb([128, 8], F32, pfx + "dtt")
    dsk = k.sb([128, 2], F32, pfx + "dsk")
    k.dma("sp", are.v, are_l.v); k.dma("sp", aim.v, aim_l.v); k.dma("sp", dtt.v, ldt_l.v); k.dma("sp", dsk.v, d_l.v)
    k.act(dtt.v, dtt.v, AF.Exp)
    th = k.sb([128, 8], F32, pfx + "th"); rho = k.sb([128, 8], F32, pfx + "rho")
    k.tt(th.v, dtt.v, aim.v, ALU.mult)
    k.tt(rho.v, dtt.v, are.v, ALU.mult)
    k.act(rho.v, rho.v, AF.Exp)
    ki = k.sb([128, 8], I32, pfx + "ki"); kf = k.sb([128, 8], F32, pfx + "kf")
    hh_ = k.sb([128, 8], F32, pfx + "hh"); sh = k.sb([128, 8], F32, pfx + "sh"); ch = k.sb([128, 8], F32, pfx + "ch")
    k.ts(kf.v, th.v, 1.0 / (2 * math.pi), ALU.mult)
    k.copy(ki.v, kf.v, eng="dve")
    k.copy(kf.v, ki.v, eng="dve")
    k.stt(hh_.v, kf.v, -2 * math.pi, th.v, ALU.mult, ALU.add)
    k.ts(hh_.v, hh_.v, 0.5, ALU.mult)
    k.act(sh.v, hh_.v, AF.Sin)
    q4 = k.sb([128, 8], F32, pfx + "q4")
    k.act(q4.v, hh_.v, AF.Sin, scale=0.5)
    k.tt(q4.v, q4.v, q4.v, ALU.mult)
    k.ts(ch.v, q4.v, -2.0, ALU.mult, 1.0, ALU.add)
    zr = k.sb([128, 8, 9], F32, pfx + "zr"); zi = k.sb([128, 8, 9], F32, pfx + "zi"); nzi = k.sb([128, 8, 9], F32, pfx + "nzi")
    tmp8 = k.sb([128, 8], F32, pfx + "tmp8"); tmp8b = k.sb([128, 8], F32, pfx + "tmp8b")
    k.tt(tmp8.v, sh.v, sh.v, ALU.mult)
    k.ts(zr[:, :, 0], tmp8.v, -2.0, ALU.mult, 1.0, ALU.add)
    k.tt(tmp8.v, sh.v, ch.v, ALU.mult)
    k.ts(zi[:, :, 0], tmp8.v, 2.0, ALU.mult)
    for m_ in range(8):
        k.tt(tmp8.v, zr[:, :, m_], zr[:, :, m_], ALU.mult)
        k.tt(tmp8b.v, zi[:, :, m_], zi[:, :, m_], ALU.mult)
        k.tt(zr[:, :, m_ + 1], tmp8.v, tmp8b.v, ALU.subtract)
        k.tt(tmp8.v, zr[:, :, m_], zi[:, :, m_], ALU.mult)
        k.ts(zi[:, :, m_ + 1], tmp8.v, 2.0, ALU.mult)
    k.ts(nzi.v, zi.v, -1.0, ALU.mult)
    abr = k.sb([128, 8], F32, pfx + "abr"); abi = k.sb([128, 8], F32, pfx + "abi"); den = k.sb([128, 8], F32, pfx + "den")
    kre = k.sb([128, 8], F32, pfx + "kre"); kim = k.sb([128, 8], F32, pfx + "kim"); nkre = k.sb([128, 8], F32, pfx + "nkre")
    k.tt(abr.v, rho.v, zr[:, :, 0], ALU.mult)
    k.ts(abr.v, abr.v, -1.0, ALU.add)
    k.tt(abi.v, rho.v, zi[:, :, 0], ALU.mult)
    k.tt(den.v, are.v, are.v, ALU.mult)
    k.tt(tmp8.v, aim.v, aim.v, ALU.mult)
    k.tt(den.v, den.v, tmp8.v, ALU.add)
    k.recip(den.v, den.v)
    k.tt(kre.v, abr.v, are.v, ALU.mult)
    k.tt(tmp8.v, abi.v, aim.v, ALU.mult)
    k.tt(kre.v, kre.v, tmp8.v, ALU.add)
    k.tt(kre.v, kre.v, den.v, ALU.mult)
    k.tt(kim.v, abi.v, are.v, ALU.mult)
    k.tt(tmp8.v, abr.v, aim.v, ALU.mult)
    k.tt(kim.v, kim.v, tmp8.v, ALU.subtract)
    k.tt(kim.v, kim.v, den.v, ALU.mult)
    k.ts(nkre.v, kre.v, -1.0, ALU.mult)
    Fc = k.sb([128, 8, L], F32, pfx + "Fc"); Fs = k.sb([128, 8, L], F32, pfx + "Fs")
    Ere = k.sb([128, 8, L], F32, pfx + "Ere"); Eim = k.sb([128, 8, L], F32, pfx + "Eim")
    rhoT = k.sb([128, 8, L], F32, pfx + "rhoT")
    Fcb = k.sb([128, 8, L], BF16, pfx + "Fcb"); Fsb = k.sb([128, 8, L], BF16, pfx + "Fsb"); NFsb = k.sb([128, 8, L], BF16, pfx + "NFsb")
    tl = k.sb([128, L], F32, pfx + "tl")
    k.memset(Fc.v, 1.0)
    k.memset(Fs.v, 0.0)
    k.memset(rhoT.v, 1.0)
    for j in range(8):
        for m_ in range(8):
            lo = slice(0, 2 ** m_)
            hi = slice(2 ** m_, 2 ** (m_ + 1))
            w_ = 2 ** m_
            k.ts(tl[:, 0:w_], Fs[:, j, lo], zi[:, j, m_:m_ + 1], ALU.mult)
            k.stt(Fc[:, j, hi], Fc[:, j, lo], zr[:, j, m_:m_ + 1], tl[:, 0:w_], ALU.mult, ALU.subtract)
            k.ts(tl[:, 0:w_], Fc[:, j, lo], zi[:, j, m_:m_ + 1], ALU.mult)
            k.stt(Fs[:, j, hi], Fs[:, j, lo], zr[:, j, m_:m_ + 1], tl[:, 0:w_], ALU.mult, ALU.add)
        k.ts(tl.v, Fs[:, j, :], kim[:, j:j + 1], ALU.mult)
        k.stt(Ere[:, j, :], Fc[:, j, :], kre[:, j:j + 1], tl.v, ALU.mult, ALU.add)
        k.ts(tl.v, Fs[:, j, :], nkre[:, j:j + 1], ALU.mult)
        k.stt(Eim[:, j, :], Fc[:, j, :], kim[:, j:j + 1], tl.v, ALU.mult, ALU.add)
        k.ts(rhoT[:, j, :], rhoT[:, j, :], rho[:, j:j + 1], ALU.mult)
        k.ts(NFsb[:, j, :], Fs[:, j, :], -1.0, ALU.mult, eng="pool")
        k.copy(Fcb[:, j, :], Fc[:, j, :], eng="act")
        k.copy(Fsb[:, j, :], Fs[:, j, :], eng="act")
    uf = [k.sb([128, BS_A], F32, pfx + f"uf{a}") for a in range(2)]
    ub = [k.sb([128, BS_A], BF16, pfx + f"ub{a}") for a in range(2)]
    go = [k.sb([128, BS_A], BF16, pfx + f"go{i}") for i in range(2)]
    L5 = {nm: [k.sb([128, BS_A], F32, pfx + f"{nm}{i}") for i in range(2)]
          for nm in ("vre", "vim", "p1", "p2", "p3", "p4", "wre", "wim")}
    L5.update({nm: [k.sb([128, BS_A], BF16, pfx + f"{nm}{i}") for i in range(2)]
               for nm in ("q1", "q2", "q3", "q4", "wrb", "wib")})
    sre = [k.sb([128, BS_A], BF16, pfx + f"sre{i}") for i in range(2)]
    sim_ = [k.sb([128, BS_A], BF16, pfx + f"sim{i}") for i in range(2)]
    wir = k.sb([128, 8], F32, pfx + "wir"); wii = k.sb([128, 8], F32, pfx + "wii")
    k.memset(wir.v, 0.0)
    k.memset(wii.v, 0.0)
    c1L = [k.sb([128, 1], F32, pfx + f"c1{i}") for i in range(2)]
    x2 = k.sb([128, BS_A], F32, pfx + "x2"); x3 = k.sb([128, BS_A], F32, pfx + "x3"); yv = k.sb([128, BS_A], F32, pfx + "yv")
    acc = [k.ps([128, 512], F32, pfx + f"acc{i}") for i in range(2)]
    PpL = [k.ps([128, 512], F32, pfx + f"Pp{i}") for i in range(2)]; QpL = [k.ps([128, 512], F32, pfx + f"Qp{i}") for i in range(2)]
    yp = [k.ps([128, 512], F32, pfx + f"yp{a}") for a in range(2)]
    for bi, (t0, t1_) in enumerate(blocks):
        n = t1_ - t0
        if bi == 0:
            h = hbm
            k.dma("sp", h.v, hn_meta.v)
        else:
            h = hb[bi % 2]
            k.dma("sp", h.v, hn_own[bi - 1])
        lev = 4 if bi == 0 else 8
        for a in range(2):
            for kt in range(32):
                k.matmul(acc[0][:, 0:n], w[:, kt, a * 128:(a + 1) * 128], h[:, kt, 0:n], start=(kt == 0), stop=(kt == 31))
            k.copy(uf[a][:, 0:n], acc[0][:, 0:n], eng="dve")
            k.copy(ub[a][:, 0:n], uf[a][:, 0:n], eng="pool")
            for kt in range(32):
                k.matmul(acc[1][:, 0:n], w[:, kt, 256 + a * 128:256 + (a + 1) * 128], h[:, kt, 0:n], start=(kt == 0), stop=(kt == 31))
            o = go[a]
            k.act(o[:, 0:n], acc[1][:, 0:n], AF.Silu)
            k.dma("act", sgT[a * 128:(a + 1) * 128, t0:t1_], o[:, 0:n])
        def chain(j):
            a = j // 4
            Pp, Qp, c1 = PpL[j % 2], QpL[j % 2], c1L[j % 2]
            k.matmul(Pp[:, 0:n], Bre[:, j, :], ub[a][:, 0:n])
            k.matmul(Qp[:, 0:n], Bim[:, j, :], ub[a][:, 0:n])
            vre, vim, p1, p2, p3, p4, wre, wim, q1, q2, q3, q4, wrb, wib = (L5[nm][j % 2] for nm in
                ("vre", "vim", "p1", "p2", "p3", "p4", "wre", "wim", "q1", "q2", "q3", "q4", "wrb", "wib"))
            k.tt(p1[:, 0:n], Pp[:, 0:n], Ere[:, j, 0:n], ALU.mult)
            k.tt(p2[:, 0:n], Qp[:, 0:n], Eim[:, j, 0:n], ALU.mult)
            k.tt(p3[:, 0:n], Qp[:, 0:n], Ere[:, j, 0:n], ALU.mult)
            k.tt(p4[:, 0:n], Pp[:, 0:n], Eim[:, j, 0:n], ALU.mult)
            k.tt(vre[:, 0:n], p1[:, 0:n], p2[:, 0:n], ALU.subtract, eng="pool")
            k.tt(vim[:, 0:n], p3[:, 0:n], p4[:, 0:n], ALU.add, eng="pool")
            k.scan(wre[:, 0:n], rhoT[:, j, 0:n], vre[:, 0:n], wir[:, j:j + 1])
            k.scan(wim[:, 0:n], rhoT[:, j, 0:n], vim[:, 0:n], wii[:, j:j + 1])
            k.ts(c1.v, wim[:, n - 1:n], nzi[:, j, lev:lev + 1], ALU.mult)
            k.stt(wir[:, j:j + 1], wre[:, n - 1:n], zr[:, j, lev:lev + 1], c1.v, ALU.mult, ALU.add)
            k.ts(c1.v, wre[:, n - 1:n], zi[:, j, lev:lev + 1], ALU.mult)
            k.stt(wii[:, j:j + 1], wim[:, n - 1:n], zr[:, j, lev:lev + 1], c1.v, ALU.mult, ALU.add)
            sr, si = sre[j % 2], sim_[j % 2]
            k.copy(wrb[:, 0:n], wre[:, 0:n], eng="act")
            k.copy(wib[:, 0:n], wim[:, 0:n], eng="act")
            k.tt(q1[:, 0:n], wrb[:, 0:n], Fcb[:, j, 0:n], ALU.mult)
            k.tt(q2[:, 0:n], wib[:, 0:n], Fsb[:, j, 0:n], ALU.mult)
            k.tt(sr[:, 0:n], q1[:, 0:n], q2[:, 0:n], ALU.subtract)
            k.tt(q3[:, 0:n], wrb[:, 0:n], NFsb[:, j, 0:n], ALU.mult)
            k.tt(q4[:, 0:n], wib[:, 0:n], Fcb[:, j, 0:n], ALU.mult)
            k.tt(si[:, 0:n], q3[:, 0:n], q4[:, 0:n], ALU.subtract)
            k.matmul(yp[a][:, 0:n], Cre[:, j, :], sr[:, 0:n], start=(j % 4 == 0), stop=False)
            k.matmul(yp[a][:, 0:n], Cim[:, j, :], si[:, 0:n], start=False, stop=(j % 4 == 3))
        for j0 in range(0, 8, 2):
            k.replay(k.record(chain, j0), k.record(chain, j0 + 1))
        for a in range(2):
            k.stt(yv[:, 0:n], uf[a][:, 0:n], dsk[:, a:a + 1], yp[a][:, 0:n], ALU.mult, ALU.add)
            k.tt(x2[:, 0:n], yv[:, 0:n], yv[:, 0:n], ALU.mult, eng="pool")
            k.ts(x2[:, 0:n], x2[:, 0:n], 0.044715, ALU.mult, 1.0, ALU.add, eng="pool")
            k.tt(x3[:, 0:n], x2[:, 0:n], yv[:, 0:n], ALU.mult, eng="pool")
            k.act(x3[:, 0:n], x3[:, 0:n], AF.Sigmoid, scale=GELU_C)
            o = go[a]
            k.tt(o[:, 0:n], x3[:, 0:n], yv[:, 0:n], ALU.mult)
            k.dma("act", gT[a * 128:(a + 1) * 128, t0:t1_], o[:, 0:n])


PERM = np.concatenate([np.arange(32, 64), np.arange(0, 32)])


def ktile(w):
    K, M = w.shape
    return np.ascontiguousarray(w.reshape(K // 128, 128, M).transpose(1, 0, 2))


def vec_tile(g):
    return np.ascontiguousarray(g.reshape(-1, 128).T)


def rope_tables(T):
    pos = np.arange(T, dtype=np.float32)
    inv_freq = (np.float32(10000.0) ** (-np.arange(0, 64, 2, dtype=np.float32) / np.float32(64))).astype(np.float32)
    ang = (pos[:, None] * inv_freq[None, :]).astype(np.float32)
    cos = np.cos(ang).astype(np.float32).T
    sin = np.sin(ang).astype(np.float32).T
    cos2 = np.ascontiguousarray(np.concatenate([cos, cos], 0))
    sin2s = np.ascontiguousarray(np.concatenate([-sin, sin], 0))
    return cos2, sin2s


def mla_weights(c, w_in, q_norm, w_uq, kv_norm, w_ukv):
    kr = w_in[:, 1536:1600]
    wkv = np.concatenate([w_in[:, 1024:1536], kr, kr[:, PERM], w_in[:, 1600 + c * 512:1600 + (c + 1) * 512]], 1)
    uq = []
    kn = []
    vv = []
    for h in range(4):
        b = (4 * c + h) * 192
        rope = w_uq[:, b + 128:b + 192]
        uq += [w_uq[:, b:b + 128], rope, rope[:, PERM]]
        b2 = (4 * c + h) * 256
        kn.append(w_ukv[:, b2:b2 + 128])
        vv.append(w_ukv[:, b2 + 128:b2 + 256])
    return dict(
        wq_in=ktile(w_in[:, 0:1024]),
        wkv_in=ktile(wkv),
        wuq=ktile(np.concatenate(uq, 1)),
        wukv=ktile(np.concatenate(kn + vv, 1)),
        gq=vec_tile(q_norm),
        gkv=vec_tile(kv_norm),
    )


def out_w_tile(W):
    K = W.shape[0]
    nkt = K // 128
    return np.ascontiguousarray(W.reshape(nkt, 128, 32, 128).transpose(2, 1, 0, 3).reshape(32, 128, nkt * 128))


def hn_layout(hnT):
    T = hnT.shape[1]
    nb = (T - 16) // 256
    v = hnT.reshape(32, 128, T)
    meta = np.ascontiguousarray(v[:, :, 0:16].transpose(1, 0, 2))
    own = np.ascontiguousarray(v[:, :, 16:].reshape(32, 128, nb, 256).transpose(2, 1, 0, 3))
    return dict(hn_meta=meta, hn_own=own)


def const_mats():
    U = np.triu(np.ones((128, 128), np.float32))
    ident = np.eye(128, dtype=np.float32)
    return U, ident


def hyb_weights(c, w_in, conv_w, conv_b, dt_bias, a_log, d_skip, norm_g,
                a_re, a_im, log_dt, b_re, b_im, c_re, c_im, s5_d):
    z = w_in[:, c * 512:(c + 1) * 512]
    x = w_in[:, 4096 + c * 512:4096 + (c + 1) * 512]
    B = w_in[:, 8192 + c * 128:8192 + (c + 1) * 128]
    C = w_in[:, 9216 + c * 128:9216 + (c + 1) * 128]
    dt = w_in[:, 10240 + c * 8:10240 + (c + 1) * 8]
    w_ssd = ktile(np.concatenate([z, x, B, C, dt], 1))
    u = w_in[:, 10304 + c * 256:10304 + (c + 1) * 256]
    gate = w_in[:, 12352 + c * 256:12352 + (c + 1) * 256]
    w_s5 = ktile(np.concatenate([u, gate], 1))
    chans = [np.arange(c * 512 + m * 128, c * 512 + (m + 1) * 128) for m in range(4)]
    chans.append(np.arange(4096 + c * 128, 4096 + (c + 1) * 128))
    chans.append(np.arange(5120 + c * 128, 5120 + (c + 1) * 128))
    convw = np.ascontiguousarray(np.stack([conv_w[:, ch].T for ch in chans], 1))
    convb = np.ascontiguousarray(np.stack([conv_b[ch] for ch in chans], 1))
    hs = slice(c * 8, (c + 1) * 8)
    dtb_bc = np.ascontiguousarray(np.broadcast_to(dt_bias[hs][None, :], (128, 8)))
    alog_bc = np.ascontiguousarray(np.broadcast_to(a_log[hs][None, :], (128, 8)))
    d_bc = np.ascontiguousarray(np.broadcast_to(np.repeat(d_skip[hs], 64)[None, :], (128, 512)))
    ng_bc = np.ascontiguousarray(np.broadcast_to(norm_g[c * 512:(c + 1) * 512][None, :], (128, 512)))
    bre = np.zeros((128, 8, 128), np.float32); bim = np.zeros((128, 8, 128), np.float32)
    cre = np.zeros((128, 8, 128), np.float32); cim = np.zeros((128, 8, 128), np.float32)
    are_l = np.zeros((128, 8), np.float32); aim_l = np.zeros((128, 8), np.float32); ldt_l = np.zeros((128, 8), np.float32)
    for j in range(8):
        a, q = j // 4, (j % 4) * 32
        for m in range(2):
            g = 16 * c + 2 * j + m
            bre[q + m * 16:q + (m + 1) * 16, j, m * 64:(m + 1) * 64] = b_re[g].T
            bim[q + m * 16:q + (m + 1) * 16, j, m * 64:(m + 1) * 64] = b_im[g].T
            cre[m * 64:(m + 1) * 64, j, q + m * 16:q + (m + 1) * 16] = c_re[g].T
            cim[m * 64:(m + 1) * 64, j, q + m * 16:q + (m + 1) * 16] = c_im[g].T
            are_l[m * 64:(m + 1) * 64, j] = a_re[g]
            aim_l[m * 64:(m + 1) * 64, j] = a_im[g]
            ldt_l[m * 64:(m + 1) * 64, j] = log_dt[g]
    d_l = np.ascontiguousarray(s5_d[c * 256:(c + 1) * 256].reshape(2, 128).T)
    return dict(w_ssd=w_ssd, w_s5=w_s5, convw=convw, convb=convb, dtb_bc=dtb_bc, alog_bc=alog_bc, d_bc=d_bc, ng_bc=ng_bc,
                bre=bre, bim=bim, cre=cre, cim=cim, are_l=are_l, aim_l=aim_l, ldt_l=ldt_l, d_l=d_l)


def glu_w_tile(W):
    return np.ascontiguousarray(W.reshape(16, 128, 16, 128).transpose(2, 1, 0, 3).reshape(16, 128, 2048))


def blk_layout(xT, nkt):
    T = xT.shape[1]
    nb = (T - 16) // 256
    v = xT.reshape(nkt, 128, T)
    meta = np.ascontiguousarray(v[:, :, 0:16].transpose(1, 0, 2))
    own = np.ascontiguousarray(v[:, :, 16:].reshape(nkt, 128, nb, 256).transpose(2, 1, 0, 3))
    return meta, own


def mla_weights2(c, w_in, q_norm, w_uq, kv_norm, w_ukv):
    kr = w_in[:, 1536:1600]
    wkv3 = np.concatenate([w_in[:, 1024:1536], kr, kr[:, PERM]], 1)
    base = mla_weights(c, w_in, q_norm, w_uq, kv_norm, w_ukv)
    return dict(wq_in=base["wq_in"], wkv3=ktile(wkv3), gq=base["gq"], gkv=base["gkv"],
                wgate=ktile(w_in[:, 1600 + c * 512:1600 + (c + 1) * 512]), wuq=base["wuq"], wukv=base["wukv"])

from concourse.bass_utils import run_bass_kernel_spmd

BFNP = ml_dtypes.bfloat16
T_ALL = 16400
TC = 2064
NCORE = 8
_PROGS = {}

HYB_SHAPES = dict(w_ssd=[128, 32, 1288], w_s5=[128, 32, 512], convw=[128, 6, 4], convb=[128, 6], dtb_bc=[128, 8],
                  alog_bc=[128, 8], d_bc=[128, 512], ng_bc=[128, 512], bre=[128, 8, 128], bim=[128, 8, 128],
                  cre=[128, 8, 128], cim=[128, 8, 128], are_l=[128, 8], aim_l=[128, 8], ldt_l=[128, 8], d_l=[128, 2],
                  Umat=[128, 128], ident=[128, 128])


def _new():
    return bass.Bass("TRN2", target_bir_lowering=False)


def prog_norm():
    nc = _new()
    with contextlib.ExitStack() as st:
        k = KB(nc, st)
        hT = k.dram("hT", [D, TC], F32, kind="ExternalInput")
        g_l = k.dram("g_l", [128, 32], F32, kind="ExternalInput")
        hnT = k.dram("hnT", [D, TC], BF16, kind="ExternalOutput")
        stage_out(k, hT, None, None, g_l, None, hnT, TC, 0, True, BF16)
        k.final_wait("sp", [hnT])
        k.emit()
    return nc


def prog_out(hyb, final):
    nc = _new()
    nkt = 48 if hyb else 32
    with contextlib.ExitStack() as st:
        k = KB(nc, st)
        hT = k.dram("hT", [D, TC], F32, kind="ExternalInput")
        yT = k.dram("yT", [D, TC], BF16, kind="ExternalInput")
        wl = k.dram("wl", [32, 128, nkt * 128], F32, kind="ExternalInput")
        g_l = k.dram("g_l", [128, 32], F32, kind="ExternalInput")
        glu = None
        if hyb:
            g_all = k.dram("g_all", [2048, TC], BF16, kind="ExternalInput")
            sg_all = k.dram("sg_all", [2048, TC], BF16, kind="ExternalInput")
            wglu_l = k.dram("wglu_l", [16, 128, 2048], F32, kind="ExternalInput")
            glu = (g_all, sg_all, wglu_l)
        outs = []
        if not final:
            hT_new = k.dram("hT_new", [D, TC], F32, kind="ExternalOutput")
            outs.append(hT_new)
        else:
            hT_new = k.dram("hT_new", [D, TC], F32)
        hnT = k.dram("hnT", [D, TC], F32 if final else BF16, kind="ExternalOutput")
        outs.append(hnT)
        ybT = None
        if hyb:
            ybT = k.dram("ybT", [2048, TC], BF16)
            with k.scope():
                stage_glu(k, g_all, sg_all, wglu_l, ybT, TC)
        with k.scope():
            stage_out(k, hT, yT, wl, g_l, hT_new, hnT, TC, nkt, False, F32 if final else BF16, glu=ybT)
        if hyb:
            wq_in = k.dram("wq_in", [128, 32, 1024], F32, kind="ExternalInput")
            wkv3 = k.dram("wkv3", [128, 32, 640], F32, kind="ExternalInput")
            gq = k.dram("gq", [128, 8], F32, kind="ExternalInput")
            gkv = k.dram("gkv", [128, 4], F32, kind="ExternalInput")
            cos2c = k.dram("cos2c", [64, TC], F32, kind="ExternalInput")
            sin2sc = k.dram("sin2sc", [64, TC], F32, kind="ExternalInput")
            cqnT = k.dram("cqnT", [1024, TC], BF16, kind="ExternalOutput")
            ckvnT = k.dram("ckvnT", [512, TC], BF16, kind="ExternalOutput")
            krT = k.dram("krT", [64, TC], BF16, kind="ExternalOutput")
            outs += [cqnT, ckvnT, krT]
            with k.scope():
                mla_pre(k, TC, hnT, wq_in, wkv3, gq, gkv, cos2c, sin2sc, cqnT, ckvnT, krT)
        k.final_wait("sp", outs)
        k.emit()
    return nc


def prog_hyb():
    nc = _new()
    T = T_ALL
    with contextlib.ExitStack() as st:
        k = KB(nc, st)
        hn_meta = k.dram("hn_meta", [128, 32, 16], BF16, kind="ExternalInput")
        hn_own = k.dram("hn_own", [(T - 16) // 256, 128, 32, 256], BF16, kind="ExternalInput")
        d = {n: k.dram(n, s, F32, kind="ExternalInput") for n, s in HYB_SHAPES.items()}
        yT = k.dram("yT", [512, T], BF16, kind="ExternalOutput")
        gT = k.dram("gT", [256, T], BF16, kind="ExternalOutput")
        sgT = k.dram("sgT", [256, T], BF16, kind="ExternalOutput")
        with k.scope():
            hyb_ssd(k, T, hn_meta, hn_own, d["w_ssd"], d["convw"], d["convb"], d["dtb_bc"], d["alog_bc"], d["d_bc"],
                    d["ng_bc"], d["Umat"], d["ident"], yT)
        with k.scope():
            hyb_s5(k, T, hn_meta, hn_own, d["w_s5"], d["bre"], d["bim"], d["cre"], d["cim"], d["are_l"], d["aim_l"],
                   d["ldt_l"], d["d_l"], gT, sgT)
        k.final_wait("sp", [yT, gT, sgT])
        k.emit()
    return nc


def prog_mla():
    nc = _new()
    T = T_ALL
    NB = (T - 16) // 256
    with contextlib.ExitStack() as st:
        k = KB(nc, st)
        hn_meta = k.dram("hn_meta", [128, 32, 16], BF16, kind="ExternalInput")
        hn_own = k.dram("hn_own", [NB, 128, 32, 256], BF16, kind="ExternalInput")
        cq_meta = k.dram("cq_meta", [128, 8, 16], BF16, kind="ExternalInput")
        cq_own = k.dram("cq_own", [NB, 128, 8, 256], BF16, kind="ExternalInput")
        ck_meta = k.dram("ck_meta", [128, 4, 16], BF16, kind="ExternalInput")
        ck_own = k.dram("ck_own", [NB, 128, 4, 256], BF16, kind="ExternalInput")
        krT = k.dram("krT", [64, T], BF16, kind="ExternalInput")
        wgate = k.dram("wgate", [128, 32, 512], F32, kind="ExternalInput")
        wuq = k.dram("wuq", [128, 8, 1024], F32, kind="ExternalInput")
        wukv = k.dram("wukv", [128, 4, 1024], F32, kind="ExternalInput")
        cos2 = k.dram("cos2", [64, T], F32, kind="ExternalInput")
        sin2s = k.dram("sin2s", [64, T], F32, kind="ExternalInput")
        yT = k.dram("yT", [512, T], BF16, kind="ExternalOutput")
        stage_mla2(k, T, hn_meta, hn_own, cq_meta, cq_own, ck_meta, ck_own, krT, wgate, wuq, wukv, cos2, sin2s, yT)
        k.final_wait("sp", [yT])
        k.emit()
    return nc


def _get(name, fn, *a):
    if name not in _PROGS:
        _PROGS[name] = fn(*a)
    return _PROGS[name]


def _run(nc, in_maps):
    res = run_bass_kernel_spmd(nc, in_maps, core_ids=list(range(NCORE)))
    return res.results


def _tok_idx(c):
    return np.concatenate([np.arange(16), 16 + 2048 * c + np.arange(2048)])


def _gather_tokens(per_core):
    return np.concatenate([per_core[0][:, 0:16]] + [per_core[c][:, 16:] for c in range(NCORE)], axis=1)


def _split_tokens(full):
    return [np.ascontiguousarray(full[:, _tok_idx(c)]) for c in range(NCORE)]


def kernel(x, meta, hyb_norm, hyb_w_in, ssd_conv_w, ssd_conv_b, ssd_dt_bias, ssd_a_log, ssd_d, ssd_norm,
           s5_a_re, s5_a_im, s5_log_dt, s5_b_re, s5_b_im, s5_c_re, s5_c_im, s5_d, s5_w_glu, hyb_w_out,
           mla_norm, mla_w_in, mla_q_norm, mla_w_uq, mla_kv_norm, mla_w_ukv, mla_w_out, final_norm):
    f32 = lambda a: np.asarray(a, dtype=np.float32)
    x, meta = f32(x), f32(meta)
    h_full_T = np.ascontiguousarray(np.concatenate([meta, x[0]], axis=0).T)
    hT = _split_tokens(h_full_T)
    del h_full_T
    U, ident = const_mats()
    cos2, sin2s = rope_tables(T_ALL)

    g0 = vec_tile(f32(hyb_norm[0]))
    res = _run(_get("norm", prog_norm), [dict(hT=hT[c], g_l=g0) for c in range(NCORE)])
    hn = [np.asarray(r["hnT"]) for r in res]

    for layer in range(4):
        i = layer // 2
        hn_l = hn_layout(_gather_tokens(hn))
        last = (layer == 3)
        if layer % 2 == 0:
            ims = []
            for c in range(NCORE):
                im = hyb_weights(c, f32(hyb_w_in[i]), f32(ssd_conv_w[i]), f32(ssd_conv_b[i]), f32(ssd_dt_bias[i]),
                                 f32(ssd_a_log[i]), f32(ssd_d[i]), f32(ssd_norm[i]), f32(s5_a_re[i]), f32(s5_a_im[i]),
                                 f32(s5_log_dt[i]), f32(s5_b_re[i]), f32(s5_b_im[i]), f32(s5_c_re[i]), f32(s5_c_im[i]),
                                 f32(s5_d[i]))
                im.update(Umat=U, ident=ident, **hn_l)
                ims.append(im)
            res = _run(_get("hyb", prog_hyb), ims)
            del ims
            y_all = _split_tokens(np.concatenate([np.asarray(r["yT"]) for r in res], axis=0))
            g_all = _split_tokens(np.concatenate([np.asarray(r["gT"]) for r in res], axis=0))
            sg_all = _split_tokens(np.concatenate([np.asarray(r["sgT"]) for r in res], axis=0))
            wl = out_w_tile(f32(hyb_w_out[i]))
            wglu_l = glu_w_tile(f32(s5_w_glu[i]))
            gn = vec_tile(f32(mla_norm[i]))
            mw = [mla_weights2(c, f32(mla_w_in[i]), f32(mla_q_norm[i]), f32(mla_w_uq[i]), f32(mla_kv_norm[i]),
                               f32(mla_w_ukv[i])) for c in range(NCORE)]
            ims = [dict(hT=hT[c], yT=y_all[c], wl=wl, g_l=gn, g_all=g_all[c], sg_all=sg_all[c], wglu_l=wglu_l,
                        wq_in=mw[c]["wq_in"], wkv3=mw[c]["wkv3"], gq=mw[c]["gq"], gkv=mw[c]["gkv"],
                        cos2c=np.ascontiguousarray(cos2[:, _tok_idx(c)]), sin2sc=np.ascontiguousarray(sin2s[:, _tok_idx(c)]))
                   for c in range(NCORE)]
            res = _run(_get("out_hyb", prog_out, True, False), ims)
            cq_l = blk_layout(_gather_tokens([np.asarray(r["cqnT"]) for r in res]), 8)
            ck_l = blk_layout(_gather_tokens([np.asarray(r["ckvnT"]) for r in res]), 4)
            kr_full = np.ascontiguousarray(_gather_tokens([np.asarray(r["krT"]) for r in res]))
        else:
            ims = []
            for c in range(NCORE):
                im = dict(wgate=mw[c]["wgate"], wuq=mw[c]["wuq"], wukv=mw[c]["wukv"], cos2=cos2, sin2s=sin2s,
                          cq_meta=cq_l[0], cq_own=cq_l[1], ck_meta=ck_l[0], ck_own=ck_l[1], krT=kr_full, **hn_l)
                ims.append(im)
            res = _run(_get("mla", prog_mla), ims)
            del ims
            y_all = _split_tokens(np.concatenate([np.asarray(r["yT"]) for r in res], axis=0))
            wl = out_w_tile(f32(mla_w_out[i]))
            gn = vec_tile(f32(final_norm) if last else f32(hyb_norm[i + 1]))
            ims = [dict(hT=hT[c], yT=y_all[c], wl=wl, g_l=gn) for c in range(NCORE)]
            res = _run(_get("out_mla_final" if last else "out_mla", prog_out, False, last), ims)
        del ims
        if not last:
            hT = [np.asarray(r["hT_new"]) for r in res]
        hn = [np.asarray(r["hnT"]) for r in res]

    out = np.concatenate([hn[c][:, 16:].T for c in range(NCORE)], axis=0)
    return np.ascontiguousarray(out[None].astype(np.float32))
```
